# Optimizing a Trainium2 kernel written in Bass

```python
import math
import jax, jax.numpy as jnp
from jax import lax
import numpy as np

D_MODEL = 1024
BATCH = 32
SEQ = 2048
DEPTH = 4

N_MIXERS = 4
ALPHA = (2 * DEPTH) ** 0.25
BETA = (8 * DEPTH) ** -0.25
LN_EPS = 1e-5

ML_HEADS = 4
ML_DV = D_MODEL // ML_HEADS
ML_DK = ML_DV // 2
ML_CHUNK = 64
ML_IN = 2 * ML_HEADS * ML_DK + 2 * D_MODEL + 4 * ML_HEADS

GLA_HEADS = 4
GLA_DK = D_MODEL // 2 // GLA_HEADS
GLA_DV = D_MODEL // GLA_HEADS
GLA_RANK = 16
GLA_TAU = 16.0
GLA_CHUNK = 32
GLA_IN = 2 * GLA_HEADS * GLA_DK + 2 * D_MODEL + 2 * GLA_RANK

LRU_WIDTH = D_MODEL
LRU_BLOCKS = 4
LRU_BW = LRU_WIDTH // LRU_BLOCKS
CONV_WIDTH = 4
LRU_C = 8.0
LRU_IN = 2 * LRU_WIDTH

MLA_HEADS = 8
MLA_NOPE = 128
MLA_ROPE = 64
MLA_DV = 128
MLA_Q_RANK = 384
MLA_KV_RANK = 256
MLA_IN = MLA_Q_RANK + MLA_KV_RANK + MLA_ROPE
ROPE_THETA = 10000.0
Q_BLOCK = 128

N_EXPERTS = 16
EXPERT_FF = 1024
CAPACITY_FACTOR = 2

N_MLSTM = (DEPTH - 0 + N_MIXERS - 1) // N_MIXERS
N_GLA = (DEPTH - 1 + N_MIXERS - 1) // N_MIXERS
N_LRU = (DEPTH - 2 + N_MIXERS - 1) // N_MIXERS
N_MLA = (DEPTH - 3 + N_MIXERS - 1) // N_MIXERS

kernel_name = 'hybrid_mlstm_gla_rglru_mla_ecmoe_encoder'


def layer_norm(x, g, b):
    xf = x.astype(jnp.float32)
    mu = jnp.mean(xf, axis=-1, keepdims=True)
    var = jnp.mean(jnp.square(xf - mu), axis=-1, keepdims=True)
    return ((xf - mu) * lax.rsqrt(var + LN_EPS) * g + b).astype(x.dtype)


def rms_norm(x, g):
    xf = x.astype(jnp.float32)
    return (xf * lax.rsqrt(jnp.mean(jnp.square(xf), axis=-1, keepdims=True) + LN_EPS) * g).astype(x.dtype)


def head_norm(h, g):
    hf = h.astype(jnp.float32)
    mu = jnp.mean(hf, axis=-1, keepdims=True)
    var = jnp.mean(jnp.square(hf - mu), axis=-1, keepdims=True)
    hn = (hf - mu) * lax.rsqrt(var + LN_EPS)
    return hn.reshape(h.shape[0], h.shape[1], -1) * g.astype(jnp.float32)


def to_chunks(t, L):
    B, S = t.shape[0], t.shape[1]
    t = t.reshape((B, S // L, L) + t.shape[2:])
    return jnp.moveaxis(t, (1, 2), (0, 3))


def from_chunks(t):
    t = jnp.moveaxis(t, (0, 3), (1, 2))
    return t.reshape((t.shape[0], t.shape[1] * t.shape[2]) + t.shape[3:])


def flip_seq(t):
    return jnp.flip(t, axis=1)


def mlstm_chunked(q, k, v, ig, fg):
    B, S, H, DK = q.shape
    DV = v.shape[-1]
    L = ML_CHUNK
    mask = jnp.tril(jnp.ones((L, L), dtype=bool))

    def step(carry, inp):
        C, n, m = carry
        qb, kb, vb, ib, fb = inp
        b = jnp.cumsum(jax.nn.log_sigmoid(fb), axis=-1)
        g = b[..., -1]
        logD = jnp.where(mask, b[..., :, None] - b[..., None, :] + ib[..., None, :], -jnp.inf)
        inter = b + m[..., None]
        m_t = jnp.maximum(inter, jnp.max(logD, axis=-1))
        s = jnp.einsum('bhtd,bhsd->bhts', qb, kb) * jnp.exp(logD - m_t[..., None])
        w_inter = jnp.exp(inter - m_t)
        num = jnp.einsum('bhts,bhsv->bhtv', s, vb) + w_inter[..., None] * jnp.einsum('bhtd,bhdv->bhtv', qb, C)
        den = jnp.sum(s, axis=-1) + w_inter * jnp.einsum('bhtd,bhd->bht', qb, n)
        h = num / jnp.maximum(jnp.abs(den), jnp.exp(-m_t))[..., None]
        lw = g[..., None] - b + ib
        m_new = jnp.maximum(g + m, jnp.max(lw, axis=-1))
        ws = jnp.exp(lw - m_new[..., None])
        wc = jnp.exp(g + m - m_new)
        kw = kb * ws[..., None]
        C = wc[..., None, None] * C + jnp.einsum('bhsd,bhsv->bhdv', kw, vb)
        n = wc[..., None] * n + jnp.sum(kw, axis=-2)
        return (C, n, m_new), h

    init = (jnp.zeros((B, H, DK, DV), jnp.float32), jnp.zeros((B, H, DK), jnp.float32),
            jnp.zeros((B, H), jnp.float32))
    xs = (to_chunks(q, L), to_chunks(k, L), to_chunks(v, L), to_chunks(ig, L), to_chunks(fg, L))
    _, h = lax.scan(step, init, xs)
    return from_chunks(h)


def mlstm_mixer(x, w_in, gate_b, norm_g, w_out):
    B, S, _ = x.shape
    f32 = jnp.float32
    qk = ML_HEADS * ML_DK
    z = x @ w_in
    q, k, v, o, gates = jnp.split(z, [qk, 2 * qk, 2 * qk + D_MODEL, 2 * qk + 2 * D_MODEL], axis=-1)
    q = q.reshape(B, S, ML_HEADS, ML_DK).astype(f32) * (ML_DK ** -0.5)
    k = k.reshape(B, S, ML_HEADS, ML_DK).astype(f32)
    v = v.reshape(B, S, ML_HEADS, ML_DV).astype(f32)
    gates = gates.reshape(B, S, 2, 2, ML_HEADS).astype(f32) + gate_b.astype(f32)
    h_fwd = mlstm_chunked(q, k, v, gates[:, :, 0, 0], gates[:, :, 0, 1])
    h_bwd = flip_seq(mlstm_chunked(flip_seq(q), flip_seq(k), flip_seq(v),
                                   flip_seq(gates[:, :, 1, 0]), flip_seq(gates[:, :, 1, 1])))
    h = head_norm(h_fwd + h_bwd, norm_g).astype(x.dtype)
    return (jax.nn.sigmoid(o) * h) @ w_out


def gla_chunked(q, k, v, loga):
    B, S, H, DK = q.shape
    DV = v.shape[-1]
    L = GLA_CHUNK
    mask = jnp.tril(jnp.ones((L, L), dtype=bool))[:, :, None]

    def step(state, inp):
        qb, kb, vb, ab = inp
        b = jnp.cumsum(ab, axis=-2)
        dec = jnp.exp(jnp.where(mask, b[:, :, :, None, :] - b[:, :, None, :, :], -jnp.inf))
        A = jnp.einsum('bhtd,bhtsd,bhsd->bhts', qb, dec, kb)
        o = jnp.einsum('bhts,bhsv->bhtv', A, vb) + jnp.einsum('bhtd,bhdv->bhtv', qb * jnp.exp(b), state)
        g = b[:, :, -1, :]
        state = jnp.exp(g)[..., None] * state + jnp.einsum('bhsd,bhsv->bhdv', kb * jnp.exp(g[:, :, None, :] - b), vb)
        return state, o

    init = jnp.zeros((B, H, DK, DV), jnp.float32)
    xs = (to_chunks(q, L), to_chunks(k, L), to_chunks(v, L), to_chunks(loga, L))
    _, o = lax.scan(step, init, xs)
    return from_chunks(o)


def gla_mixer(x, w_in, gate_w, gate_b, norm_g, w_out):
    B, S, _ = x.shape
    f32 = jnp.float32
    qk = GLA_HEADS * GLA_DK
    z = x @ w_in
    q, k, v, r, glr = jnp.split(z, [qk, 2 * qk, 2 * qk + D_MODEL, 2 * qk + 2 * D_MODEL], axis=-1)
    q = q.reshape(B, S, GLA_HEADS, GLA_DK).astype(f32) * (GLA_DK ** -0.5)
    k = k.reshape(B, S, GLA_HEADS, GLA_DK).astype(f32)
    v = v.reshape(B, S, GLA_HEADS, GLA_DV).astype(f32)
    glr = glr.reshape(B, S, 2, GLA_RANK).astype(f32)
    loga = jax.nn.log_sigmoid(jnp.einsum('bsgr,grk->bsgk', glr, gate_w.astype(f32)) + gate_b.astype(f32)) / GLA_TAU
    loga = loga.reshape(B, S, 2, GLA_HEADS, GLA_DK)
    o_fwd = gla_chunked(q, k, v, loga[:, :, 0])
    o_bwd = flip_seq(gla_chunked(flip_seq(q), flip_seq(k), flip_seq(v), flip_seq(loga[:, :, 1])))
    o = head_norm(o_fwd + o_bwd, norm_g).astype(x.dtype)
    return (jax.nn.silu(r) * o) @ w_out


def rglru_direction(u, wa, ba, wx, bx, lam):
    B, S, W = u.shape
    ub = u.reshape(B, S, LRU_BLOCKS, LRU_BW)
    r = jax.nn.sigmoid(jnp.einsum('bsni,nij->bsnj', ub, wa).reshape(B, S, W) + ba)
    i = jax.nn.sigmoid(jnp.einsum('bsni,nij->bsnj', ub, wx).reshape(B, S, W) + bx)
    log_a = LRU_C * r * jax.nn.log_sigmoid(lam)
    a = jnp.exp(log_a)
    gated = jnp.sqrt(-jnp.expm1(2.0 * log_a)) * (i * u)

    def step(h, inp):
        a_t, g_t = inp
        h = a_t * h + g_t
        return h, h

    _, hs = lax.scan(step, jnp.zeros((B, W), jnp.float32), (jnp.swapaxes(a, 0, 1), jnp.swapaxes(gated, 0, 1)))
    return jnp.swapaxes(hs, 0, 1)


def rglru_mixer(x, w_in, conv_w, conv_b, gate_a_w, gate_a_b, gate_x_w, gate_x_b, lam, w_out):
    f32 = jnp.float32
    z = x @ w_in
    gate_branch, u = jnp.split(z, [LRU_WIDTH], axis=-1)
    u = lax.conv_general_dilated(u, conv_w[:, None, :].astype(u.dtype), window_strides=(1,),
                                 padding=[(CONV_WIDTH // 2, CONV_WIDTH - 1 - CONV_WIDTH // 2)],
                                 dimension_numbers=('NWC', 'WIO', 'NWC'),
                                 feature_group_count=LRU_WIDTH) + conv_b
    u = u.astype(f32)
    h_fwd = rglru_direction(u, gate_a_w[0].astype(f32), gate_a_b[0].astype(f32), gate_x_w[0].astype(f32),
                            gate_x_b[0].astype(f32), lam[0].astype(f32))
    h_bwd = flip_seq(rglru_direction(flip_seq(u), gate_a_w[1].astype(f32), gate_a_b[1].astype(f32),
                                     gate_x_w[1].astype(f32), gate_x_b[1].astype(f32), lam[1].astype(f32)))
    h = (h_fwd + h_bwd).astype(x.dtype)
    return (jax.nn.gelu(gate_branch) * h) @ w_out


def apply_rope(t, cos, sin):
    half = t.shape[-1] // 2
    t1, t2 = t[..., :half], t[..., half:]
    return jnp.concatenate([t1 * cos - t2 * sin, t2 * cos + t1 * sin], axis=-1)


def mla_mixer(x, positions, w_in, q_norm_g, kv_norm_g, w_uq, w_ukv, w_out):
    B, S, _ = x.shape
    f32 = jnp.float32
    z = x @ w_in
    cq, ckv, k_rope = jnp.split(z, [MLA_Q_RANK, MLA_Q_RANK + MLA_KV_RANK], axis=-1)
    q = (rms_norm(cq, q_norm_g) @ w_uq).reshape(B, S, MLA_HEADS, MLA_NOPE + MLA_ROPE)
    kv = (rms_norm(ckv, kv_norm_g) @ w_ukv).reshape(B, S, MLA_HEADS, MLA_NOPE + MLA_DV)
    q_nope, q_rope = jnp.split(q, [MLA_NOPE], axis=-1)
    k_nope, v = jnp.split(kv, [MLA_NOPE], axis=-1)
    half = MLA_ROPE // 2
    freq = ROPE_THETA ** (-jnp.arange(half, dtype=f32) / half)
    ang = positions.astype(f32)[..., None] * freq
    cos = jnp.cos(ang).astype(x.dtype)
    sin = jnp.sin(ang).astype(x.dtype)
    q_rope = apply_rope(q_rope, cos[:, :, None, :], sin[:, :, None, :])
    k_rope = apply_rope(k_rope, cos, sin)
    scale = (MLA_NOPE + MLA_ROPE) ** -0.5
    nq = S // Q_BLOCK
    qn_blocks = jnp.moveaxis(q_nope.reshape(B, nq, Q_BLOCK, MLA_HEADS, MLA_NOPE), 1, 0)
    qr_blocks = jnp.moveaxis(q_rope.reshape(B, nq, Q_BLOCK, MLA_HEADS, MLA_ROPE), 1, 0)

    def attend(blk):
        qn, qr = blk
        s = (jnp.einsum('bqhd,bkhd->bhqk', qn, k_nope) + jnp.einsum('bqhd,bkd->bhqk', qr, k_rope)).astype(f32) * scale
        p = jax.nn.softmax(s, axis=-1).astype(v.dtype)
        return jnp.einsum('bhqk,bkhv->bqhv', p, v)

    out = lax.map(attend, (qn_blocks, qr_blocks))
    out = jnp.moveaxis(out, 0, 1).reshape(B, S, MLA_HEADS * MLA_DV)
    return out @ w_out


def ec_moe(x, w_router, w_gate, w_up, w_down):
    B, N, _ = x.shape
    cap = CAPACITY_FACTOR * N // N_EXPERTS
    aff = jax.nn.softmax(jnp.einsum('bnd,de->bne', x, w_router).astype(jnp.float32), axis=-1)
    gates, idx = lax.top_k(jnp.swapaxes(aff, 1, 2), cap)
    bidx = jnp.arange(B)[:, None, None]
    xin = x[bidx, idx]
    h = jax.nn.silu(jnp.einsum('becd,edf->becf', xin, w_gate)) * jnp.einsum('becd,edf->becf', xin, w_up)
    out = jnp.einsum('becf,efd->becd', h, w_down) * gates[..., None].astype(x.dtype)
    return jnp.zeros_like(x).at[bidx, idx].add(out.astype(x.dtype))


def setup_inputs(seed: int = 0) -> dict:
    key = jax.random.key(seed)
    keys = iter(jax.random.split(key, 64))
    f32 = jnp.float32

    def nrm(shape, scale):
        return jax.random.normal(next(keys), shape, f32) * scale

    def gain(shape):
        return 1.0 + nrm(shape, 0.02)

    D = D_MODEL
    x = jax.random.normal(next(keys), (BATCH, SEQ, D), f32)
    positions = (jnp.arange(SEQ, dtype=jnp.int32)[None, :]
                 + jax.random.randint(next(keys), (BATCH, 1), 0, SEQ, dtype=jnp.int32))
    mlstm_w_in = nrm((N_MLSTM, D, ML_IN), D ** -0.5)
    ig_b = nrm((N_MLSTM, 2, 1, ML_HEADS), 0.1)
    fg_b = 3.0 + 3.0 * jax.random.uniform(next(keys), (N_MLSTM, 2, 1, ML_HEADS), f32)
    mlstm_gate_b = jnp.concatenate([ig_b, fg_b], axis=2)
    mlstm_norm_g = gain((N_MLSTM, D))
    mlstm_w_out = nrm((N_MLSTM, D, D), BETA * D ** -0.5)
    gla_w_in = nrm((N_GLA, D, GLA_IN), D ** -0.5)
    gla_gate_w = nrm((N_GLA, 2, GLA_RANK, GLA_HEADS * GLA_DK), GLA_RANK ** -0.5)
    gla_gate_b = nrm((N_GLA, 2, GLA_HEADS * GLA_DK), 0.1)
    gla_norm_g = gain((N_GLA, D))
    gla_w_out = nrm((N_GLA, D, D), BETA * D ** -0.5)
    lru_w_in = nrm((N_LRU, D, LRU_IN), D ** -0.5)
    lru_conv_w = nrm((N_LRU, CONV_WIDTH, LRU_WIDTH), CONV_WIDTH ** -0.5)
    lru_conv_b = nrm((N_LRU, LRU_WIDTH), 0.02)
    lru_gate_a_w = nrm((N_LRU, 2, LRU_BLOCKS, LRU_BW, LRU_BW), LRU_BW ** -0.5)
    lru_gate_a_b = nrm((N_LRU, 2, LRU_WIDTH), 0.1)
    lru_gate_x_w = nrm((N_LRU, 2, LRU_BLOCKS, LRU_BW, LRU_BW), LRU_BW ** -0.5)
    lru_gate_x_b = nrm((N_LRU, 2, LRU_WIDTH), 0.1)
    a0 = jax.random.uniform(next(keys), (N_LRU, 2, LRU_WIDTH), f32, minval=0.9, maxval=0.999)
    s0 = a0 ** (1.0 / LRU_C)
    lru_lambda = jnp.log(s0) - jnp.log1p(-s0)
    lru_w_out = nrm((N_LRU, LRU_WIDTH, D), BETA * LRU_WIDTH ** -0.5)
    mla_w_in = nrm((N_MLA, D, MLA_IN), D ** -0.5)
    mla_q_norm_g = gain((N_MLA, MLA_Q_RANK))
    mla_kv_norm_g = gain((N_MLA, MLA_KV_RANK))
    mla_w_uq = nrm((N_MLA, MLA_Q_RANK, MLA_HEADS * (MLA_NOPE + MLA_ROPE)), MLA_Q_RANK ** -0.5)
    mla_w_ukv = nrm((N_MLA, MLA_KV_RANK, MLA_HEADS * (MLA_NOPE + MLA_DV)), MLA_KV_RANK ** -0.5)
    mla_w_out = nrm((N_MLA, MLA_HEADS * MLA_DV, D), BETA * (MLA_HEADS * MLA_DV) ** -0.5)
    moe_router = nrm((DEPTH, D, N_EXPERTS), D ** -0.5)
    moe_w_gate = nrm((DEPTH, N_EXPERTS, D, EXPERT_FF), D ** -0.5)
    moe_w_up = nrm((DEPTH, N_EXPERTS, D, EXPERT_FF), D ** -0.5)
    moe_w_down = nrm((DEPTH, N_EXPERTS, EXPERT_FF, D), BETA * EXPERT_FF ** -0.5)
    ln_g = gain((DEPTH, 2, D))
    ln_b = nrm((DEPTH, 2, D), 0.02)
    return {
        'x': x, 'positions': positions,
        'mlstm_w_in': mlstm_w_in, 'mlstm_gate_b': mlstm_gate_b, 'mlstm_norm_g': mlstm_norm_g, 'mlstm_w_out': mlstm_w_out,
        'gla_w_in': gla_w_in, 'gla_gate_w': gla_gate_w, 'gla_gate_b': gla_gate_b, 'gla_norm_g': gla_norm_g, 'gla_w_out': gla_w_out,
        'lru_w_in': lru_w_in, 'lru_conv_w': lru_conv_w, 'lru_conv_b': lru_conv_b,
        'lru_gate_a_w': lru_gate_a_w, 'lru_gate_a_b': lru_gate_a_b, 'lru_gate_x_w': lru_gate_x_w, 'lru_gate_x_b': lru_gate_x_b,
        'lru_lambda': lru_lambda, 'lru_w_out': lru_w_out,
        'mla_w_in': mla_w_in, 'mla_q_norm_g': mla_q_norm_g, 'mla_kv_norm_g': mla_kv_norm_g,
        'mla_w_uq': mla_w_uq, 'mla_w_ukv': mla_w_ukv, 'mla_w_out': mla_w_out,
        'moe_router': moe_router, 'moe_w_gate': moe_w_gate, 'moe_w_up': moe_w_up, 'moe_w_down': moe_w_down,
        'ln_g': ln_g, 'ln_b': ln_b,
    }


def reference(x, positions,
              mlstm_w_in, mlstm_gate_b, mlstm_norm_g, mlstm_w_out,
              gla_w_in, gla_gate_w, gla_gate_b, gla_norm_g, gla_w_out,
              lru_w_in, lru_conv_w, lru_conv_b, lru_gate_a_w, lru_gate_a_b, lru_gate_x_w, lru_gate_x_b,
              lru_lambda, lru_w_out,
              mla_w_in, mla_q_norm_g, mla_kv_norm_g, mla_w_uq, mla_w_ukv, mla_w_out,
              moe_router, moe_w_gate, moe_w_up, moe_w_down,
              ln_g, ln_b):
    for i in range(DEPTH):
        m = i % N_MIXERS
        j = i // N_MIXERS
        if m == 0:
            h = mlstm_mixer(x, mlstm_w_in[j], mlstm_gate_b[j], mlstm_norm_g[j], mlstm_w_out[j])
        elif m == 1:
            h = gla_mixer(x, gla_w_in[j], gla_gate_w[j], gla_gate_b[j], gla_norm_g[j], gla_w_out[j])
        elif m == 2:
            h = rglru_mixer(x, lru_w_in[j], lru_conv_w[j], lru_conv_b[j], lru_gate_a_w[j], lru_gate_a_b[j],
                            lru_gate_x_w[j], lru_gate_x_b[j], lru_lambda[j], lru_w_out[j])
        else:
            h = mla_mixer(x, positions, mla_w_in[j], mla_q_norm_g[j], mla_kv_norm_g[j],
                          mla_w_uq[j], mla_w_ukv[j], mla_w_out[j])
        x = layer_norm(ALPHA * x + h, ln_g[i, 0], ln_b[i, 0])
        y = ec_moe(x, moe_router[i], moe_w_gate[i], moe_w_up[i], moe_w_down[i])
        x = layer_norm(ALPHA * x + y, ln_g[i, 1], ln_b[i, 1])
    return x
```

```python
import contextlib
import numpy as np
import concourse.bass as bass
import concourse.mybir as mybir
from concourse.bass_utils import run_bass_kernel_spmd

F32 = mybir.dt.float32
BF16 = mybir.dt.bfloat16
I32 = mybir.dt.int32
AF = mybir.ActivationFunctionType
ALU = mybir.AluOpType
AX = mybir.AxisListType

D = 1024
S = 2048
NT = 16
NK = 8
DEPTH = 4
ALPHA = (2 * DEPTH) ** 0.25
LN_EPS = 1e-5
NE = 16
CAP = 256
NCORES = 8
SEQ_PER_CORE = 4

SAME_ENGINE_RAW = True


_UN = [0]


def un(name):
    _UN[0] += 1
    return "%s_%d" % (name, _UN[0])


class Reg:
    __slots__ = ("w", "r", "name")

    def __init__(self, name=""):
        self.w = None
        self.r = {}
        self.name = name


class Prog:
    ENG = ("pe", "act", "dve", "pool", "sp")
    KDMA = 8

    def __init__(self, nc, stack):
        self.nc = nc
        self.eng = {"pe": nc.tensor, "act": nc.scalar, "dve": nc.vector, "pool": nc.gpsimd, "sp": nc.sync}
        self.sem = {}
        for e in self.ENG:
            self.sem[("c", e)] = stack.enter_context(nc.semaphore("s_" + e))
        self.dq = ("sp", "pool", "act")
        for q in self.dq:
            for k in range(self.KDMA):
                self.sem[("d", q, k)] = stack.enter_context(nc.semaphore("d_%s%d" % (q, k)))
        self.cnt = {e: 0 for e in self.ENG}
        self.dcnt = {q: 0 for q in self.dq}
        self.known = {e: {} for e in self.ENG}
        self.nwaits = 0
        self.ninstr = 0

    def _wait(self, eng, key, val):
        if self.known[eng].get(key, 0) >= val:
            return
        self.eng[eng].wait_ge(self.sem[key], val)
        self.known[eng][key] = val
        self.nwaits += 1

    def _deps(self, eng, reads, writes):
        me = ("c", eng)
        for r in reads:
            if r.w is not None:
                key, val = r.w
                if key == me:
                    if eng != "pe" and SAME_ENGINE_RAW:
                        self._wait(eng, key, val)
                else:
                    self._wait(eng, key, val)
        for w in writes:
            if w.w is not None:
                key, val = w.w
                if key != me:
                    self._wait(eng, key, val)
            for key, val in w.r.items():
                if key != me:
                    self._wait(eng, key, val)

    def _update(self, tok, reads, writes):
        key, val = tok
        for r in reads:
            if r.r.get(key, 0) < val:
                r.r[key] = val
        for w in writes:
            w.w = tok
            w.r = {}

    def op(self, eng, fn, reads=(), writes=()):
        self._deps(eng, reads, writes)
        ins = fn(self.eng[eng])
        self.cnt[eng] += 1
        self.ninstr += 1
        tok = (("c", eng), self.cnt[eng])
        ins.then_inc(self.sem[tok[0]], 1)
        self._update(tok, reads, writes)
        return tok

    def dma(self, q, out, in_, reads=(), writes=(), **kw):
        k = self.dcnt[q]
        slot = k % self.KDMA
        key = ("d", q, slot)
        if k >= self.KDMA:
            self._wait(q, key, 16 * (k // self.KDMA))
        self._deps(q, reads, writes)
        ins = self.eng[q].dma_start(out=out, in_=in_, **kw)
        tok = (key, 16 * (k // self.KDMA + 1))
        ins.then_inc(self.sem[key], 16)
        self.dcnt[q] = k + 1
        self.ninstr += 1
        self._update(tok, reads, writes)
        return tok

    def latest_tokens(self):
        toks = []
        for e in self.ENG:
            if self.cnt[e] > 0:
                toks.append((("c", e), self.cnt[e]))
        for q in self.dq:
            k = self.dcnt[q]
            for slot in range(self.KDMA):
                n = (k - slot + self.KDMA - 1) // self.KDMA
                if n > 0:
                    toks.append((("d", q, slot), 16 * n))
        return toks

    def barrier(self, engines=None):
        toks = self.latest_tokens()
        for e in (engines or self.ENG):
            for key, val in toks:
                if key != ("c", e):
                    self._wait(e, key, val)


C_IDENT = 0
C_IOTA = 128
C_PIDX = 384
C_TRIU = 386
C_TRIL = 514
C_ONES = 642
C_FREQ = 770
C_LNQ = 802
C_END = 804


def make_consts():
    c = np.zeros((128, C_END), np.float32)
    c[:, C_IDENT:C_IDENT + 128] = np.eye(128, dtype=np.float32)
    c[:, C_IOTA:C_IOTA + 256] = np.arange(256, dtype=np.float32)[None, :]
    c[:, C_PIDX] = np.arange(128)
    c[:, C_PIDX + 1] = np.arange(128) + 128
    s = np.arange(128)[:, None]
    t = np.arange(128)[None, :]
    c[:, C_TRIU:C_TRIU + 128] = (s <= t)
    c[:, C_TRIL:C_TRIL + 128] = (s >= t)
    c[:, C_ONES:C_ONES + 128] = 1.0
    c[:, C_LNQ] = np.float32(np.log(128.0 ** -0.5))
    c[:, C_FREQ:C_FREQ + 32] = (10000.0 ** (-np.arange(32, dtype=np.float32) / np.float32(32))).astype(np.float32)[None, :]
    return c


def make_consts16():
    c = np.zeros((16, 16 * 128), np.float32)
    for e in range(16):
        c[e, e * 128:(e + 1) * 128] = 1.0
    return c


class Ctx:
    pass


def setup_ctx(nc, stack, P):
    cx = Ctx()
    cx.nc = nc
    cx.P = P
    sb = lambda name, shape, dt: stack.enter_context(nc.sbuf_tensor(name, shape, dt))
    cx.R = sb("R", [128, NT, D], F32)
    cx.rR = [Reg("R%d" % i) for i in range(NT)]
    cx.XA = sb("XA", [128, NT * D], BF16)
    cx.rXA = [Reg("XA%d" % i) for i in range(NT)]
    cx.cst = sb("cst", [128, C_END], F32)
    cx.rcst = Reg("cst")
    cx.identb = sb("identb", [128, 128], BF16)
    cx.logits = sb("logits", [128, NT, NE], F32)
    cx.rlogits = [Reg("lg%d" % i) for i in range(NT)]
    cx.ps = []
    cx.rps = []
    for b in range(8):
        cx.ps.append(stack.enter_context(nc.psum_tensor("ps%d" % b, [128, 512], F32)))
        cx.rps.append(Reg("ps%d" % b))
    return cx


def load_consts(cx, consts_ap, consts16_ap):
    P = cx.P
    P.dma("sp", cx.cst[:], consts_ap[:, :], writes=[cx.rcst])
    P.op("dve", lambda e: e.tensor_copy(out=cx.identb[:], in_=cx.cst[:, C_IDENT:C_IDENT + 128]),
         reads=[cx.rcst], writes=[cx.rcst])


def emit_ln(cx, lng_ap, lnb_ap, mode, wr=None, rwr=None, do_ln=True, out_ap=None, scratch=None):
    P = cx.P
    nc = cx.nc
    with contextlib.ExitStack() as lst:
        _emit_ln(cx, lst, lng_ap, lnb_ap, mode, wr, rwr, do_ln, out_ap)
    P.barrier()


def _emit_ln(cx, lst, lng_ap, lnb_ap, mode, wr, rwr, do_ln, out_ap):
    P = cx.P
    nc = cx.nc
    cx.lng = lst.enter_context(nc.sbuf_tensor(un("lng"), [128, 2, D], F32))
    cx.rlng = Reg("lng")
    tmpT, rtmpT, stat, rstat, xbt, rxbt = alloc_ln_scratch(cx, lst)
    if do_ln:
        P.dma("sp", cx.lng[:, 0, :], lng_ap.partition_broadcast(128), writes=[cx.rlng])
        P.dma("sp", cx.lng[:, 1, :], lnb_ap.partition_broadcast(128), writes=[cx.rlng])
    for i in range(NT):
        Ri = cx.R[:, i, :]
        rRi = cx.rR[i]
        st = stat[i % 2]
        rst = rstat[i % 2]
        if do_ln:
            for h in range(2):
                P.op("dve", lambda e, h=h: e.bn_stats(out=st[:, h * 6:(h + 1) * 6], in_=Ri[:, h * 512:(h + 1) * 512]),
                     reads=[rRi], writes=[rst])
            P.op("dve", lambda e: e.bn_aggr(out=st[:, 12:14], in_=st[:, 0:12]), reads=[rst], writes=[rst])
            P.op("dve", lambda e: e.tensor_scalar(out=st[:, 14:15], in0=st[:, 13:14], scalar1=LN_EPS, scalar2=None, op0=ALU.add),
                 reads=[rst], writes=[rst])
            P.op("act", lambda e: e.activation(out=st[:, 14:15], in_=st[:, 14:15], func=AF.Sqrt), reads=[rst], writes=[rst])
            P.op("dve", lambda e: e.reciprocal(out=st[:, 15:16], in_=st[:, 14:15]), reads=[rst], writes=[rst])
            P.op("dve", lambda e: e.tensor_scalar(out=Ri, in0=Ri, scalar1=st[:, 12:13], scalar2=st[:, 15:16],
                                                   op0=ALU.subtract, op1=ALU.mult), reads=[rRi, rst], writes=[rRi])
            P.op("pool", lambda e: e.tensor_tensor(out=Ri, in0=Ri, in1=cx.lng[:, 0, :], op=ALU.mult),
                 reads=[rRi, cx.rlng], writes=[rRi])
            P.op("pool", lambda e: e.tensor_tensor(out=Ri, in0=Ri, in1=cx.lng[:, 1, :], op=ALU.add),
                 reads=[rRi, cx.rlng], writes=[rRi])
        if mode == "none":
            if out_ap is not None:
                P.dma("sp", out_ap[:, i, :], Ri, reads=[rRi])
            continue
        if mode == "xb":
            xb = cx.XA[:, i * D:(i + 1) * D]
            rxb = cx.rXA[i]
        else:
            xb = xbt[i % 2][:]
            rxb = rxbt[i % 2]
        P.op("act", lambda e: e.activation(out=xb, in_=Ri, func=AF.Copy), reads=[rRi], writes=[rxb])
        P.op("pool", lambda e: e.tensor_scalar(out=Ri, in0=Ri, scalar1=float(ALPHA), scalar2=0.0, op0=ALU.mult, op1=ALU.add),
             reads=[rRi], writes=[rRi])
        pb = 6 + (i % 2)
        pst = cx.ps[pb][:].bitcast(BF16)
        for k in range(NK):
            P.op("pe", lambda e, k=k: e.transpose(out=pst[:, k * 128:(k + 1) * 128], in_=xb[:, k * 128:(k + 1) * 128],
                                                  identity=cx.identb[:]),
                 reads=[rxb, cx.rcst], writes=[cx.rps[pb]])
        if mode == "xt":
            xt = cx.XA[:].rearrange("p (k t) -> p k t", k=NK)
            P.op("dve", lambda e: e.tensor_copy(out=xt[:, :, i * 128:(i + 1) * 128],
                                                in_=pst.rearrange("p (k t) -> p k t", k=NK)),
                 reads=[cx.rps[pb]], writes=[cx.rXA[i]])
        else:
            tt = tmpT[i % 2]
            rtt = rtmpT[i % 2]
            P.op("dve", lambda e: e.tensor_copy(out=tt[:], in_=pst), reads=[cx.rps[pb]], writes=[rtt])
            lb = 4 + (i % 2)
            for k in range(NK):
                P.op("pe", lambda e, k=k: e.matmul(cx.ps[lb][:, 0:NE], lhsT=tt[:, k * 128:(k + 1) * 128], rhs=wr[:, k, :],
                                                   start=(k == 0), stop=(k == NK - 1)),
                     reads=[rtt, rwr], writes=[cx.rps[lb]])
            P.op("act", lambda e: e.activation(out=cx.logits[:, i, :], in_=cx.ps[lb][:, 0:NE], func=AF.Copy),
                 reads=[cx.rps[lb]], writes=[cx.rlogits[i]])


def alloc_ln_scratch(cx, stack):
    nc = cx.nc
    tmpT = [stack.enter_context(nc.sbuf_tensor(un("tmpT"), [128, D], BF16)) for j in range(2)]
    rtmpT = [Reg(), Reg()]
    stat = [stack.enter_context(nc.sbuf_tensor(un("stat"), [128, 16], F32)) for j in range(2)]
    rstat = [Reg(), Reg()]
    xbt = [stack.enter_context(nc.sbuf_tensor(un("xbt"), [128, D], BF16)) for j in range(2)]
    rxbt = [Reg(), Reg()]
    return (tmpT, rtmpT, stat, rstat, xbt, rxbt)


def emit_moe(cx, wg_ap, wu_ap, wd_ap):
    P = cx.P
    nc = cx.nc
    with contextlib.ExitStack() as st:
        sb = lambda name, shape, dt: st.enter_context(nc.sbuf_tensor(un(name), shape, dt))
        aff = sb("aff", [128, NT, NE], F32)
        r_aff = Reg()
        sm = sb("sm", [128, NT, 4], F32)
        AG = sb("AG", [128, NT, NE, 2], BF16)
        r_AG = Reg()
        slot = sb("slot", [128, NT, NE], F32)
        r_slot = Reg()
        slotT = sb("slotT", [16, S], F32)
        r_slotT = Reg()
        m8 = sb("m8", [16, 8], F32)
        r_m8 = Reg()
        wbuf = [sb("wbuf%d" % j, [128, NK, D], BF16) for j in range(3)]
        r_wbuf = [Reg() for j in range(3)]
        Pm = sb("Pm", [128, NT, CAP], BF16)
        r_Pm = Reg()
        PT = sb("PT", [128, 2, S], BF16)
        r_PT = Reg()
        xinT = sb("xinT", [128, NK, CAP], BF16)
        r_xinT = Reg()
        hT = sb("hT", [128, NK, CAP], BF16)
        r_hT = Reg()
        sg = [sb("sg%d" % j, [128, CAP], F32) for j in range(2)]
        r_sg = [Reg(), Reg()]
        Oe = sb("Oe", [128, 2, D], BF16)
        r_Oe = Reg()
        gs = sb("gs", [128, 4], F32)
        r_gs = Reg()
        affT = Pm[:].rearrange("p a b -> p (a b)").bitcast(F32)[0:16, :]
        r_affT = Reg()
        work = PT[:].rearrange("p a b -> p (a b)").bitcast(F32)[0:16, :]
        r_work = Reg()

        c16 = sb("c16", [16, 16 * 128], F32)
        P.dma("sp", c16[:], cx.A["consts16"][:, :], writes=[cx.rcst])
        ident = cx.cst[:, C_IDENT:C_IDENT + 128]

        rl = cx.rlogits
        P.op("dve", lambda e: e.tensor_reduce(out=sm[:, :, 0], in_=cx.logits[:], axis=AX.X, op=ALU.max),
             reads=rl, writes=[r_aff])
        P.op("dve", lambda e: e.tensor_tensor(out=aff[:], in0=cx.logits[:],
                                              in1=sm[:, :, 0:1].to_broadcast([128, NT, NE]), op=ALU.subtract),
             reads=rl + [r_aff], writes=[r_aff])
        P.op("act", lambda e: e.activation(out=aff[:], in_=aff[:], func=AF.Exp), reads=[r_aff], writes=[r_aff])
        P.op("dve", lambda e: e.tensor_reduce(out=sm[:, :, 1], in_=aff[:], axis=AX.X, op=ALU.add),
             reads=[r_aff], writes=[r_aff])
        P.op("dve", lambda e: e.reciprocal(out=sm[:, :, 2], in_=sm[:, :, 1]), reads=[r_aff], writes=[r_aff])
        P.op("dve", lambda e: e.tensor_tensor(out=aff[:], in0=aff[:],
                                              in1=sm[:, :, 2:3].to_broadcast([128, NT, NE]), op=ALU.mult),
             reads=[r_aff], writes=[r_aff])
        P.op("dve", lambda e: e.tensor_copy(out=AG[:, :, :, 0], in_=aff[:]), reads=[r_aff], writes=[r_AG])
        P.op("dve", lambda e: e.tensor_tensor(out=AG[:, :, :, 1], in0=aff[:], in1=AG[:, :, :, 0], op=ALU.subtract),
             reads=[r_aff, r_AG], writes=[r_AG])
        for g in range(4):
            for j in range(4):
                i = g * 4 + j
                P.op("pe", lambda e, i=i, j=j, g=g: e.transpose(out=cx.ps[g][0:16, j * 128:(j + 1) * 128],
                                                                in_=aff[:, i, :], identity=ident),
                     reads=[r_aff, cx.rcst], writes=[cx.rps[g]])
            P.op("act", lambda e, g=g: e.activation(out=affT[:, g * 512:(g + 1) * 512], in_=cx.ps[g][0:16, :], func=AF.Copy),
                 reads=[cx.rps[g]], writes=[r_affT])
        P.op("pool", lambda e: e.tensor_copy(out=work[:], in_=affT[:]), reads=[r_affT], writes=[r_work])
        nround = CAP // 8
        for r in range(nround):
            P.op("dve", lambda e: e.max(out=m8[:], in_=work[:]), reads=[r_work], writes=[r_m8])
            if r < nround - 1:
                P.op("dve", lambda e: e.match_replace(out=work[:], in_to_replace=m8[:], in_values=work[:], imm_value=-1.0),
                     reads=[r_work, r_m8], writes=[r_work])
        P.op("dve", lambda e: e.tensor_scalar(out=work[:], in0=affT[:], scalar1=m8[:, 7:8], scalar2=None, op0=ALU.is_ge),
             reads=[r_affT, r_m8], writes=[r_work])
        P.op("dve", lambda e: e.tensor_tensor_scan(out=slotT[:], data0=cx.cst[0:16, C_ONES:C_ONES + 1].to_broadcast([16, S]), data1=work[:], initial=0.0,
                                                    op0=ALU.mult, op1=ALU.add),
             reads=[r_work, cx.rcst], writes=[r_slotT])
        P.op("dve", lambda e: e.tensor_tensor(out=slotT[:], in0=slotT[:], in1=work[:], op=ALU.mult),
             reads=[r_work, r_slotT], writes=[r_slotT])
        P.op("dve", lambda e: e.tensor_scalar(out=slotT[:], in0=slotT[:], scalar1=-1.0, scalar2=None, op0=ALU.add),
             reads=[r_slotT], writes=[r_slotT])
        for i in range(NT):
            P.op("pe", lambda e, i=i: e.transpose(out=cx.ps[5][:, i * NE:(i + 1) * NE], in_=slotT[:, i * 128:(i + 1) * 128],
                                                  identity=ident[0:16, 0:16]),
                 reads=[r_slotT, cx.rcst], writes=[cx.rps[5]])
        P.op("act", lambda e: e.activation(out=slot[:].rearrange("p a b -> p (a b)"), in_=cx.ps[5][:, 0:NT * NE], func=AF.Copy),
             reads=[cx.rps[5]], writes=[r_slot])

        P.barrier()
        r_xk = [Reg() for k in range(NK)]
        r_hj = [Reg() for j in range(NK)]
        r_Oq = [[Reg() for n in range(2)] for c in range(2)]
        slotTb = sb("slotTb", [16, S], BF16)
        c16b = sb("c16b", [16, 16 * 128], BF16)
        P.op("act", lambda e: e.activation(out=slotTb[:], in_=slotT[:], func=AF.Copy), reads=[r_slotT], writes=[r_slotT])
        P.op("act", lambda e: e.activation(out=c16b[:], in_=c16[:], func=AF.Copy), reads=[cx.rcst], writes=[cx.rcst])
        wq = [0]

        def load_w(ap2d):
            j = wq[0] % 3
            wq[0] += 1
            P.dma("pool", wbuf[j][:], ap2d.rearrange("(k p) f -> p k f", p=128), writes=[r_wbuf[j]])
            return j

        def build_P(ex):
            for i in range(NT):
                P.op("dve", lambda e, i=i: e.tensor_scalar(out=Pm[:, i, :], in0=cx.cst[:, C_IOTA:C_IOTA + CAP],
                                                           scalar1=slot[:, i, ex:ex + 1], scalar2=None, op0=ALU.is_equal),
                     reads=[r_slot, cx.rcst], writes=[r_Pm])

        build_P(0)
        for ex in range(NE):
            jg = load_w(wg_ap[ex])
            ju = load_w(wu_ap[ex])
            for k in range(NK):
                pb = 4 + (k % 2)
                for i in range(NT):
                    P.op("pe", lambda e, i=i, k=k, pb=pb: e.matmul(cx.ps[pb][:, 0:CAP],
                                                                   lhsT=cx.XA[:, i * D + k * 128:i * D + (k + 1) * 128],
                                                                   rhs=Pm[:, i, :], start=(i == 0), stop=(i == NT - 1)),
                         reads=[cx.rXA[i], r_Pm], writes=[cx.rps[pb]])
                P.op("act", lambda e, k=k, pb=pb: e.activation(out=xinT[:, k, :], in_=cx.ps[pb][:, 0:CAP], func=AF.Copy),
                     reads=[cx.rps[pb]], writes=[r_xk[k]])
            for c in range(2):
                for i in range(NT):
                    P.op("pe", lambda e, i=i, c=c: e.matmul(cx.ps[6][:, c * 2:c * 2 + 2], lhsT=Pm[:, i, c * 128:(c + 1) * 128],
                                                            rhs=AG[:, i, ex, :], start=(i == 0), stop=(i == NT - 1)),
                         reads=[r_Pm, r_AG], writes=[cx.rps[6]])
            P.op("dve", lambda e: e.tensor_copy(out=gs[:, 0:4], in_=cx.ps[6][:, 0:4]), reads=[cx.rps[6]], writes=[r_gs])
            P.op("dve", lambda e: e.tensor_tensor(out=gs[:, 0:1], in0=gs[:, 0:1], in1=gs[:, 1:2], op=ALU.add),
                 reads=[r_gs], writes=[r_gs])
            P.op("dve", lambda e: e.tensor_tensor(out=gs[:, 1:2], in0=gs[:, 2:3], in1=gs[:, 3:4], op=ALU.add),
                 reads=[r_gs], writes=[r_gs])
            if ex + 1 < NE:
                build_P(ex + 1)
            for j in range(NK):
                pg = 4 + (j % 2)
                pu = 6 + (j % 2)
                for k in range(NK):
                    P.op("pe", lambda e, j=j, k=k, pg=pg: e.matmul(cx.ps[pg][:, 0:CAP], lhsT=wbuf[jg][:, k, j * 128:(j + 1) * 128],
                                                                   rhs=xinT[:, k, :], start=(k == 0), stop=(k == NK - 1)),
                         reads=[r_wbuf[jg], r_xk[k]], writes=[cx.rps[pg]])
                for k in range(NK):
                    P.op("pe", lambda e, j=j, k=k, pu=pu: e.matmul(cx.ps[pu][:, 0:CAP], lhsT=wbuf[ju][:, k, j * 128:(j + 1) * 128],
                                                                   rhs=xinT[:, k, :], start=(k == 0), stop=(k == NK - 1)),
                         reads=[r_wbuf[ju], r_xk[k]], writes=[cx.rps[pu]])
                P.op("act", lambda e, j=j, pg=pg: e.activation(out=sg[j % 2][:], in_=cx.ps[pg][:, 0:CAP], func=AF.Silu),
                     reads=[cx.rps[pg]], writes=[r_sg[j % 2]])
                P.op("dve", lambda e, j=j, pu=pu: e.tensor_tensor(out=hT[:, j, :], in0=sg[j % 2][:], in1=cx.ps[pu][:, 0:CAP], op=ALU.mult),
                     reads=[cx.rps[pu], r_sg[j % 2]], writes=[r_hj[j]])
            jd = load_w(wd_ap[ex])
            for g in range(4):
                P.op("pe", lambda e, g=g: e.matmul(cx.ps[g][:, :], lhsT=c16b[:, ex * 128:(ex + 1) * 128],
                                                   rhs=slotTb[:, g * 512:(g + 1) * 512], start=True, stop=True),
                     reads=[r_slotT, cx.rcst], writes=[cx.rps[g]])
            for g in range(4):
                for c in range(2):
                    P.op("dve", lambda e, g=g, c=c: e.tensor_scalar(out=PT[:, c, g * 512:(g + 1) * 512], in0=cx.ps[g][:, :],
                                                                    scalar1=cx.cst[:, C_PIDX + c:C_PIDX + c + 1], scalar2=None,
                                                                    op0=ALU.is_equal),
                         reads=[cx.rps[g], cx.rcst], writes=[r_PT])
            for c in range(2):
                for n in range(2):
                    pb = 4 + ((c * 2 + n) % 4)
                    for j in range(NK):
                        P.op("pe", lambda e, j=j, c=c, n=n, pb=pb: e.matmul(cx.ps[pb][:, :], lhsT=hT[:, j, c * 128:(c + 1) * 128],
                                                                            rhs=wbuf[jd][:, j, n * 512:(n + 1) * 512],
                                                                            start=(j == 0), stop=(j == NK - 1)),
                             reads=[r_wbuf[jd], r_hj[j]], writes=[cx.rps[pb]])
                    P.op("act", lambda e, c=c, n=n, pb=pb: e.activation(out=Oe[:, c, n * 512:(n + 1) * 512], in_=cx.ps[pb][:, :],
                                                                        func=AF.Copy, scale=gs[:, c:c + 1]),
                         reads=[cx.rps[pb], r_gs], writes=[r_Oq[c][n]])
            for i in range(NT):
                for n in range(2):
                    pb = (i * 2 + n) % 4
                    for c in range(2):
                        P.op("pe", lambda e, i=i, n=n, c=c, pb=pb: e.matmul(cx.ps[pb][:, :], lhsT=PT[:, c, i * 128:(i + 1) * 128],
                                                                            rhs=Oe[:, c, n * 512:(n + 1) * 512],
                                                                            start=(c == 0), stop=(c == 1)),
                             reads=[r_PT, r_Oq[c][n]], writes=[cx.rps[pb]])
                    P.op("dve", lambda e, i=i, n=n, pb=pb: e.tensor_tensor(out=cx.R[:, i, n * 512:(n + 1) * 512],
                                                                           in0=cx.R[:, i, n * 512:(n + 1) * 512],
                                                                           in1=cx.ps[pb][:, :], op=ALU.add),
                         reads=[cx.rps[pb], cx.rR[i]], writes=[cx.rR[i]])
    P.barrier()


def xt_view(cx):
    return cx.XA[:].rearrange("p (k t) -> p k t", k=NK)


def small_load(cx, out, in_, reg):
    cx.P.dma("sp", out, in_, writes=[reg], allow_slow_non_contiguous=True)


def gelu_tanh(cx, out, src, t1, t2, r_src, r_t1, r_t2, r_out, extra_mul=None, r_extra=None):
    P = cx.P
    P.op("act", lambda e: e.activation(out=t1, in_=src, func=AF.Square), reads=[r_src], writes=[r_t1])
    P.op("dve", lambda e: e.tensor_scalar(out=t1, in0=t1, scalar1=0.044715, scalar2=1.0, op0=ALU.mult, op1=ALU.add),
         reads=[r_t1], writes=[r_t1])
    P.op("dve", lambda e: e.tensor_tensor(out=t1, in0=t1, in1=src, op=ALU.mult), reads=[r_t1, r_src], writes=[r_t1])
    P.op("act", lambda e: e.activation(out=t1, in_=t1, func=AF.Sigmoid, scale=1.5957691216057308), reads=[r_t1], writes=[r_t1])
    if extra_mul is None:
        P.op("dve", lambda e: e.tensor_tensor(out=out, in0=t1, in1=src, op=ALU.mult), reads=[r_t1, r_src], writes=[r_out])
    else:
        P.op("dve", lambda e: e.tensor_tensor(out=t2, in0=t1, in1=src, op=ALU.mult), reads=[r_t1, r_src], writes=[r_t2])
        P.op("pool", lambda e: e.tensor_tensor(out=out, in0=t2, in1=extra_mul, op=ALU.mult), reads=[r_t2, r_extra], writes=[r_out])


def emit_lru(cx):
    P = cx.P
    nc = cx.nc
    A = cx.A
    XT = xt_view(cx)
    with contextlib.ExitStack() as st:
        sb = lambda name, shape, dt: st.enter_context(nc.sbuf_tensor(un(name), shape, dt))
        cw = sb("cw", [128, NK, 4], F32)
        cb = sb("cb", [128, NK], F32)
        gab = sb("gab", [128, 2, NK], F32)
        gxb = sb("gxb", [128, 2, NK], F32)
        lam = sb("lam", [128, 2, NK], F32)
        c8 = sb("c8", [128, 2, NK], F32)
        c16 = sb("c16", [128, 2, NK], F32)
        r_par = Reg()
        for j in range(4):
            small_load(cx, cw[:, :, j], A["lru_conv_w"][0, j].rearrange("(c p) -> p c", p=128), r_par)
        small_load(cx, cb[:], A["lru_conv_b"][0].rearrange("(c p) -> p c", p=128), r_par)
        for g in range(2):
            small_load(cx, gab[:, g, :], A["lru_gate_a_b"][0, g].rearrange("(c p) -> p c", p=128), r_par)
        for g in range(2):
            small_load(cx, gxb[:, g, :], A["lru_gate_x_b"][0, g].rearrange("(c p) -> p c", p=128), r_par)
        for g in range(2):
            small_load(cx, lam[:, g, :], A["lru_lambda"][0, g].rearrange("(c p) -> p c", p=128), r_par)
        P.op("act", lambda e: e.activation(out=c8[:], in_=lam[:], func=AF.Exp, scale=-1.0), reads=[r_par], writes=[r_par])
        P.op("act", lambda e: e.activation(out=c8[:], in_=c8[:], func=AF.Ln, bias=1.0), reads=[r_par], writes=[r_par])
        P.op("dve", lambda e: e.tensor_scalar(out=c16[:], in0=c8[:], scalar1=-16.0, scalar2=None, op0=ALU.mult), reads=[r_par], writes=[r_par])
        P.op("dve", lambda e: e.tensor_scalar(out=c8[:], in0=c8[:], scalar1=-8.0, scalar2=None, op0=ALU.mult), reads=[r_par], writes=[r_par])

        wgb = sb("wgb", [128, NK, 256], BF16)
        r_wgb = Reg()
        wu = sb("wu", [128, NK, 256], BF16)
        r_wu = Reg()
        wg4 = [[sb("wga", [128, 2, 256], BF16) for d in range(2)] for ax in range(2)]
        r_wg4 = [[Reg() for d in range(2)] for ax in range(2)]
        wo = sb("wo", [128, D], BF16)
        r_wo = Reg()
        upad = sb("upad", [128, S + 4], F32)
        r_upad = Reg()
        uc = sb("uc", [128, 2, S], F32)
        r_uc = [Reg(), Reg()]
        ucb = sb("ucb", [128, 2, S], BF16)
        r_ucb = [Reg(), Reg()]
        rbuf = sb("rbuf", [128, S], F32)
        r_rbuf = Reg()
        abuf = sb("abuf", [128, S], F32)
        r_abuf = Reg()
        tmp = sb("tmp", [128, S], F32)
        r_tmp = Reg()
        hsum = sb("hsum", [128, S], F32)
        r_hsum = Reg()
        gtc = sb("gtc", [128, S], BF16)
        r_gtc = Reg()
        P.op("pool", lambda e: e.memset(upad[:], 0.0), writes=[r_upad])

        def psum4(base):
            return [cx.ps[base + j] for j in range(4)], [cx.rps[base + j] for j in range(4)]

        pcount = [0]

        def next_ps4():
            base = 4 * (pcount[0] % 2)
            pcount[0] += 1
            return psum4(base)

        for n in range(4):
            P.dma("pool", wgb[:], A["lru_w_in"][0][:, n * 256:(n + 1) * 256].rearrange("(k p) c -> p k c", p=128), writes=[r_wgb])
            P.dma("pool", wu[:], A["lru_w_in"][0][:, D + n * 256:D + (n + 1) * 256].rearrange("(k p) c -> p k c", p=128), writes=[r_wu])
            for ax, nm in enumerate(("lru_gate_a_w", "lru_gate_x_w")):
                for d in range(2):
                    P.dma("pool", wg4[ax][d][:], A[nm][0, d, n].rearrange("(i p) j -> p i j", p=128), writes=[r_wg4[ax][d]])
            for cc in range(2):
                c = 2 * n + cc
                ps, rps = next_ps4()
                for tb in range(4):
                    for k in range(NK):
                        P.op("pe", lambda e, tb=tb, k=k: e.matmul(ps[tb][:, :], lhsT=wu[:, k, cc * 128:(cc + 1) * 128],
                                                                  rhs=XT[:, k, tb * 512:(tb + 1) * 512], start=(k == 0), stop=(k == NK - 1)),
                             reads=[r_wu] + cx.rXA[tb * 4:(tb + 1) * 4], writes=[rps[tb]])
                    P.op("act", lambda e, tb=tb: e.activation(out=upad[:, 2 + tb * 512:2 + (tb + 1) * 512], in_=ps[tb][:, :], func=AF.Copy),
                         reads=[rps[tb]], writes=[r_upad])
                ucc = uc[:, cc, :]
                P.op("dve", lambda e: e.tensor_scalar(out=ucc, in0=upad[:, 0:S], scalar1=cw[:, c, 0:1], scalar2=cb[:, c:c + 1],
                                                       op0=ALU.mult, op1=ALU.add), reads=[r_upad, r_par], writes=[r_uc[cc]])
                for j in range(1, 4):
                    P.op("dve", lambda e, j=j: e.scalar_tensor_tensor(out=ucc, in0=upad[:, j:j + S], scalar=cw[:, c, j:j + 1], in1=ucc,
                                                                      op0=ALU.mult, op1=ALU.add), reads=[r_upad, r_par, r_uc[cc]], writes=[r_uc[cc]])
                P.op("pool", lambda e: e.tensor_copy(out=ucb[:, cc, :], in_=ucc), reads=[r_uc[cc]], writes=[r_ucb[cc]])
            for cc in range(2):
                c = 2 * n + cc
                ucc = uc[:, cc, :]
                for d in range(2):
                    ps, rps = next_ps4()
                    for tb in range(4):
                        for i in range(2):
                            P.op("pe", lambda e, tb=tb, i=i: e.matmul(ps[tb][:, :], lhsT=wg4[0][d][:, i, cc * 128:(cc + 1) * 128],
                                                                      rhs=ucb[:, i, tb * 512:(tb + 1) * 512], start=(i == 0), stop=(i == 1)),
                                 reads=[r_wg4[0][d], r_ucb[0], r_ucb[1]], writes=[rps[tb]])
                        P.op("act", lambda e, tb=tb: e.activation(out=rbuf[:, tb * 512:(tb + 1) * 512], in_=ps[tb][:, :], func=AF.Sigmoid,
                                                                  bias=gab[:, d, c:c + 1]), reads=[rps[tb], r_par], writes=[r_rbuf])
                    P.op("act", lambda e: e.activation(out=abuf[:], in_=rbuf[:], func=AF.Exp, scale=c8[:, d, c:c + 1]),
                         reads=[r_rbuf, r_par], writes=[r_abuf])
                    P.op("act", lambda e: e.activation(out=tmp[:], in_=rbuf[:], func=AF.Exp, scale=c16[:, d, c:c + 1]),
                         reads=[r_rbuf, r_par], writes=[r_tmp])
                    P.op("act", lambda e: e.activation(out=tmp[:], in_=tmp[:], func=AF.Sqrt, scale=-1.0, bias=1.0),
                         reads=[r_tmp], writes=[r_tmp])
                    ps, rps = next_ps4()
                    for tb in range(4):
                        for i in range(2):
                            P.op("pe", lambda e, tb=tb, i=i: e.matmul(ps[tb][:, :], lhsT=wg4[1][d][:, i, cc * 128:(cc + 1) * 128],
                                                                      rhs=ucb[:, i, tb * 512:(tb + 1) * 512], start=(i == 0), stop=(i == 1)),
                                 reads=[r_wg4[1][d], r_ucb[0], r_ucb[1]], writes=[rps[tb]])
                        P.op("act", lambda e, tb=tb: e.activation(out=rbuf[:, tb * 512:(tb + 1) * 512], in_=ps[tb][:, :], func=AF.Sigmoid,
                                                                  bias=gxb[:, d, c:c + 1]), reads=[rps[tb], r_par], writes=[r_rbuf])
                    P.op("dve", lambda e: e.tensor_tensor(out=tmp[:], in0=tmp[:], in1=rbuf[:], op=ALU.mult), reads=[r_tmp, r_rbuf], writes=[r_tmp])
                    P.op("dve", lambda e: e.tensor_tensor(out=tmp[:], in0=tmp[:], in1=ucc, op=ALU.mult), reads=[r_tmp, r_uc[cc]], writes=[r_tmp])
                    if d == 0:
                        P.op("dve", lambda e: e.tensor_tensor_scan(out=hsum[:], data0=abuf[:], data1=tmp[:], initial=0.0,
                                                                    op0=ALU.mult, op1=ALU.add), reads=[r_abuf, r_tmp], writes=[r_hsum])
                    else:
                        P.op("dve", lambda e: e.tensor_tensor_scan(out=rbuf[:, ::-1], data0=abuf[:, ::-1], data1=tmp[:, ::-1], initial=0.0,
                                                                    op0=ALU.mult, op1=ALU.add), reads=[r_abuf, r_tmp], writes=[r_rbuf])
                        P.op("pool", lambda e: e.tensor_tensor(out=hsum[:], in0=hsum[:], in1=rbuf[:], op=ALU.add),
                             reads=[r_rbuf, r_hsum], writes=[r_hsum])
                ps, rps = next_ps4()
                for tb in range(4):
                    for k in range(NK):
                        P.op("pe", lambda e, tb=tb, k=k: e.matmul(ps[tb][:, :], lhsT=wgb[:, k, cc * 128:(cc + 1) * 128],
                                                                  rhs=XT[:, k, tb * 512:(tb + 1) * 512], start=(k == 0), stop=(k == NK - 1)),
                             reads=[r_wgb] + cx.rXA[tb * 4:(tb + 1) * 4], writes=[rps[tb]])
                    sl = slice(tb * 512, (tb + 1) * 512)
                    gelu_tanh(cx, gtc[:, sl], ps[tb][:, :], abuf[:, sl], tmp[:, sl], rps[tb], r_abuf, r_tmp, r_gtc,
                              extra_mul=hsum[:, sl], r_extra=r_hsum)
                P.dma("pool", wo[:], A["lru_w_out"][0][c * 128:(c + 1) * 128, :], writes=[r_wo])
                for i in range(NT):
                    for hh in range(2):
                        pb = (i * 2 + hh) % 8
                        P.op("pe", lambda e, i=i, hh=hh, pb=pb: e.matmul(cx.ps[pb][:, :], lhsT=gtc[:, i * 128:(i + 1) * 128],
                                                                         rhs=wo[:, hh * 512:(hh + 1) * 512], start=True, stop=True),
                             reads=[r_gtc, r_wo], writes=[cx.rps[pb]])
                        P.op("dve", lambda e, i=i, hh=hh, pb=pb: e.tensor_tensor(out=cx.R[:, i, hh * 512:(hh + 1) * 512],
                                                                                 in0=cx.R[:, i, hh * 512:(hh + 1) * 512],
                                                                                 in1=cx.ps[pb][:, :], op=ALU.add),
                             reads=[cx.rps[pb], cx.rR[i]], writes=[cx.rR[i]])
    P.barrier()


MLA_H = 8
MLA_SCALE = 192.0 ** -0.5
TWO_PI_HI = 6.28125
TWO_PI_LO = 2.0 * np.pi - 6.28125
MAGIC = 12582912.0


def rope_apply(cx, out_bf, src, cos, sin, t1, t2, nb, r_src, r_tab, r_t, r_out):
    P = cx.P
    a1 = src[:, :, 0:32]
    a2 = src[:, :, 32:64]
    P.op("dve", lambda e: e.tensor_tensor(out=t1, in0=a1, in1=cos, op=ALU.mult), reads=[r_src, r_tab], writes=[r_t])
    P.op("dve", lambda e: e.tensor_tensor(out=t2, in0=a2, in1=sin, op=ALU.mult), reads=[r_src, r_tab], writes=[r_t])
    P.op("dve", lambda e: e.tensor_tensor(out=out_bf[:, :, 0:32], in0=t1, in1=t2, op=ALU.subtract), reads=[r_t], writes=[r_out])
    P.op("dve", lambda e: e.tensor_tensor(out=t1, in0=a2, in1=cos, op=ALU.mult), reads=[r_src, r_tab, r_out], writes=[r_t])
    P.op("dve", lambda e: e.tensor_tensor(out=t2, in0=a1, in1=sin, op=ALU.mult), reads=[r_src, r_tab], writes=[r_t])
    P.op("dve", lambda e: e.tensor_tensor(out=out_bf[:, :, 32:64], in0=t1, in1=t2, op=ALU.add), reads=[r_t], writes=[r_out])


def emit_mla(cx, seq):
    P = cx.P
    nc = cx.nc
    A = cx.A
    XT = xt_view(cx)
    with contextlib.ExitStack() as st:
        sb = lambda name, shape, dt: st.enter_context(nc.sbuf_tensor(un(name), shape, dt))
        posi = sb("posi", [128, NT], I32)
        ang = sb("ang", [128, NT, 32], F32)
        nn = sb("nn", [128, NT, 32], F32)
        cosT = sb("cosT", [128, NT, 32], F32)
        sinT = sb("sinT", [128, NT, 32], F32)
        r_tab = Reg()
        small_load(cx, posi[:], A["positions"][seq].rearrange("(i p) -> p i", p=128), r_tab)
        P.op("dve", lambda e: e.tensor_copy(out=nn[:, :, 0], in_=posi[:]), reads=[r_tab], writes=[r_tab])
        P.op("dve", lambda e: e.tensor_tensor(out=ang[:], in0=nn[:, :, 0:1].to_broadcast([128, NT, 32]),
                                              in1=cx.cst[:, None, C_FREQ:C_FREQ + 32].to_broadcast([128, NT, 32]), op=ALU.mult),
             reads=[r_tab, cx.rcst], writes=[r_tab])
        P.op("dve", lambda e: e.tensor_scalar(out=nn[:], in0=ang[:], scalar1=float(1.0 / (2.0 * np.pi)), scalar2=None, op0=ALU.mult),
             reads=[r_tab], writes=[r_tab])
        P.op("dve", lambda e: e.tensor_scalar(out=nn[:], in0=nn[:], scalar1=MAGIC, scalar2=None, op0=ALU.add), reads=[r_tab], writes=[r_tab])
        P.op("dve", lambda e: e.tensor_scalar(out=nn[:], in0=nn[:], scalar1=-MAGIC, scalar2=None, op0=ALU.add), reads=[r_tab], writes=[r_tab])
        P.op("dve", lambda e: e.scalar_tensor_tensor(out=ang[:], in0=nn[:], scalar=-TWO_PI_HI, in1=ang[:], op0=ALU.mult, op1=ALU.add),
             reads=[r_tab], writes=[r_tab])
        P.op("dve", lambda e: e.scalar_tensor_tensor(out=ang[:], in0=nn[:], scalar=-TWO_PI_LO, in1=ang[:], op0=ALU.mult, op1=ALU.add),
             reads=[r_tab], writes=[r_tab])
        PI_S = 3.1415925
        P.op("dve", lambda e: e.tensor_scalar(out=ang[:], in0=ang[:], scalar1=PI_S, scalar2=-PI_S, op0=ALU.min, op1=ALU.max),
             reads=[r_tab], writes=[r_tab])
        P.op("act", lambda e: e.activation(out=sinT[:], in_=ang[:], func=AF.Sin), reads=[r_tab], writes=[r_tab])
        P.op("act", lambda e: e.activation(out=nn[:], in_=ang[:], func=AF.Abs), reads=[r_tab], writes=[r_tab])
        P.op("dve", lambda e: e.tensor_scalar(out=nn[:], in0=nn[:], scalar1=-1.0, scalar2=float(np.pi / 2), op0=ALU.mult, op1=ALU.add),
             reads=[r_tab], writes=[r_tab])
        P.op("act", lambda e: e.activation(out=cosT[:], in_=nn[:], func=AF.Sin), reads=[r_tab], writes=[r_tab])

        wuq = sb("wuq", [128, 3, 1536], BF16)
        wukv = sb("wukv", [128, 2, 2048], BF16)
        r_w = Reg()
        P.dma("pool", wuq[:], A["mla_w_uq"][0].rearrange("(k p) c -> p k c", p=128), writes=[r_w])
        P.dma("pool", wukv[:], A["mla_w_ukv"][0].rearrange("(k p) c -> p k c", p=128), writes=[r_w])
        cqnT = sb("cqnT", [128, 3, S], BF16)
        ckvnT = sb("ckvnT", [128, 2, S], BF16)
        kropeT = sb("kropeT", [64, S], BF16)
        r_lat = [Reg() for i in range(NT)]
        r_krT = Reg()
        with contextlib.ExitStack() as st1:
            sb1 = lambda name, shape, dt: st1.enter_context(nc.sbuf_tensor(un(name), shape, dt))
            wi = sb1("wi", [128, NK, 704], BF16)
            r_wi = Reg()
            P.dma("pool", wi[:], A["mla_w_in"][0].rearrange("(k p) c -> p k c", p=128), writes=[r_wi])
            gq = sb1("gq", [128, 384], F32)
            gkv = sb1("gkv", [128, 256], F32)
            r_g = Reg()
            P.dma("sp", gq[:], A["mla_q_norm_g"][0].partition_broadcast(128), writes=[r_g])
            P.dma("sp", gkv[:], A["mla_kv_norm_g"][0].partition_broadcast(128), writes=[r_g])
            zs = [sb1("zs", [128, 704], F32) for j in range(2)]
            r_zs = [Reg(), Reg()]
            junk = sb1("junk", [128, 384], F32)
            r_junk = Reg()
            ms = [sb1("ms", [128, 4], F32) for j in range(2)]
            r_ms = [Reg(), Reg()]
            cn = [sb1("cn", [128, 640], BF16) for j in range(2)]
            r_cn = [Reg(), Reg()]
            krz = sb1("krz", [128, NT, 64], F32)
            r_krz = Reg()
            krb = sb1("krb", [128, NT, 64], BF16)
            r_krb = Reg()
            rt1 = sb1("rt1", [128, NT, 32], F32)
            rt2 = sb1("rt2", [128, NT, 32], F32)
            r_rt = Reg()
            for i in range(NT):
                j2 = i % 2
                pa = 0 + 2 * j2
                pbk = 1 + 2 * j2
                for k in range(NK):
                    P.op("pe", lambda e, k=k: e.matmul(cx.ps[pa][:, :], lhsT=XT[:, k, i * 128:(i + 1) * 128], rhs=wi[:, k, 0:512],
                                                       start=(k == 0), stop=(k == NK - 1)), reads=[cx.rXA[i], r_wi], writes=[cx.rps[pa]])
                for k in range(NK):
                    P.op("pe", lambda e, k=k: e.matmul(cx.ps[pbk][:, 0:192], lhsT=XT[:, k, i * 128:(i + 1) * 128], rhs=wi[:, k, 512:704],
                                                       start=(k == 0), stop=(k == NK - 1)), reads=[cx.rXA[i], r_wi], writes=[cx.rps[pbk]])
                z = zs[j2]
                P.op("act", lambda e: e.activation(out=z[:, 0:512], in_=cx.ps[pa][:, :], func=AF.Copy), reads=[cx.rps[pa]], writes=[r_zs[j2]])
                P.op("dve", lambda e: e.tensor_copy(out=z[:, 512:704], in_=cx.ps[pbk][:, 0:192]), reads=[cx.rps[pbk]], writes=[r_zs[j2]])
                m = ms[j2]
                P.op("act", lambda e: e.activation(out=junk[:, 0:384], in_=z[:, 0:384], func=AF.Square, accum_out=m[:, 0:1]),
                     reads=[r_zs[j2]], writes=[r_junk, r_ms[j2]])
                P.op("act", lambda e: e.activation(out=junk[:, 0:256], in_=z[:, 384:640], func=AF.Square, accum_out=m[:, 1:2]),
                     reads=[r_zs[j2]], writes=[r_junk, r_ms[j2]])
                P.op("dve", lambda e: e.tensor_scalar(out=m[:, 0:1], in0=m[:, 0:1], scalar1=1.0 / 384.0, scalar2=LN_EPS, op0=ALU.mult, op1=ALU.add),
                     reads=[r_ms[j2]], writes=[r_ms[j2]])
                P.op("dve", lambda e: e.tensor_scalar(out=m[:, 1:2], in0=m[:, 1:2], scalar1=1.0 / 256.0, scalar2=LN_EPS, op0=ALU.mult, op1=ALU.add),
                     reads=[r_ms[j2]], writes=[r_ms[j2]])
                P.op("act", lambda e: e.activation(out=m[:, 0:2], in_=m[:, 0:2], func=AF.Sqrt), reads=[r_ms[j2]], writes=[r_ms[j2]])
                P.op("dve", lambda e: e.reciprocal(out=m[:, 2:4], in_=m[:, 0:2]), reads=[r_ms[j2]], writes=[r_ms[j2]])
                c_ = cn[j2]
                P.op("dve", lambda e: e.scalar_tensor_tensor(out=c_[:, 0:384], in0=z[:, 0:384], scalar=m[:, 2:3], in1=gq[:], op0=ALU.mult, op1=ALU.mult),
                     reads=[r_zs[j2], r_ms[j2], r_g], writes=[r_cn[j2]])
                P.op("dve", lambda e: e.scalar_tensor_tensor(out=c_[:, 384:640], in0=z[:, 384:640], scalar=m[:, 3:4], in1=gkv[:], op0=ALU.mult, op1=ALU.mult),
                     reads=[r_zs[j2], r_ms[j2], r_g], writes=[r_cn[j2]])
                P.op("pool", lambda e: e.tensor_copy(out=krz[:, i, :], in_=z[:, 640:704]), reads=[r_zs[j2]], writes=[r_krz])
                pt = 6 + j2
                pst = cx.ps[pt][:].bitcast(BF16)
                for k in range(5):
                    P.op("pe", lambda e, k=k: e.transpose(out=pst[:, k * 128:(k + 1) * 128], in_=c_[:, k * 128:(k + 1) * 128], identity=cx.identb[:]),
                         reads=[r_cn[j2], cx.rcst], writes=[cx.rps[pt]])
                P.op("act", lambda e: e.activation(out=cqnT[:, :, i * 128:(i + 1) * 128], in_=pst[:, 0:384].rearrange("p (k t) -> p k t", k=3), func=AF.Copy),
                     reads=[cx.rps[pt]], writes=[r_lat[i]])
                P.op("dve", lambda e: e.tensor_copy(out=ckvnT[:, :, i * 128:(i + 1) * 128], in_=pst[:, 384:640].rearrange("p (k t) -> p k t", k=2)),
                     reads=[cx.rps[pt]], writes=[r_lat[i]])
            rope_apply(cx, krb[:], krz[:], cosT[:], sinT[:], rt1[:], rt2[:], NT, r_krz, r_tab, r_rt, r_krb)
            for g in range(2):
                pst = cx.ps[4 + g][:].bitcast(BF16)
                for j in range(8):
                    i = g * 8 + j
                    P.op("pe", lambda e, i=i, j=j: e.transpose(out=pst[0:64, j * 128:(j + 1) * 128], in_=krb[:, i, :], identity=cx.identb[:]),
                         reads=[r_krb, cx.rcst], writes=[cx.rps[4 + g]])
                P.op("act", lambda e, g=g: e.activation(out=kropeT[:, g * 1024:(g + 1) * 1024], in_=pst[0:64, :], func=AF.Copy),
                     reads=[cx.rps[4 + g]], writes=[r_krT])
        P.barrier()
        qnT = sb("qnT", [128, S], BF16)
        knT = sb("knT", [128, S], BF16)
        qrT = sb("qrT", [64, S], BF16)
        vh = sb("vh", [128, NT, 128], BF16)
        r_qnT, r_knT, r_qrT, r_vh = Reg(), Reg(), Reg(), Reg()
        qrb = sb("qrb", [128, 8, 64], BF16)
        r_qrb = Reg()
        rt1 = sb("rt1b", [128, 8, 32], F32)
        rt2 = sb("rt2b", [128, 8, 32], F32)
        r_rt = Reg()
        Pb = sb("Pb", [128, S], BF16)
        r_Pb = Reg()
        PTs = sb("PTs", [128, NT, 128], BF16)
        r_PTs = Reg()
        oT = sb("oT", [128, 128], BF16)
        r_oT = Reg()
        sm = [sb("smx", [128, 4], F32) for j in range(2)]
        r_sm = [Reg(), Reg()]
        wo = sb("wo", [128, D], BF16)
        r_wo = Reg()
        all_lat = r_lat
        for h in range(MLA_H):
            P.dma("pool", wo[:], A["mla_w_out"][0][h * 128:(h + 1) * 128, :], writes=[r_wo])
            for tb in range(4):
                for k in range(3):
                    P.op("pe", lambda e, tb=tb, k=k: e.matmul(cx.ps[tb][:, :], lhsT=wuq[:, k, h * 192:h * 192 + 128],
                                                              rhs=cqnT[:, k, tb * 512:(tb + 1) * 512], start=(k == 0), stop=(k == 2)),
                         reads=[r_w] + all_lat[tb * 4:(tb + 1) * 4], writes=[cx.rps[tb]])
                P.op("act", lambda e, tb=tb: e.activation(out=qnT[:, tb * 512:(tb + 1) * 512], in_=cx.ps[tb][:, :], func=AF.Copy),
                     reads=[cx.rps[tb]], writes=[r_qnT])
            for tb in range(4):
                for k in range(2):
                    P.op("pe", lambda e, tb=tb, k=k: e.matmul(cx.ps[4 + tb][:, :], lhsT=wukv[:, k, h * 256:h * 256 + 128],
                                                              rhs=ckvnT[:, k, tb * 512:(tb + 1) * 512], start=(k == 0), stop=(k == 1)),
                         reads=[r_w] + all_lat[tb * 4:(tb + 1) * 4], writes=[cx.rps[4 + tb]])
                P.op("dve", lambda e, tb=tb: e.tensor_copy(out=knT[:, tb * 512:(tb + 1) * 512], in_=cx.ps[4 + tb][:, :]),
                     reads=[cx.rps[4 + tb]], writes=[r_knT])
            for g in range(4):
                for j in range(4):
                    i = g * 4 + j
                    for k in range(2):
                        P.op("pe", lambda e, i=i, j=j, k=k, g=g: e.matmul(cx.ps[g][:, j * 128:(j + 1) * 128], lhsT=ckvnT[:, k, i * 128:(i + 1) * 128],
                                                                          rhs=wukv[:, k, h * 256 + 128:h * 256 + 256], start=(k == 0), stop=(k == 1)),
                             reads=[r_w, all_lat[i]], writes=[cx.rps[g]])
                P.op("act", lambda e, g=g: e.activation(out=vh[:, g * 4:(g + 1) * 4, :], in_=cx.ps[g][:, :].rearrange("p (a b) -> p a b", a=4), func=AF.Copy),
                     reads=[cx.rps[g]], writes=[r_vh])
            for g in range(2):
                pb = 4 + g
                for j in range(8):
                    i = g * 8 + j
                    for k in range(3):
                        P.op("pe", lambda e, i=i, j=j, k=k, pb=pb: e.matmul(cx.ps[pb][:, j * 64:(j + 1) * 64], lhsT=cqnT[:, k, i * 128:(i + 1) * 128],
                                                                            rhs=wuq[:, k, h * 192 + 128:h * 192 + 192], start=(k == 0), stop=(k == 2)),
                             reads=[r_w, all_lat[i]], writes=[cx.rps[pb]])
                rope_apply(cx, qrb[:], cx.ps[pb][:, :].rearrange("p (a b) -> p a b", a=8), cosT[:, g * 8:(g + 1) * 8, :], sinT[:, g * 8:(g + 1) * 8, :],
                           rt1[:], rt2[:], 8, cx.rps[pb], r_tab, r_rt, r_qrb)
                pst = cx.ps[6 + g][:].bitcast(BF16)
                for j in range(8):
                    P.op("pe", lambda e, j=j: e.transpose(out=pst[0:64, j * 128:(j + 1) * 128], in_=qrb[:, j, :], identity=cx.identb[:]),
                         reads=[r_qrb, cx.rcst], writes=[cx.rps[6 + g]])
                P.op("act", lambda e, g=g: e.activation(out=qrT[:, g * 1024:(g + 1) * 1024], in_=pst[0:64, :], func=AF.Copy),
                     reads=[cx.rps[6 + g]], writes=[r_qrT])
            for qb in range(NT):
                qs = slice(qb * 128, (qb + 1) * 128)
                for tb in range(4):
                    P.op("pe", lambda e, tb=tb: e.matmul(cx.ps[tb][:, :], lhsT=qnT[:, qs], rhs=knT[:, tb * 512:(tb + 1) * 512], start=True, stop=False),
                         reads=[r_qnT, r_knT], writes=[cx.rps[tb]])
                    P.op("pe", lambda e, tb=tb: e.matmul(cx.ps[tb][:, :], lhsT=qrT[:, qs], rhs=kropeT[:, tb * 512:(tb + 1) * 512], start=False, stop=True),
                         reads=[r_qrT, r_krT], writes=[cx.rps[tb]])
                Sps = cx.ps[0][:, :]
                m = sm[qb % 2]
                r_m = r_sm[qb % 2]
                for tb in range(4):
                    P.op("dve", lambda e, tb=tb: e.tensor_reduce(out=m[:, tb:tb + 1], in_=cx.ps[tb][:, :], axis=AX.X, op=ALU.max),
                         reads=[cx.rps[tb]], writes=[r_m])
                P.op("dve", lambda e: e.tensor_reduce(out=m[:, 0:1], in_=m[:, 0:4], axis=AX.X, op=ALU.max), reads=[r_m], writes=[r_m])
                P.op("dve", lambda e: e.tensor_scalar(out=m[:, 1:2], in0=m[:, 0:1], scalar1=-MLA_SCALE, scalar2=None, op0=ALU.mult), reads=[r_m], writes=[r_m])
                for tb in range(4):
                    P.op("act", lambda e, tb=tb: e.activation(out=Pb[:, tb * 512:(tb + 1) * 512], in_=cx.ps[tb][:, :], func=AF.Exp, scale=MLA_SCALE,
                                                              bias=m[:, 1:2]), reads=[cx.rps[tb], r_m], writes=[r_Pb])
                P.op("dve", lambda e: e.tensor_reduce(out=m[:, 2:3], in_=Pb[:], axis=AX.X, op=ALU.add), reads=[r_Pb], writes=[r_m])
                P.op("dve", lambda e: e.reciprocal(out=m[:, 3:4], in_=m[:, 2:3]), reads=[r_m], writes=[r_m])
                for g in range(2):
                    pst = cx.ps[4 + g][:].bitcast(BF16)
                    for j in range(8):
                        c = g * 8 + j
                        P.op("pe", lambda e, c=c, j=j: e.transpose(out=pst[:, j * 128:(j + 1) * 128], in_=Pb[:, c * 128:(c + 1) * 128], identity=cx.identb[:]),
                             reads=[r_Pb, cx.rcst], writes=[cx.rps[4 + g]])
                    if g == 0:
                        P.op("act", lambda e: e.activation(out=PTs[:, 0:8, :], in_=pst.rearrange("p (a b) -> p a b", a=8), func=AF.Copy),
                             reads=[cx.rps[4]], writes=[r_PTs])
                    else:
                        P.op("dve", lambda e: e.tensor_copy(out=PTs[:, 8:16, :], in_=pst.rearrange("p (a b) -> p a b", a=8)),
                             reads=[cx.rps[5]], writes=[r_PTs])
                for c in range(NT):
                    P.op("pe", lambda e, c=c: e.matmul(cx.ps[6][:, 0:128], lhsT=vh[:, c, :], rhs=PTs[:, c, :], start=(c == 0), stop=(c == NT - 1)),
                         reads=[r_vh, r_PTs], writes=[cx.rps[6]])
                P.op("act", lambda e: e.activation(out=oT[:], in_=cx.ps[6][:, 0:128], func=AF.Copy), reads=[cx.rps[6]], writes=[r_oT])
                for hh in range(2):
                    pb = 7 if hh == 0 else 6
                    P.op("pe", lambda e, hh=hh, pb=pb: e.matmul(cx.ps[pb][:, :], lhsT=oT[:], rhs=wo[:, hh * 512:(hh + 1) * 512], start=True, stop=True),
                         reads=[r_oT, r_wo], writes=[cx.rps[pb]])
                    P.op("dve", lambda e, hh=hh, pb=pb: e.scalar_tensor_tensor(out=cx.R[:, qb, hh * 512:(hh + 1) * 512], in0=cx.ps[pb][:, :], scalar=m[:, 3:4],
                                                                               in1=cx.R[:, qb, hh * 512:(hh + 1) * 512], op0=ALU.mult, op1=ALU.add),
                         reads=[cx.rps[pb], r_m, cx.rR[qb]], writes=[cx.rR[qb]])
    P.barrier()


LN_QSCALE = float(np.log(128.0 ** -0.5))


def emit_linattn(cx, kind):
    P = cx.P
    nc = cx.nc
    A = cx.A
    XT = xt_view(cx)
    ml = (kind == "mlstm")
    pre = "mlstm" if ml else "gla"
    W_in = A[pre + "_w_in"][0]
    DV1 = 257 if ml else 256
    cs = 1.0 if ml else 1.0 / 16.0
    with contextlib.ExitStack() as st:
        sb = lambda name, shape, dt: st.enter_context(nc.sbuf_tensor(un(name), shape, dt))
        ng = sb("ng", [128, 256], F32)
        r_ng = Reg()
        r_gw = Reg()
        if ml:
            wgates = sb("wgates", [128, NK, 16], BF16)
            P.dma("pool", wgates[:], W_in[:, 3072:3088].rearrange("(k p) c -> p k c", p=128), writes=[r_gw])
            gbias = sb("gbias", [128, 16], F32)
            P.dma("sp", gbias[:], A["mlstm_gate_b"][0].rearrange("a b c -> (a b c)").partition_broadcast(128), writes=[r_gw])
            ngbias = sb("ngbias", [128, 16], F32)
            P.op("dve", lambda e: e.tensor_scalar(out=ngbias[:], in0=gbias[:], scalar1=-1.0, scalar2=None, op0=ALU.mult), reads=[r_gw], writes=[r_gw])
            wrep = [sb("wrep", [128, NK, 128], BF16) for j in range(2)]
            r_wrep = [Reg(), Reg()]
            igb = sb("igb", [128, S], F32)
            r_igb = Reg()
        else:
            wglr = sb("wglr", [128, NK, 32], BF16)
            P.dma("pool", wglr[:], W_in[:, 3072:3104].rearrange("(k p) c -> p k c", p=128), writes=[r_gw])
            gwpad = sb("gwpad", [32, 2, 512], BF16)
            P.op("pool", lambda e: e.memset(gwpad[:], 0.0), writes=[r_gw])
            for d in range(2):
                P.dma("pool", gwpad[d * 16:(d + 1) * 16, d, :], A["gla_gate_w"][0, d], writes=[r_gw])
            ngb = sb("ngb", [128, 2, 4], F32)
            for d in range(2):
                small_load(cx, ngb[:, d, :], A["gla_gate_b"][0, d].rearrange("(h p) -> p h", p=128), r_gw)
            P.op("dve", lambda e: e.tensor_scalar(out=ngb[:], in0=ngb[:], scalar1=-1.0, scalar2=None, op0=ALU.mult), reads=[r_gw], writes=[r_gw])
            glrT = sb("glrT", [32, S], BF16)
            r_glrT = Reg()
            for tb in range(4):
                for k in range(NK):
                    P.op("pe", lambda e, tb=tb, k=k: e.matmul(cx.ps[tb][0:32, :], lhsT=wglr[:, k, :], rhs=XT[:, k, tb * 512:(tb + 1) * 512],
                                                              start=(k == 0), stop=(k == NK - 1)), reads=[r_gw] + cx.rXA[tb * 4:(tb + 1) * 4], writes=[cx.rps[tb]])
                P.op("act", lambda e, tb=tb: e.activation(out=glrT[:, tb * 512:(tb + 1) * 512], in_=cx.ps[tb][0:32, :], func=AF.Copy),
                     reads=[cx.rps[tb]], writes=[r_glrT])

        WA = sb("WA", [128, NK, 256], BF16)
        WB = sb("WB", [128, NK, 256], BF16)
        r_WA, r_WB = Reg(), Reg()
        wq = WA[:, :, 0:128]
        wk = WA[:, :, 128:256]
        wo = WA[:].rearrange("p k c -> p (k c)").rearrange("p (j f) -> p j f", j=2)
        wv = WB
        wor = WB
        r_wq, r_wk, r_wv, r_wor, r_wo = r_WA, r_WA, r_WB, r_WB, r_WA
        qT = sb("qT", [128, S], F32)
        kT = sb("kT", [128, S], F32)
        r_qT, r_kT = Reg(), Reg()
        vb = sb("vb", [128, NT, DV1], BF16)
        r_vb = Reg()
        buf1 = sb("buf1", [128, S], F32)
        buf2 = sb("buf2", [128, S], F32)
        r_b1, r_b2 = Reg(), Reg()
        qe = sb("qe", [128, S], BF16)
        ke = sb("ke", [128, S], BF16)
        kgT = sb("kgT", [128, NT, 128], BF16)
        r_qe, r_ke, r_kgT = Reg(), Reg(), Reg()
        Oacc = sb("Oacc", [128, NT, 256], F32)
        r_O = [Reg() for i in range(NT)]
        offs = sb("offs", [128, NT], F32)
        gsv = sb("gsv", [128, NT], F32)
        eg = sb("eg", [128, NT], F32)
        r_small = Reg()
        Sst = sb("Sst", [128, DV1], F32)
        Sb = sb("Sb", [128, DV1], BF16)
        r_S, r_Sb = Reg(), Reg()
        atm = [sb("atm", [128, 128], BF16) for j in range(2)]
        r_atm = [Reg(), Reg()]
        dn = [sb("dn", [128, 4], F32) for j in range(2)]
        r_dn = [Reg(), Reg()]
        hst = [sb("hst", [128, 16], F32) for j in range(2)]
        r_hst = [Reg(), Reg()]
        gact = [sb("gact", [128, 256], F32) for j in range(2)]
        r_gact = [Reg(), Reg()]
        xn = [sb("xn", [128, 256], F32) for j in range(2)]
        r_xn = [Reg(), Reg()]
        gbt = [sb("gbt", [128, 256], BF16) for j in range(2)]
        r_gbt = [Reg(), Reg()]
        gTt = [sb("gTt", [128, 2, 128], BF16) for j in range(2)]
        r_gTt = [Reg(), Reg()]
        P.op("pool", lambda e: e.memset(offs[:, 0:1], 0.0), writes=[r_small])
        if ml:
            P.op("pool", lambda e: e.memset(vb[:, :, 256:257], 1.0), writes=[r_vb])

        b1v = buf1[:].rearrange("p (c j) -> p c j", j=128)
        b2v = buf2[:].rearrange("p (c j) -> p c j", j=128)
        allXA = cx.rXA

        def wload(dst, reg, c0, c1):
            P.dma("pool", dst, W_in[:, c0:c1].rearrange("(k p) c -> p k c", p=128), writes=[reg])

        for h in range(4):
            wload(wq, r_wq, h * 128, (h + 1) * 128)
            wload(wk, r_wk, 512 + h * 128, 512 + (h + 1) * 128)
            wload(wv[:], r_wv, 1024 + h * 256, 1024 + (h + 1) * 256)
            P.dma("sp", ng[:], A[pre + "_norm_g"][0][h * 256:(h + 1) * 256].partition_broadcast(128), writes=[r_ng])
            for (wsrc, r_w, dst, r_dst, base) in ((wq, r_wq, qT, r_qT, 0), (wk, r_wk, kT, r_kT, 4)):
                for tb in range(4):
                    pb = base + tb
                    for k in range(NK):
                        P.op("pe", lambda e, k=k, pb=pb, tb=tb, wsrc=wsrc: e.matmul(cx.ps[pb][:, :], lhsT=wsrc[:, k, :], rhs=XT[:, k, tb * 512:(tb + 1) * 512],
                                                                                    start=(k == 0), stop=(k == NK - 1)),
                             reads=[r_w] + allXA[tb * 4:(tb + 1) * 4], writes=[cx.rps[pb]])
                    eng = "act" if base == 0 else "dve"
                    if eng == "act":
                        P.op("act", lambda e, pb=pb, tb=tb, dst=dst: e.activation(out=dst[:, tb * 512:(tb + 1) * 512], in_=cx.ps[pb][:, :], func=AF.Copy),
                             reads=[cx.rps[pb]], writes=[r_dst])
                    else:
                        P.op("dve", lambda e, pb=pb, tb=tb, dst=dst: e.tensor_copy(out=dst[:, tb * 512:(tb + 1) * 512], in_=cx.ps[pb][:, :]),
                             reads=[cx.rps[pb]], writes=[r_dst])
            for i in range(NT):
                pb = (i // 2) % 8
                off = (i % 2) * 256
                for k in range(NK):
                    P.op("pe", lambda e, k=k, pb=pb, off=off, i=i: e.matmul(cx.ps[pb][:, off:off + 256], lhsT=XT[:, k, i * 128:(i + 1) * 128], rhs=wv[:, k, :],
                                                                            start=(k == 0), stop=(k == NK - 1)), reads=[r_wv, allXA[i]], writes=[cx.rps[pb]])
                P.op("act", lambda e, pb=pb, off=off, i=i: e.activation(out=vb[:, i, 0:256], in_=cx.ps[pb][:, off:off + 256], func=AF.Copy),
                     reads=[cx.rps[pb]], writes=[r_vb])
            wload(wor[:], r_wor, 2048 + h * 256, 2048 + (h + 1) * 256)
            P.dma("pool", wo, A[pre + "_w_out"][0][h * 256:(h + 1) * 256, :].rearrange("(j p) f -> p j f", p=128), writes=[r_wo])
            for d in range(2):
                if ml:
                    jf = d * 8 + 4 + h
                    ji = d * 8 + h
                    for jj, gidx in enumerate((jf, ji)):
                        P.op("dve", lambda e, jj=jj, gidx=gidx: e.tensor_copy(out=wrep[jj][:], in_=wgates[:, :, gidx:gidx + 1].to_broadcast([128, NK, 128])),
                             reads=[r_gw], writes=[r_wrep[jj]])
                    for tb in range(4):
                        for k in range(NK):
                            P.op("pe", lambda e, k=k, tb=tb: e.matmul(cx.ps[tb][:, :], lhsT=wrep[0][:, k, :], rhs=XT[:, k, tb * 512:(tb + 1) * 512],
                                                                      start=(k == 0), stop=(k == NK - 1)), reads=[r_wrep[0]] + allXA[tb * 4:(tb + 1) * 4], writes=[cx.rps[tb]])
                        P.op("act", lambda e, tb=tb: e.activation(out=buf1[:, tb * 512:(tb + 1) * 512], in_=cx.ps[tb][:, :], func=AF.Exp, scale=-1.0,
                                                                  bias=ngbias[:, jf:jf + 1]), reads=[cx.rps[tb], r_gw], writes=[r_b1])
                    for tb in range(4):
                        for k in range(NK):
                            P.op("pe", lambda e, k=k, tb=tb: e.matmul(cx.ps[4 + tb][:, :], lhsT=wrep[1][:, k, :], rhs=XT[:, k, tb * 512:(tb + 1) * 512],
                                                                      start=(k == 0), stop=(k == NK - 1)), reads=[r_wrep[1]] + allXA[tb * 4:(tb + 1) * 4], writes=[cx.rps[4 + tb]])
                        P.op("act", lambda e, tb=tb: e.activation(out=igb[:, tb * 512:(tb + 1) * 512], in_=cx.ps[4 + tb][:, :], func=AF.Identity,
                                                                  bias=gbias[:, ji:ji + 1]), reads=[cx.rps[4 + tb], r_gw], writes=[r_igb])
                else:
                    for tb in range(4):
                        P.op("pe", lambda e, tb=tb: e.matmul(cx.ps[tb][:, :], lhsT=gwpad[:, d, h * 128:(h + 1) * 128], rhs=glrT[:, tb * 512:(tb + 1) * 512],
                                                             start=True, stop=True), reads=[r_gw, r_glrT], writes=[cx.rps[tb]])
                        P.op("act", lambda e, tb=tb: e.activation(out=buf1[:, tb * 512:(tb + 1) * 512], in_=cx.ps[tb][:, :], func=AF.Exp, scale=-1.0,
                                                                  bias=ngb[:, d, h:h + 1]), reads=[cx.rps[tb], r_gw], writes=[r_b1])
                P.op("act", lambda e: e.activation(out=buf1[:], in_=buf1[:], func=AF.Ln, bias=1.0), reads=[r_b1], writes=[r_b1])
                P.op("dve", lambda e: e.tensor_tensor_scan(out=buf2[:], data0=cx.cst[:, C_ONES:C_ONES + 1].to_broadcast([128, S]), data1=buf1[:], initial=0.0,
                                                            op0=ALU.mult, op1=ALU.add), reads=[r_b1, cx.rcst], writes=[r_b2])
                P.op("dve", lambda e: e.tensor_copy(out=offs[:, 1:NT], in_=b2v[:, 0:NT - 1, 127]), reads=[r_b2], writes=[r_small])
                P.op("dve", lambda e: e.tensor_tensor(out=b2v, in0=b2v, in1=offs[:, :, None].to_broadcast([128, NT, 128]), op=ALU.subtract),
                     reads=[r_b2, r_small], writes=[r_b2])
                P.op("dve", lambda e: e.tensor_copy(out=gsv[:], in_=b2v[:, :, 127]), reads=[r_b2], writes=[r_small])
                P.op("act", lambda e: e.activation(out=eg[:], in_=gsv[:], func=AF.Exp, scale=-cs), reads=[r_small], writes=[r_small])
                gs_bc = gsv[:, :, None].to_broadcast([128, NT, 128])
                if d == 1:
                    P.op("dve", lambda e: e.scalar_tensor_tensor(out=buf2[:], in0=buf2[:], scalar=-1.0, in1=buf1[:], op0=ALU.mult, op1=ALU.add),
                         reads=[r_b1, r_b2], writes=[r_b2])
                    P.op("dve", lambda e: e.tensor_tensor(out=b2v, in0=b2v, in1=gs_bc, op=ALU.add), reads=[r_b2, r_small], writes=[r_b2])
                P.op("dve", lambda e: e.scalar_tensor_tensor(out=b1v, in0=b2v, scalar=-1.0, in1=gs_bc, op0=ALU.mult, op1=ALU.add),
                     reads=[r_b2, r_small], writes=[r_b1])
                if ml:
                    P.op("dve", lambda e: e.scalar_tensor_tensor(out=buf1[:], in0=buf1[:], scalar=-cs, in1=igb[:], op0=ALU.mult, op1=ALU.add),
                         reads=[r_b1, r_igb], writes=[r_b1])
                    P.op("act", lambda e: e.activation(out=buf1[:], in_=buf1[:], func=AF.Exp), reads=[r_b1], writes=[r_b1])
                else:
                    P.op("act", lambda e: e.activation(out=buf1[:], in_=buf1[:], func=AF.Exp, scale=-cs), reads=[r_b1], writes=[r_b1])
                P.op("pool", lambda e: e.tensor_tensor(out=ke[:], in0=kT[:], in1=buf1[:], op=ALU.mult), reads=[r_kT, r_b1], writes=[r_ke])
                for g in range(2):
                    pst = cx.ps[6 + g][:].bitcast(BF16)
                    for j in range(8):
                        c = g * 8 + j
                        P.op("pe", lambda e, c=c, j=j: e.transpose(out=pst[:, j * 128:(j + 1) * 128], in_=ke[:, c * 128:(c + 1) * 128], identity=cx.identb[:]),
                             reads=[r_ke, cx.rcst], writes=[cx.rps[6 + g]])
                    P.op("act", lambda e, g=g: e.activation(out=kgT[:, g * 8:(g + 1) * 8, :], in_=pst.rearrange("p (a b) -> p a b", a=8), func=AF.Copy),
                         reads=[cx.rps[6 + g]], writes=[r_kgT])
                if ml:
                    P.op("dve", lambda e: e.scalar_tensor_tensor(out=buf1[:], in0=buf2[:], scalar=cs, in1=igb[:], op0=ALU.mult, op1=ALU.add),
                         reads=[r_b2, r_igb], writes=[r_b1])
                    P.op("act", lambda e: e.activation(out=buf1[:], in_=buf1[:], func=AF.Exp), reads=[r_b1], writes=[r_b1])
                else:
                    P.op("act", lambda e: e.activation(out=buf1[:], in_=buf2[:], func=AF.Exp, scale=cs), reads=[r_b2], writes=[r_b1])
                P.op("pool", lambda e: e.tensor_tensor(out=ke[:], in0=kT[:], in1=buf1[:], op=ALU.mult), reads=[r_kT, r_b1], writes=[r_ke])
                P.op("act", lambda e: e.activation(out=buf2[:], in_=buf2[:], func=AF.Exp, scale=-cs, bias=cx.cst[:, C_LNQ:C_LNQ + 1]), reads=[r_b2, cx.rcst], writes=[r_b2])
                P.op("dve", lambda e: e.tensor_tensor(out=qe[:], in0=qT[:], in1=buf2[:], op=ALU.mult), reads=[r_qT, r_b2], writes=[r_qe])
                order = list(range(NT)) if d == 0 else list(range(NT - 1, -1, -1))
                mcol = C_TRIU if d == 0 else C_TRIL
                for n_, c in enumerate(order):
                    csl = slice(c * 128, (c + 1) * 128)
                    pa = n_ % 2
                    po = 2 + (n_ % 2)
                    pss = 4 + (n_ % 2)
                    first = (n_ == 0)
                    last = (n_ == NT - 1)
                    P.op("pe", lambda e: e.matmul(cx.ps[pa][:, 0:128], lhsT=ke[:, csl], rhs=qe[:, csl], start=True, stop=True),
                         reads=[r_ke, r_qe], writes=[cx.rps[pa]])
                    am = atm[n_ % 2]
                    P.op("dve", lambda e: e.tensor_tensor(out=am[:], in0=cx.cst[:, mcol:mcol + 128], in1=cx.ps[pa][:, 0:128], op=ALU.mult),
                         reads=[cx.rps[pa], cx.rcst], writes=[r_atm[n_ % 2]])
                    P.op("pe", lambda e: e.matmul(cx.ps[po][:, 0:DV1], lhsT=am[:], rhs=vb[:, c, :], start=True, stop=first),
                         reads=[r_atm[n_ % 2], r_vb], writes=[cx.rps[po]])
                    if not first:
                        P.op("pe", lambda e: e.matmul(cx.ps[po][:, 0:DV1], lhsT=qe[:, csl], rhs=Sb[:], start=False, stop=True),
                             reads=[r_qe, r_Sb], writes=[cx.rps[po]])
                    if not last:
                        P.op("pe", lambda e: e.matmul(cx.ps[pss][:, 0:DV1], lhsT=kgT[:, c, :], rhs=vb[:, c, :], start=True, stop=True),
                             reads=[r_kgT, r_vb], writes=[cx.rps[pss]])
                        if first:
                            P.op("dve", lambda e: e.tensor_copy(out=Sst[:], in_=cx.ps[pss][:, 0:DV1]), reads=[cx.rps[pss]], writes=[r_S])
                        else:
                            P.op("dve", lambda e: e.scalar_tensor_tensor(out=Sst[:], in0=Sst[:], scalar=eg[:, c:c + 1], in1=cx.ps[pss][:, 0:DV1],
                                                                          op0=ALU.mult, op1=ALU.add), reads=[cx.rps[pss], r_S, r_small], writes=[r_S])
                        P.op("act", lambda e: e.activation(out=Sb[:], in_=Sst[:], func=AF.Copy), reads=[r_S], writes=[r_Sb])
                    oc = Oacc[:, c, :]
                    if ml:
                        dd = dn[n_ % 2]
                        r_dd = r_dn[n_ % 2]
                        P.op("act", lambda e: e.activation(out=dd[:, 0:1], in_=cx.ps[po][:, 256:257], func=AF.Abs), reads=[cx.rps[po]], writes=[r_dd])
                        P.op("dve", lambda e: e.tensor_scalar(out=dd[:, 0:1], in0=dd[:, 0:1], scalar1=1.0, scalar2=None, op0=ALU.max), reads=[r_dd], writes=[r_dd])
                        P.op("dve", lambda e: e.reciprocal(out=dd[:, 1:2], in_=dd[:, 0:1]), reads=[r_dd], writes=[r_dd])
                        if d == 0:
                            P.op("act", lambda e: e.activation(out=oc, in_=cx.ps[po][:, 0:256], func=AF.Copy, scale=dd[:, 1:2]),
                                 reads=[cx.rps[po], r_dd], writes=[r_O[c]])
                        else:
                            P.op("dve", lambda e: e.scalar_tensor_tensor(out=oc, in0=cx.ps[po][:, 0:256], scalar=dd[:, 1:2], in1=oc, op0=ALU.mult, op1=ALU.add),
                                 reads=[cx.rps[po], r_dd, r_O[c]], writes=[r_O[c]])
                    else:
                        if d == 0:
                            P.op("act", lambda e: e.activation(out=oc, in_=cx.ps[po][:, 0:256], func=AF.Copy), reads=[cx.rps[po]], writes=[r_O[c]])
                        else:
                            P.op("dve", lambda e: e.tensor_tensor(out=oc, in0=oc, in1=cx.ps[po][:, 0:256], op=ALU.add), reads=[cx.rps[po], r_O[c]], writes=[r_O[c]])
            for i in range(NT):
                j2 = i % 2
                hs = hst[j2]
                r_hs = r_hst[j2]
                oc = Oacc[:, i, :]
                P.op("dve", lambda e: e.bn_stats(out=hs[:, 0:6], in_=oc), reads=[r_O[i]], writes=[r_hs])
                P.op("dve", lambda e: e.bn_aggr(out=hs[:, 6:8], in_=hs[:, 0:6]), reads=[r_hs], writes=[r_hs])
                P.op("dve", lambda e: e.tensor_scalar(out=hs[:, 8:9], in0=hs[:, 7:8], scalar1=LN_EPS, scalar2=None, op0=ALU.add), reads=[r_hs], writes=[r_hs])
                P.op("act", lambda e: e.activation(out=hs[:, 8:9], in_=hs[:, 8:9], func=AF.Sqrt), reads=[r_hs], writes=[r_hs])
                P.op("dve", lambda e: e.reciprocal(out=hs[:, 9:10], in_=hs[:, 8:9]), reads=[r_hs], writes=[r_hs])
                x_ = xn[j2]
                P.op("dve", lambda e: e.tensor_scalar(out=x_[:], in0=oc, scalar1=hs[:, 6:7], scalar2=hs[:, 9:10], op0=ALU.subtract, op1=ALU.mult),
                     reads=[r_O[i], r_hs], writes=[r_xn[j2]])
                P.op("pool", lambda e: e.tensor_tensor(out=x_[:], in0=x_[:], in1=ng[:], op=ALU.mult), reads=[r_xn[j2], r_ng], writes=[r_xn[j2]])
                pg = 6 + j2
                for k in range(NK):
                    P.op("pe", lambda e, k=k: e.matmul(cx.ps[pg][:, 0:256], lhsT=XT[:, k, i * 128:(i + 1) * 128], rhs=wor[:, k, :],
                                                       start=(k == 0), stop=(k == NK - 1)), reads=[r_wor, allXA[i]], writes=[cx.rps[pg]])
                ga = gact[j2]
                P.op("act", lambda e: e.activation(out=ga[:], in_=cx.ps[pg][:, 0:256], func=(AF.Sigmoid if ml else AF.Silu)),
                     reads=[cx.rps[pg]], writes=[r_gact[j2]])
                gb_ = gbt[j2]
                P.op("dve", lambda e: e.tensor_tensor(out=gb_[:], in0=x_[:], in1=ga[:], op=ALU.mult), reads=[r_xn[j2], r_gact[j2]], writes=[r_gbt[j2]])
                pt = 4 + j2
                pst = cx.ps[pt][:].bitcast(BF16)
                for j in range(2):
                    P.op("pe", lambda e, j=j: e.transpose(out=pst[:, j * 128:(j + 1) * 128], in_=gb_[:, j * 128:(j + 1) * 128], identity=cx.identb[:]),
                         reads=[r_gbt[j2], cx.rcst], writes=[cx.rps[pt]])
                gT_ = gTt[j2]
                P.op("act", lambda e: e.activation(out=gT_[:], in_=pst[:, 0:256].rearrange("p (a b) -> p a b", a=2), func=AF.Copy),
                     reads=[cx.rps[pt]], writes=[r_gTt[j2]])
                for hh in range(2):
                    pb = (i * 2 + hh) % 4
                    for j in range(2):
                        P.op("pe", lambda e, j=j, hh=hh, pb=pb: e.matmul(cx.ps[pb][:, :], lhsT=gT_[:, j, :], rhs=wo[:, j, hh * 512:(hh + 1) * 512],
                                                                         start=(j == 0), stop=(j == 1)), reads=[r_gTt[j2], r_wo], writes=[cx.rps[pb]])
                    P.op("dve", lambda e, hh=hh, pb=pb: e.tensor_tensor(out=cx.R[:, i, hh * 512:(hh + 1) * 512], in0=cx.R[:, i, hh * 512:(hh + 1) * 512],
                                                                        in1=cx.ps[pb][:, :], op=ALU.add), reads=[cx.rps[pb], cx.rR[i]], writes=[cx.rR[i]])
    P.barrier()


def declare_inputs(nc, nseq, names_shapes):
    aps = {}
    for name, shape, dt in names_shapes:
        aps[name] = nc.dram_tensor(name, list(shape), dt, kind="ExternalInput").ap()
    return aps


WEIGHT_SPECS = [
    ("mlstm_w_in", (1, 1024, 3088)), ("mlstm_gate_b", (1, 2, 2, 4)), ("mlstm_norm_g", (1, 1024)), ("mlstm_w_out", (1, 1024, 1024)),
    ("gla_w_in", (1, 1024, 3104)), ("gla_gate_w", (1, 2, 16, 512)), ("gla_gate_b", (1, 2, 512)), ("gla_norm_g", (1, 1024)),
    ("gla_w_out", (1, 1024, 1024)),
    ("lru_w_in", (1, 1024, 2048)), ("lru_conv_w", (1, 4, 1024)), ("lru_conv_b", (1, 1024)),
    ("lru_gate_a_w", (1, 2, 4, 256, 256)), ("lru_gate_a_b", (1, 2, 1024)), ("lru_gate_x_w", (1, 2, 4, 256, 256)),
    ("lru_gate_x_b", (1, 2, 1024)), ("lru_lambda", (1, 2, 1024)), ("lru_w_out", (1, 1024, 1024)),
    ("mla_w_in", (1, 1024, 704)), ("mla_q_norm_g", (1, 384)), ("mla_kv_norm_g", (1, 256)), ("mla_w_uq", (1, 384, 1536)),
    ("mla_w_ukv", (1, 256, 2048)), ("mla_w_out", (1, 1024, 1024)),
    ("moe_router", (4, 1024, 16)), ("moe_w_gate", (4, 16, 1024, 1024)), ("moe_w_up", (4, 16, 1024, 1024)),
    ("moe_w_down", (4, 16, 1024, 1024)), ("ln_g", (4, 2, 1024)), ("ln_b", (4, 2, 1024)),
]


def build_program(nseq, stages):
    nc = bass.Bass("TRN2", target_bir_lowering=False)
    specs = [("x", (nseq, S, D), F32), ("positions", (nseq, S), I32)]
    specs += [(n, s, F32) for n, s in WEIGHT_SPECS]
    specs += [("consts", (128, C_END), F32), ("consts16", (16, 16 * 128), F32)]
    A = declare_inputs(nc, nseq, specs)
    out = nc.dram_tensor("out", [nseq, S, D], F32, kind="ExternalOutput").ap()
    with contextlib.ExitStack() as stack:
        P = Prog(nc, stack)
        cx = setup_ctx(nc, stack, P)
        cx.A = A
        load_consts(cx, A["consts"], A["consts16"])
        wr = stack.enter_context(nc.sbuf_tensor("wr", [128, NK, NE], BF16))
        rwr = Reg()
        lnscr = None
        for s in range(nseq):
            for stg in stages:
                kind = stg[0]
                if kind == "load":
                    xin = A["x"][s].rearrange("(i p) d -> p i d", p=128)
                    for i in range(NT):
                        P.dma("sp", cx.R[:, i, :], xin[:, i, :], writes=[cx.rR[i]])
                elif kind == "ln":
                    _, L, j, mode = stg
                    if mode == "xb":
                        P.dma("pool", wr[:], A["moe_router"][L].rearrange("(k p) e -> p k e", p=128), writes=[rwr])
                    emit_ln(cx, A["ln_g"][L, j], A["ln_b"][L, j], mode, wr=wr, rwr=rwr, do_ln=True,
                            out_ap=out[s].rearrange("(i p) d -> p i d", p=128), scratch=lnscr)
                elif kind == "prep":
                    _, L, mode = stg
                    if mode == "xb":
                        P.dma("pool", wr[:], A["moe_router"][L].rearrange("(k p) e -> p k e", p=128), writes=[rwr])
                    emit_ln(cx, None, None, mode, wr=wr, rwr=rwr, do_ln=False, scratch=lnscr)
                elif kind == "moe":
                    _, L = stg
                    emit_moe(cx, A["moe_w_gate"][L], A["moe_w_up"][L], A["moe_w_down"][L])
                elif kind == "mla":
                    emit_mla(cx, s)
                elif kind in ("mlstm", "gla"):
                    emit_linattn(cx, kind)
                elif kind == "lru":
                    emit_lru(cx)
                elif kind == "store":
                    oo = out[s].rearrange("(i p) d -> p i d", p=128)
                    for i in range(NT):
                        P.dma("sp", oo[:, i, :], cx.R[:, i, :], reads=[cx.rR[i]])
                else:
                    raise ValueError(kind)
        P.barrier(engines=["sp"])
        print("instructions", P.ninstr, "waits", P.nwaits)
    return nc


MIXERS = ("mlstm", "gla", "lru", "mla")


def full_stages():
    st = [("load",), ("prep", 0, "xt")]
    for L in range(DEPTH):
        st.append((MIXERS[L % 4],))
        st.append(("ln", L, 0, "xb"))
        st.append(("moe", L))
        st.append(("ln", L, 1, "xt" if L < DEPTH - 1 else "none"))
    return st


NSEQ_PER_LAUNCH = 4
_PROG_CACHE = {}


def kernel(**inputs):
    nseq = NSEQ_PER_LAUNCH
    x = np.ascontiguousarray(np.asarray(inputs["x"], dtype=np.float32))
    pos = np.ascontiguousarray(np.asarray(inputs["positions"], dtype=np.int32))
    B = x.shape[0]
    per_core = B // NCORES
    consts = make_consts()
    consts16 = make_consts16()
    weights = {n: np.ascontiguousarray(np.asarray(inputs[n], dtype=np.float32)) for n, _ in WEIGHT_SPECS}
    out = np.empty((B, S, D), np.float32)
    for l0 in range(0, per_core, nseq):
        if nseq not in _PROG_CACHE:
            _PROG_CACHE[nseq] = build_program(nseq, full_stages())
        nc = _PROG_CACHE[nseq]
        in_maps = []
        for c in range(NCORES):
            b0 = c * per_core + l0
            m = {"x": x[b0:b0 + nseq], "positions": pos[b0:b0 + nseq], "consts": consts, "consts16": consts16}
            m.update(weights)
            in_maps.append(m)
        res = run_bass_kernel_spmd(nc, in_maps, core_ids=list(range(NCORES)))
        for c in range(NCORES):
            b0 = c * per_core + l0
            out[b0:b0 + nseq] = np.asarray(res.results[c]["out"]).reshape(nseq, S, D)
    return out
```

```python
import contextlib
import numpy as np
import concourse.bass as bass
import concourse.mybir as mybir
from concourse.bass_utils import run_bass_kernel_spmd

F32 = mybir.dt.float32
BF16 = mybir.dt.bfloat16
I32 = mybir.dt.int32
AF = mybir.ActivationFunctionType
ALU = mybir.AluOpType
AX = mybir.AxisListType

D = 1024
S = 2048
NT = 16
NK = 8
DEPTH = 4
ALPHA = (2 * DEPTH) ** 0.25
LN_EPS = 1e-5
NE = 16
CAP = 256
NCORES = 8
SEQ_PER_CORE = 4

SAME_ENGINE_RAW = True


_UN = [0]


def un(name):
    _UN[0] += 1
    return "%s_%d" % (name, _UN[0])


class Reg:
    __slots__ = ("w", "r", "name")

    def __init__(self, name=""):
        self.w = None
        self.r = {}
        self.name = name


class Prog:
    ENG = ("pe", "act", "dve", "pool", "sp")
    KDMA = 8

    def __init__(self, nc, stack):
        self.nc = nc
        self.eng = {"pe": nc.tensor, "act": nc.scalar, "dve": nc.vector, "pool": nc.gpsimd, "sp": nc.sync}
        self.sem = {}
        for e in self.ENG:
            self.sem[("c", e)] = stack.enter_context(nc.semaphore("s_" + e))
        self.dq = ("sp", "pool", "act")
        for q in self.dq:
            for k in range(self.KDMA):
                self.sem[("d", q, k)] = stack.enter_context(nc.semaphore("d_%s%d" % (q, k)))
        self.cnt = {e: 0 for e in self.ENG}
        self.dcnt = {q: 0 for q in self.dq}
        self.known = {e: {} for e in self.ENG}
        self.nwaits = 0
        self.ninstr = 0

    def _wait(self, eng, key, val):
        if self.known[eng].get(key, 0) >= val:
            return
        self.eng[eng].wait_ge(self.sem[key], val)
        self.known[eng][key] = val
        self.nwaits += 1

    def _deps(self, eng, reads, writes):
        me = ("c", eng)
        for r in reads:
            if r.w is not None:
                key, val = r.w
                if key == me:
                    if eng != "pe" and SAME_ENGINE_RAW:
                        self._wait(eng, key, val)
                else:
                    self._wait(eng, key, val)
        for w in writes:
            if w.w is not None:
                key, val = w.w
                if key != me:
                    self._wait(eng, key, val)
            for key, val in w.r.items():
                if key != me:
                    self._wait(eng, key, val)

    def _update(self, tok, reads, writes):
        key, val = tok
        for r in reads:
            if r.r.get(key, 0) < val:
                r.r[key] = val
        for w in writes:
            w.w = tok
            w.r = {}

    def op(self, eng, fn, reads=(), writes=()):
        self._deps(eng, reads, writes)
        ins = fn(self.eng[eng])
        self.cnt[eng] += 1
        self.ninstr += 1
        tok = (("c", eng), self.cnt[eng])
        ins.then_inc(self.sem[tok[0]], 1)
        self._update(tok, reads, writes)
        return tok

    def dma(self, q, out, in_, reads=(), writes=(), **kw):
        k = self.dcnt[q]
        slot = k % self.KDMA
        key = ("d", q, slot)
        if k >= self.KDMA:
            self._wait(q, key, 16 * (k // self.KDMA))
        self._deps(q, reads, writes)
        ins = self.eng[q].dma_start(out=out, in_=in_, **kw)
        tok = (key, 16 * (k // self.KDMA + 1))
        ins.then_inc(self.sem[key], 16)
        self.dcnt[q] = k + 1
        self.ninstr += 1
        self._update(tok, reads, writes)
        return tok

    def latest_tokens(self):
        toks = []
        for e in self.ENG:
            if self.cnt[e] > 0:
                toks.append((("c", e), self.cnt[e]))
        for q in self.dq:
            k = self.dcnt[q]
            for slot in range(self.KDMA):
                n = (k - slot + self.KDMA - 1) // self.KDMA
                if n > 0:
                    toks.append((("d", q, slot), 16 * n))
        return toks

    def barrier(self, engines=None):
        toks = self.latest_tokens()
        for e in (engines or self.ENG):
            for key, val in toks:
                if key != ("c", e):
                    self._wait(e, key, val)


C_IDENT = 0
C_IOTA = 128
C_PIDX = 384
C_TRIU = 386
C_TRIL = 514
C_ONES = 642
C_FREQ = 770
C_LNQ = 802
C_END = 804


def make_consts():
    c = np.zeros((128, C_END), np.float32)
    c[:, C_IDENT:C_IDENT + 128] = np.eye(128, dtype=np.float32)
    c[:, C_IOTA:C_IOTA + 256] = np.arange(256, dtype=np.float32)[None, :]
    c[:, C_PIDX] = np.arange(128)
    c[:, C_PIDX + 1] = np.arange(128) + 128
    s = np.arange(128)[:, None]
    t = np.arange(128)[None, :]
    c[:, C_TRIU:C_TRIU + 128] = (s <= t)
    c[:, C_TRIL:C_TRIL + 128] = (s >= t)
    c[:, C_ONES:C_ONES + 128] = 1.0
    c[:, C_LNQ] = np.float32(np.log(128.0 ** -0.5))
    c[:, C_FREQ:C_FREQ + 32] = (10000.0 ** (-np.arange(32, dtype=np.float32) / np.float32(32))).astype(np.float32)[None, :]
    return c


def make_consts16():
    c = np.zeros((16, 16 * 128), np.float32)
    for e in range(16):
        c[e, e * 128:(e + 1) * 128] = 1.0
    return c


class Ctx:
    pass


def setup_ctx(nc, stack, P):
    cx = Ctx()
    cx.nc = nc
    cx.P = P
    sb = lambda name, shape, dt: stack.enter_context(nc.sbuf_tensor(name, shape, dt))
    cx.R = sb("R", [128, NT, D], F32)
    cx.rR = [Reg("R%d" % i) for i in range(NT)]
    cx.XA = sb("XA", [128, NT * D], BF16)
    cx.rXA = [Reg("XA%d" % i) for i in range(NT)]
    cx.cst = sb("cst", [128, C_END], F32)
    cx.rcst = Reg("cst")
    cx.identb = sb("identb", [128, 128], BF16)
    cx.logits = sb("logits", [128, NT, NE], F32)
    cx.rlogits = [Reg("lg%d" % i) for i in range(NT)]
    cx.ps = []
    cx.rps = []
    for b in range(8):
        cx.ps.append(stack.enter_context(nc.psum_tensor("ps%d" % b, [128, 512], F32)))
        cx.rps.append(Reg("ps%d" % b))
    return cx


def load_consts(cx, consts_ap, consts16_ap):
    P = cx.P
    P.dma("sp", cx.cst[:], consts_ap[:, :], writes=[cx.rcst])
    P.op("dve", lambda e: e.tensor_copy(out=cx.identb[:], in_=cx.cst[:, C_IDENT:C_IDENT + 128]),
         reads=[cx.rcst], writes=[cx.rcst])


def emit_ln(cx, lng_ap, lnb_ap, mode, wr=None, rwr=None, do_ln=True, out_ap=None, scratch=None):
    P = cx.P
    nc = cx.nc
    with contextlib.ExitStack() as lst:
        _emit_ln(cx, lst, lng_ap, lnb_ap, mode, wr, rwr, do_ln, out_ap)
    P.barrier()


def _emit_ln(cx, lst, lng_ap, lnb_ap, mode, wr, rwr, do_ln, out_ap):
    P = cx.P
    nc = cx.nc
    cx.lng = lst.enter_context(nc.sbuf_tensor(un("lng"), [128, 2, D], F32))
    cx.rlng = Reg("lng")
    tmpT, rtmpT, stat, rstat, xbt, rxbt = alloc_ln_scratch(cx, lst)
    if do_ln:
        P.dma("sp", cx.lng[:, 0, :], lng_ap.partition_broadcast(128), writes=[cx.rlng])
        P.dma("sp", cx.lng[:, 1, :], lnb_ap.partition_broadcast(128), writes=[cx.rlng])
    for i in range(NT):
        Ri = cx.R[:, i, :]
        rRi = cx.rR[i]
        st = stat[i % 2]
        rst = rstat[i % 2]
        if do_ln:
            for h in range(2):
                P.op("dve", lambda e, h=h: e.bn_stats(out=st[:, h * 6:(h + 1) * 6], in_=Ri[:, h * 512:(h + 1) * 512]),
                     reads=[rRi], writes=[rst])
            P.op("dve", lambda e: e.bn_aggr(out=st[:, 12:14], in_=st[:, 0:12]), reads=[rst], writes=[rst])
            P.op("dve", lambda e: e.tensor_scalar(out=st[:, 14:15], in0=st[:, 13:14], scalar1=LN_EPS, scalar2=None, op0=ALU.add),
                 reads=[rst], writes=[rst])
            P.op("act", lambda e: e.activation(out=st[:, 14:15], in_=st[:, 14:15], func=AF.Sqrt), reads=[rst], writes=[rst])
            P.op("dve", lambda e: e.reciprocal(out=st[:, 15:16], in_=st[:, 14:15]), reads=[rst], writes=[rst])
            P.op("dve", lambda e: e.tensor_scalar(out=Ri, in0=Ri, scalar1=st[:, 12:13], scalar2=st[:, 15:16],
                                                   op0=ALU.subtract, op1=ALU.mult), reads=[rRi, rst], writes=[rRi])
            P.op("pool", lambda e: e.tensor_tensor(out=Ri, in0=Ri, in1=cx.lng[:, 0, :], op=ALU.mult),
                 reads=[rRi, cx.rlng], writes=[rRi])
            P.op("pool", lambda e: e.tensor_tensor(out=Ri, in0=Ri, in1=cx.lng[:, 1, :], op=ALU.add),
                 reads=[rRi, cx.rlng], writes=[rRi])
        if mode == "none":
            if out_ap is not None:
                P.dma("sp", out_ap[:, i, :], Ri, reads=[rRi])
            continue
        if mode == "xb":
            xb = cx.XA[:, i * D:(i + 1) * D]
            rxb = cx.rXA[i]
        else:
            xb = xbt[i % 2][:]
            rxb = rxbt[i % 2]
        P.op("act", lambda e: e.activation(out=xb, in_=Ri, func=AF.Copy), reads=[rRi], writes=[rxb])
        P.op("pool", lambda e: e.tensor_scalar(out=Ri, in0=Ri, scalar1=float(ALPHA), scalar2=0.0, op0=ALU.mult, op1=ALU.add),
             reads=[rRi], writes=[rRi])
        pb = 6 + (i % 2)
        pst = cx.ps[pb][:].bitcast(BF16)
        for k in range(NK):
            P.op("pe", lambda e, k=k: e.transpose(out=pst[:, k * 128:(k + 1) * 128], in_=xb[:, k * 128:(k + 1) * 128],
                                                  identity=cx.identb[:]),
                 reads=[rxb, cx.rcst], writes=[cx.rps[pb]])
        if mode == "xt":
            xt = cx.XA[:].rearrange("p (k t) -> p k t", k=NK)
            P.op("dve", lambda e: e.tensor_copy(out=xt[:, :, i * 128:(i + 1) * 128],
                                                in_=pst.rearrange("p (k t) -> p k t", k=NK)),
                 reads=[cx.rps[pb]], writes=[cx.rXA[i]])
        else:
            tt = tmpT[i % 2]
            rtt = rtmpT[i % 2]
            P.op("dve", lambda e: e.tensor_copy(out=tt[:], in_=pst), reads=[cx.rps[pb]], writes=[rtt])
            lb = 4 + (i % 2)
            for k in range(NK):
                P.op("pe", lambda e, k=k: e.matmul(cx.ps[lb][:, 0:NE], lhsT=tt[:, k * 128:(k + 1) * 128], rhs=wr[:, k, :],
                                                   start=(k == 0), stop=(k == NK - 1)),
                     reads=[rtt, rwr], writes=[cx.rps[lb]])
            P.op("act", lambda e: e.activation(out=cx.logits[:, i, :], in_=cx.ps[lb][:, 0:NE], func=AF.Copy),
                 reads=[cx.rps[lb]], writes=[cx.rlogits[i]])


def alloc_ln_scratch(cx, stack):
    nc = cx.nc
    tmpT = [stack.enter_context(nc.sbuf_tensor(un("tmpT"), [128, D], BF16)) for j in range(2)]
    rtmpT = [Reg(), Reg()]
    stat = [stack.enter_context(nc.sbuf_tensor(un("stat"), [128, 16], F32)) for j in range(2)]
    rstat = [Reg(), Reg()]
    xbt = [stack.enter_context(nc.sbuf_tensor(un("xbt"), [128, D], BF16)) for j in range(2)]
    rxbt = [Reg(), Reg()]
    return (tmpT, rtmpT, stat, rstat, xbt, rxbt)


def emit_moe(cx, wg_ap, wu_ap, wd_ap):
    P = cx.P
    nc = cx.nc
    with contextlib.ExitStack() as st:
        sb = lambda name, shape, dt: st.enter_context(nc.sbuf_tensor(un(name), shape, dt))
        aff = sb("aff", [128, NT, NE], F32)
        r_aff = Reg()
        sm = sb("sm", [128, NT, 4], F32)
        AG = sb("AG", [128, NT, NE, 2], BF16)
        r_AG = Reg()
        slot = sb("slot", [128, NT, NE], F32)
        r_slot = Reg()
        slotT = sb("slotT", [16, S], F32)
        r_slotT = Reg()
        m8 = sb("m8", [16, 8], F32)
        r_m8 = Reg()
        wbuf = [sb("wbuf%d" % j, [128, NK, D], BF16) for j in range(3)]
        r_wbuf = [Reg() for j in range(3)]
        Pm = sb("Pm", [128, NT, CAP], BF16)
        r_Pm = Reg()
        PT = sb("PT", [128, 2, S], BF16)
        r_PT = Reg()
        xinT = sb("xinT", [128, NK, CAP], BF16)
        r_xinT = Reg()
        hT = sb("hT", [128, NK, CAP], BF16)
        r_hT = Reg()
        sg = [sb("sg%d" % j, [128, CAP], F32) for j in range(2)]
        r_sg = [Reg(), Reg()]
        Oe = sb("Oe", [128, 2, D], BF16)
        r_Oe = Reg()
        gs = sb("gs", [128, 4], F32)
        r_gs = Reg()
        affT = Pm[:].rearrange("p a b -> p (a b)").bitcast(F32)[0:16, :]
        r_affT = Reg()
        work = PT[:].rearrange("p a b -> p (a b)").bitcast(F32)[0:16, :]
        r_work = Reg()

        c16 = sb("c16", [16, 16 * 128], F32)
        P.dma("sp", c16[:], cx.A["consts16"][:, :], writes=[cx.rcst])
        ident = cx.cst[:, C_IDENT:C_IDENT + 128]

        rl = cx.rlogits
        P.op("dve", lambda e: e.tensor_reduce(out=sm[:, :, 0], in_=cx.logits[:], axis=AX.X, op=ALU.max),
             reads=rl, writes=[r_aff])
        P.op("dve", lambda e: e.tensor_tensor(out=aff[:], in0=cx.logits[:],
                                              in1=sm[:, :, 0:1].to_broadcast([128, NT, NE]), op=ALU.subtract),
             reads=rl + [r_aff], writes=[r_aff])
        P.op("act", lambda e: e.activation(out=aff[:], in_=aff[:], func=AF.Exp), reads=[r_aff], writes=[r_aff])
        P.op("dve", lambda e: e.tensor_reduce(out=sm[:, :, 1], in_=aff[:], axis=AX.X, op=ALU.add),
             reads=[r_aff], writes=[r_aff])
        P.op("dve", lambda e: e.reciprocal(out=sm[:, :, 2], in_=sm[:, :, 1]), reads=[r_aff], writes=[r_aff])
        P.op("dve", lambda e: e.tensor_tensor(out=aff[:], in0=aff[:],
                                              in1=sm[:, :, 2:3].to_broadcast([128, NT, NE]), op=ALU.mult),
             reads=[r_aff], writes=[r_aff])
        P.op("dve", lambda e: e.tensor_copy(out=AG[:, :, :, 0], in_=aff[:]), reads=[r_aff], writes=[r_AG])
        P.op("dve", lambda e: e.tensor_tensor(out=AG[:, :, :, 1], in0=aff[:], in1=AG[:, :, :, 0], op=ALU.subtract),
             reads=[r_aff, r_AG], writes=[r_AG])
        for g in range(4):
            for j in range(4):
                i = g * 4 + j
                P.op("pe", lambda e, i=i, j=j, g=g: e.transpose(out=cx.ps[g][0:16, j * 128:(j + 1) * 128],
                                                                in_=aff[:, i, :], identity=ident),
                     reads=[r_aff, cx.rcst], writes=[cx.rps[g]])
            P.op("act", lambda e, g=g: e.activation(out=affT[:, g * 512:(g + 1) * 512], in_=cx.ps[g][0:16, :], func=AF.Copy),
                 reads=[cx.rps[g]], writes=[r_affT])
        P.op("pool", lambda e: e.tensor_copy(out=work[:], in_=affT[:]), reads=[r_affT], writes=[r_work])
        nround = CAP // 8
        for r in range(nround):
            P.op("dve", lambda e: e.max(out=m8[:], in_=work[:]), reads=[r_work], writes=[r_m8])
            if r < nround - 1:
                P.op("dve", lambda e: e.match_replace(out=work[:], in_to_replace=m8[:], in_values=work[:], imm_value=-1.0),
                     reads=[r_work, r_m8], writes=[r_work])
        P.op("dve", lambda e: e.tensor_scalar(out=work[:], in0=affT[:], scalar1=m8[:, 7:8], scalar2=None, op0=ALU.is_ge),
             reads=[r_affT, r_m8], writes=[r_work])
        P.op("dve", lambda e: e.tensor_tensor_scan(out=slotT[:], data0=cx.cst[0:16, C_ONES:C_ONES + 1].to_broadcast([16, S]), data1=work[:], initial=0.0,
                                                    op0=ALU.mult, op1=ALU.add),
             reads=[r_work, cx.rcst], writes=[r_slotT])
        P.op("dve", lambda e: e.tensor_tensor(out=slotT[:], in0=slotT[:], in1=work[:], op=ALU.mult),
             reads=[r_work, r_slotT], writes=[r_slotT])
        P.op("dve", lambda e: e.tensor_scalar(out=slotT[:], in0=slotT[:], scalar1=-1.0, scalar2=None, op0=ALU.add),
             reads=[r_slotT], writes=[r_slotT])
        for i in range(NT):
            P.op("pe", lambda e, i=i: e.transpose(out=cx.ps[5][:, i * NE:(i + 1) * NE], in_=slotT[:, i * 128:(i + 1) * 128],
                                                  identity=ident[0:16, 0:16]),
                 reads=[r_slotT, cx.rcst], writes=[cx.rps[5]])
        P.op("act", lambda e: e.activation(out=slot[:].rearrange("p a b -> p (a b)"), in_=cx.ps[5][:, 0:NT * NE], func=AF.Copy),
             reads=[cx.rps[5]], writes=[r_slot])

        P.barrier()
        r_xk = [Reg() for k in range(NK)]
        r_hj = [Reg() for j in range(NK)]
        r_Oq = [[Reg() for n in range(2)] for c in range(2)]
        slotTb = sb("slotTb", [16, S], BF16)
        c16b = sb("c16b", [16, 16 * 128], BF16)
        P.op("act", lambda e: e.activation(out=slotTb[:], in_=slotT[:], func=AF.Copy), reads=[r_slotT], writes=[r_slotT])
        P.op("act", lambda e: e.activation(out=c16b[:], in_=c16[:], func=AF.Copy), reads=[cx.rcst], writes=[cx.rcst])
        wq = [0]

        def load_w(ap2d):
            j = wq[0] % 3
            wq[0] += 1
            P.dma("pool", wbuf[j][:], ap2d.rearrange("(k p) f -> p k f", p=128), writes=[r_wbuf[j]])
            return j

        def build_P(ex):
            for i in range(NT):
                P.op("dve", lambda e, i=i: e.tensor_scalar(out=Pm[:, i, :], in0=cx.cst[:, C_IOTA:C_IOTA + CAP],
                                                           scalar1=slot[:, i, ex:ex + 1], scalar2=None, op0=ALU.is_equal),
                     reads=[r_slot, cx.rcst], writes=[r_Pm])

        build_P(0)
        for ex in range(NE):
            jg = load_w(wg_ap[ex])
            ju = load_w(wu_ap[ex])
            for k in range(NK):
                pb = 4 + (k % 2)
                for i in range(NT):
                    P.op("pe", lambda e, i=i, k=k, pb=pb: e.matmul(cx.ps[pb][:, 0:CAP],
                                                                   lhsT=cx.XA[:, i * D + k * 128:i * D + (k + 1) * 128],
                                                                   rhs=Pm[:, i, :], start=(i == 0), stop=(i == NT - 1)),
                         reads=[cx.rXA[i], r_Pm], writes=[cx.rps[pb]])
                P.op("act", lambda e, k=k, pb=pb: e.activation(out=xinT[:, k, :], in_=cx.ps[pb][:, 0:CAP], func=AF.Copy),
                     reads=[cx.rps[pb]], writes=[r_xk[k]])
            for c in range(2):
                for i in range(NT):
                    P.op("pe", lambda e, i=i, c=c: e.matmul(cx.ps[6][:, c * 2:c * 2 + 2], lhsT=Pm[:, i, c * 128:(c + 1) * 128],
                                                            rhs=AG[:, i, ex, :], start=(i == 0), stop=(i == NT - 1)),
                         reads=[r_Pm, r_AG], writes=[cx.rps[6]])
            P.op("dve", lambda e: e.tensor_copy(out=gs[:, 0:4], in_=cx.ps[6][:, 0:4]), reads=[cx.rps[6]], writes=[r_gs])
            P.op("dve", lambda e: e.tensor_tensor(out=gs[:, 0:1], in0=gs[:, 0:1], in1=gs[:, 1:2], op=ALU.add),
                 reads=[r_gs], writes=[r_gs])
            P.op("dve", lambda e: e.tensor_tensor(out=gs[:, 1:2], in0=gs[:, 2:3], in1=gs[:, 3:4], op=ALU.add),
                 reads=[r_gs], writes=[r_gs])
            if ex + 1 < NE:
                build_P(ex + 1)
            for j in range(NK):
                pg = 4 + (j % 2)
                pu = 6 + (j % 2)
                for k in range(NK):
                    P.op("pe", lambda e, j=j, k=k, pg=pg: e.matmul(cx.ps[pg][:, 0:CAP], lhsT=wbuf[jg][:, k, j * 128:(j + 1) * 128],
                                                                   rhs=xinT[:, k, :], start=(k == 0), stop=(k == NK - 1)),
                         reads=[r_wbuf[jg], r_xk[k]], writes=[cx.rps[pg]])
                for k in range(NK):
                    P.op("pe", lambda e, j=j, k=k, pu=pu: e.matmul(cx.ps[pu][:, 0:CAP], lhsT=wbuf[ju][:, k, j * 128:(j + 1) * 128],
                                                                   rhs=xinT[:, k, :], start=(k == 0), stop=(k == NK - 1)),
                         reads=[r_wbuf[ju], r_xk[k]], writes=[cx.rps[pu]])
                P.op("act", lambda e, j=j, pg=pg: e.activation(out=sg[j % 2][:], in_=cx.ps[pg][:, 0:CAP], func=AF.Silu),
                     reads=[cx.rps[pg]], writes=[r_sg[j % 2]])
                P.op("dve", lambda e, j=j, pu=pu: e.tensor_tensor(out=hT[:, j, :], in0=sg[j % 2][:], in1=cx.ps[pu][:, 0:CAP], op=ALU.mult),
                     reads=[cx.rps[pu], r_sg[j % 2]], writes=[r_hj[j]])
            jd = load_w(wd_ap[ex])
            for g in range(4):
                P.op("pe", lambda e, g=g: e.matmul(cx.ps[g][:, :], lhsT=c16b[:, ex * 128:(ex + 1) * 128],
                                                   rhs=slotTb[:, g * 512:(g + 1) * 512], start=True, stop=True),
                     reads=[r_slotT, cx.rcst], writes=[cx.rps[g]])
            for g in range(4):
                for c in range(2):
                    P.op("dve", lambda e, g=g, c=c: e.tensor_scalar(out=PT[:, c, g * 512:(g + 1) * 512], in0=cx.ps[g][:, :],
                                                                    scalar1=cx.cst[:, C_PIDX + c:C_PIDX + c + 1], scalar2=None,
                                                                    op0=ALU.is_equal),
                         reads=[cx.rps[g], cx.rcst], writes=[r_PT])
            for c in range(2):
                for n in range(2):
                    pb = 4 + ((c * 2 + n) % 4)
                    for j in range(NK):
                        P.op("pe", lambda e, j=j, c=c, n=n, pb=pb: e.matmul(cx.ps[pb][:, :], lhsT=hT[:, j, c * 128:(c + 1) * 128],
                                                                            rhs=wbuf[jd][:, j, n * 512:(n + 1) * 512],
                                                                            start=(j == 0), stop=(j == NK - 1)),
                             reads=[r_wbuf[jd], r_hj[j]], writes=[cx.rps[pb]])
                    P.op("act", lambda e, c=c, n=n, pb=pb: e.activation(out=Oe[:, c, n * 512:(n + 1) * 512], in_=cx.ps[pb][:, :],
                                                                        func=AF.Copy, scale=gs[:, c:c + 1]),
                         reads=[cx.rps[pb], r_gs], writes=[r_Oq[c][n]])
            for i in range(NT):
                for n in range(2):
                    pb = (i * 2 + n) % 4
                    for c in range(2):
                        P.op("pe", lambda e, i=i, n=n, c=c, pb=pb: e.matmul(cx.ps[pb][:, :], lhsT=PT[:, c, i * 128:(i + 1) * 128],
                                                                            rhs=Oe[:, c, n * 512:(n + 1) * 512],
                                                                            start=(c == 0), stop=(c == 1)),
                             reads=[r_PT, r_Oq[c][n]], writes=[cx.rps[pb]])
                    P.op("dve", lambda e, i=i, n=n, pb=pb: e.tensor_tensor(out=cx.R[:, i, n * 512:(n + 1) * 512],
                                                                           in0=cx.R[:, i, n * 512:(n + 1) * 512],
                                                                           in1=cx.ps[pb][:, :], op=ALU.add),
                         reads=[cx.rps[pb], cx.rR[i]], writes=[cx.rR[i]])
    P.barrier()


def xt_view(cx):
    return cx.XA[:].rearrange("p (k t) -> p k t", k=NK)


def small_load(cx, out, in_, reg):
    cx.P.dma("sp", out, in_, writes=[reg], allow_slow_non_contiguous=True)


def gelu_tanh(cx, out, src, t1, t2, r_src, r_t1, r_t2, r_out, extra_mul=None, r_extra=None):
    P = cx.P
    P.op("act", lambda e: e.activation(out=t1, in_=src, func=AF.Square), reads=[r_src], writes=[r_t1])
    P.op("dve", lambda e: e.tensor_scalar(out=t1, in0=t1, scalar1=0.044715, scalar2=1.0, op0=ALU.mult, op1=ALU.add),
         reads=[r_t1], writes=[r_t1])
    P.op("dve", lambda e: e.tensor_tensor(out=t1, in0=t1, in1=src, op=ALU.mult), reads=[r_t1, r_src], writes=[r_t1])
    P.op("act", lambda e: e.activation(out=t1, in_=t1, func=AF.Sigmoid, scale=1.5957691216057308), reads=[r_t1], writes=[r_t1])
    if extra_mul is None:
        P.op("dve", lambda e: e.tensor_tensor(out=out, in0=t1, in1=src, op=ALU.mult), reads=[r_t1, r_src], writes=[r_out])
    else:
        P.op("dve", lambda e: e.tensor_tensor(out=t2, in0=t1, in1=src, op=ALU.mult), reads=[r_t1, r_src], writes=[r_t2])
        P.op("pool", lambda e: e.tensor_tensor(out=out, in0=t2, in1=extra_mul, op=ALU.mult), reads=[r_t2, r_extra], writes=[r_out])


def emit_lru(cx):
    P = cx.P
    nc = cx.nc
    A = cx.A
    XT = xt_view(cx)
    with contextlib.ExitStack() as st:
        sb = lambda name, shape, dt: st.enter_context(nc.sbuf_tensor(un(name), shape, dt))
        cw = sb("cw", [128, NK, 4], F32)
        cb = sb("cb", [128, NK], F32)
        gab = sb("gab", [128, 2, NK], F32)
        gxb = sb("gxb", [128, 2, NK], F32)
        lam = sb("lam", [128, 2, NK], F32)
        c8 = sb("c8", [128, 2, NK], F32)
        c16 = sb("c16", [128, 2, NK], F32)
        r_par = Reg()
        for j in range(4):
            small_load(cx, cw[:, :, j], A["lru_conv_w"][0, j].rearrange("(c p) -> p c", p=128), r_par)
        small_load(cx, cb[:], A["lru_conv_b"][0].rearrange("(c p) -> p c", p=128), r_par)
        for g in range(2):
            small_load(cx, gab[:, g, :], A["lru_gate_a_b"][0, g].rearrange("(c p) -> p c", p=128), r_par)
        for g in range(2):
            small_load(cx, gxb[:, g, :], A["lru_gate_x_b"][0, g].rearrange("(c p) -> p c", p=128), r_par)
        for g in range(2):
            small_load(cx, lam[:, g, :], A["lru_lambda"][0, g].rearrange("(c p) -> p c", p=128), r_par)
        P.op("act", lambda e: e.activation(out=c8[:], in_=lam[:], func=AF.Exp, scale=-1.0), reads=[r_par], writes=[r_par])
        P.op("act", lambda e: e.activation(out=c8[:], in_=c8[:], func=AF.Ln, bias=1.0), reads=[r_par], writes=[r_par])
        P.op("dve", lambda e: e.tensor_scalar(out=c16[:], in0=c8[:], scalar1=-16.0, scalar2=None, op0=ALU.mult), reads=[r_par], writes=[r_par])
        P.op("dve", lambda e: e.tensor_scalar(out=c8[:], in0=c8[:], scalar1=-8.0, scalar2=None, op0=ALU.mult), reads=[r_par], writes=[r_par])

        wgb = sb("wgb", [128, NK, 256], BF16)
        r_wgb = Reg()
        wu = sb("wu", [128, NK, 256], BF16)
        r_wu = Reg()
        wg4 = [[sb("wga", [128, 2, 256], BF16) for d in range(2)] for ax in range(2)]
        r_wg4 = [[Reg() for d in range(2)] for ax in range(2)]
        wo = sb("wo", [128, D], BF16)
        r_wo = Reg()
        upad = sb("upad", [128, S + 4], F32)
        r_upad = Reg()
        uc = sb("uc", [128, 2, S], F32)
        r_uc = [Reg(), Reg()]
        ucb = sb("ucb", [128, 2, S], BF16)
        r_ucb = [Reg(), Reg()]
        rbuf = sb("rbuf", [128, S], F32)
        r_rbuf = Reg()
        abuf = sb("abuf", [128, S], F32)
        r_abuf = Reg()
        tmp = sb("tmp", [128, S], F32)
        r_tmp = Reg()
        hsum = sb("hsum", [128, S], F32)
        r_hsum = Reg()
        gtc = sb("gtc", [128, S], BF16)
        r_gtc = Reg()
        P.op("pool", lambda e: e.memset(upad[:], 0.0), writes=[r_upad])

        def psum4(base):
            return [cx.ps[base + j] for j in range(4)], [cx.rps[base + j] for j in range(4)]

        pcount = [0]

        def next_ps4():
            base = 4 * (pcount[0] % 2)
            pcount[0] += 1
            return psum4(base)

        for n in range(4):
            P.dma("pool", wgb[:], A["lru_w_in"][0][:, n * 256:(n + 1) * 256].rearrange("(k p) c -> p k c", p=128), writes=[r_wgb])
            P.dma("pool", wu[:], A["lru_w_in"][0][:, D + n * 256:D + (n + 1) * 256].rearrange("(k p) c -> p k c", p=128), writes=[r_wu])
            for ax, nm in enumerate(("lru_gate_a_w", "lru_gate_x_w")):
                for d in range(2):
                    P.dma("pool", wg4[ax][d][:], A[nm][0, d, n].rearrange("(i p) j -> p i j", p=128), writes=[r_wg4[ax][d]])
            for cc in range(2):
                c = 2 * n + cc
                ps, rps = next_ps4()
                for tb in range(4):
                    for k in range(NK):
                        P.op("pe", lambda e, tb=tb, k=k: e.matmul(ps[tb][:, :], lhsT=wu[:, k, cc * 128:(cc + 1) * 128],
                                                                  rhs=XT[:, k, tb * 512:(tb + 1) * 512], start=(k == 0), stop=(k == NK - 1)),
                             reads=[r_wu] + cx.rXA[tb * 4:(tb + 1) * 4], writes=[rps[tb]])
                    P.op("act", lambda e, tb=tb: e.activation(out=upad[:, 2 + tb * 512:2 + (tb + 1) * 512], in_=ps[tb][:, :], func=AF.Copy),
                         reads=[rps[tb]], writes=[r_upad])
                ucc = uc[:, cc, :]
                P.op("dve", lambda e: e.tensor_scalar(out=ucc, in0=upad[:, 0:S], scalar1=cw[:, c, 0:1], scalar2=cb[:, c:c + 1],
                                                       op0=ALU.mult, op1=ALU.add), reads=[r_upad, r_par], writes=[r_uc[cc]])
                for j in range(1, 4):
                    P.op("dve", lambda e, j=j: e.scalar_tensor_tensor(out=ucc, in0=upad[:, j:j + S], scalar=cw[:, c, j:j + 1], in1=ucc,
                                                                      op0=ALU.mult, op1=ALU.add), reads=[r_upad, r_par, r_uc[cc]], writes=[r_uc[cc]])
                P.op("pool", lambda e: e.tensor_copy(out=ucb[:, cc, :], in_=ucc), reads=[r_uc[cc]], writes=[r_ucb[cc]])
            for cc in range(2):
                c = 2 * n + cc
                ucc = uc[:, cc, :]
                for d in range(2):
                    ps, rps = next_ps4()
                    for tb in range(4):
                        for i in range(2):
                            P.op("pe", lambda e, tb=tb, i=i: e.matmul(ps[tb][:, :], lhsT=wg4[0][d][:, i, cc * 128:(cc + 1) * 128],
                                                                      rhs=ucb[:, i, tb * 512:(tb + 1) * 512], start=(i == 0), stop=(i == 1)),
                                 reads=[r_wg4[0][d], r_ucb[0], r_ucb[1]], writes=[rps[tb]])
                        P.op("act", lambda e, tb=tb: e.activation(out=rbuf[:, tb * 512:(tb + 1) * 512], in_=ps[tb][:, :], func=AF.Sigmoid,
                                                                  bias=gab[:, d, c:c + 1]), reads=[rps[tb], r_par], writes=[r_rbuf])
                    P.op("act", lambda e: e.activation(out=abuf[:], in_=rbuf[:], func=AF.Exp, scale=c8[:, d, c:c + 1]),
                         reads=[r_rbuf, r_par], writes=[r_abuf])
                    P.op("act", lambda e: e.activation(out=tmp[:], in_=rbuf[:], func=AF.Exp, scale=c16[:, d, c:c + 1]),
                         reads=[r_rbuf, r_par], writes=[r_tmp])
                    P.op("act", lambda e: e.activation(out=tmp[:], in_=tmp[:], func=AF.Sqrt, scale=-1.0, bias=1.0),
                         reads=[r_tmp], writes=[r_tmp])
                    ps, rps = next_ps4()
                    for tb in range(4):
                        for i in range(2):
                            P.op("pe", lambda e, tb=tb, i=i: e.matmul(ps[tb][:, :], lhsT=wg4[1][d][:, i, cc * 128:(cc + 1) * 128],
                                                                      rhs=ucb[:, i, tb * 512:(tb + 1) * 512], start=(i == 0), stop=(i == 1)),
                                 reads=[r_wg4[1][d], r_ucb[0], r_ucb[1]], writes=[rps[tb]])
                        P.op("act", lambda e, tb=tb: e.activation(out=rbuf[:, tb * 512:(tb + 1) * 512], in_=ps[tb][:, :], func=AF.Sigmoid,
                                                                  bias=gxb[:, d, c:c + 1]), reads=[rps[tb], r_par], writes=[r_rbuf])
                    P.op("dve", lambda e: e.tensor_tensor(out=tmp[:], in0=tmp[:], in1=rbuf[:], op=ALU.mult), reads=[r_tmp, r_rbuf], writes=[r_tmp])
                    P.op("dve", lambda e: e.tensor_tensor(out=tmp[:], in0=tmp[:], in1=ucc, op=ALU.mult), reads=[r_tmp, r_uc[cc]], writes=[r_tmp])
                    if d == 0:
                        P.op("dve", lambda e: e.tensor_tensor_scan(out=hsum[:], data0=abuf[:], data1=tmp[:], initial=0.0,
                                                                    op0=ALU.mult, op1=ALU.add), reads=[r_abuf, r_tmp], writes=[r_hsum])
                    else:
                        P.op("dve", lambda e: e.tensor_tensor_scan(out=rbuf[:, ::-1], data0=abuf[:, ::-1], data1=tmp[:, ::-1], initial=0.0,
                                                                    op0=ALU.mult, op1=ALU.add), reads=[r_abuf, r_tmp], writes=[r_rbuf])
                        P.op("pool", lambda e: e.tensor_tensor(out=hsum[:], in0=hsum[:], in1=rbuf[:], op=ALU.add),
                             reads=[r_rbuf, r_hsum], writes=[r_hsum])
                ps, rps = next_ps4()
                for tb in range(4):
                    for k in range(NK):
                        P.op("pe", lambda e, tb=tb, k=k: e.matmul(ps[tb][:, :], lhsT=wgb[:, k, cc * 128:(cc + 1) * 128],
                                                                  rhs=XT[:, k, tb * 512:(tb + 1) * 512], start=(k == 0), stop=(k == NK - 1)),
                             reads=[r_wgb] + cx.rXA[tb * 4:(tb + 1) * 4], writes=[rps[tb]])
                    sl = slice(tb * 512, (tb + 1) * 512)
                    gelu_tanh(cx, gtc[:, sl], ps[tb][:, :], abuf[:, sl], tmp[:, sl], rps[tb], r_abuf, r_tmp, r_gtc,
                              extra_mul=hsum[:, sl], r_extra=r_hsum)
                P.dma("pool", wo[:], A["lru_w_out"][0][c * 128:(c + 1) * 128, :], writes=[r_wo])
                for i in range(NT):
                    for hh in range(2):
                        pb = (i * 2 + hh) % 8
                        P.op("pe", lambda e, i=i, hh=hh, pb=pb: e.matmul(cx.ps[pb][:, :], lhsT=gtc[:, i * 128:(i + 1) * 128],
                                                                         rhs=wo[:, hh * 512:(hh + 1) * 512], start=True, stop=True),
                             reads=[r_gtc, r_wo], writes=[cx.rps[pb]])
                        P.op("dve", lambda e, i=i, hh=hh, pb=pb: e.tensor_tensor(out=cx.R[:, i, hh * 512:(hh + 1) * 512],
                                                                                 in0=cx.R[:, i, hh * 512:(hh + 1) * 512],
                                                                                 in1=cx.ps[pb][:, :], op=ALU.add),
                             reads=[cx.rps[pb], cx.rR[i]], writes=[cx.rR[i]])
    P.barrier()


MLA_H = 8
MLA_SCALE = 192.0 ** -0.5
TWO_PI_HI = 6.28125
TWO_PI_LO = 2.0 * np.pi - 6.28125
MAGIC = 12582912.0


def rope_apply(cx, out_bf, src, cos, sin, t1, t2, nb, r_src, r_tab, r_t, r_out):
    P = cx.P
    a1 = src[:, :, 0:32]
    a2 = src[:, :, 32:64]
    P.op("dve", lambda e: e.tensor_tensor(out=t1, in0=a1, in1=cos, op=ALU.mult), reads=[r_src, r_tab], writes=[r_t])
    P.op("dve", lambda e: e.tensor_tensor(out=t2, in0=a2, in1=sin, op=ALU.mult), reads=[r_src, r_tab], writes=[r_t])
    P.op("dve", lambda e: e.tensor_tensor(out=out_bf[:, :, 0:32], in0=t1, in1=t2, op=ALU.subtract), reads=[r_t], writes=[r_out])
    P.op("dve", lambda e: e.tensor_tensor(out=t1, in0=a2, in1=cos, op=ALU.mult), reads=[r_src, r_tab, r_out], writes=[r_t])
    P.op("dve", lambda e: e.tensor_tensor(out=t2, in0=a1, in1=sin, op=ALU.mult), reads=[r_src, r_tab], writes=[r_t])
    P.op("dve", lambda e: e.tensor_tensor(out=out_bf[:, :, 32:64], in0=t1, in1=t2, op=ALU.add), reads=[r_t], writes=[r_out])


def emit_mla(cx, seq):
    P = cx.P
    nc = cx.nc
    A = cx.A
    XT = xt_view(cx)
    with contextlib.ExitStack() as st:
        sb = lambda name, shape, dt: st.enter_context(nc.sbuf_tensor(un(name), shape, dt))
        posi = sb("posi", [128, NT], I32)
        ang = sb("ang", [128, NT, 32], F32)
        nn = sb("nn", [128, NT, 32], F32)
        cosT = sb("cosT", [128, NT, 32], F32)
        sinT = sb("sinT", [128, NT, 32], F32)
        r_tab = Reg()
        small_load(cx, posi[:], A["positions"][seq].rearrange("(i p) -> p i", p=128), r_tab)
        P.op("dve", lambda e: e.tensor_copy(out=nn[:, :, 0], in_=posi[:]), reads=[r_tab], writes=[r_tab])
        P.op("dve", lambda e: e.tensor_tensor(out=ang[:], in0=nn[:, :, 0:1].to_broadcast([128, NT, 32]),
                                              in1=cx.cst[:, None, C_FREQ:C_FREQ + 32].to_broadcast([128, NT, 32]), op=ALU.mult),
             reads=[r_tab, cx.rcst], writes=[r_tab])
        P.op("dve", lambda e: e.tensor_scalar(out=nn[:], in0=ang[:], scalar1=float(1.0 / (2.0 * np.pi)), scalar2=None, op0=ALU.mult),
             reads=[r_tab], writes=[r_tab])
        P.op("dve", lambda e: e.tensor_scalar(out=nn[:], in0=nn[:], scalar1=MAGIC, scalar2=None, op0=ALU.add), reads=[r_tab], writes=[r_tab])
        P.op("dve", lambda e: e.tensor_scalar(out=nn[:], in0=nn[:], scalar1=-MAGIC, scalar2=None, op0=ALU.add), reads=[r_tab], writes=[r_tab])
        P.op("dve", lambda e: e.scalar_tensor_tensor(out=ang[:], in0=nn[:], scalar=-TWO_PI_HI, in1=ang[:], op0=ALU.mult, op1=ALU.add),
             reads=[r_tab], writes=[r_tab])
        P.op("dve", lambda e: e.scalar_tensor_tensor(out=ang[:], in0=nn[:], scalar=-TWO_PI_LO, in1=ang[:], op0=ALU.mult, op1=ALU.add),
             reads=[r_tab], writes=[r_tab])
        PI_S = 3.1415925
        P.op("dve", lambda e: e.tensor_scalar(out=ang[:], in0=ang[:], scalar1=PI_S, scalar2=-PI_S, op0=ALU.min, op1=ALU.max),
             reads=[r_tab], writes=[r_tab])
        P.op("act", lambda e: e.activation(out=sinT[:], in_=ang[:], func=AF.Sin), reads=[r_tab], writes=[r_tab])
        P.op("act", lambda e: e.activation(out=nn[:], in_=ang[:], func=AF.Abs), reads=[r_tab], writes=[r_tab])
        P.op("dve", lambda e: e.tensor_scalar(out=nn[:], in0=nn[:], scalar1=-1.0, scalar2=float(np.pi / 2), op0=ALU.mult, op1=ALU.add),
             reads=[r_tab], writes=[r_tab])
        P.op("act", lambda e: e.activation(out=cosT[:], in_=nn[:], func=AF.Sin), reads=[r_tab], writes=[r_tab])

        wuq = sb("wuq", [128, 3, 1536], BF16)
        wukv = sb("wukv", [128, 2, 2048], BF16)
        r_w = Reg()
        P.dma("pool", wuq[:], A["mla_w_uq"][0].rearrange("(k p) c -> p k c", p=128), writes=[r_w])
        P.dma("pool", wukv[:], A["mla_w_ukv"][0].rearrange("(k p) c -> p k c", p=128), writes=[r_w])
        cqnT = sb("cqnT", [128, 3, S], BF16)
        ckvnT = sb("ckvnT", [128, 2, S], BF16)
        kropeT = sb("kropeT", [64, S], BF16)
        r_lat = [Reg() for i in range(NT)]
        r_krT = Reg()
        with contextlib.ExitStack() as st1:
            sb1 = lambda name, shape, dt: st1.enter_context(nc.sbuf_tensor(un(name), shape, dt))
            wi = sb1("wi", [128, NK, 704], BF16)
            r_wi = Reg()
            P.dma("pool", wi[:], A["mla_w_in"][0].rearrange("(k p) c -> p k c", p=128), writes=[r_wi])
            gq = sb1("gq", [128, 384], F32)
            gkv = sb1("gkv", [128, 256], F32)
            r_g = Reg()
            P.dma("sp", gq[:], A["mla_q_norm_g"][0].partition_broadcast(128), writes=[r_g])
            P.dma("sp", gkv[:], A["mla_kv_norm_g"][0].partition_broadcast(128), writes=[r_g])
            zs = [sb1("zs", [128, 704], F32) for j in range(2)]
            r_zs = [Reg(), Reg()]
            junk = sb1("junk", [128, 384], F32)
            r_junk = Reg()
            ms = [sb1("ms", [128, 4], F32) for j in range(2)]
            r_ms = [Reg(), Reg()]
            cn = [sb1("cn", [128, 640], BF16) for j in range(2)]
            r_cn = [Reg(), Reg()]
            krz = sb1("krz", [128, NT, 64], F32)
            r_krz = Reg()
            krb = sb1("krb", [128, NT, 64], BF16)
            r_krb = Reg()
            rt1 = sb1("rt1", [128, NT, 32], F32)
            rt2 = sb1("rt2", [128, NT, 32], F32)
            r_rt = Reg()
            for i in range(NT):
                j2 = i % 2
                pa = 0 + 2 * j2
                pbk = 1 + 2 * j2
                for k in range(NK):
                    P.op("pe", lambda e, k=k: e.matmul(cx.ps[pa][:, :], lhsT=XT[:, k, i * 128:(i + 1) * 128], rhs=wi[:, k, 0:512],
                                                       start=(k == 0), stop=(k == NK - 1)), reads=[cx.rXA[i], r_wi], writes=[cx.rps[pa]])
                for k in range(NK):
                    P.op("pe", lambda e, k=k: e.matmul(cx.ps[pbk][:, 0:192], lhsT=XT[:, k, i * 128:(i + 1) * 128], rhs=wi[:, k, 512:704],
                                                       start=(k == 0), stop=(k == NK - 1)), reads=[cx.rXA[i], r_wi], writes=[cx.rps[pbk]])
                z = zs[j2]
                P.op("act", lambda e: e.activation(out=z[:, 0:512], in_=cx.ps[pa][:, :], func=AF.Copy), reads=[cx.rps[pa]], writes=[r_zs[j2]])
                P.op("dve", lambda e: e.tensor_copy(out=z[:, 512:704], in_=cx.ps[pbk][:, 0:192]), reads=[cx.rps[pbk]], writes=[r_zs[j2]])
                m = ms[j2]
                P.op("act", lambda e: e.activation(out=junk[:, 0:384], in_=z[:, 0:384], func=AF.Square, accum_out=m[:, 0:1]),
                     reads=[r_zs[j2]], writes=[r_junk, r_ms[j2]])
                P.op("act", lambda e: e.activation(out=junk[:, 0:256], in_=z[:, 384:640], func=AF.Square, accum_out=m[:, 1:2]),
                     reads=[r_zs[j2]], writes=[r_junk, r_ms[j2]])
                P.op("dve", lambda e: e.tensor_scalar(out=m[:, 0:1], in0=m[:, 0:1], scalar1=1.0 / 384.0, scalar2=LN_EPS, op0=ALU.mult, op1=ALU.add),
                     reads=[r_ms[j2]], writes=[r_ms[j2]])
                P.op("dve", lambda e: e.tensor_scalar(out=m[:, 1:2], in0=m[:, 1:2], scalar1=1.0 / 256.0, scalar2=LN_EPS, op0=ALU.mult, op1=ALU.add),
                     reads=[r_ms[j2]], writes=[r_ms[j2]])
                P.op("act", lambda e: e.activation(out=m[:, 0:2], in_=m[:, 0:2], func=AF.Sqrt), reads=[r_ms[j2]], writes=[r_ms[j2]])
                P.op("dve", lambda e: e.reciprocal(out=m[:, 2:4], in_=m[:, 0:2]), reads=[r_ms[j2]], writes=[r_ms[j2]])
                c_ = cn[j2]
                P.op("dve", lambda e: e.scalar_tensor_tensor(out=c_[:, 0:384], in0=z[:, 0:384], scalar=m[:, 2:3], in1=gq[:], op0=ALU.mult, op1=ALU.mult),
                     reads=[r_zs[j2], r_ms[j2], r_g], writes=[r_cn[j2]])
                P.op("dve", lambda e: e.scalar_tensor_tensor(out=c_[:, 384:640], in0=z[:, 384:640], scalar=m[:, 3:4], in1=gkv[:], op0=ALU.mult, op1=ALU.mult),
                     reads=[r_zs[j2], r_ms[j2], r_g], writes=[r_cn[j2]])
                P.op("pool", lambda e: e.tensor_copy(out=krz[:, i, :], in_=z[:, 640:704]), reads=[r_zs[j2]], writes=[r_krz])
                pt = 6 + j2
                pst = cx.ps[pt][:].bitcast(BF16)
                for k in range(5):
                    P.op("pe", lambda e, k=k: e.transpose(out=pst[:, k * 128:(k + 1) * 128], in_=c_[:, k * 128:(k + 1) * 128], identity=cx.identb[:]),
                         reads=[r_cn[j2], cx.rcst], writes=[cx.rps[pt]])
                P.op("act", lambda e: e.activation(out=cqnT[:, :, i * 128:(i + 1) * 128], in_=pst[:, 0:384].rearrange("p (k t) -> p k t", k=3), func=AF.Copy),
                     reads=[cx.rps[pt]], writes=[r_lat[i]])
                P.op("dve", lambda e: e.tensor_copy(out=ckvnT[:, :, i * 128:(i + 1) * 128], in_=pst[:, 384:640].rearrange("p (k t) -> p k t", k=2)),
                     reads=[cx.rps[pt]], writes=[r_lat[i]])
            rope_apply(cx, krb[:], krz[:], cosT[:], sinT[:], rt1[:], rt2[:], NT, r_krz, r_tab, r_rt, r_krb)
            for g in range(2):
                pst = cx.ps[4 + g][:].bitcast(BF16)
                for j in range(8):
                    i = g * 8 + j
                    P.op("pe", lambda e, i=i, j=j: e.transpose(out=pst[0:64, j * 128:(j + 1) * 128], in_=krb[:, i, :], identity=cx.identb[:]),
                         reads=[r_krb, cx.rcst], writes=[cx.rps[4 + g]])
                P.op("act", lambda e, g=g: e.activation(out=kropeT[:, g * 1024:(g + 1) * 1024], in_=pst[0:64, :], func=AF.Copy),
                     reads=[cx.rps[4 + g]], writes=[r_krT])
        P.barrier()
        qnT = sb("qnT", [128, S], BF16)
        knT = sb("knT", [128, S], BF16)
        qrT = sb("qrT", [64, S], BF16)
        vh = sb("vh", [128, NT, 128], BF16)
        r_qnT, r_knT, r_qrT, r_vh = Reg(), Reg(), Reg(), Reg()
        qrb = sb("qrb", [128, 8, 64], BF16)
        r_qrb = Reg()
        rt1 = sb("rt1b", [128, 8, 32], F32)
        rt2 = sb("rt2b", [128, 8, 32], F32)
        r_rt = Reg()
        Pb = sb("Pb", [128, S], BF16)
        r_Pb = Reg()
        PTs = sb("PTs", [128, NT, 128], BF16)
        r_PTs = Reg()
        oT = sb("oT", [128, 128], BF16)
        r_oT = Reg()
        sm = [sb("smx", [128, 8], F32) for j in range(2)]
        r_sm = [Reg(), Reg()]
        wo = sb("wo", [128, D], BF16)
        r_wo = Reg()
        all_lat = r_lat
        for h in range(MLA_H):
            P.dma("pool", wo[:], A["mla_w_out"][0][h * 128:(h + 1) * 128, :], writes=[r_wo])
            for tb in range(4):
                for k in range(3):
                    P.op("pe", lambda e, tb=tb, k=k: e.matmul(cx.ps[tb][:, :], lhsT=wuq[:, k, h * 192:h * 192 + 128],
                                                              rhs=cqnT[:, k, tb * 512:(tb + 1) * 512], start=(k == 0), stop=(k == 2)),
                         reads=[r_w] + all_lat[tb * 4:(tb + 1) * 4], writes=[cx.rps[tb]])
                P.op("act", lambda e, tb=tb: e.activation(out=qnT[:, tb * 512:(tb + 1) * 512], in_=cx.ps[tb][:, :], func=AF.Copy),
                     reads=[cx.rps[tb]], writes=[r_qnT])
            for tb in range(4):
                for k in range(2):
                    P.op("pe", lambda e, tb=tb, k=k: e.matmul(cx.ps[4 + tb][:, :], lhsT=wukv[:, k, h * 256:h * 256 + 128],
                                                              rhs=ckvnT[:, k, tb * 512:(tb + 1) * 512], start=(k == 0), stop=(k == 1)),
                         reads=[r_w] + all_lat[tb * 4:(tb + 1) * 4], writes=[cx.rps[4 + tb]])
                P.op("dve", lambda e, tb=tb: e.tensor_copy(out=knT[:, tb * 512:(tb + 1) * 512], in_=cx.ps[4 + tb][:, :]),
                     reads=[cx.rps[4 + tb]], writes=[r_knT])
            for g in range(4):
                for j in range(4):
                    i = g * 4 + j
                    for k in range(2):
                        P.op("pe", lambda e, i=i, j=j, k=k, g=g: e.matmul(cx.ps[g][:, j * 128:(j + 1) * 128], lhsT=ckvnT[:, k, i * 128:(i + 1) * 128],
                                                                          rhs=wukv[:, k, h * 256 + 128:h * 256 + 256], start=(k == 0), stop=(k == 1)),
                             reads=[r_w, all_lat[i]], writes=[cx.rps[g]])
                P.op("act", lambda e, g=g: e.activation(out=vh[:, g * 4:(g + 1) * 4, :], in_=cx.ps[g][:, :].rearrange("p (a b) -> p a b", a=4), func=AF.Copy),
                     reads=[cx.rps[g]], writes=[r_vh])
            for g in range(2):
                pb = 4 + g
                for j in range(8):
                    i = g * 8 + j
                    for k in range(3):
                        P.op("pe", lambda e, i=i, j=j, k=k, pb=pb: e.matmul(cx.ps[pb][:, j * 64:(j + 1) * 64], lhsT=cqnT[:, k, i * 128:(i + 1) * 128],
                                                                            rhs=wuq[:, k, h * 192 + 128:h * 192 + 192], start=(k == 0), stop=(k == 2)),
                             reads=[r_w, all_lat[i]], writes=[cx.rps[pb]])
                rope_apply(cx, qrb[:], cx.ps[pb][:, :].rearrange("p (a b) -> p a b", a=8), cosT[:, g * 8:(g + 1) * 8, :], sinT[:, g * 8:(g + 1) * 8, :],
                           rt1[:], rt2[:], 8, cx.rps[pb], r_tab, r_rt, r_qrb)
                pst = cx.ps[6 + g][:].bitcast(BF16)
                for j in range(8):
                    P.op("pe", lambda e, j=j: e.transpose(out=pst[0:64, j * 128:(j + 1) * 128], in_=qrb[:, j, :], identity=cx.identb[:]),
                         reads=[r_qrb, cx.rcst], writes=[cx.rps[6 + g]])
                P.op("act", lambda e, g=g: e.activation(out=qrT[:, g * 1024:(g + 1) * 1024], in_=pst[0:64, :], func=AF.Copy),
                     reads=[cx.rps[6 + g]], writes=[r_qrT])
            def emit_S(qb):
                base = 4 * (qb % 2)
                qs = slice(qb * 128, (qb + 1) * 128)
                for tb in range(4):
                    P.op("pe", lambda e, tb=tb: e.matmul(cx.ps[base + tb][:, :], lhsT=qnT[:, qs], rhs=knT[:, tb * 512:(tb + 1) * 512], start=True, stop=False),
                         reads=[r_qnT, r_knT], writes=[cx.rps[base + tb]])
                    P.op("pe", lambda e, tb=tb: e.matmul(cx.ps[base + tb][:, :], lhsT=qrT[:, qs], rhs=kropeT[:, tb * 512:(tb + 1) * 512], start=False, stop=True),
                         reads=[r_qrT, r_krT], writes=[cx.rps[base + tb]])

            def emit_softmax(qb):
                base = 4 * (qb % 2)
                m = sm[qb % 2]
                r_m = r_sm[qb % 2]
                for tb in range(4):
                    P.op("dve", lambda e, tb=tb: e.tensor_reduce(out=m[:, tb:tb + 1], in_=cx.ps[base + tb][:, :], axis=AX.X, op=ALU.max),
                         reads=[cx.rps[base + tb]], writes=[r_m])
                P.op("dve", lambda e: e.tensor_reduce(out=m[:, 0:1], in_=m[:, 0:4], axis=AX.X, op=ALU.max), reads=[r_m], writes=[r_m])
                P.op("dve", lambda e: e.tensor_scalar(out=m[:, 1:2], in0=m[:, 0:1], scalar1=-MLA_SCALE, scalar2=None, op0=ALU.mult), reads=[r_m], writes=[r_m])
                for tb in range(4):
                    P.op("act", lambda e, tb=tb: e.activation(out=Pb[:, tb * 512:(tb + 1) * 512], in_=cx.ps[base + tb][:, :], func=AF.Exp, scale=MLA_SCALE,
                                                              bias=m[:, 1:2], accum_out=m[:, 4 + tb:5 + tb]), reads=[cx.rps[base + tb], r_m], writes=[r_Pb, r_m])
                P.op("dve", lambda e: e.tensor_reduce(out=m[:, 2:3], in_=m[:, 4:8], axis=AX.X, op=ALU.add), reads=[r_m], writes=[r_m])
                P.op("dve", lambda e: e.reciprocal(out=m[:, 3:4], in_=m[:, 2:3]), reads=[r_m], writes=[r_m])

            def emit_rest(qb):
                base = 4 * (qb % 2)
                m = sm[qb % 2]
                r_m = r_sm[qb % 2]
                for g in range(2):
                    pst = cx.ps[base + g][:].bitcast(BF16)
                    for j in range(8):
                        c = g * 8 + j
                        P.op("pe", lambda e, c=c, j=j: e.transpose(out=pst[:, j * 128:(j + 1) * 128], in_=Pb[:, c * 128:(c + 1) * 128], identity=cx.identb[:]),
                             reads=[r_Pb, cx.rcst], writes=[cx.rps[base + g]])
                    P.op("act", lambda e, g=g: e.activation(out=PTs[:, g * 8:(g + 1) * 8, :], in_=pst.rearrange("p (a b) -> p a b", a=8), func=AF.Copy),
                         reads=[cx.rps[base + g]], writes=[r_PTs])
                for c in range(NT):
                    P.op("pe", lambda e, c=c: e.matmul(cx.ps[base + 2][:, 0:128], lhsT=vh[:, c, :], rhs=PTs[:, c, :], start=(c == 0), stop=(c == NT - 1)),
                         reads=[r_vh, r_PTs], writes=[cx.rps[base + 2]])
                P.op("act", lambda e: e.activation(out=oT[:], in_=cx.ps[base + 2][:, 0:128], func=AF.Copy), reads=[cx.rps[base + 2]], writes=[r_oT])
                for hh in range(2):
                    pb = base + 3 - hh
                    P.op("pe", lambda e, hh=hh, pb=pb: e.matmul(cx.ps[pb][:, :], lhsT=oT[:], rhs=wo[:, hh * 512:(hh + 1) * 512], start=True, stop=True),
                         reads=[r_oT, r_wo], writes=[cx.rps[pb]])
                    P.op("dve", lambda e, hh=hh, pb=pb: e.scalar_tensor_tensor(out=cx.R[:, qb, hh * 512:(hh + 1) * 512], in0=cx.ps[pb][:, :], scalar=m[:, 3:4],
                                                                               in1=cx.R[:, qb, hh * 512:(hh + 1) * 512], op0=ALU.mult, op1=ALU.add),
                         reads=[cx.rps[pb], r_m, cx.rR[qb]], writes=[cx.rR[qb]])

            emit_S(0)
            for qb in range(NT):
                if qb + 1 < NT:
                    emit_S(qb + 1)
                emit_softmax(qb)
                emit_rest(qb)
    P.barrier()


LN_QSCALE = float(np.log(128.0 ** -0.5))


def emit_linattn(cx, kind):
    P = cx.P
    nc = cx.nc
    A = cx.A
    XT = xt_view(cx)
    ml = (kind == "mlstm")
    pre = "mlstm" if ml else "gla"
    W_in = A[pre + "_w_in"][0]
    DV1 = 257 if ml else 256
    cs = 1.0 if ml else 1.0 / 16.0
    with contextlib.ExitStack() as st:
        sb = lambda name, shape, dt: st.enter_context(nc.sbuf_tensor(un(name), shape, dt))
        ng = sb("ng", [128, 256], F32)
        r_ng = Reg()
        r_gw = Reg()
        if ml:
            wgates = sb("wgates", [128, NK, 16], BF16)
            P.dma("pool", wgates[:], W_in[:, 3072:3088].rearrange("(k p) c -> p k c", p=128), writes=[r_gw])
            gbias = sb("gbias", [128, 16], F32)
            P.dma("sp", gbias[:], A["mlstm_gate_b"][0].rearrange("a b c -> (a b c)").partition_broadcast(128), writes=[r_gw])
            ngbias = sb("ngbias", [128, 16], F32)
            P.op("dve", lambda e: e.tensor_scalar(out=ngbias[:], in0=gbias[:], scalar1=-1.0, scalar2=None, op0=ALU.mult), reads=[r_gw], writes=[r_gw])
            wrep = [sb("wrep", [128, NK, 128], BF16) for j in range(2)]
            r_wrep = [Reg(), Reg()]
            igb = sb("igb", [128, S], F32)
            r_igb = Reg()
        else:
            wglr = sb("wglr", [128, NK, 32], BF16)
            P.dma("pool", wglr[:], W_in[:, 3072:3104].rearrange("(k p) c -> p k c", p=128), writes=[r_gw])
            gwpad = sb("gwpad", [32, 2, 512], BF16)
            P.op("pool", lambda e: e.memset(gwpad[:], 0.0), writes=[r_gw])
            for d in range(2):
                P.dma("pool", gwpad[d * 16:(d + 1) * 16, d, :], A["gla_gate_w"][0, d], writes=[r_gw])
            ngb = sb("ngb", [128, 2, 4], F32)
            for d in range(2):
                small_load(cx, ngb[:, d, :], A["gla_gate_b"][0, d].rearrange("(h p) -> p h", p=128), r_gw)
            P.op("dve", lambda e: e.tensor_scalar(out=ngb[:], in0=ngb[:], scalar1=-1.0, scalar2=None, op0=ALU.mult), reads=[r_gw], writes=[r_gw])
            glrT = sb("glrT", [32, S], BF16)
            r_glrT = Reg()
            for tb in range(4):
                for k in range(NK):
                    P.op("pe", lambda e, tb=tb, k=k: e.matmul(cx.ps[tb][0:32, :], lhsT=wglr[:, k, :], rhs=XT[:, k, tb * 512:(tb + 1) * 512],
                                                              start=(k == 0), stop=(k == NK - 1)), reads=[r_gw] + cx.rXA[tb * 4:(tb + 1) * 4], writes=[cx.rps[tb]])
                P.op("act", lambda e, tb=tb: e.activation(out=glrT[:, tb * 512:(tb + 1) * 512], in_=cx.ps[tb][0:32, :], func=AF.Copy),
                     reads=[cx.rps[tb]], writes=[r_glrT])

        WA = sb("WA", [128, NK, 256], BF16)
        WB = sb("WB", [128, NK, 256], BF16)
        r_WA, r_WB = Reg(), Reg()
        wq = WA[:, :, 0:128]
        wk = WA[:, :, 128:256]
        wo = WA[:].rearrange("p k c -> p (k c)").rearrange("p (j f) -> p j f", j=2)
        wv = WB
        wor = WB
        r_wq, r_wk, r_wv, r_wor, r_wo = r_WA, r_WA, r_WB, r_WB, r_WA
        qT = sb("qT", [128, S], F32)
        kT = sb("kT", [128, S], F32)
        r_qT, r_kT = Reg(), Reg()
        vb = sb("vb", [128, NT, DV1], BF16)
        r_vb = Reg()
        buf1 = sb("buf1", [128, S], F32)
        buf2 = sb("buf2", [128, S], F32)
        r_b1, r_b2 = Reg(), Reg()
        qe = sb("qe", [128, S], BF16)
        ke = sb("ke", [128, S], BF16)
        kgT = sb("kgT", [128, NT, 128], BF16)
        r_qe, r_ke, r_kgT = Reg(), Reg(), Reg()
        Oacc = sb("Oacc", [128, NT, 256], F32)
        r_O = [Reg() for i in range(NT)]
        offs = sb("offs", [128, NT], F32)
        gsv = sb("gsv", [128, NT], F32)
        eg = sb("eg", [128, NT], F32)
        r_small = Reg()
        Sst = sb("Sst", [128, DV1], F32)
        Sb = sb("Sb", [128, DV1], BF16)
        r_S, r_Sb = Reg(), Reg()
        atm = [sb("atm", [128, 128], BF16) for j in range(2)]
        r_atm = [Reg(), Reg()]
        dn = [sb("dn", [128, 4], F32) for j in range(2)]
        r_dn = [Reg(), Reg()]
        hsall = sb("hsall", [128, NT, 10], F32)
        r_hsall = Reg()
        gact = [sb("gact", [128, 256], F32) for j in range(2)]
        r_gact = [Reg(), Reg()]
        xn = [sb("xn", [128, 256], F32) for j in range(2)]
        r_xn = [Reg(), Reg()]
        gbt = [sb("gbt", [128, 256], BF16) for j in range(2)]
        r_gbt = [Reg(), Reg()]
        gTt = [sb("gTt", [128, 2, 128], BF16) for j in range(2)]
        r_gTt = [Reg(), Reg()]
        P.op("pool", lambda e: e.memset(offs[:, 0:1], 0.0), writes=[r_small])
        if ml:
            P.op("pool", lambda e: e.memset(vb[:, :, 256:257], 1.0), writes=[r_vb])

        b1v = buf1[:].rearrange("p (c j) -> p c j", j=128)
        b2v = buf2[:].rearrange("p (c j) -> p c j", j=128)
        allXA = cx.rXA

        def wload(dst, reg, c0, c1):
            P.dma("pool", dst, W_in[:, c0:c1].rearrange("(k p) c -> p k c", p=128), writes=[reg])

        for h in range(4):
            wload(wq, r_wq, h * 128, (h + 1) * 128)
            wload(wk, r_wk, 512 + h * 128, 512 + (h + 1) * 128)
            wload(wv[:], r_wv, 1024 + h * 256, 1024 + (h + 1) * 256)
            P.dma("sp", ng[:], A[pre + "_norm_g"][0][h * 256:(h + 1) * 256].partition_broadcast(128), writes=[r_ng])
            for (wsrc, r_w, dst, r_dst, base) in ((wq, r_wq, qT, r_qT, 0), (wk, r_wk, kT, r_kT, 4)):
                for tb in range(4):
                    pb = base + tb
                    for k in range(NK):
                        P.op("pe", lambda e, k=k, pb=pb, tb=tb, wsrc=wsrc: e.matmul(cx.ps[pb][:, :], lhsT=wsrc[:, k, :], rhs=XT[:, k, tb * 512:(tb + 1) * 512],
                                                                                    start=(k == 0), stop=(k == NK - 1)),
                             reads=[r_w] + allXA[tb * 4:(tb + 1) * 4], writes=[cx.rps[pb]])
                    eng = "act" if base == 0 else "dve"
                    if eng == "act":
                        P.op("act", lambda e, pb=pb, tb=tb, dst=dst: e.activation(out=dst[:, tb * 512:(tb + 1) * 512], in_=cx.ps[pb][:, :], func=AF.Copy),
                             reads=[cx.rps[pb]], writes=[r_dst])
                    else:
                        P.op("dve", lambda e, pb=pb, tb=tb, dst=dst: e.tensor_copy(out=dst[:, tb * 512:(tb + 1) * 512], in_=cx.ps[pb][:, :]),
                             reads=[cx.rps[pb]], writes=[r_dst])
            for i in range(NT):
                pb = (i // 2) % 8
                off = (i % 2) * 256
                for k in range(NK):
                    P.op("pe", lambda e, k=k, pb=pb, off=off, i=i: e.matmul(cx.ps[pb][:, off:off + 256], lhsT=XT[:, k, i * 128:(i + 1) * 128], rhs=wv[:, k, :],
                                                                            start=(k == 0), stop=(k == NK - 1)), reads=[r_wv, allXA[i]], writes=[cx.rps[pb]])
                P.op("act", lambda e, pb=pb, off=off, i=i: e.activation(out=vb[:, i, 0:256], in_=cx.ps[pb][:, off:off + 256], func=AF.Copy),
                     reads=[cx.rps[pb]], writes=[r_vb])
            wload(wor[:], r_wor, 2048 + h * 256, 2048 + (h + 1) * 256)
            P.dma("pool", wo, A[pre + "_w_out"][0][h * 256:(h + 1) * 256, :].rearrange("(j p) f -> p j f", p=128), writes=[r_wo])
            for d in range(2):
                if ml:
                    jf = d * 8 + 4 + h
                    ji = d * 8 + h
                    for jj, gidx in enumerate((jf, ji)):
                        P.op("dve", lambda e, jj=jj, gidx=gidx: e.tensor_copy(out=wrep[jj][:], in_=wgates[:, :, gidx:gidx + 1].to_broadcast([128, NK, 128])),
                             reads=[r_gw], writes=[r_wrep[jj]])
                    for tb in range(4):
                        for k in range(NK):
                            P.op("pe", lambda e, k=k, tb=tb: e.matmul(cx.ps[tb][:, :], lhsT=wrep[0][:, k, :], rhs=XT[:, k, tb * 512:(tb + 1) * 512],
                                                                      start=(k == 0), stop=(k == NK - 1)), reads=[r_wrep[0]] + allXA[tb * 4:(tb + 1) * 4], writes=[cx.rps[tb]])
                        P.op("act", lambda e, tb=tb: e.activation(out=buf1[:, tb * 512:(tb + 1) * 512], in_=cx.ps[tb][:, :], func=AF.Exp, scale=-1.0,
                                                                  bias=ngbias[:, jf:jf + 1]), reads=[cx.rps[tb], r_gw], writes=[r_b1])
                    for tb in range(4):
                        for k in range(NK):
                            P.op("pe", lambda e, k=k, tb=tb: e.matmul(cx.ps[4 + tb][:, :], lhsT=wrep[1][:, k, :], rhs=XT[:, k, tb * 512:(tb + 1) * 512],
                                                                      start=(k == 0), stop=(k == NK - 1)), reads=[r_wrep[1]] + allXA[tb * 4:(tb + 1) * 4], writes=[cx.rps[4 + tb]])
                        P.op("act", lambda e, tb=tb: e.activation(out=igb[:, tb * 512:(tb + 1) * 512], in_=cx.ps[4 + tb][:, :], func=AF.Identity,
                                                                  bias=gbias[:, ji:ji + 1]), reads=[cx.rps[4 + tb], r_gw], writes=[r_igb])
                else:
                    for tb in range(4):
                        P.op("pe", lambda e, tb=tb: e.matmul(cx.ps[tb][:, :], lhsT=gwpad[:, d, h * 128:(h + 1) * 128], rhs=glrT[:, tb * 512:(tb + 1) * 512],
                                                             start=True, stop=True), reads=[r_gw, r_glrT], writes=[cx.rps[tb]])
                        P.op("act", lambda e, tb=tb: e.activation(out=buf1[:, tb * 512:(tb + 1) * 512], in_=cx.ps[tb][:, :], func=AF.Exp, scale=-1.0,
                                                                  bias=ngb[:, d, h:h + 1]), reads=[cx.rps[tb], r_gw], writes=[r_b1])
                P.op("act", lambda e: e.activation(out=buf1[:], in_=buf1[:], func=AF.Ln, bias=1.0), reads=[r_b1], writes=[r_b1])
                P.op("dve", lambda e: e.tensor_tensor_scan(out=buf2[:], data0=cx.cst[:, C_ONES:C_ONES + 1].to_broadcast([128, S]), data1=buf1[:], initial=0.0,
                                                            op0=ALU.mult, op1=ALU.add), reads=[r_b1, cx.rcst], writes=[r_b2])
                P.op("dve", lambda e: e.tensor_copy(out=offs[:, 1:NT], in_=b2v[:, 0:NT - 1, 127]), reads=[r_b2], writes=[r_small])
                P.op("dve", lambda e: e.tensor_tensor(out=b2v, in0=b2v, in1=offs[:, :, None].to_broadcast([128, NT, 128]), op=ALU.subtract),
                     reads=[r_b2, r_small], writes=[r_b2])
                P.op("dve", lambda e: e.tensor_copy(out=gsv[:], in_=b2v[:, :, 127]), reads=[r_b2], writes=[r_small])
                P.op("act", lambda e: e.activation(out=eg[:], in_=gsv[:], func=AF.Exp, scale=-cs), reads=[r_small], writes=[r_small])
                gs_bc = gsv[:, :, None].to_broadcast([128, NT, 128])
                if d == 1:
                    P.op("dve", lambda e: e.scalar_tensor_tensor(out=buf2[:], in0=buf2[:], scalar=-1.0, in1=buf1[:], op0=ALU.mult, op1=ALU.add),
                         reads=[r_b1, r_b2], writes=[r_b2])
                    P.op("dve", lambda e: e.tensor_tensor(out=b2v, in0=b2v, in1=gs_bc, op=ALU.add), reads=[r_b2, r_small], writes=[r_b2])
                P.op("dve", lambda e: e.scalar_tensor_tensor(out=b1v, in0=b2v, scalar=-1.0, in1=gs_bc, op0=ALU.mult, op1=ALU.add),
                     reads=[r_b2, r_small], writes=[r_b1])
                if ml:
                    P.op("dve", lambda e: e.scalar_tensor_tensor(out=buf1[:], in0=buf1[:], scalar=-cs, in1=igb[:], op0=ALU.mult, op1=ALU.add),
                         reads=[r_b1, r_igb], writes=[r_b1])
                    P.op("act", lambda e: e.activation(out=buf1[:], in_=buf1[:], func=AF.Exp), reads=[r_b1], writes=[r_b1])
                else:
                    P.op("act", lambda e: e.activation(out=buf1[:], in_=buf1[:], func=AF.Exp, scale=-cs), reads=[r_b1], writes=[r_b1])
                P.op("dve", lambda e: e.tensor_tensor(out=ke[:], in0=kT[:], in1=buf1[:], op=ALU.mult), reads=[r_kT, r_b1], writes=[r_ke])
                for g in range(2):
                    pst = cx.ps[6 + g][:].bitcast(BF16)
                    for j in range(8):
                        c = g * 8 + j
                        P.op("pe", lambda e, c=c, j=j: e.transpose(out=pst[:, j * 128:(j + 1) * 128], in_=ke[:, c * 128:(c + 1) * 128], identity=cx.identb[:]),
                             reads=[r_ke, cx.rcst], writes=[cx.rps[6 + g]])
                    P.op("act", lambda e, g=g: e.activation(out=kgT[:, g * 8:(g + 1) * 8, :], in_=pst.rearrange("p (a b) -> p a b", a=8), func=AF.Copy),
                         reads=[cx.rps[6 + g]], writes=[r_kgT])
                if ml:
                    P.op("dve", lambda e: e.scalar_tensor_tensor(out=buf1[:], in0=buf2[:], scalar=cs, in1=igb[:], op0=ALU.mult, op1=ALU.add),
                         reads=[r_b2, r_igb], writes=[r_b1])
                    P.op("act", lambda e: e.activation(out=buf1[:], in_=buf1[:], func=AF.Exp), reads=[r_b1], writes=[r_b1])
                else:
                    P.op("act", lambda e: e.activation(out=buf1[:], in_=buf2[:], func=AF.Exp, scale=cs), reads=[r_b2], writes=[r_b1])
                P.op("dve", lambda e: e.tensor_tensor(out=ke[:], in0=kT[:], in1=buf1[:], op=ALU.mult), reads=[r_kT, r_b1], writes=[r_ke])
                P.op("act", lambda e: e.activation(out=buf2[:], in_=buf2[:], func=AF.Exp, scale=-cs, bias=cx.cst[:, C_LNQ:C_LNQ + 1]), reads=[r_b2, cx.rcst], writes=[r_b2])
                P.op("dve", lambda e: e.tensor_tensor(out=qe[:], in0=qT[:], in1=buf2[:], op=ALU.mult), reads=[r_qT, r_b2], writes=[r_qe])
                order = list(range(NT)) if d == 0 else list(range(NT - 1, -1, -1))
                mcol = C_TRIU if d == 0 else C_TRIL
                for n_, c in enumerate(order):
                    csl = slice(c * 128, (c + 1) * 128)
                    pa = n_ % 2
                    po = 2 + (n_ % 2)
                    pss = 4 + (n_ % 2)
                    first = (n_ == 0)
                    last = (n_ == NT - 1)
                    P.op("pe", lambda e: e.matmul(cx.ps[pa][:, 0:128], lhsT=ke[:, csl], rhs=qe[:, csl], start=True, stop=True),
                         reads=[r_ke, r_qe], writes=[cx.rps[pa]])
                    am = atm[n_ % 2]
                    P.op("dve", lambda e: e.tensor_tensor(out=am[:], in0=cx.cst[:, mcol:mcol + 128], in1=cx.ps[pa][:, 0:128], op=ALU.mult),
                         reads=[cx.rps[pa], cx.rcst], writes=[r_atm[n_ % 2]])
                    P.op("pe", lambda e: e.matmul(cx.ps[po][:, 0:DV1], lhsT=am[:], rhs=vb[:, c, :], start=True, stop=first),
                         reads=[r_atm[n_ % 2], r_vb], writes=[cx.rps[po]])
                    if not first:
                        P.op("pe", lambda e: e.matmul(cx.ps[po][:, 0:DV1], lhsT=qe[:, csl], rhs=Sb[:], start=False, stop=True),
                             reads=[r_qe, r_Sb], writes=[cx.rps[po]])
                    if not last:
                        P.op("pe", lambda e: e.matmul(cx.ps[pss][:, 0:DV1], lhsT=kgT[:, c, :], rhs=vb[:, c, :], start=True, stop=True),
                             reads=[r_kgT, r_vb], writes=[cx.rps[pss]])
                        if first:
                            P.op("dve", lambda e: e.tensor_copy(out=Sst[:], in_=cx.ps[pss][:, 0:DV1]), reads=[cx.rps[pss]], writes=[r_S])
                        else:
                            P.op("dve", lambda e: e.scalar_tensor_tensor(out=Sst[:], in0=Sst[:], scalar=eg[:, c:c + 1], in1=cx.ps[pss][:, 0:DV1],
                                                                          op0=ALU.mult, op1=ALU.add), reads=[cx.rps[pss], r_S, r_small], writes=[r_S])
                        P.op("act", lambda e: e.activation(out=Sb[:], in_=Sst[:], func=AF.Copy), reads=[r_S], writes=[r_Sb])
                    oc = Oacc[:, c, :]
                    if ml:
                        dd = dn[n_ % 2]
                        r_dd = r_dn[n_ % 2]
                        P.op("act", lambda e: e.activation(out=dd[:, 0:1], in_=cx.ps[po][:, 256:257], func=AF.Abs), reads=[cx.rps[po]], writes=[r_dd])
                        P.op("dve", lambda e: e.tensor_scalar(out=dd[:, 0:1], in0=dd[:, 0:1], scalar1=1.0, scalar2=None, op0=ALU.max), reads=[r_dd], writes=[r_dd])
                        P.op("dve", lambda e: e.reciprocal(out=dd[:, 1:2], in_=dd[:, 0:1]), reads=[r_dd], writes=[r_dd])
                        if d == 0:
                            P.op("act", lambda e: e.activation(out=oc, in_=cx.ps[po][:, 0:256], func=AF.Copy, scale=dd[:, 1:2]),
                                 reads=[cx.rps[po], r_dd], writes=[r_O[c]])
                        else:
                            P.op("dve", lambda e: e.scalar_tensor_tensor(out=oc, in0=cx.ps[po][:, 0:256], scalar=dd[:, 1:2], in1=oc, op0=ALU.mult, op1=ALU.add),
                                 reads=[cx.rps[po], r_dd, r_O[c]], writes=[r_O[c]])
                    else:
                        if d == 0:
                            P.op("act", lambda e: e.activation(out=oc, in_=cx.ps[po][:, 0:256], func=AF.Copy), reads=[cx.rps[po]], writes=[r_O[c]])
                        else:
                            P.op("dve", lambda e: e.tensor_tensor(out=oc, in0=oc, in1=cx.ps[po][:, 0:256], op=ALU.add), reads=[cx.rps[po], r_O[c]], writes=[r_O[c]])
            for i in range(NT):
                P.op("dve", lambda e, i=i: e.bn_stats(out=hsall[:, i, 0:6], in_=Oacc[:, i, :]), reads=[r_O[i]], writes=[r_hsall])
            for i in range(NT):
                P.op("dve", lambda e, i=i: e.bn_aggr(out=hsall[:, i, 6:8], in_=hsall[:, i, 0:6]), reads=[r_hsall], writes=[r_hsall])
            P.op("dve", lambda e: e.tensor_scalar(out=hsall[:, :, 8], in0=hsall[:, :, 7], scalar1=LN_EPS, scalar2=None, op0=ALU.add), reads=[r_hsall], writes=[r_hsall])
            P.op("act", lambda e: e.activation(out=hsall[:, :, 8], in_=hsall[:, :, 8], func=AF.Sqrt), reads=[r_hsall], writes=[r_hsall])
            P.op("dve", lambda e: e.reciprocal(out=hsall[:, :, 9], in_=hsall[:, :, 8]), reads=[r_hsall], writes=[r_hsall])
            for i in range(NT):
                j2 = i % 2
                hs = hsall[:, i, :]
                r_hs = r_hsall
                oc = Oacc[:, i, :]
                x_ = xn[j2]
                P.op("dve", lambda e: e.tensor_scalar(out=x_[:], in0=oc, scalar1=hs[:, 6:7], scalar2=hs[:, 9:10], op0=ALU.subtract, op1=ALU.mult),
                     reads=[r_O[i], r_hs], writes=[r_xn[j2]])
                P.op("pool", lambda e: e.tensor_tensor(out=x_[:], in0=x_[:], in1=ng[:], op=ALU.mult), reads=[r_xn[j2], r_ng], writes=[r_xn[j2]])
                pg = 6 + j2
                for k in range(NK):
                    P.op("pe", lambda e, k=k: e.matmul(cx.ps[pg][:, 0:256], lhsT=XT[:, k, i * 128:(i + 1) * 128], rhs=wor[:, k, :],
                                                       start=(k == 0), stop=(k == NK - 1)), reads=[r_wor, allXA[i]], writes=[cx.rps[pg]])
                ga = gact[j2]
                P.op("act", lambda e: e.activation(out=ga[:], in_=cx.ps[pg][:, 0:256], func=(AF.Sigmoid if ml else AF.Silu)),
                     reads=[cx.rps[pg]], writes=[r_gact[j2]])
                gb_ = gbt[j2]
                P.op("dve", lambda e: e.tensor_tensor(out=gb_[:], in0=x_[:], in1=ga[:], op=ALU.mult), reads=[r_xn[j2], r_gact[j2]], writes=[r_gbt[j2]])
                pt = 4 + j2
                pst = cx.ps[pt][:].bitcast(BF16)
                for j in range(2):
                    P.op("pe", lambda e, j=j: e.transpose(out=pst[:, j * 128:(j + 1) * 128], in_=gb_[:, j * 128:(j + 1) * 128], identity=cx.identb[:]),
                         reads=[r_gbt[j2], cx.rcst], writes=[cx.rps[pt]])
                gT_ = gTt[j2]
                P.op("act", lambda e: e.activation(out=gT_[:], in_=pst[:, 0:256].rearrange("p (a b) -> p a b", a=2), func=AF.Copy),
                     reads=[cx.rps[pt]], writes=[r_gTt[j2]])
                for hh in range(2):
                    pb = (i * 2 + hh) % 4
                    for j in range(2):
                        P.op("pe", lambda e, j=j, hh=hh, pb=pb: e.matmul(cx.ps[pb][:, :], lhsT=gT_[:, j, :], rhs=wo[:, j, hh * 512:(hh + 1) * 512],
                                                                         start=(j == 0), stop=(j == 1)), reads=[r_gTt[j2], r_wo], writes=[cx.rps[pb]])
                    P.op("dve", lambda e, hh=hh, pb=pb: e.tensor_tensor(out=cx.R[:, i, hh * 512:(hh + 1) * 512], in0=cx.R[:, i, hh * 512:(hh + 1) * 512],
                                                                        in1=cx.ps[pb][:, :], op=ALU.add), reads=[cx.rps[pb], cx.rR[i]], writes=[cx.rR[i]])
    P.barrier()


def declare_inputs(nc, nseq, names_shapes):
    aps = {}
    for name, shape, dt in names_shapes:
        aps[name] = nc.dram_tensor(name, list(shape), dt, kind="ExternalInput").ap()
    return aps


WEIGHT_SPECS = [
    ("mlstm_w_in", (1, 1024, 3088)), ("mlstm_gate_b", (1, 2, 2, 4)), ("mlstm_norm_g", (1, 1024)), ("mlstm_w_out", (1, 1024, 1024)),
    ("gla_w_in", (1, 1024, 3104)), ("gla_gate_w", (1, 2, 16, 512)), ("gla_gate_b", (1, 2, 512)), ("gla_norm_g", (1, 1024)),
    ("gla_w_out", (1, 1024, 1024)),
    ("lru_w_in", (1, 1024, 2048)), ("lru_conv_w", (1, 4, 1024)), ("lru_conv_b", (1, 1024)),
    ("lru_gate_a_w", (1, 2, 4, 256, 256)), ("lru_gate_a_b", (1, 2, 1024)), ("lru_gate_x_w", (1, 2, 4, 256, 256)),
    ("lru_gate_x_b", (1, 2, 1024)), ("lru_lambda", (1, 2, 1024)), ("lru_w_out", (1, 1024, 1024)),
    ("mla_w_in", (1, 1024, 704)), ("mla_q_norm_g", (1, 384)), ("mla_kv_norm_g", (1, 256)), ("mla_w_uq", (1, 384, 1536)),
    ("mla_w_ukv", (1, 256, 2048)), ("mla_w_out", (1, 1024, 1024)),
    ("moe_router", (4, 1024, 16)), ("moe_w_gate", (4, 16, 1024, 1024)), ("moe_w_up", (4, 16, 1024, 1024)),
    ("moe_w_down", (4, 16, 1024, 1024)), ("ln_g", (4, 2, 1024)), ("ln_b", (4, 2, 1024)),
]


def build_program(nseq, stages):
    nc = bass.Bass("TRN2", target_bir_lowering=False)
    specs = [("x", (nseq, S, D), F32), ("positions", (nseq, S), I32)]
    specs += [(n, s, F32) for n, s in WEIGHT_SPECS]
    specs += [("consts", (128, C_END), F32), ("consts16", (16, 16 * 128), F32)]
    A = declare_inputs(nc, nseq, specs)
    out = nc.dram_tensor("out", [nseq, S, D], F32, kind="ExternalOutput").ap()
    with contextlib.ExitStack() as stack:
        P = Prog(nc, stack)
        cx = setup_ctx(nc, stack, P)
        cx.A = A
        load_consts(cx, A["consts"], A["consts16"])
        wr = stack.enter_context(nc.sbuf_tensor("wr", [128, NK, NE], BF16))
        rwr = Reg()
        lnscr = None
        for s in range(nseq):
            for stg in stages:
                kind = stg[0]
                if kind == "load":
                    xin = A["x"][s].rearrange("(i p) d -> p i d", p=128)
                    for i in range(NT):
                        P.dma("sp", cx.R[:, i, :], xin[:, i, :], writes=[cx.rR[i]])
                elif kind == "ln":
                    _, L, j, mode = stg
                    if mode == "xb":
                        P.dma("pool", wr[:], A["moe_router"][L].rearrange("(k p) e -> p k e", p=128), writes=[rwr])
                    emit_ln(cx, A["ln_g"][L, j], A["ln_b"][L, j], mode, wr=wr, rwr=rwr, do_ln=True,
                            out_ap=out[s].rearrange("(i p) d -> p i d", p=128), scratch=lnscr)
                elif kind == "prep":
                    _, L, mode = stg
                    if mode == "xb":
                        P.dma("pool", wr[:], A["moe_router"][L].rearrange("(k p) e -> p k e", p=128), writes=[rwr])
                    emit_ln(cx, None, None, mode, wr=wr, rwr=rwr, do_ln=False, scratch=lnscr)
                elif kind == "moe":
                    _, L = stg
                    emit_moe(cx, A["moe_w_gate"][L], A["moe_w_up"][L], A["moe_w_down"][L])
                elif kind == "mla":
                    emit_mla(cx, s)
                elif kind in ("mlstm", "gla"):
                    emit_linattn(cx, kind)
                elif kind == "lru":
                    emit_lru(cx)
                elif kind == "store":
                    oo = out[s].rearrange("(i p) d -> p i d", p=128)
                    for i in range(NT):
                        P.dma("sp", oo[:, i, :], cx.R[:, i, :], reads=[cx.rR[i]])
                else:
                    raise ValueError(kind)
        P.barrier(engines=["sp"])
        print("instructions", P.ninstr, "waits", P.nwaits)
    return nc


MIXERS = ("mlstm", "gla", "lru", "mla")


def full_stages():
    st = [("load",), ("prep", 0, "xt")]
    for L in range(DEPTH):
        st.append((MIXERS[L % 4],))
        st.append(("ln", L, 0, "xb"))
        st.append(("moe", L))
        st.append(("ln", L, 1, "xt" if L < DEPTH - 1 else "none"))
    return st


NSEQ_PER_LAUNCH = 4
_PROG_CACHE = {}


def kernel(**inputs):
    nseq = NSEQ_PER_LAUNCH
    x = np.ascontiguousarray(np.asarray(inputs["x"], dtype=np.float32))
    pos = np.ascontiguousarray(np.asarray(inputs["positions"], dtype=np.int32))
    B = x.shape[0]
    per_core = B // NCORES
    consts = make_consts()
    consts16 = make_consts16()
    weights = {n: np.ascontiguousarray(np.asarray(inputs[n], dtype=np.float32)) for n, _ in WEIGHT_SPECS}
    out = np.empty((B, S, D), np.float32)
    for l0 in range(0, per_core, nseq):
        if nseq not in _PROG_CACHE:
            _PROG_CACHE[nseq] = build_program(nseq, full_stages())
        nc = _PROG_CACHE[nseq]
        in_maps = []
        for c in range(NCORES):
            b0 = c * per_core + l0
            m = {"x": x[b0:b0 + nseq], "positions": pos[b0:b0 + nseq], "consts": consts, "consts16": consts16}
            m.update(weights)
            in_maps.append(m)
        res = run_bass_kernel_spmd(nc, in_maps, core_ids=list(range(NCORES)))
        for c in range(NCORES):
            b0 = c * per_core + l0
            out[b0:b0 + nseq] = np.asarray(res.results[c]["out"]).reshape(nseq, S, D)
    return out
```

```python
import contextlib
import numpy as np
import concourse.bass as bass
import concourse.mybir as mybir
from concourse.bass_utils import run_bass_kernel_spmd

F32 = mybir.dt.float32
BF16 = mybir.dt.bfloat16
I32 = mybir.dt.int32
AF = mybir.ActivationFunctionType
ALU = mybir.AluOpType
AX = mybir.AxisListType

D = 1024
S = 2048
NT = 16
NK = 8
DEPTH = 4
ALPHA = (2 * DEPTH) ** 0.25
LN_EPS = 1e-5
NE = 16
CAP = 256
NCORES = 8
SEQ_PER_CORE = 4

SAME_ENGINE_RAW = True


_UN = [0]


def un(name):
    _UN[0] += 1
    return "%s_%d" % (name, _UN[0])


class Reg:
    __slots__ = ("w", "r", "name")

    def __init__(self, name=""):
        self.w = None
        self.r = {}
        self.name = name


class Prog:
    ENG = ("pe", "act", "dve", "pool", "sp")
    KDMA = 8

    def __init__(self, nc, stack):
        self.nc = nc
        self.eng = {"pe": nc.tensor, "act": nc.scalar, "dve": nc.vector, "pool": nc.gpsimd, "sp": nc.sync}
        self.sem = {}
        for e in self.ENG:
            self.sem[("c", e)] = stack.enter_context(nc.semaphore("s_" + e))
        self.dq = ("sp", "pool", "act")
        for q in self.dq:
            for k in range(self.KDMA):
                self.sem[("d", q, k)] = stack.enter_context(nc.semaphore("d_%s%d" % (q, k)))
        self.cnt = {e: 0 for e in self.ENG}
        self.dcnt = {q: 0 for q in self.dq}
        self.known = {e: {} for e in self.ENG}
        self.nwaits = 0
        self.ninstr = 0

    def _wait(self, eng, key, val):
        if self.known[eng].get(key, 0) >= val:
            return
        self.eng[eng].wait_ge(self.sem[key], val)
        self.known[eng][key] = val
        self.nwaits += 1

    def _deps(self, eng, reads, writes):
        me = ("c", eng)
        for r in reads:
            if r.w is not None:
                key, val = r.w
                if key == me:
                    if eng != "pe" and SAME_ENGINE_RAW:
                        self._wait(eng, key, val)
                else:
                    self._wait(eng, key, val)
        for w in writes:
            if w.w is not None:
                key, val = w.w
                if key != me:
                    self._wait(eng, key, val)
            for key, val in w.r.items():
                if key != me:
                    self._wait(eng, key, val)

    def _update(self, tok, reads, writes):
        key, val = tok
        for r in reads:
            if r.r.get(key, 0) < val:
                r.r[key] = val
        for w in writes:
            w.w = tok
            w.r = {}

    def op(self, eng, fn, reads=(), writes=()):
        self._deps(eng, reads, writes)
        ins = fn(self.eng[eng])
        self.cnt[eng] += 1
        self.ninstr += 1
        tok = (("c", eng), self.cnt[eng])
        ins.then_inc(self.sem[tok[0]], 1)
        self._update(tok, reads, writes)
        return tok

    def dma(self, q, out, in_, reads=(), writes=(), **kw):
        k = self.dcnt[q]
        slot = k % self.KDMA
        key = ("d", q, slot)
        if k >= self.KDMA:
            self._wait(q, key, 16 * (k // self.KDMA))
        self._deps(q, reads, writes)
        ins = self.eng[q].dma_start(out=out, in_=in_, **kw)
        tok = (key, 16 * (k // self.KDMA + 1))
        ins.then_inc(self.sem[key], 16)
        self.dcnt[q] = k + 1
        self.ninstr += 1
        self._update(tok, reads, writes)
        return tok

    def idma_gather(self, out, in_dram, idx_ap, reads=(), writes=()):
        q = "pool"
        k = self.dcnt[q]
        slot = k % self.KDMA
        key = ("d", q, slot)
        if k >= self.KDMA:
            self._wait(q, key, 16 * (k // self.KDMA))
        self._deps(q, reads, writes)
        ins = self.eng[q].indirect_dma_start(out=out, out_offset=None, in_=in_dram,
                                             in_offset=bass.IndirectOffsetOnAxis(ap=idx_ap, axis=0))
        tok = (key, 16 * (k // self.KDMA + 1))
        ins.then_inc(self.sem[key], 16)
        self.dcnt[q] = k + 1
        self.ninstr += 1
        self._update(tok, reads, writes)
        return tok

    def latest_tokens(self):
        toks = []
        for e in self.ENG:
            if self.cnt[e] > 0:
                toks.append((("c", e), self.cnt[e]))
        for q in self.dq:
            k = self.dcnt[q]
            for slot in range(self.KDMA):
                n = (k - slot + self.KDMA - 1) // self.KDMA
                if n > 0:
                    toks.append((("d", q, slot), 16 * n))
        return toks

    def barrier(self, engines=None):
        toks = self.latest_tokens()
        for e in (engines or self.ENG):
            for key, val in toks:
                if key != ("c", e):
                    self._wait(e, key, val)


C_IDENT = 0
C_IOTA = 128
C_PIDX = 384
C_TRIU = 386
C_TRIL = 514
C_ONES = 642
C_FREQ = 770
C_LNQ = 802
C_END = 804


def make_consts():
    c = np.zeros((128, C_END), np.float32)
    c[:, C_IDENT:C_IDENT + 128] = np.eye(128, dtype=np.float32)
    c[:, C_IOTA:C_IOTA + 256] = np.arange(256, dtype=np.float32)[None, :]
    c[:, C_PIDX] = np.arange(128)
    c[:, C_PIDX + 1] = np.arange(128) + 128
    s = np.arange(128)[:, None]
    t = np.arange(128)[None, :]
    c[:, C_TRIU:C_TRIU + 128] = (s <= t)
    c[:, C_TRIL:C_TRIL + 128] = (s >= t)
    c[:, C_ONES:C_ONES + 128] = 1.0
    c[:, C_LNQ] = np.float32(np.log(128.0 ** -0.5))
    c[:, C_FREQ:C_FREQ + 32] = (10000.0 ** (-np.arange(32, dtype=np.float32) / np.float32(32))).astype(np.float32)[None, :]
    return c


def make_consts16():
    c = np.zeros((16, 16 * 128), np.float32)
    for e in range(16):
        c[e, e * 128:(e + 1) * 128] = 1.0
    return c


class Ctx:
    pass


def setup_ctx(nc, stack, P):
    cx = Ctx()
    cx.nc = nc
    cx.P = P
    sb = lambda name, shape, dt: stack.enter_context(nc.sbuf_tensor(name, shape, dt))
    cx.R = sb("R", [128, NT, D], F32)
    cx.rR = [Reg("R%d" % i) for i in range(NT)]
    cx.XA = sb("XA", [128, NT * D], BF16)
    cx.rXA = [Reg("XA%d" % i) for i in range(NT)]
    cx.cst = sb("cst", [128, C_END], F32)
    cx.rcst = Reg("cst")
    cx.identb = sb("identb", [128, 128], BF16)
    cx.logits = sb("logits", [128, NT, NE], F32)
    cx.rlogits = [Reg("lg%d" % i) for i in range(NT)]
    cx.ps = []
    cx.rps = []
    for b in range(8):
        cx.ps.append(stack.enter_context(nc.psum_tensor("ps%d" % b, [128, 512], F32)))
        cx.rps.append(Reg("ps%d" % b))
    return cx


def load_consts(cx, consts_ap, consts16_ap):
    P = cx.P
    P.dma("sp", cx.cst[:], consts_ap[:, :], writes=[cx.rcst])
    P.op("dve", lambda e: e.tensor_copy(out=cx.identb[:], in_=cx.cst[:, C_IDENT:C_IDENT + 128]),
         reads=[cx.rcst], writes=[cx.rcst])


def emit_ln(cx, lng_ap, lnb_ap, mode, wr=None, rwr=None, do_ln=True, out_ap=None, scratch=None):
    P = cx.P
    nc = cx.nc
    with contextlib.ExitStack() as lst:
        _emit_ln(cx, lst, lng_ap, lnb_ap, mode, wr, rwr, do_ln, out_ap)
    P.barrier()


def _emit_ln(cx, lst, lng_ap, lnb_ap, mode, wr, rwr, do_ln, out_ap):
    P = cx.P
    nc = cx.nc
    cx.lng = lst.enter_context(nc.sbuf_tensor(un("lng"), [128, 2, D], F32))
    cx.rlng = Reg("lng")
    tmpT, rtmpT, stat, rstat, xbt, rxbt = alloc_ln_scratch(cx, lst)
    if do_ln:
        P.dma("sp", cx.lng[:, 0, :], lng_ap.partition_broadcast(128), writes=[cx.rlng])
        P.dma("sp", cx.lng[:, 1, :], lnb_ap.partition_broadcast(128), writes=[cx.rlng])
    for i in range(NT):
        Ri = cx.R[:, i, :]
        rRi = cx.rR[i]
        st = stat[i % 2]
        rst = rstat[i % 2]
        if do_ln:
            for h in range(2):
                P.op("dve", lambda e, h=h: e.bn_stats(out=st[:, h * 6:(h + 1) * 6], in_=Ri[:, h * 512:(h + 1) * 512]),
                     reads=[rRi], writes=[rst])
            P.op("dve", lambda e: e.bn_aggr(out=st[:, 12:14], in_=st[:, 0:12]), reads=[rst], writes=[rst])
            P.op("dve", lambda e: e.tensor_scalar(out=st[:, 14:15], in0=st[:, 13:14], scalar1=LN_EPS, scalar2=None, op0=ALU.add),
                 reads=[rst], writes=[rst])
            P.op("act", lambda e: e.activation(out=st[:, 14:15], in_=st[:, 14:15], func=AF.Sqrt), reads=[rst], writes=[rst])
            P.op("dve", lambda e: e.reciprocal(out=st[:, 15:16], in_=st[:, 14:15]), reads=[rst], writes=[rst])
            P.op("dve", lambda e: e.tensor_scalar(out=Ri, in0=Ri, scalar1=st[:, 12:13], scalar2=st[:, 15:16],
                                                   op0=ALU.subtract, op1=ALU.mult), reads=[rRi, rst], writes=[rRi])
            P.op("pool", lambda e: e.tensor_tensor(out=Ri, in0=Ri, in1=cx.lng[:, 0, :], op=ALU.mult),
                 reads=[rRi, cx.rlng], writes=[rRi])
            P.op("pool", lambda e: e.tensor_tensor(out=Ri, in0=Ri, in1=cx.lng[:, 1, :], op=ALU.add),
                 reads=[rRi, cx.rlng], writes=[rRi])
        if mode == "none":
            if out_ap is not None:
                P.dma("sp", out_ap[:, i, :], Ri, reads=[rRi])
            continue
        if mode == "xb":
            xb = cx.XA[:, i * D:(i + 1) * D]
            rxb = cx.rXA[i]
        else:
            xb = xbt[i % 2][:]
            rxb = rxbt[i % 2]
        P.op("act", lambda e: e.activation(out=xb, in_=Ri, func=AF.Copy), reads=[rRi], writes=[rxb])
        P.op("pool", lambda e: e.tensor_scalar(out=Ri, in0=Ri, scalar1=float(ALPHA), scalar2=0.0, op0=ALU.mult, op1=ALU.add),
             reads=[rRi], writes=[rRi])
        pb = 6 + (i % 2)
        pst = cx.ps[pb][:].bitcast(BF16)
        for k in range(NK):
            P.op("pe", lambda e, k=k: e.transpose(out=pst[:, k * 128:(k + 1) * 128], in_=xb[:, k * 128:(k + 1) * 128],
                                                  identity=cx.identb[:]),
                 reads=[rxb, cx.rcst], writes=[cx.rps[pb]])
        if mode == "xt":
            xt = cx.XA[:].rearrange("p (k t) -> p k t", k=NK)
            P.op("dve", lambda e: e.tensor_copy(out=xt[:, :, i * 128:(i + 1) * 128],
                                                in_=pst.rearrange("p (k t) -> p k t", k=NK)),
                 reads=[cx.rps[pb]], writes=[cx.rXA[i]])
        else:
            tt = tmpT[i % 2]
            rtt = rtmpT[i % 2]
            P.op("dve", lambda e: e.tensor_copy(out=tt[:], in_=pst), reads=[cx.rps[pb]], writes=[rtt])
            lb = 4 + (i % 2)
            for k in range(NK):
                P.op("pe", lambda e, k=k: e.matmul(cx.ps[lb][:, 0:NE], lhsT=tt[:, k * 128:(k + 1) * 128], rhs=wr[:, k, :],
                                                   start=(k == 0), stop=(k == NK - 1)),
                     reads=[rtt, rwr], writes=[cx.rps[lb]])
            P.op("act", lambda e: e.activation(out=cx.logits[:, i, :], in_=cx.ps[lb][:, 0:NE], func=AF.Copy),
                 reads=[cx.rps[lb]], writes=[cx.rlogits[i]])


def alloc_ln_scratch(cx, stack):
    nc = cx.nc
    tmpT = [stack.enter_context(nc.sbuf_tensor(un("tmpT"), [128, D], BF16)) for j in range(2)]
    rtmpT = [Reg(), Reg()]
    stat = [stack.enter_context(nc.sbuf_tensor(un("stat"), [128, 16], F32)) for j in range(2)]
    rstat = [Reg(), Reg()]
    xbt = [stack.enter_context(nc.sbuf_tensor(un("xbt"), [128, D], BF16)) for j in range(2)]
    rxbt = [Reg(), Reg()]
    return (tmpT, rtmpT, stat, rstat, xbt, rxbt)


def emit_moe(cx, wg_ap, wu_ap, wd_ap):
    P = cx.P
    nc = cx.nc
    with contextlib.ExitStack() as st:
        sb = lambda name, shape, dt: st.enter_context(nc.sbuf_tensor(un(name), shape, dt))
        aff = sb("aff", [128, NT, NE], F32)
        r_aff = Reg()
        sm = sb("sm", [128, NT, 4], F32)
        AG = sb("AG", [128, NT, NE, 4], BF16)
        r_AG = Reg()
        slot = sb("slot", [128, NT, NE], F32)
        r_slot = Reg()
        slotT = sb("slotT", [16, S], F32)
        r_slotT = Reg()
        m8 = sb("m8", [16, 8], F32)
        r_m8 = Reg()
        wbuf = [sb("wbuf%d" % j, [128, NK, D], BF16) for j in range(3)]
        r_wbuf = [Reg() for j in range(3)]
        Pm = sb("Pm", [128, NT, CAP], BF16)
        r_Pm = Reg()
        PT = sb("PT", [128, 2, S], BF16)
        r_PT = Reg()
        xinT = sb("xinT", [128, NK, CAP], BF16)
        r_xinT = Reg()
        hT = sb("hT", [128, NK, CAP], BF16)
        r_hT = Reg()
        sg = [sb("sg%d" % j, [128, CAP], F32) for j in range(2)]
        r_sg = [Reg(), Reg()]
        Oe = sb("Oe", [128, 2, D], BF16)
        r_Oe = Reg()
        gs = sb("gs", [128, 4], F32)
        r_gs = Reg()
        affT = Pm[:].rearrange("p a b -> p (a b)").bitcast(F32)[0:16, :]
        r_affT = Reg()
        work = PT[:].rearrange("p a b -> p (a b)").bitcast(F32)[0:16, :]
        r_work = Reg()

        ident = cx.cst[:, C_IDENT:C_IDENT + 128]

        rl = cx.rlogits
        P.op("dve", lambda e: e.tensor_reduce(out=sm[:, :, 0], in_=cx.logits[:], axis=AX.X, op=ALU.max),
             reads=rl, writes=[r_aff])
        P.op("dve", lambda e: e.tensor_tensor(out=aff[:], in0=cx.logits[:],
                                              in1=sm[:, :, 0:1].to_broadcast([128, NT, NE]), op=ALU.subtract),
             reads=rl + [r_aff], writes=[r_aff])
        P.op("act", lambda e: e.activation(out=aff[:], in_=aff[:], func=AF.Exp), reads=[r_aff], writes=[r_aff])
        P.op("dve", lambda e: e.tensor_reduce(out=sm[:, :, 1], in_=aff[:], axis=AX.X, op=ALU.add),
             reads=[r_aff], writes=[r_aff])
        P.op("dve", lambda e: e.reciprocal(out=sm[:, :, 2], in_=sm[:, :, 1]), reads=[r_aff], writes=[r_aff])
        P.op("dve", lambda e: e.tensor_tensor(out=aff[:], in0=aff[:],
                                              in1=sm[:, :, 2:3].to_broadcast([128, NT, NE]), op=ALU.mult),
             reads=[r_aff], writes=[r_aff])
        P.op("dve", lambda e: e.tensor_copy(out=AG[:, :, :, 0], in_=aff[:]), reads=[r_aff], writes=[r_AG])
        P.op("dve", lambda e: e.tensor_tensor(out=AG[:, :, :, 1], in0=aff[:], in1=AG[:, :, :, 0], op=ALU.subtract),
             reads=[r_aff, r_AG], writes=[r_AG])
        for i in range(NT):
            P.op("pool", lambda e, i=i: e.memset(AG[:, i, :, 2], float(i)), writes=[r_AG])
        P.op("dve", lambda e: e.tensor_copy(out=AG[:, :, :, 3], in_=cx.cst[:, C_PIDX:C_PIDX + 1].to_broadcast([128, NT * NE]).rearrange("p (a b) -> p a b", a=NT)),
             reads=[cx.rcst], writes=[r_AG])
        for i in range(NT):
            P.dma("sp", cx.xd[i * 128:(i + 1) * 128, :], cx.XA[:, i * D:(i + 1) * D], reads=[cx.rXA[i]], writes=[cx.r_xd[i]])
        for g in range(4):
            for j in range(4):
                i = g * 4 + j
                P.op("pe", lambda e, i=i, j=j, g=g: e.transpose(out=cx.ps[g][0:16, j * 128:(j + 1) * 128],
                                                                in_=aff[:, i, :], identity=ident),
                     reads=[r_aff, cx.rcst], writes=[cx.rps[g]])
            P.op("act", lambda e, g=g: e.activation(out=affT[:, g * 512:(g + 1) * 512], in_=cx.ps[g][0:16, :], func=AF.Copy),
                 reads=[cx.rps[g]], writes=[r_affT])
        P.op("pool", lambda e: e.tensor_copy(out=work[:], in_=affT[:]), reads=[r_affT], writes=[r_work])
        nround = CAP // 8
        for r in range(nround):
            P.op("dve", lambda e: e.max(out=m8[:], in_=work[:]), reads=[r_work], writes=[r_m8])
            if r < nround - 1:
                P.op("dve", lambda e: e.match_replace(out=work[:], in_to_replace=m8[:], in_values=work[:], imm_value=-1.0),
                     reads=[r_work, r_m8], writes=[r_work])
        P.op("dve", lambda e: e.tensor_scalar(out=work[:], in0=affT[:], scalar1=m8[:, 7:8], scalar2=None, op0=ALU.is_ge),
             reads=[r_affT, r_m8], writes=[r_work])
        P.op("dve", lambda e: e.tensor_tensor_scan(out=slotT[:], data0=cx.cst[0:16, C_ONES:C_ONES + 1].to_broadcast([16, S]), data1=work[:], initial=0.0,
                                                    op0=ALU.mult, op1=ALU.add),
             reads=[r_work, cx.rcst], writes=[r_slotT])
        P.op("dve", lambda e: e.tensor_tensor(out=slotT[:], in0=slotT[:], in1=work[:], op=ALU.mult),
             reads=[r_work, r_slotT], writes=[r_slotT])
        P.op("dve", lambda e: e.tensor_scalar(out=slotT[:], in0=slotT[:], scalar1=-1.0, scalar2=None, op0=ALU.add),
             reads=[r_slotT], writes=[r_slotT])
        for i in range(NT):
            P.op("pe", lambda e, i=i: e.transpose(out=cx.ps[5][:, i * NE:(i + 1) * NE], in_=slotT[:, i * 128:(i + 1) * 128],
                                                  identity=ident[0:16, 0:16]),
                 reads=[r_slotT, cx.rcst], writes=[cx.rps[5]])
        P.op("act", lambda e: e.activation(out=slot[:].rearrange("p a b -> p (a b)"), in_=cx.ps[5][:, 0:NT * NE], func=AF.Copy),
             reads=[cx.rps[5]], writes=[r_slot])

        P.barrier()
        r_xk = [Reg() for k in range(NK)]
        r_hj = [Reg() for j in range(NK)]
        r_Oq = [[Reg() for n in range(2)] for c in range(2)]
        slotTb = sb("slotTb", [16, S], BF16)
        c16b = sb("c16b", [16, 16 * 128], BF16)
        P.op("act", lambda e: e.activation(out=slotTb[:], in_=slotT[:], func=AF.Copy), reads=[r_slotT], writes=[r_slotT])
        P.dma("pool", c16b[:], cx.A["consts16"][:, :], writes=[cx.rcst])
        wq = [0]
        xa3 = cx.XA[:].rearrange("p (s k f) -> p s k f", s=2, k=NK)
        wslot = [wbuf[0][:], wbuf[1][:], wbuf[2][:], xa3[:, 0], xa3[:, 1]]
        rw = [[r_wbuf[0]], [r_wbuf[1]], [r_wbuf[2]], [Reg()] + cx.rXA[0:8], [Reg()] + cx.rXA[8:16]]

        def load_w(ap2d):
            j = wq[0] % 5
            wq[0] += 1
            P.dma("pool", wslot[j], ap2d.rearrange("(k p) f -> p k f", p=128), writes=rw[j])
            return j

        def build_P(ex):
            for i in range(NT):
                P.op("dve", lambda e, i=i: e.tensor_scalar(out=Pm[:, i, :], in0=cx.cst[:, C_IOTA:C_IOTA + CAP],
                                                           scalar1=slot[:, i, ex:ex + 1], scalar2=None, op0=ALU.is_equal),
                     reads=[r_slot, cx.rcst], writes=[r_Pm])

        xin = sb("xin", [128, 2, D], BF16)
        r_xin = [Reg(), Reg()]
        gs2 = [sb("gs2", [128, 8], F32) for j in range(2)]
        r_gs2 = [Reg(), Reg()]
        idx2 = [sb("idx2", [128, 2], I32) for j in range(2)]
        r_idx2 = [Reg(), Reg()]

        def gsidx_and_gather(ex):
            g_ = gs2[ex % 2]
            r_g = r_gs2[ex % 2]
            for c in range(2):
                for i in range(NT):
                    P.op("pe", lambda e, i=i, c=c: e.matmul(cx.ps[6][:, 8 + c * 4:8 + c * 4 + 4], lhsT=Pm[:, i, c * 128:(c + 1) * 128],
                                                            rhs=AG[:, i, ex, :], start=(i == 0), stop=(i == NT - 1)),
                         reads=[r_Pm, r_AG], writes=[cx.rps[6]])
            P.op("dve", lambda e: e.tensor_copy(out=g_[:, 0:8], in_=cx.ps[6][:, 8:16]), reads=[cx.rps[6]], writes=[r_g])
            gv = g_[:, 0:8].rearrange("p (c f) -> p c f", c=2)
            P.op("dve", lambda e: e.tensor_tensor(out=gv[:, :, 0], in0=gv[:, :, 0], in1=gv[:, :, 1], op=ALU.add), reads=[r_g], writes=[r_g])
            P.op("dve", lambda e: e.scalar_tensor_tensor(out=gv[:, :, 2], in0=gv[:, :, 2], scalar=128.0, in1=gv[:, :, 3], op0=ALU.mult, op1=ALU.add),
                 reads=[r_g], writes=[r_g])
            ix = idx2[ex % 2]
            P.op("dve", lambda e: e.tensor_copy(out=ix[:], in_=gv[:, :, 2]), reads=[r_g], writes=[r_idx2[ex % 2]])
            for c in range(2):
                P.idma_gather(xin[:, c, :], cx.xd[:, :], ix[:, c:c + 1], reads=[r_idx2[ex % 2]] + cx.r_xd, writes=[r_xin[c]])

        build_P(0)
        gsidx_and_gather(0)
        jnext = (load_w(wg_ap[0]), load_w(wu_ap[0]), load_w(wd_ap[0]))
        for ex in range(NE):
            jg, ju, jd = jnext
            if ex + 1 < NE:
                jg1 = load_w(wg_ap[ex + 1])
                ju1 = load_w(wu_ap[ex + 1])
            gsc = gs2[ex % 2][:, 0:8].rearrange("p (c f) -> p c f", c=2)
            r_gs = r_gs2[ex % 2]
            for c in range(2):
                pb = 4 + c
                pst = cx.ps[pb][:].bitcast(BF16)
                for k in range(NK):
                    P.op("pe", lambda e, k=k, c=c: e.transpose(out=pst[:, k * 128:(k + 1) * 128], in_=xin[:, c, k * 128:(k + 1) * 128], identity=cx.identb[:]),
                         reads=[r_xin[c], cx.rcst], writes=[cx.rps[pb]])
                P.op("act", lambda e, c=c: e.activation(out=xinT[:, :, c * 128:(c + 1) * 128], in_=pst.rearrange("p (k t) -> p k t", k=NK), func=AF.Copy),
                     reads=[cx.rps[pb]], writes=r_xk)
            for j in range(NK):
                pg = 4 + (j % 2)
                pu = 6 + (j % 2)
                for k in range(NK):
                    P.op("pe", lambda e, j=j, k=k, pg=pg: e.matmul(cx.ps[pg][:, 0:CAP], lhsT=wslot[jg][:, k, j * 128:(j + 1) * 128],
                                                                   rhs=xinT[:, k, :], start=(k == 0), stop=(k == NK - 1)),
                         reads=rw[jg] + [r_xk[k]], writes=[cx.rps[pg]])
                for k in range(NK):
                    P.op("pe", lambda e, j=j, k=k, pu=pu: e.matmul(cx.ps[pu][:, 0:CAP], lhsT=wslot[ju][:, k, j * 128:(j + 1) * 128],
                                                                   rhs=xinT[:, k, :], start=(k == 0), stop=(k == NK - 1)),
                         reads=rw[ju] + [r_xk[k]], writes=[cx.rps[pu]])
                P.op("act", lambda e, j=j, pg=pg: e.activation(out=sg[j % 2][:], in_=cx.ps[pg][:, 0:CAP], func=AF.Silu),
                     reads=[cx.rps[pg]], writes=[r_sg[j % 2]])
                P.op("dve", lambda e, j=j, pu=pu: e.tensor_tensor(out=hT[:, j, :], in0=sg[j % 2][:], in1=cx.ps[pu][:, 0:CAP], op=ALU.mult),
                     reads=[cx.rps[pu], r_sg[j % 2]], writes=[r_hj[j]])
                if j == 1 and ex + 1 < NE:
                    build_P(ex + 1)
                if j == 5 and ex + 1 < NE:
                    gsidx_and_gather(ex + 1)
            if ex + 1 < NE:
                jnext = (jg1, ju1, load_w(wd_ap[ex + 1]))
            for g in range(4):
                P.op("pe", lambda e, g=g: e.matmul(cx.ps[g][:, :], lhsT=c16b[:, ex * 128:(ex + 1) * 128],
                                                   rhs=slotTb[:, g * 512:(g + 1) * 512], start=True, stop=True),
                     reads=[r_slotT, cx.rcst], writes=[cx.rps[g]])
            for g in range(4):
                for c in range(2):
                    P.op("dve", lambda e, g=g, c=c: e.tensor_scalar(out=PT[:, c, g * 512:(g + 1) * 512], in0=cx.ps[g][:, :],
                                                                    scalar1=cx.cst[:, C_PIDX + c:C_PIDX + c + 1], scalar2=None,
                                                                    op0=ALU.is_equal),
                         reads=[cx.rps[g], cx.rcst], writes=[r_PT])
            for c in range(2):
                for n in range(2):
                    pb = 4 + ((c * 2 + n) % 4)
                    for j in range(NK):
                        P.op("pe", lambda e, j=j, c=c, n=n, pb=pb: e.matmul(cx.ps[pb][:, :], lhsT=hT[:, j, c * 128:(c + 1) * 128],
                                                                            rhs=wslot[jd][:, j, n * 512:(n + 1) * 512],
                                                                            start=(j == 0), stop=(j == NK - 1)),
                             reads=rw[jd] + [r_hj[j]], writes=[cx.rps[pb]])
                    P.op("act", lambda e, c=c, n=n, pb=pb: e.activation(out=Oe[:, c, n * 512:(n + 1) * 512], in_=cx.ps[pb][:, :],
                                                                        func=AF.Copy, scale=gsc[:, c, 0:1]),
                         reads=[cx.rps[pb], r_gs], writes=[r_Oq[c][n]])
            for i in range(NT):
                for n in range(2):
                    pb = (i * 2 + n) % 4
                    for c in range(2):
                        P.op("pe", lambda e, i=i, n=n, c=c, pb=pb: e.matmul(cx.ps[pb][:, :], lhsT=PT[:, c, i * 128:(i + 1) * 128],
                                                                            rhs=Oe[:, c, n * 512:(n + 1) * 512],
                                                                            start=(c == 0), stop=(c == 1)),
                             reads=[r_PT, r_Oq[c][n]], writes=[cx.rps[pb]])
                    P.op("dve", lambda e, i=i, n=n, pb=pb: e.tensor_tensor(out=cx.R[:, i, n * 512:(n + 1) * 512],
                                                                           in0=cx.R[:, i, n * 512:(n + 1) * 512],
                                                                           in1=cx.ps[pb][:, :], op=ALU.add),
                         reads=[cx.rps[pb], cx.rR[i]], writes=[cx.rR[i]])
    P.barrier()


def xt_view(cx):
    return cx.XA[:].rearrange("p (k t) -> p k t", k=NK)


def small_load(cx, out, in_, reg):
    cx.P.dma("sp", out, in_, writes=[reg], allow_slow_non_contiguous=True)


def gelu_tanh(cx, out, src, t1, t2, r_src, r_t1, r_t2, r_out, extra_mul=None, r_extra=None):
    P = cx.P
    P.op("act", lambda e: e.activation(out=t1, in_=src, func=AF.Square), reads=[r_src], writes=[r_t1])
    P.op("dve", lambda e: e.tensor_scalar(out=t1, in0=t1, scalar1=0.044715, scalar2=1.0, op0=ALU.mult, op1=ALU.add),
         reads=[r_t1], writes=[r_t1])
    P.op("dve", lambda e: e.tensor_tensor(out=t1, in0=t1, in1=src, op=ALU.mult), reads=[r_t1, r_src], writes=[r_t1])
    P.op("act", lambda e: e.activation(out=t1, in_=t1, func=AF.Sigmoid, scale=1.5957691216057308), reads=[r_t1], writes=[r_t1])
    if extra_mul is None:
        P.op("dve", lambda e: e.tensor_tensor(out=out, in0=t1, in1=src, op=ALU.mult), reads=[r_t1, r_src], writes=[r_out])
    else:
        P.op("dve", lambda e: e.tensor_tensor(out=t2, in0=t1, in1=src, op=ALU.mult), reads=[r_t1, r_src], writes=[r_t2])
        P.op("pool", lambda e: e.tensor_tensor(out=out, in0=t2, in1=extra_mul, op=ALU.mult), reads=[r_t2, r_extra], writes=[r_out])


def emit_lru(cx):
    P = cx.P
    nc = cx.nc
    A = cx.A
    XT = xt_view(cx)
    with contextlib.ExitStack() as st:
        sb = lambda name, shape, dt: st.enter_context(nc.sbuf_tensor(un(name), shape, dt))
        cw = sb("cw", [128, NK, 4], F32)
        cb = sb("cb", [128, NK], F32)
        gab = sb("gab", [128, 2, NK], F32)
        gxb = sb("gxb", [128, 2, NK], F32)
        lam = sb("lam", [128, 2, NK], F32)
        c8 = sb("c8", [128, 2, NK], F32)
        c16 = sb("c16", [128, 2, NK], F32)
        r_par = Reg()
        for j in range(4):
            small_load(cx, cw[:, :, j], A["lru_conv_w"][0, j].rearrange("(c p) -> p c", p=128), r_par)
        small_load(cx, cb[:], A["lru_conv_b"][0].rearrange("(c p) -> p c", p=128), r_par)
        for g in range(2):
            small_load(cx, gab[:, g, :], A["lru_gate_a_b"][0, g].rearrange("(c p) -> p c", p=128), r_par)
        for g in range(2):
            small_load(cx, gxb[:, g, :], A["lru_gate_x_b"][0, g].rearrange("(c p) -> p c", p=128), r_par)
        for g in range(2):
            small_load(cx, lam[:, g, :], A["lru_lambda"][0, g].rearrange("(c p) -> p c", p=128), r_par)
        P.op("act", lambda e: e.activation(out=c8[:], in_=lam[:], func=AF.Exp, scale=-1.0), reads=[r_par], writes=[r_par])
        P.op("act", lambda e: e.activation(out=c8[:], in_=c8[:], func=AF.Ln, bias=1.0), reads=[r_par], writes=[r_par])
        P.op("dve", lambda e: e.tensor_scalar(out=c16[:], in0=c8[:], scalar1=-16.0, scalar2=None, op0=ALU.mult), reads=[r_par], writes=[r_par])
        P.op("dve", lambda e: e.tensor_scalar(out=c8[:], in0=c8[:], scalar1=-8.0, scalar2=None, op0=ALU.mult), reads=[r_par], writes=[r_par])

        wgb = sb("wgb", [128, NK, 256], BF16)
        r_wgb = Reg()
        wu = sb("wu", [128, NK, 256], BF16)
        r_wu = Reg()
        wg4 = [[sb("wga", [128, 2, 256], BF16) for d in range(2)] for ax in range(2)]
        r_wg4 = [[Reg() for d in range(2)] for ax in range(2)]
        wo = sb("wo", [128, D], BF16)
        r_wo = Reg()
        upad = sb("upad", [128, S + 4], F32)
        r_upad = Reg()
        uc = sb("uc", [128, 2, S], F32)
        r_uc = [Reg(), Reg()]
        ucb = sb("ucb", [128, 2, S], BF16)
        r_ucb = [Reg(), Reg()]
        rbuf = sb("rbuf", [128, S], F32)
        r_rbuf = Reg()
        abuf = sb("abuf", [128, S], F32)
        r_abuf = Reg()
        tmp = sb("tmp", [128, S], F32)
        r_tmp = Reg()
        hsum = sb("hsum", [128, S], F32)
        r_hsum = Reg()
        gtc = sb("gtc", [128, S], BF16)
        r_gtc = Reg()
        P.op("pool", lambda e: e.memset(upad[:], 0.0), writes=[r_upad])

        def psum4(base):
            return [cx.ps[base + j] for j in range(4)], [cx.rps[base + j] for j in range(4)]

        pcount = [0]

        def next_ps4():
            base = 4 * (pcount[0] % 2)
            pcount[0] += 1
            return psum4(base)

        for n in range(4):
            P.dma("pool", wgb[:], A["lru_w_in"][0][:, n * 256:(n + 1) * 256].rearrange("(k p) c -> p k c", p=128), writes=[r_wgb])
            P.dma("pool", wu[:], A["lru_w_in"][0][:, D + n * 256:D + (n + 1) * 256].rearrange("(k p) c -> p k c", p=128), writes=[r_wu])
            for ax, nm in enumerate(("lru_gate_a_w", "lru_gate_x_w")):
                for d in range(2):
                    P.dma("pool", wg4[ax][d][:], A[nm][0, d, n].rearrange("(i p) j -> p i j", p=128), writes=[r_wg4[ax][d]])
            for cc in range(2):
                c = 2 * n + cc
                ps, rps = next_ps4()
                for tb in range(4):
                    for k in range(NK):
                        P.op("pe", lambda e, tb=tb, k=k: e.matmul(ps[tb][:, :], lhsT=wu[:, k, cc * 128:(cc + 1) * 128],
                                                                  rhs=XT[:, k, tb * 512:(tb + 1) * 512], start=(k == 0), stop=(k == NK - 1)),
                             reads=[r_wu] + cx.rXA[tb * 4:(tb + 1) * 4], writes=[rps[tb]])
                    P.op("act", lambda e, tb=tb: e.activation(out=upad[:, 2 + tb * 512:2 + (tb + 1) * 512], in_=ps[tb][:, :], func=AF.Copy),
                         reads=[rps[tb]], writes=[r_upad])
                ucc = uc[:, cc, :]
                P.op("dve", lambda e: e.tensor_scalar(out=ucc, in0=upad[:, 0:S], scalar1=cw[:, c, 0:1], scalar2=cb[:, c:c + 1],
                                                       op0=ALU.mult, op1=ALU.add), reads=[r_upad, r_par], writes=[r_uc[cc]])
                for j in range(1, 4):
                    P.op("dve", lambda e, j=j: e.scalar_tensor_tensor(out=ucc, in0=upad[:, j:j + S], scalar=cw[:, c, j:j + 1], in1=ucc,
                                                                      op0=ALU.mult, op1=ALU.add), reads=[r_upad, r_par, r_uc[cc]], writes=[r_uc[cc]])
                P.op("pool", lambda e: e.tensor_copy(out=ucb[:, cc, :], in_=ucc), reads=[r_uc[cc]], writes=[r_ucb[cc]])
            for cc in range(2):
                c = 2 * n + cc
                ucc = uc[:, cc, :]
                for d in range(2):
                    ps, rps = next_ps4()
                    for tb in range(4):
                        for i in range(2):
                            P.op("pe", lambda e, tb=tb, i=i: e.matmul(ps[tb][:, :], lhsT=wg4[0][d][:, i, cc * 128:(cc + 1) * 128],
                                                                      rhs=ucb[:, i, tb * 512:(tb + 1) * 512], start=(i == 0), stop=(i == 1)),
                                 reads=[r_wg4[0][d], r_ucb[0], r_ucb[1]], writes=[rps[tb]])
                        P.op("act", lambda e, tb=tb: e.activation(out=rbuf[:, tb * 512:(tb + 1) * 512], in_=ps[tb][:, :], func=AF.Sigmoid,
                                                                  bias=gab[:, d, c:c + 1]), reads=[rps[tb], r_par], writes=[r_rbuf])
                    P.op("act", lambda e: e.activation(out=abuf[:], in_=rbuf[:], func=AF.Exp, scale=c8[:, d, c:c + 1]),
                         reads=[r_rbuf, r_par], writes=[r_abuf])
                    P.op("act", lambda e: e.activation(out=tmp[:], in_=rbuf[:], func=AF.Exp, scale=c16[:, d, c:c + 1]),
                         reads=[r_rbuf, r_par], writes=[r_tmp])
                    P.op("act", lambda e: e.activation(out=tmp[:], in_=tmp[:], func=AF.Sqrt, scale=-1.0, bias=1.0),
                         reads=[r_tmp], writes=[r_tmp])
                    ps, rps = next_ps4()
                    for tb in range(4):
                        for i in range(2):
                            P.op("pe", lambda e, tb=tb, i=i: e.matmul(ps[tb][:, :], lhsT=wg4[1][d][:, i, cc * 128:(cc + 1) * 128],
                                                                      rhs=ucb[:, i, tb * 512:(tb + 1) * 512], start=(i == 0), stop=(i == 1)),
                                 reads=[r_wg4[1][d], r_ucb[0], r_ucb[1]], writes=[rps[tb]])
                        P.op("act", lambda e, tb=tb: e.activation(out=rbuf[:, tb * 512:(tb + 1) * 512], in_=ps[tb][:, :], func=AF.Sigmoid,
                                                                  bias=gxb[:, d, c:c + 1]), reads=[rps[tb], r_par], writes=[r_rbuf])
                    P.op("dve", lambda e: e.tensor_tensor(out=tmp[:], in0=tmp[:], in1=rbuf[:], op=ALU.mult), reads=[r_tmp, r_rbuf], writes=[r_tmp])
                    P.op("dve", lambda e: e.tensor_tensor(out=tmp[:], in0=tmp[:], in1=ucc, op=ALU.mult), reads=[r_tmp, r_uc[cc]], writes=[r_tmp])
                    if d == 0:
                        P.op("dve", lambda e: e.tensor_tensor_scan(out=hsum[:], data0=abuf[:], data1=tmp[:], initial=0.0,
                                                                    op0=ALU.mult, op1=ALU.add), reads=[r_abuf, r_tmp], writes=[r_hsum])
                    else:
                        P.op("dve", lambda e: e.tensor_tensor_scan(out=rbuf[:, ::-1], data0=abuf[:, ::-1], data1=tmp[:, ::-1], initial=0.0,
                                                                    op0=ALU.mult, op1=ALU.add), reads=[r_abuf, r_tmp], writes=[r_rbuf])
                        P.op("pool", lambda e: e.tensor_tensor(out=hsum[:], in0=hsum[:], in1=rbuf[:], op=ALU.add),
                             reads=[r_rbuf, r_hsum], writes=[r_hsum])
                ps, rps = next_ps4()
                for tb in range(4):
                    for k in range(NK):
                        P.op("pe", lambda e, tb=tb, k=k: e.matmul(ps[tb][:, :], lhsT=wgb[:, k, cc * 128:(cc + 1) * 128],
                                                                  rhs=XT[:, k, tb * 512:(tb + 1) * 512], start=(k == 0), stop=(k == NK - 1)),
                             reads=[r_wgb] + cx.rXA[tb * 4:(tb + 1) * 4], writes=[rps[tb]])
                    sl = slice(tb * 512, (tb + 1) * 512)
                    gelu_tanh(cx, gtc[:, sl], ps[tb][:, :], abuf[:, sl], tmp[:, sl], rps[tb], r_abuf, r_tmp, r_gtc,
                              extra_mul=hsum[:, sl], r_extra=r_hsum)
                P.dma("pool", wo[:], A["lru_w_out"][0][c * 128:(c + 1) * 128, :], writes=[r_wo])
                for i in range(NT):
                    for hh in range(2):
                        pb = (i * 2 + hh) % 8
                        P.op("pe", lambda e, i=i, hh=hh, pb=pb: e.matmul(cx.ps[pb][:, :], lhsT=gtc[:, i * 128:(i + 1) * 128],
                                                                         rhs=wo[:, hh * 512:(hh + 1) * 512], start=True, stop=True),
                             reads=[r_gtc, r_wo], writes=[cx.rps[pb]])
                        P.op("dve", lambda e, i=i, hh=hh, pb=pb: e.tensor_tensor(out=cx.R[:, i, hh * 512:(hh + 1) * 512],
                                                                                 in0=cx.R[:, i, hh * 512:(hh + 1) * 512],
                                                                                 in1=cx.ps[pb][:, :], op=ALU.add),
                             reads=[cx.rps[pb], cx.rR[i]], writes=[cx.rR[i]])
    P.barrier()


MLA_H = 8
MLA_SCALE = 192.0 ** -0.5
TWO_PI_HI = 6.28125
TWO_PI_LO = 2.0 * np.pi - 6.28125
MAGIC = 12582912.0


def rope_apply(cx, out_bf, src, cos, sin, t1, t2, nb, r_src, r_tab, r_t, r_out):
    P = cx.P
    a1 = src[:, :, 0:32]
    a2 = src[:, :, 32:64]
    P.op("dve", lambda e: e.tensor_tensor(out=t1, in0=a1, in1=cos, op=ALU.mult), reads=[r_src, r_tab], writes=[r_t])
    P.op("dve", lambda e: e.tensor_tensor(out=t2, in0=a2, in1=sin, op=ALU.mult), reads=[r_src, r_tab], writes=[r_t])
    P.op("dve", lambda e: e.tensor_tensor(out=out_bf[:, :, 0:32], in0=t1, in1=t2, op=ALU.subtract), reads=[r_t], writes=[r_out])
    P.op("dve", lambda e: e.tensor_tensor(out=t1, in0=a2, in1=cos, op=ALU.mult), reads=[r_src, r_tab, r_out], writes=[r_t])
    P.op("dve", lambda e: e.tensor_tensor(out=t2, in0=a1, in1=sin, op=ALU.mult), reads=[r_src, r_tab], writes=[r_t])
    P.op("dve", lambda e: e.tensor_tensor(out=out_bf[:, :, 32:64], in0=t1, in1=t2, op=ALU.add), reads=[r_t], writes=[r_out])


def emit_mla(cx, seq):
    P = cx.P
    nc = cx.nc
    A = cx.A
    XT = xt_view(cx)
    with contextlib.ExitStack() as st:
        sb = lambda name, shape, dt: st.enter_context(nc.sbuf_tensor(un(name), shape, dt))
        posi = sb("posi", [128, NT], I32)
        ang = sb("ang", [128, NT, 32], F32)
        nn = sb("nn", [128, NT, 32], F32)
        cosT = sb("cosT", [128, NT, 32], F32)
        sinT = sb("sinT", [128, NT, 32], F32)
        r_tab = Reg()
        small_load(cx, posi[:], A["positions"][seq].rearrange("(i p) -> p i", p=128), r_tab)
        P.op("dve", lambda e: e.tensor_copy(out=nn[:, :, 0], in_=posi[:]), reads=[r_tab], writes=[r_tab])
        P.op("dve", lambda e: e.tensor_tensor(out=ang[:], in0=nn[:, :, 0:1].to_broadcast([128, NT, 32]),
                                              in1=cx.cst[:, None, C_FREQ:C_FREQ + 32].to_broadcast([128, NT, 32]), op=ALU.mult),
             reads=[r_tab, cx.rcst], writes=[r_tab])
        P.op("dve", lambda e: e.tensor_scalar(out=nn[:], in0=ang[:], scalar1=float(1.0 / (2.0 * np.pi)), scalar2=None, op0=ALU.mult),
             reads=[r_tab], writes=[r_tab])
        P.op("dve", lambda e: e.tensor_scalar(out=nn[:], in0=nn[:], scalar1=MAGIC, scalar2=None, op0=ALU.add), reads=[r_tab], writes=[r_tab])
        P.op("dve", lambda e: e.tensor_scalar(out=nn[:], in0=nn[:], scalar1=-MAGIC, scalar2=None, op0=ALU.add), reads=[r_tab], writes=[r_tab])
        P.op("dve", lambda e: e.scalar_tensor_tensor(out=ang[:], in0=nn[:], scalar=-TWO_PI_HI, in1=ang[:], op0=ALU.mult, op1=ALU.add),
             reads=[r_tab], writes=[r_tab])
        P.op("dve", lambda e: e.scalar_tensor_tensor(out=ang[:], in0=nn[:], scalar=-TWO_PI_LO, in1=ang[:], op0=ALU.mult, op1=ALU.add),
             reads=[r_tab], writes=[r_tab])
        PI_S = 3.1415925
        P.op("dve", lambda e: e.tensor_scalar(out=ang[:], in0=ang[:], scalar1=PI_S, scalar2=-PI_S, op0=ALU.min, op1=ALU.max),
             reads=[r_tab], writes=[r_tab])
        P.op("act", lambda e: e.activation(out=sinT[:], in_=ang[:], func=AF.Sin), reads=[r_tab], writes=[r_tab])
        P.op("act", lambda e: e.activation(out=nn[:], in_=ang[:], func=AF.Abs), reads=[r_tab], writes=[r_tab])
        P.op("dve", lambda e: e.tensor_scalar(out=nn[:], in0=nn[:], scalar1=-1.0, scalar2=float(np.pi / 2), op0=ALU.mult, op1=ALU.add),
             reads=[r_tab], writes=[r_tab])
        P.op("act", lambda e: e.activation(out=cosT[:], in_=nn[:], func=AF.Sin), reads=[r_tab], writes=[r_tab])

        wuq = sb("wuq", [128, 3, 1536], BF16)
        wukv = sb("wukv", [128, 2, 2048], BF16)
        r_w = Reg()
        P.dma("pool", wuq[:], A["mla_w_uq"][0].rearrange("(k p) c -> p k c", p=128), writes=[r_w])
        P.dma("pool", wukv[:], A["mla_w_ukv"][0].rearrange("(k p) c -> p k c", p=128), writes=[r_w])
        cqnT = sb("cqnT", [128, 3, S], BF16)
        ckvnT = sb("ckvnT", [128, 2, S], BF16)
        kropeT = sb("kropeT", [64, S], BF16)
        r_lat = [Reg() for i in range(NT)]
        r_krT = Reg()
        with contextlib.ExitStack() as st1:
            sb1 = lambda name, shape, dt: st1.enter_context(nc.sbuf_tensor(un(name), shape, dt))
            wi = sb1("wi", [128, NK, 704], BF16)
            r_wi = Reg()
            P.dma("pool", wi[:], A["mla_w_in"][0].rearrange("(k p) c -> p k c", p=128), writes=[r_wi])
            gq = sb1("gq", [128, 384], F32)
            gkv = sb1("gkv", [128, 256], F32)
            r_g = Reg()
            P.dma("sp", gq[:], A["mla_q_norm_g"][0].partition_broadcast(128), writes=[r_g])
            P.dma("sp", gkv[:], A["mla_kv_norm_g"][0].partition_broadcast(128), writes=[r_g])
            zs = [sb1("zs", [128, 704], F32) for j in range(2)]
            r_zs = [Reg(), Reg()]
            junk = sb1("junk", [128, 384], F32)
            r_junk = Reg()
            ms = [sb1("ms", [128, 4], F32) for j in range(2)]
            r_ms = [Reg(), Reg()]
            cn = [sb1("cn", [128, 640], BF16) for j in range(2)]
            r_cn = [Reg(), Reg()]
            krz = sb1("krz", [128, NT, 64], F32)
            r_krz = Reg()
            krb = sb1("krb", [128, NT, 64], BF16)
            r_krb = Reg()
            rt1 = sb1("rt1", [128, NT, 32], F32)
            rt2 = sb1("rt2", [128, NT, 32], F32)
            r_rt = Reg()
            for i in range(NT):
                j2 = i % 2
                pa = 0 + 2 * j2
                pbk = 1 + 2 * j2
                for k in range(NK):
                    P.op("pe", lambda e, k=k: e.matmul(cx.ps[pa][:, :], lhsT=XT[:, k, i * 128:(i + 1) * 128], rhs=wi[:, k, 0:512],
                                                       start=(k == 0), stop=(k == NK - 1)), reads=[cx.rXA[i], r_wi], writes=[cx.rps[pa]])
                for k in range(NK):
                    P.op("pe", lambda e, k=k: e.matmul(cx.ps[pbk][:, 0:192], lhsT=XT[:, k, i * 128:(i + 1) * 128], rhs=wi[:, k, 512:704],
                                                       start=(k == 0), stop=(k == NK - 1)), reads=[cx.rXA[i], r_wi], writes=[cx.rps[pbk]])
                z = zs[j2]
                P.op("act", lambda e: e.activation(out=z[:, 0:512], in_=cx.ps[pa][:, :], func=AF.Copy), reads=[cx.rps[pa]], writes=[r_zs[j2]])
                P.op("dve", lambda e: e.tensor_copy(out=z[:, 512:704], in_=cx.ps[pbk][:, 0:192]), reads=[cx.rps[pbk]], writes=[r_zs[j2]])
                m = ms[j2]
                P.op("act", lambda e: e.activation(out=junk[:, 0:384], in_=z[:, 0:384], func=AF.Square, accum_out=m[:, 0:1]),
                     reads=[r_zs[j2]], writes=[r_junk, r_ms[j2]])
                P.op("act", lambda e: e.activation(out=junk[:, 0:256], in_=z[:, 384:640], func=AF.Square, accum_out=m[:, 1:2]),
                     reads=[r_zs[j2]], writes=[r_junk, r_ms[j2]])
                P.op("dve", lambda e: e.tensor_scalar(out=m[:, 0:1], in0=m[:, 0:1], scalar1=1.0 / 384.0, scalar2=LN_EPS, op0=ALU.mult, op1=ALU.add),
                     reads=[r_ms[j2]], writes=[r_ms[j2]])
                P.op("dve", lambda e: e.tensor_scalar(out=m[:, 1:2], in0=m[:, 1:2], scalar1=1.0 / 256.0, scalar2=LN_EPS, op0=ALU.mult, op1=ALU.add),
                     reads=[r_ms[j2]], writes=[r_ms[j2]])
                P.op("act", lambda e: e.activation(out=m[:, 0:2], in_=m[:, 0:2], func=AF.Sqrt), reads=[r_ms[j2]], writes=[r_ms[j2]])
                P.op("dve", lambda e: e.reciprocal(out=m[:, 2:4], in_=m[:, 0:2]), reads=[r_ms[j2]], writes=[r_ms[j2]])
                c_ = cn[j2]
                P.op("dve", lambda e: e.scalar_tensor_tensor(out=c_[:, 0:384], in0=z[:, 0:384], scalar=m[:, 2:3], in1=gq[:], op0=ALU.mult, op1=ALU.mult),
                     reads=[r_zs[j2], r_ms[j2], r_g], writes=[r_cn[j2]])
                P.op("dve", lambda e: e.scalar_tensor_tensor(out=c_[:, 384:640], in0=z[:, 384:640], scalar=m[:, 3:4], in1=gkv[:], op0=ALU.mult, op1=ALU.mult),
                     reads=[r_zs[j2], r_ms[j2], r_g], writes=[r_cn[j2]])
                P.op("pool", lambda e: e.tensor_copy(out=krz[:, i, :], in_=z[:, 640:704]), reads=[r_zs[j2]], writes=[r_krz])
                pt = 6 + j2
                pst = cx.ps[pt][:].bitcast(BF16)
                for k in range(5):
                    P.op("pe", lambda e, k=k: e.transpose(out=pst[:, k * 128:(k + 1) * 128], in_=c_[:, k * 128:(k + 1) * 128], identity=cx.identb[:]),
                         reads=[r_cn[j2], cx.rcst], writes=[cx.rps[pt]])
                P.op("act", lambda e: e.activation(out=cqnT[:, :, i * 128:(i + 1) * 128], in_=pst[:, 0:384].rearrange("p (k t) -> p k t", k=3), func=AF.Copy),
                     reads=[cx.rps[pt]], writes=[r_lat[i]])
                P.op("dve", lambda e: e.tensor_copy(out=ckvnT[:, :, i * 128:(i + 1) * 128], in_=pst[:, 384:640].rearrange("p (k t) -> p k t", k=2)),
                     reads=[cx.rps[pt]], writes=[r_lat[i]])
            rope_apply(cx, krb[:], krz[:], cosT[:], sinT[:], rt1[:], rt2[:], NT, r_krz, r_tab, r_rt, r_krb)
            for g in range(2):
                pst = cx.ps[4 + g][:].bitcast(BF16)
                for j in range(8):
                    i = g * 8 + j
                    P.op("pe", lambda e, i=i, j=j: e.transpose(out=pst[0:64, j * 128:(j + 1) * 128], in_=krb[:, i, :], identity=cx.identb[:]),
                         reads=[r_krb, cx.rcst], writes=[cx.rps[4 + g]])
                P.op("act", lambda e, g=g: e.activation(out=kropeT[:, g * 1024:(g + 1) * 1024], in_=pst[0:64, :], func=AF.Copy),
                     reads=[cx.rps[4 + g]], writes=[r_krT])
        P.barrier()
        qnT = sb("qnT", [128, S], BF16)
        knT = sb("knT", [128, S], BF16)
        qrT = sb("qrT", [64, S], BF16)
        vh = sb("vh", [128, NT, 128], BF16)
        r_qnT, r_knT, r_qrT, r_vh = Reg(), Reg(), Reg(), Reg()
        qrb = sb("qrb", [128, 8, 64], BF16)
        r_qrb = Reg()
        rt1 = sb("rt1b", [128, 8, 32], F32)
        rt2 = sb("rt2b", [128, 8, 32], F32)
        r_rt = Reg()
        Pb = sb("Pb", [128, S], BF16)
        r_Pb = Reg()
        PTs = sb("PTs", [128, NT, 128], BF16)
        r_PTs = Reg()
        oT = sb("oT", [128, 128], BF16)
        r_oT = Reg()
        sm = [sb("smx", [128, 8], F32) for j in range(2)]
        r_sm = [Reg(), Reg()]
        wo = sb("wo", [128, D], BF16)
        r_wo = Reg()
        all_lat = r_lat
        for h in range(MLA_H):
            P.dma("pool", wo[:], A["mla_w_out"][0][h * 128:(h + 1) * 128, :], writes=[r_wo])
            for tb in range(4):
                for k in range(3):
                    P.op("pe", lambda e, tb=tb, k=k: e.matmul(cx.ps[tb][:, :], lhsT=wuq[:, k, h * 192:h * 192 + 128],
                                                              rhs=cqnT[:, k, tb * 512:(tb + 1) * 512], start=(k == 0), stop=(k == 2)),
                         reads=[r_w] + all_lat[tb * 4:(tb + 1) * 4], writes=[cx.rps[tb]])
                P.op("act", lambda e, tb=tb: e.activation(out=qnT[:, tb * 512:(tb + 1) * 512], in_=cx.ps[tb][:, :], func=AF.Copy),
                     reads=[cx.rps[tb]], writes=[r_qnT])
            for tb in range(4):
                for k in range(2):
                    P.op("pe", lambda e, tb=tb, k=k: e.matmul(cx.ps[4 + tb][:, :], lhsT=wukv[:, k, h * 256:h * 256 + 128],
                                                              rhs=ckvnT[:, k, tb * 512:(tb + 1) * 512], start=(k == 0), stop=(k == 1)),
                         reads=[r_w] + all_lat[tb * 4:(tb + 1) * 4], writes=[cx.rps[4 + tb]])
                P.op("dve", lambda e, tb=tb: e.tensor_copy(out=knT[:, tb * 512:(tb + 1) * 512], in_=cx.ps[4 + tb][:, :]),
                     reads=[cx.rps[4 + tb]], writes=[r_knT])
            for g in range(4):
                for j in range(4):
                    i = g * 4 + j
                    for k in range(2):
                        P.op("pe", lambda e, i=i, j=j, k=k, g=g: e.matmul(cx.ps[g][:, j * 128:(j + 1) * 128], lhsT=ckvnT[:, k, i * 128:(i + 1) * 128],
                                                                          rhs=wukv[:, k, h * 256 + 128:h * 256 + 256], start=(k == 0), stop=(k == 1)),
                             reads=[r_w, all_lat[i]], writes=[cx.rps[g]])
                P.op("act", lambda e, g=g: e.activation(out=vh[:, g * 4:(g + 1) * 4, :], in_=cx.ps[g][:, :].rearrange("p (a b) -> p a b", a=4), func=AF.Copy),
                     reads=[cx.rps[g]], writes=[r_vh])
            for g in range(2):
                pb = 4 + g
                for j in range(8):
                    i = g * 8 + j
                    for k in range(3):
                        P.op("pe", lambda e, i=i, j=j, k=k, pb=pb: e.matmul(cx.ps[pb][:, j * 64:(j + 1) * 64], lhsT=cqnT[:, k, i * 128:(i + 1) * 128],
                                                                            rhs=wuq[:, k, h * 192 + 128:h * 192 + 192], start=(k == 0), stop=(k == 2)),
                             reads=[r_w, all_lat[i]], writes=[cx.rps[pb]])
                rope_apply(cx, qrb[:], cx.ps[pb][:, :].rearrange("p (a b) -> p a b", a=8), cosT[:, g * 8:(g + 1) * 8, :], sinT[:, g * 8:(g + 1) * 8, :],
                           rt1[:], rt2[:], 8, cx.rps[pb], r_tab, r_rt, r_qrb)
                pst = cx.ps[6 + g][:].bitcast(BF16)
                for j in range(8):
                    P.op("pe", lambda e, j=j: e.transpose(out=pst[0:64, j * 128:(j + 1) * 128], in_=qrb[:, j, :], identity=cx.identb[:]),
                         reads=[r_qrb, cx.rcst], writes=[cx.rps[6 + g]])
                P.op("act", lambda e, g=g: e.activation(out=qrT[:, g * 1024:(g + 1) * 1024], in_=pst[0:64, :], func=AF.Copy),
                     reads=[cx.rps[6 + g]], writes=[r_qrT])
            def emit_S(qb):
                base = 4 * (qb % 2)
                qs = slice(qb * 128, (qb + 1) * 128)
                for tb in range(4):
                    P.op("pe", lambda e, tb=tb: e.matmul(cx.ps[base + tb][:, :], lhsT=qnT[:, qs], rhs=knT[:, tb * 512:(tb + 1) * 512], start=True, stop=False),
                         reads=[r_qnT, r_knT], writes=[cx.rps[base + tb]])
                    P.op("pe", lambda e, tb=tb: e.matmul(cx.ps[base + tb][:, :], lhsT=qrT[:, qs], rhs=kropeT[:, tb * 512:(tb + 1) * 512], start=False, stop=True),
                         reads=[r_qrT, r_krT], writes=[cx.rps[base + tb]])

            def emit_softmax(qb):
                base = 4 * (qb % 2)
                m = sm[qb % 2]
                r_m = r_sm[qb % 2]
                for tb in range(4):
                    P.op("dve", lambda e, tb=tb: e.tensor_reduce(out=m[:, tb:tb + 1], in_=cx.ps[base + tb][:, :], axis=AX.X, op=ALU.max),
                         reads=[cx.rps[base + tb]], writes=[r_m])
                P.op("dve", lambda e: e.tensor_reduce(out=m[:, 0:1], in_=m[:, 0:4], axis=AX.X, op=ALU.max), reads=[r_m], writes=[r_m])
                P.op("dve", lambda e: e.tensor_scalar(out=m[:, 1:2], in0=m[:, 0:1], scalar1=-MLA_SCALE, scalar2=None, op0=ALU.mult), reads=[r_m], writes=[r_m])
                for tb in range(4):
                    P.op("act", lambda e, tb=tb: e.activation(out=Pb[:, tb * 512:(tb + 1) * 512], in_=cx.ps[base + tb][:, :], func=AF.Exp, scale=MLA_SCALE,
                                                              bias=m[:, 1:2], accum_out=m[:, 4 + tb:5 + tb]), reads=[cx.rps[base + tb], r_m], writes=[r_Pb, r_m])
                P.op("dve", lambda e: e.tensor_reduce(out=m[:, 2:3], in_=m[:, 4:8], axis=AX.X, op=ALU.add), reads=[r_m], writes=[r_m])
                P.op("dve", lambda e: e.reciprocal(out=m[:, 3:4], in_=m[:, 2:3]), reads=[r_m], writes=[r_m])

            def emit_rest(qb):
                base = 4 * (qb % 2)
                m = sm[qb % 2]
                r_m = r_sm[qb % 2]
                for g in range(2):
                    pst = cx.ps[base + g][:].bitcast(BF16)
                    for j in range(8):
                        c = g * 8 + j
                        P.op("pe", lambda e, c=c, j=j: e.transpose(out=pst[:, j * 128:(j + 1) * 128], in_=Pb[:, c * 128:(c + 1) * 128], identity=cx.identb[:]),
                             reads=[r_Pb, cx.rcst], writes=[cx.rps[base + g]])
                    P.op("act", lambda e, g=g: e.activation(out=PTs[:, g * 8:(g + 1) * 8, :], in_=pst.rearrange("p (a b) -> p a b", a=8), func=AF.Copy),
                         reads=[cx.rps[base + g]], writes=[r_PTs])
                for c in range(NT):
                    P.op("pe", lambda e, c=c: e.matmul(cx.ps[base + 2][:, 0:128], lhsT=vh[:, c, :], rhs=PTs[:, c, :], start=(c == 0), stop=(c == NT - 1)),
                         reads=[r_vh, r_PTs], writes=[cx.rps[base + 2]])
                P.op("act", lambda e: e.activation(out=oT[:], in_=cx.ps[base + 2][:, 0:128], func=AF.Copy), reads=[cx.rps[base + 2]], writes=[r_oT])
                for hh in range(2):
                    pb = base + 3 - hh
                    P.op("pe", lambda e, hh=hh, pb=pb: e.matmul(cx.ps[pb][:, :], lhsT=oT[:], rhs=wo[:, hh * 512:(hh + 1) * 512], start=True, stop=True),
                         reads=[r_oT, r_wo], writes=[cx.rps[pb]])
                    P.op("dve", lambda e, hh=hh, pb=pb: e.scalar_tensor_tensor(out=cx.R[:, qb, hh * 512:(hh + 1) * 512], in0=cx.ps[pb][:, :], scalar=m[:, 3:4],
                                                                               in1=cx.R[:, qb, hh * 512:(hh + 1) * 512], op0=ALU.mult, op1=ALU.add),
                         reads=[cx.rps[pb], r_m, cx.rR[qb]], writes=[cx.rR[qb]])

            emit_S(0)
            for qb in range(NT):
                if qb + 1 < NT:
                    emit_S(qb + 1)
                emit_softmax(qb)
                emit_rest(qb)
    P.barrier()


LN_QSCALE = float(np.log(128.0 ** -0.5))


def emit_linattn(cx, kind):
    P = cx.P
    nc = cx.nc
    A = cx.A
    XT = xt_view(cx)
    ml = (kind == "mlstm")
    pre = "mlstm" if ml else "gla"
    W_in = A[pre + "_w_in"][0]
    DV1 = 257 if ml else 256
    cs = 1.0 if ml else 1.0 / 16.0
    with contextlib.ExitStack() as st:
        sb = lambda name, shape, dt: st.enter_context(nc.sbuf_tensor(un(name), shape, dt))
        ng = sb("ng", [128, 256], F32)
        r_ng = Reg()
        r_gw = Reg()
        if ml:
            wgates = sb("wgates", [128, NK, 16], BF16)
            P.dma("pool", wgates[:], W_in[:, 3072:3088].rearrange("(k p) c -> p k c", p=128), writes=[r_gw])
            gbias = sb("gbias", [128, 16], F32)
            P.dma("sp", gbias[:], A["mlstm_gate_b"][0].rearrange("a b c -> (a b c)").partition_broadcast(128), writes=[r_gw])
            ngbias = sb("ngbias", [128, 16], F32)
            P.op("dve", lambda e: e.tensor_scalar(out=ngbias[:], in0=gbias[:], scalar1=-1.0, scalar2=None, op0=ALU.mult), reads=[r_gw], writes=[r_gw])
            wrep = [sb("wrep", [128, NK, 128], BF16) for j in range(2)]
            r_wrep = [Reg(), Reg()]
            igb = sb("igb", [128, S], F32)
            r_igb = Reg()
        else:
            wglr = sb("wglr", [128, NK, 32], BF16)
            P.dma("pool", wglr[:], W_in[:, 3072:3104].rearrange("(k p) c -> p k c", p=128), writes=[r_gw])
            gwpad = sb("gwpad", [32, 2, 512], BF16)
            P.op("pool", lambda e: e.memset(gwpad[:], 0.0), writes=[r_gw])
            for d in range(2):
                P.dma("pool", gwpad[d * 16:(d + 1) * 16, d, :], A["gla_gate_w"][0, d], writes=[r_gw])
            ngb = sb("ngb", [128, 2, 4], F32)
            for d in range(2):
                small_load(cx, ngb[:, d, :], A["gla_gate_b"][0, d].rearrange("(h p) -> p h", p=128), r_gw)
            P.op("dve", lambda e: e.tensor_scalar(out=ngb[:], in0=ngb[:], scalar1=-1.0, scalar2=None, op0=ALU.mult), reads=[r_gw], writes=[r_gw])
            glrT = sb("glrT", [32, S], BF16)
            r_glrT = Reg()
            for tb in range(4):
                for k in range(NK):
                    P.op("pe", lambda e, tb=tb, k=k: e.matmul(cx.ps[tb][0:32, :], lhsT=wglr[:, k, :], rhs=XT[:, k, tb * 512:(tb + 1) * 512],
                                                              start=(k == 0), stop=(k == NK - 1)), reads=[r_gw] + cx.rXA[tb * 4:(tb + 1) * 4], writes=[cx.rps[tb]])
                P.op("act", lambda e, tb=tb: e.activation(out=glrT[:, tb * 512:(tb + 1) * 512], in_=cx.ps[tb][0:32, :], func=AF.Copy),
                     reads=[cx.rps[tb]], writes=[r_glrT])

        WA = sb("WA", [128, NK, 256], BF16)
        WB = sb("WB", [128, NK, 256], BF16)
        r_WA, r_WB = Reg(), Reg()
        wq = WA[:, :, 0:128]
        wk = WA[:, :, 128:256]
        wo = WA[:].rearrange("p k c -> p (k c)").rearrange("p (j f) -> p j f", j=2)
        wv = WB
        wor = WB
        r_wq, r_wk, r_wv, r_wor, r_wo = r_WA, r_WA, r_WB, r_WB, r_WA
        qT = sb("qT", [128, S], F32)
        kT = sb("kT", [128, S], F32)
        r_qT, r_kT = Reg(), Reg()
        vb = sb("vb", [128, NT, DV1], BF16)
        r_vb = Reg()
        buf1 = sb("buf1", [128, S], F32)
        buf2 = sb("buf2", [128, S], F32)
        r_b1, r_b2 = Reg(), Reg()
        qe = sb("qe", [128, S], BF16)
        ke = sb("ke", [128, S], BF16)
        kgT = sb("kgT", [128, NT, 128], BF16)
        r_qe, r_ke, r_kgT = Reg(), Reg(), Reg()
        Oacc = sb("Oacc", [128, NT, 256], F32)
        r_O = [Reg() for i in range(NT)]
        offs = sb("offs", [128, NT], F32)
        gsv = sb("gsv", [128, NT], F32)
        eg = sb("eg", [128, NT], F32)
        r_small = Reg()
        Sst = sb("Sst", [128, DV1], F32)
        Sb = sb("Sb", [128, DV1], BF16)
        r_S, r_Sb = Reg(), Reg()
        atm = [sb("atm", [128, 128], BF16) for j in range(2)]
        r_atm = [Reg(), Reg()]
        dn = [sb("dn", [128, 4], F32) for j in range(2)]
        r_dn = [Reg(), Reg()]
        hsall = sb("hsall", [128, NT, 10], F32)
        r_hsall = Reg()
        gact = [sb("gact", [128, 256], F32) for j in range(2)]
        r_gact = [Reg(), Reg()]
        xn = [sb("xn", [128, 256], F32) for j in range(2)]
        r_xn = [Reg(), Reg()]
        gbt = [sb("gbt", [128, 256], BF16) for j in range(2)]
        r_gbt = [Reg(), Reg()]
        gTt = [sb("gTt", [128, 2, 128], BF16) for j in range(2)]
        r_gTt = [Reg(), Reg()]
        P.op("pool", lambda e: e.memset(offs[:, 0:1], 0.0), writes=[r_small])
        if ml:
            P.op("pool", lambda e: e.memset(vb[:, :, 256:257], 1.0), writes=[r_vb])

        b1v = buf1[:].rearrange("p (c j) -> p c j", j=128)
        b2v = buf2[:].rearrange("p (c j) -> p c j", j=128)
        allXA = cx.rXA

        def wload(dst, reg, c0, c1):
            P.dma("pool", dst, W_in[:, c0:c1].rearrange("(k p) c -> p k c", p=128), writes=[reg])

        for h in range(4):
            wload(wq, r_wq, h * 128, (h + 1) * 128)
            wload(wk, r_wk, 512 + h * 128, 512 + (h + 1) * 128)
            wload(wv[:], r_wv, 1024 + h * 256, 1024 + (h + 1) * 256)
            P.dma("sp", ng[:], A[pre + "_norm_g"][0][h * 256:(h + 1) * 256].partition_broadcast(128), writes=[r_ng])
            for (wsrc, r_w, dst, r_dst, base) in ((wq, r_wq, qT, r_qT, 0), (wk, r_wk, kT, r_kT, 4)):
                for tb in range(4):
                    pb = base + tb
                    for k in range(NK):
                        P.op("pe", lambda e, k=k, pb=pb, tb=tb, wsrc=wsrc: e.matmul(cx.ps[pb][:, :], lhsT=wsrc[:, k, :], rhs=XT[:, k, tb * 512:(tb + 1) * 512],
                                                                                    start=(k == 0), stop=(k == NK - 1)),
                             reads=[r_w] + allXA[tb * 4:(tb + 1) * 4], writes=[cx.rps[pb]])
                    eng = "act" if base == 0 else "dve"
                    if eng == "act":
                        P.op("act", lambda e, pb=pb, tb=tb, dst=dst: e.activation(out=dst[:, tb * 512:(tb + 1) * 512], in_=cx.ps[pb][:, :], func=AF.Copy),
                             reads=[cx.rps[pb]], writes=[r_dst])
                    else:
                        P.op("dve", lambda e, pb=pb, tb=tb, dst=dst: e.tensor_copy(out=dst[:, tb * 512:(tb + 1) * 512], in_=cx.ps[pb][:, :]),
                             reads=[cx.rps[pb]], writes=[r_dst])
            for i in range(NT):
                pb = (i // 2) % 8
                off = (i % 2) * 256
                for k in range(NK):
                    P.op("pe", lambda e, k=k, pb=pb, off=off, i=i: e.matmul(cx.ps[pb][:, off:off + 256], lhsT=XT[:, k, i * 128:(i + 1) * 128], rhs=wv[:, k, :],
                                                                            start=(k == 0), stop=(k == NK - 1)), reads=[r_wv, allXA[i]], writes=[cx.rps[pb]])
                P.op("act", lambda e, pb=pb, off=off, i=i: e.activation(out=vb[:, i, 0:256], in_=cx.ps[pb][:, off:off + 256], func=AF.Copy),
                     reads=[cx.rps[pb]], writes=[r_vb])
            wload(wor[:], r_wor, 2048 + h * 256, 2048 + (h + 1) * 256)
            P.dma("pool", wo, A[pre + "_w_out"][0][h * 256:(h + 1) * 256, :].rearrange("(j p) f -> p j f", p=128), writes=[r_wo])
            for d in range(2):
                if ml:
                    jf = d * 8 + 4 + h
                    ji = d * 8 + h
                    for jj, gidx in enumerate((jf, ji)):
                        P.op("dve", lambda e, jj=jj, gidx=gidx: e.tensor_copy(out=wrep[jj][:], in_=wgates[:, :, gidx:gidx + 1].to_broadcast([128, NK, 128])),
                             reads=[r_gw], writes=[r_wrep[jj]])
                    for tb in range(4):
                        for k in range(NK):
                            P.op("pe", lambda e, k=k, tb=tb: e.matmul(cx.ps[tb][:, :], lhsT=wrep[0][:, k, :], rhs=XT[:, k, tb * 512:(tb + 1) * 512],
                                                                      start=(k == 0), stop=(k == NK - 1)), reads=[r_wrep[0]] + allXA[tb * 4:(tb + 1) * 4], writes=[cx.rps[tb]])
                        P.op("act", lambda e, tb=tb: e.activation(out=buf1[:, tb * 512:(tb + 1) * 512], in_=cx.ps[tb][:, :], func=AF.Exp, scale=-1.0,
                                                                  bias=ngbias[:, jf:jf + 1]), reads=[cx.rps[tb], r_gw], writes=[r_b1])
                    for tb in range(4):
                        for k in range(NK):
                            P.op("pe", lambda e, k=k, tb=tb: e.matmul(cx.ps[4 + tb][:, :], lhsT=wrep[1][:, k, :], rhs=XT[:, k, tb * 512:(tb + 1) * 512],
                                                                      start=(k == 0), stop=(k == NK - 1)), reads=[r_wrep[1]] + allXA[tb * 4:(tb + 1) * 4], writes=[cx.rps[4 + tb]])
                        P.op("act", lambda e, tb=tb: e.activation(out=igb[:, tb * 512:(tb + 1) * 512], in_=cx.ps[4 + tb][:, :], func=AF.Identity,
                                                                  bias=gbias[:, ji:ji + 1]), reads=[cx.rps[4 + tb], r_gw], writes=[r_igb])
                else:
                    for tb in range(4):
                        P.op("pe", lambda e, tb=tb: e.matmul(cx.ps[tb][:, :], lhsT=gwpad[:, d, h * 128:(h + 1) * 128], rhs=glrT[:, tb * 512:(tb + 1) * 512],
                                                             start=True, stop=True), reads=[r_gw, r_glrT], writes=[cx.rps[tb]])
                        P.op("act", lambda e, tb=tb: e.activation(out=buf1[:, tb * 512:(tb + 1) * 512], in_=cx.ps[tb][:, :], func=AF.Exp, scale=-1.0,
                                                                  bias=ngb[:, d, h:h + 1]), reads=[cx.rps[tb], r_gw], writes=[r_b1])
                P.op("act", lambda e: e.activation(out=buf1[:], in_=buf1[:], func=AF.Ln, bias=1.0), reads=[r_b1], writes=[r_b1])
                P.op("dve", lambda e: e.tensor_tensor_scan(out=buf2[:], data0=cx.cst[:, C_ONES:C_ONES + 1].to_broadcast([128, S]), data1=buf1[:], initial=0.0,
                                                            op0=ALU.mult, op1=ALU.add), reads=[r_b1, cx.rcst], writes=[r_b2])
                P.op("dve", lambda e: e.tensor_copy(out=offs[:, 1:NT], in_=b2v[:, 0:NT - 1, 127]), reads=[r_b2], writes=[r_small])
                P.op("dve", lambda e: e.tensor_tensor(out=b2v, in0=b2v, in1=offs[:, :, None].to_broadcast([128, NT, 128]), op=ALU.subtract),
                     reads=[r_b2, r_small], writes=[r_b2])
                P.op("dve", lambda e: e.tensor_copy(out=gsv[:], in_=b2v[:, :, 127]), reads=[r_b2], writes=[r_small])
                P.op("act", lambda e: e.activation(out=eg[:], in_=gsv[:], func=AF.Exp, scale=-cs), reads=[r_small], writes=[r_small])
                gs_bc = gsv[:, :, None].to_broadcast([128, NT, 128])
                if d == 1:
                    P.op("dve", lambda e: e.scalar_tensor_tensor(out=buf2[:], in0=buf2[:], scalar=-1.0, in1=buf1[:], op0=ALU.mult, op1=ALU.add),
                         reads=[r_b1, r_b2], writes=[r_b2])
                    P.op("dve", lambda e: e.tensor_tensor(out=b2v, in0=b2v, in1=gs_bc, op=ALU.add), reads=[r_b2, r_small], writes=[r_b2])
                P.op("dve", lambda e: e.scalar_tensor_tensor(out=b1v, in0=b2v, scalar=-1.0, in1=gs_bc, op0=ALU.mult, op1=ALU.add),
                     reads=[r_b2, r_small], writes=[r_b1])
                if ml:
                    P.op("dve", lambda e: e.scalar_tensor_tensor(out=buf1[:], in0=buf1[:], scalar=-cs, in1=igb[:], op0=ALU.mult, op1=ALU.add),
                         reads=[r_b1, r_igb], writes=[r_b1])
                    P.op("act", lambda e: e.activation(out=buf1[:], in_=buf1[:], func=AF.Exp), reads=[r_b1], writes=[r_b1])
                else:
                    P.op("act", lambda e: e.activation(out=buf1[:], in_=buf1[:], func=AF.Exp, scale=-cs), reads=[r_b1], writes=[r_b1])
                P.op("dve", lambda e: e.tensor_tensor(out=ke[:], in0=kT[:], in1=buf1[:], op=ALU.mult), reads=[r_kT, r_b1], writes=[r_ke])
                for g in range(2):
                    pst = cx.ps[6 + g][:].bitcast(BF16)
                    for j in range(8):
                        c = g * 8 + j
                        P.op("pe", lambda e, c=c, j=j: e.transpose(out=pst[:, j * 128:(j + 1) * 128], in_=ke[:, c * 128:(c + 1) * 128], identity=cx.identb[:]),
                             reads=[r_ke, cx.rcst], writes=[cx.rps[6 + g]])
                    P.op("act", lambda e, g=g: e.activation(out=kgT[:, g * 8:(g + 1) * 8, :], in_=pst.rearrange("p (a b) -> p a b", a=8), func=AF.Copy),
                         reads=[cx.rps[6 + g]], writes=[r_kgT])
                if ml:
                    P.op("dve", lambda e: e.scalar_tensor_tensor(out=buf1[:], in0=buf2[:], scalar=cs, in1=igb[:], op0=ALU.mult, op1=ALU.add),
                         reads=[r_b2, r_igb], writes=[r_b1])
                    P.op("act", lambda e: e.activation(out=buf1[:], in_=buf1[:], func=AF.Exp), reads=[r_b1], writes=[r_b1])
                else:
                    P.op("act", lambda e: e.activation(out=buf1[:], in_=buf2[:], func=AF.Exp, scale=cs), reads=[r_b2], writes=[r_b1])
                P.op("dve", lambda e: e.tensor_tensor(out=ke[:], in0=kT[:], in1=buf1[:], op=ALU.mult), reads=[r_kT, r_b1], writes=[r_ke])
                P.op("act", lambda e: e.activation(out=buf2[:], in_=buf2[:], func=AF.Exp, scale=-cs, bias=cx.cst[:, C_LNQ:C_LNQ + 1]), reads=[r_b2, cx.rcst], writes=[r_b2])
                P.op("dve", lambda e: e.tensor_tensor(out=qe[:], in0=qT[:], in1=buf2[:], op=ALU.mult), reads=[r_qT, r_b2], writes=[r_qe])
                order = list(range(NT)) if d == 0 else list(range(NT - 1, -1, -1))
                mcol = C_TRIU if d == 0 else C_TRIL
                for n_, c in enumerate(order):
                    csl = slice(c * 128, (c + 1) * 128)
                    pa = n_ % 2
                    po = 2 + (n_ % 2)
                    pss = 4 + (n_ % 2)
                    first = (n_ == 0)
                    last = (n_ == NT - 1)
                    P.op("pe", lambda e: e.matmul(cx.ps[pa][:, 0:128], lhsT=ke[:, csl], rhs=qe[:, csl], start=True, stop=True),
                         reads=[r_ke, r_qe], writes=[cx.rps[pa]])
                    am = atm[n_ % 2]
                    P.op("dve", lambda e: e.tensor_tensor(out=am[:], in0=cx.cst[:, mcol:mcol + 128], in1=cx.ps[pa][:, 0:128], op=ALU.mult),
                         reads=[cx.rps[pa], cx.rcst], writes=[r_atm[n_ % 2]])
                    P.op("pe", lambda e: e.matmul(cx.ps[po][:, 0:DV1], lhsT=am[:], rhs=vb[:, c, :], start=True, stop=first),
                         reads=[r_atm[n_ % 2], r_vb], writes=[cx.rps[po]])
                    if not first:
                        P.op("pe", lambda e: e.matmul(cx.ps[po][:, 0:DV1], lhsT=qe[:, csl], rhs=Sb[:], start=False, stop=True),
                             reads=[r_qe, r_Sb], writes=[cx.rps[po]])
                    if not last:
                        P.op("pe", lambda e: e.matmul(cx.ps[pss][:, 0:DV1], lhsT=kgT[:, c, :], rhs=vb[:, c, :], start=True, stop=True),
                             reads=[r_kgT, r_vb], writes=[cx.rps[pss]])
                        if first:
                            P.op("dve", lambda e: e.tensor_copy(out=Sst[:], in_=cx.ps[pss][:, 0:DV1]), reads=[cx.rps[pss]], writes=[r_S])
                        else:
                            P.op("dve", lambda e: e.scalar_tensor_tensor(out=Sst[:], in0=Sst[:], scalar=eg[:, c:c + 1], in1=cx.ps[pss][:, 0:DV1],
                                                                          op0=ALU.mult, op1=ALU.add), reads=[cx.rps[pss], r_S, r_small], writes=[r_S])
                        P.op("act", lambda e: e.activation(out=Sb[:], in_=Sst[:], func=AF.Copy), reads=[r_S], writes=[r_Sb])
                    oc = Oacc[:, c, :]
                    if ml:
                        dd = dn[n_ % 2]
                        r_dd = r_dn[n_ % 2]
                        P.op("act", lambda e: e.activation(out=dd[:, 0:1], in_=cx.ps[po][:, 256:257], func=AF.Abs), reads=[cx.rps[po]], writes=[r_dd])
                        P.op("dve", lambda e: e.tensor_scalar(out=dd[:, 0:1], in0=dd[:, 0:1], scalar1=1.0, scalar2=None, op0=ALU.max), reads=[r_dd], writes=[r_dd])
                        P.op("dve", lambda e: e.reciprocal(out=dd[:, 1:2], in_=dd[:, 0:1]), reads=[r_dd], writes=[r_dd])
                        if d == 0:
                            P.op("act", lambda e: e.activation(out=oc, in_=cx.ps[po][:, 0:256], func=AF.Copy, scale=dd[:, 1:2]),
                                 reads=[cx.rps[po], r_dd], writes=[r_O[c]])
                        else:
                            P.op("dve", lambda e: e.scalar_tensor_tensor(out=oc, in0=cx.ps[po][:, 0:256], scalar=dd[:, 1:2], in1=oc, op0=ALU.mult, op1=ALU.add),
                                 reads=[cx.rps[po], r_dd, r_O[c]], writes=[r_O[c]])
                    else:
                        if d == 0:
                            P.op("act", lambda e: e.activation(out=oc, in_=cx.ps[po][:, 0:256], func=AF.Copy), reads=[cx.rps[po]], writes=[r_O[c]])
                        else:
                            P.op("dve", lambda e: e.tensor_tensor(out=oc, in0=oc, in1=cx.ps[po][:, 0:256], op=ALU.add), reads=[cx.rps[po], r_O[c]], writes=[r_O[c]])
            for i in range(NT):
                P.op("dve", lambda e, i=i: e.bn_stats(out=hsall[:, i, 0:6], in_=Oacc[:, i, :]), reads=[r_O[i]], writes=[r_hsall])
            for i in range(NT):
                P.op("dve", lambda e, i=i: e.bn_aggr(out=hsall[:, i, 6:8], in_=hsall[:, i, 0:6]), reads=[r_hsall], writes=[r_hsall])
            P.op("dve", lambda e: e.tensor_scalar(out=hsall[:, :, 8], in0=hsall[:, :, 7], scalar1=LN_EPS, scalar2=None, op0=ALU.add), reads=[r_hsall], writes=[r_hsall])
            P.op("act", lambda e: e.activation(out=hsall[:, :, 8], in_=hsall[:, :, 8], func=AF.Sqrt), reads=[r_hsall], writes=[r_hsall])
            P.op("dve", lambda e: e.reciprocal(out=hsall[:, :, 9], in_=hsall[:, :, 8]), reads=[r_hsall], writes=[r_hsall])
            for i in range(NT):
                j2 = i % 2
                hs = hsall[:, i, :]
                r_hs = r_hsall
                oc = Oacc[:, i, :]
                x_ = xn[j2]
                P.op("dve", lambda e: e.tensor_scalar(out=x_[:], in0=oc, scalar1=hs[:, 6:7], scalar2=hs[:, 9:10], op0=ALU.subtract, op1=ALU.mult),
                     reads=[r_O[i], r_hs], writes=[r_xn[j2]])
                P.op("pool", lambda e: e.tensor_tensor(out=x_[:], in0=x_[:], in1=ng[:], op=ALU.mult), reads=[r_xn[j2], r_ng], writes=[r_xn[j2]])
                pg = 6 + j2
                for k in range(NK):
                    P.op("pe", lambda e, k=k: e.matmul(cx.ps[pg][:, 0:256], lhsT=XT[:, k, i * 128:(i + 1) * 128], rhs=wor[:, k, :],
                                                       start=(k == 0), stop=(k == NK - 1)), reads=[r_wor, allXA[i]], writes=[cx.rps[pg]])
                ga = gact[j2]
                P.op("act", lambda e: e.activation(out=ga[:], in_=cx.ps[pg][:, 0:256], func=(AF.Sigmoid if ml else AF.Silu)),
                     reads=[cx.rps[pg]], writes=[r_gact[j2]])
                gb_ = gbt[j2]
                P.op("dve", lambda e: e.tensor_tensor(out=gb_[:], in0=x_[:], in1=ga[:], op=ALU.mult), reads=[r_xn[j2], r_gact[j2]], writes=[r_gbt[j2]])
                pt = 4 + j2
                pst = cx.ps[pt][:].bitcast(BF16)
                for j in range(2):
                    P.op("pe", lambda e, j=j: e.transpose(out=pst[:, j * 128:(j + 1) * 128], in_=gb_[:, j * 128:(j + 1) * 128], identity=cx.identb[:]),
                         reads=[r_gbt[j2], cx.rcst], writes=[cx.rps[pt]])
                gT_ = gTt[j2]
                P.op("act", lambda e: e.activation(out=gT_[:], in_=pst[:, 0:256].rearrange("p (a b) -> p a b", a=2), func=AF.Copy),
                     reads=[cx.rps[pt]], writes=[r_gTt[j2]])
                for hh in range(2):
                    pb = (i * 2 + hh) % 4
                    for j in range(2):
                        P.op("pe", lambda e, j=j, hh=hh, pb=pb: e.matmul(cx.ps[pb][:, :], lhsT=gT_[:, j, :], rhs=wo[:, j, hh * 512:(hh + 1) * 512],
                                                                         start=(j == 0), stop=(j == 1)), reads=[r_gTt[j2], r_wo], writes=[cx.rps[pb]])
                    P.op("dve", lambda e, hh=hh, pb=pb: e.tensor_tensor(out=cx.R[:, i, hh * 512:(hh + 1) * 512], in0=cx.R[:, i, hh * 512:(hh + 1) * 512],
                                                                        in1=cx.ps[pb][:, :], op=ALU.add), reads=[cx.rps[pb], cx.rR[i]], writes=[cx.rR[i]])
    P.barrier()


def declare_inputs(nc, nseq, names_shapes):
    aps = {}
    for name, shape, dt in names_shapes:
        aps[name] = nc.dram_tensor(name, list(shape), dt, kind="ExternalInput").ap()
    return aps


WEIGHT_SPECS = [
    ("mlstm_w_in", (1, 1024, 3088)), ("mlstm_gate_b", (1, 2, 2, 4)), ("mlstm_norm_g", (1, 1024)), ("mlstm_w_out", (1, 1024, 1024)),
    ("gla_w_in", (1, 1024, 3104)), ("gla_gate_w", (1, 2, 16, 512)), ("gla_gate_b", (1, 2, 512)), ("gla_norm_g", (1, 1024)),
    ("gla_w_out", (1, 1024, 1024)),
    ("lru_w_in", (1, 1024, 2048)), ("lru_conv_w", (1, 4, 1024)), ("lru_conv_b", (1, 1024)),
    ("lru_gate_a_w", (1, 2, 4, 256, 256)), ("lru_gate_a_b", (1, 2, 1024)), ("lru_gate_x_w", (1, 2, 4, 256, 256)),
    ("lru_gate_x_b", (1, 2, 1024)), ("lru_lambda", (1, 2, 1024)), ("lru_w_out", (1, 1024, 1024)),
    ("mla_w_in", (1, 1024, 704)), ("mla_q_norm_g", (1, 384)), ("mla_kv_norm_g", (1, 256)), ("mla_w_uq", (1, 384, 1536)),
    ("mla_w_ukv", (1, 256, 2048)), ("mla_w_out", (1, 1024, 1024)),
    ("moe_router", (4, 1024, 16)), ("moe_w_gate", (4, 16, 1024, 1024)), ("moe_w_up", (4, 16, 1024, 1024)),
    ("moe_w_down", (4, 16, 1024, 1024)), ("ln_g", (4, 2, 1024)), ("ln_b", (4, 2, 1024)),
]


def build_program(nseq, stages):
    nc = bass.Bass("TRN2", target_bir_lowering=False)
    specs = [("x", (nseq, S, D), F32), ("positions", (nseq, S), I32)]
    specs += [(n, s, F32) for n, s in WEIGHT_SPECS]
    specs += [("consts", (128, C_END), F32), ("consts16", (16, 16 * 128), F32)]
    A = declare_inputs(nc, nseq, specs)
    out = nc.dram_tensor("out", [nseq, S, D], F32, kind="ExternalOutput").ap()
    xd = nc.dram_tensor("xd_scratch", [S, D], BF16, kind="ExternalOutput").ap()
    with contextlib.ExitStack() as stack:
        P = Prog(nc, stack)
        cx = setup_ctx(nc, stack, P)
        cx.A = A
        cx.xd = xd
        cx.r_xd = [Reg() for i in range(NT)]
        load_consts(cx, A["consts"], A["consts16"])
        wr = stack.enter_context(nc.sbuf_tensor("wr", [128, NK, NE], BF16))
        rwr = Reg()
        lnscr = None
        for s in range(nseq):
            for stg in stages:
                kind = stg[0]
                if kind == "load":
                    xin = A["x"][s].rearrange("(i p) d -> p i d", p=128)
                    for i in range(NT):
                        P.dma("sp", cx.R[:, i, :], xin[:, i, :], writes=[cx.rR[i]])
                elif kind == "ln":
                    _, L, j, mode = stg
                    if mode == "xb":
                        P.dma("pool", wr[:], A["moe_router"][L].rearrange("(k p) e -> p k e", p=128), writes=[rwr])
                    emit_ln(cx, A["ln_g"][L, j], A["ln_b"][L, j], mode, wr=wr, rwr=rwr, do_ln=True,
                            out_ap=out[s].rearrange("(i p) d -> p i d", p=128), scratch=lnscr)
                elif kind == "prep":
                    _, L, mode = stg
                    if mode == "xb":
                        P.dma("pool", wr[:], A["moe_router"][L].rearrange("(k p) e -> p k e", p=128), writes=[rwr])
                    emit_ln(cx, None, None, mode, wr=wr, rwr=rwr, do_ln=False, scratch=lnscr)
                elif kind == "moe":
                    _, L = stg
                    emit_moe(cx, A["moe_w_gate"][L], A["moe_w_up"][L], A["moe_w_down"][L])
                elif kind == "mla":
                    emit_mla(cx, s)
                elif kind in ("mlstm", "gla"):
                    emit_linattn(cx, kind)
                elif kind == "lru":
                    emit_lru(cx)
                elif kind == "store":
                    oo = out[s].rearrange("(i p) d -> p i d", p=128)
                    for i in range(NT):
                        P.dma("sp", oo[:, i, :], cx.R[:, i, :], reads=[cx.rR[i]])
                else:
                    raise ValueError(kind)
        P.barrier(engines=["sp"])
        print("instructions", P.ninstr, "waits", P.nwaits)
    return nc


MIXERS = ("mlstm", "gla", "lru", "mla")


def full_stages():
    st = [("load",), ("prep", 0, "xt")]
    for L in range(DEPTH):
        st.append((MIXERS[L % 4],))
        st.append(("ln", L, 0, "xb"))
        st.append(("moe", L))
        st.append(("ln", L, 1, "xt" if L < DEPTH - 1 else "none"))
    return st


NSEQ_PER_LAUNCH = 4
_PROG_CACHE = {}


def kernel(**inputs):
    nseq = NSEQ_PER_LAUNCH
    x = np.ascontiguousarray(np.asarray(inputs["x"], dtype=np.float32))
    pos = np.ascontiguousarray(np.asarray(inputs["positions"], dtype=np.int32))
    B = x.shape[0]
    per_core = B // NCORES
    consts = make_consts()
    consts16 = make_consts16()
    weights = {n: np.ascontiguousarray(np.asarray(inputs[n], dtype=np.float32)) for n, _ in WEIGHT_SPECS}
    out = np.empty((B, S, D), np.float32)
    for l0 in range(0, per_core, nseq):
        if nseq not in _PROG_CACHE:
            _PROG_CACHE[nseq] = build_program(nseq, full_stages())
        nc = _PROG_CACHE[nseq]
        in_maps = []
        for c in range(NCORES):
            b0 = c * per_core + l0
            m = {"x": x[b0:b0 + nseq], "positions": pos[b0:b0 + nseq], "consts": consts, "consts16": consts16}
            m.update(weights)
            in_maps.append(m)
        res = run_bass_kernel_spmd(nc, in_maps, core_ids=list(range(NCORES)))
        for c in range(NCORES):
            b0 = c * per_core + l0
            out[b0:b0 + nseq] = np.asarray(res.results[c]["out"]).reshape(nseq, S, D)
    return out
```

```python
import contextlib
import numpy as np
import concourse.bass as bass
import concourse.mybir as mybir
from concourse.bass_utils import run_bass_kernel_spmd

F32 = mybir.dt.float32
BF16 = mybir.dt.bfloat16
I32 = mybir.dt.int32
AF = mybir.ActivationFunctionType
ALU = mybir.AluOpType
AX = mybir.AxisListType

D = 1024
S = 2048
NT = 16
NK = 8
DEPTH = 4
ALPHA = (2 * DEPTH) ** 0.25
LN_EPS = 1e-5
NE = 16
CAP = 256
NCORES = 8
SEQ_PER_CORE = 4

SAME_ENGINE_RAW = True


_UN = [0]


def un(name):
    _UN[0] += 1
    return "%s_%d" % (name, _UN[0])


class Reg:
    __slots__ = ("w", "r", "name")

    def __init__(self, name=""):
        self.w = None
        self.r = {}
        self.name = name


class Prog:
    ENG = ("pe", "act", "dve", "pool", "sp")
    KDMA = 8

    def __init__(self, nc, stack):
        self.nc = nc
        self.eng = {"pe": nc.tensor, "act": nc.scalar, "dve": nc.vector, "pool": nc.gpsimd, "sp": nc.sync}
        self.sem = {}
        for e in self.ENG:
            self.sem[("c", e)] = stack.enter_context(nc.semaphore("s_" + e))
        self.dq = ("sp", "pool", "act")
        for q in self.dq:
            for k in range(self.KDMA):
                self.sem[("d", q, k)] = stack.enter_context(nc.semaphore("d_%s%d" % (q, k)))
        self.cnt = {e: 0 for e in self.ENG}
        self.dcnt = {q: 0 for q in self.dq}
        self.known = {e: {} for e in self.ENG}
        self.nwaits = 0
        self.ninstr = 0

    def _wait(self, eng, key, val):
        if self.known[eng].get(key, 0) >= val:
            return
        self.eng[eng].wait_ge(self.sem[key], val)
        self.known[eng][key] = val
        self.nwaits += 1

    def _deps(self, eng, reads, writes):
        me = ("c", eng)
        for r in reads:
            if r.w is not None:
                key, val = r.w
                if key == me:
                    if eng != "pe" and SAME_ENGINE_RAW:
                        self._wait(eng, key, val)
                else:
                    self._wait(eng, key, val)
        for w in writes:
            if w.w is not None:
                key, val = w.w
                if key != me:
                    self._wait(eng, key, val)
            for key, val in w.r.items():
                if key != me:
                    self._wait(eng, key, val)

    def _update(self, tok, reads, writes):
        key, val = tok
        for r in reads:
            if r.r.get(key, 0) < val:
                r.r[key] = val
        for w in writes:
            w.w = tok
            w.r = {}

    def op(self, eng, fn, reads=(), writes=()):
        self._deps(eng, reads, writes)
        ins = fn(self.eng[eng])
        self.cnt[eng] += 1
        self.ninstr += 1
        tok = (("c", eng), self.cnt[eng])
        ins.then_inc(self.sem[tok[0]], 1)
        self._update(tok, reads, writes)
        return tok

    def dma(self, q, out, in_, reads=(), writes=(), **kw):
        k = self.dcnt[q]
        slot = k % self.KDMA
        key = ("d", q, slot)
        if k >= self.KDMA:
            self._wait(q, key, 16 * (k // self.KDMA))
        self._deps(q, reads, writes)
        ins = self.eng[q].dma_start(out=out, in_=in_, **kw)
        tok = (key, 16 * (k // self.KDMA + 1))
        ins.then_inc(self.sem[key], 16)
        self.dcnt[q] = k + 1
        self.ninstr += 1
        self._update(tok, reads, writes)
        return tok

    def idma_gather(self, out, in_dram, idx_ap, reads=(), writes=()):
        q = "pool"
        k = self.dcnt[q]
        slot = k % self.KDMA
        key = ("d", q, slot)
        if k >= self.KDMA:
            self._wait(q, key, 16 * (k // self.KDMA))
        self._deps(q, reads, writes)
        ins = self.eng[q].indirect_dma_start(out=out, out_offset=None, in_=in_dram,
                                             in_offset=bass.IndirectOffsetOnAxis(ap=idx_ap, axis=0))
        tok = (key, 16 * (k // self.KDMA + 1))
        ins.then_inc(self.sem[key], 16)
        self.dcnt[q] = k + 1
        self.ninstr += 1
        self._update(tok, reads, writes)
        return tok

    def idma_scatter_add(self, out_dram, idx_ap, in_, reads=(), writes=(), after=()):
        q = "pool"
        k = self.dcnt[q]
        slot = k % self.KDMA
        key = ("d", q, slot)
        if k >= self.KDMA:
            self._wait(q, key, 16 * (k // self.KDMA))
        for akey, aval in after:
            self._wait(q, akey, aval)
        self._deps(q, reads, writes)
        ins = self.eng[q].indirect_dma_start(out=out_dram, out_offset=bass.IndirectOffsetOnAxis(ap=idx_ap, axis=0),
                                             in_=in_, in_offset=None, compute_op=ALU.add)
        tok = (key, 16 * (k // self.KDMA + 1))
        ins.then_inc(self.sem[key], 16)
        self.dcnt[q] = k + 1
        self.ninstr += 1
        self._update(tok, reads, writes)
        return tok

    def latest_tokens(self):
        toks = []
        for e in self.ENG:
            if self.cnt[e] > 0:
                toks.append((("c", e), self.cnt[e]))
        for q in self.dq:
            k = self.dcnt[q]
            for slot in range(self.KDMA):
                n = (k - slot + self.KDMA - 1) // self.KDMA
                if n > 0:
                    toks.append((("d", q, slot), 16 * n))
        return toks

    def barrier(self, engines=None):
        toks = self.latest_tokens()
        for e in (engines or self.ENG):
            for key, val in toks:
                if key != ("c", e):
                    self._wait(e, key, val)


C_IDENT = 0
C_IOTA = 128
C_PIDX = 384
C_TRIU = 386
C_TRIL = 514
C_ONES = 642
C_FREQ = 770
C_LNQ = 802
C_END = 804


def make_consts():
    c = np.zeros((128, C_END), np.float32)
    c[:, C_IDENT:C_IDENT + 128] = np.eye(128, dtype=np.float32)
    c[:, C_IOTA:C_IOTA + 256] = np.arange(256, dtype=np.float32)[None, :]
    c[:, C_PIDX] = np.arange(128)
    c[:, C_PIDX + 1] = np.arange(128) + 128
    s = np.arange(128)[:, None]
    t = np.arange(128)[None, :]
    c[:, C_TRIU:C_TRIU + 128] = (s <= t)
    c[:, C_TRIL:C_TRIL + 128] = (s >= t)
    c[:, C_ONES:C_ONES + 128] = 1.0
    c[:, C_LNQ] = np.float32(np.log(128.0 ** -0.5))
    c[:, C_FREQ:C_FREQ + 32] = (10000.0 ** (-np.arange(32, dtype=np.float32) / np.float32(32))).astype(np.float32)[None, :]
    return c


def make_consts16():
    c = np.zeros((16, 16 * 128), np.float32)
    for e in range(16):
        c[e, e * 128:(e + 1) * 128] = 1.0
    return c


class Ctx:
    pass


def setup_ctx(nc, stack, P):
    cx = Ctx()
    cx.nc = nc
    cx.P = P
    sb = lambda name, shape, dt: stack.enter_context(nc.sbuf_tensor(name, shape, dt))
    cx.R = sb("R", [128, NT, D], F32)
    cx.rR = [Reg("R%d" % i) for i in range(NT)]
    cx.XA = sb("XA", [128, NT * D], BF16)
    cx.rXA = [Reg("XA%d" % i) for i in range(NT)]
    cx.cst = sb("cst", [128, C_END], F32)
    cx.rcst = Reg("cst")
    cx.identb = sb("identb", [128, 128], BF16)
    cx.logits = sb("logits", [128, NT, NE], F32)
    cx.rlogits = [Reg("lg%d" % i) for i in range(NT)]
    cx.ps = []
    cx.rps = []
    for b in range(8):
        cx.ps.append(stack.enter_context(nc.psum_tensor("ps%d" % b, [128, 512], F32)))
        cx.rps.append(Reg("ps%d" % b))
    return cx


def load_consts(cx, consts_ap, consts16_ap):
    P = cx.P
    P.dma("sp", cx.cst[:], consts_ap[:, :], writes=[cx.rcst])
    P.op("dve", lambda e: e.tensor_copy(out=cx.identb[:], in_=cx.cst[:, C_IDENT:C_IDENT + 128]),
         reads=[cx.rcst], writes=[cx.rcst])


def emit_ln(cx, lng_ap, lnb_ap, mode, wr=None, rwr=None, do_ln=True, out_ap=None, scratch=None):
    P = cx.P
    nc = cx.nc
    with contextlib.ExitStack() as lst:
        _emit_ln(cx, lst, lng_ap, lnb_ap, mode, wr, rwr, do_ln, out_ap)
    P.barrier()


def _emit_ln(cx, lst, lng_ap, lnb_ap, mode, wr, rwr, do_ln, out_ap):
    P = cx.P
    nc = cx.nc
    cx.lng = lst.enter_context(nc.sbuf_tensor(un("lng"), [128, 2, D], F32))
    cx.rlng = Reg("lng")
    tmpT, rtmpT, stat, rstat, xbt, rxbt = alloc_ln_scratch(cx, lst)
    if do_ln:
        P.dma("sp", cx.lng[:, 0, :], lng_ap.partition_broadcast(128), writes=[cx.rlng])
        P.dma("sp", cx.lng[:, 1, :], lnb_ap.partition_broadcast(128), writes=[cx.rlng])
    for i in range(NT):
        Ri = cx.R[:, i, :]
        rRi = cx.rR[i]
        st = stat[i % 2]
        rst = rstat[i % 2]
        if do_ln:
            for h in range(2):
                P.op("dve", lambda e, h=h: e.bn_stats(out=st[:, h * 6:(h + 1) * 6], in_=Ri[:, h * 512:(h + 1) * 512]),
                     reads=[rRi], writes=[rst])
            P.op("dve", lambda e: e.bn_aggr(out=st[:, 12:14], in_=st[:, 0:12]), reads=[rst], writes=[rst])
            P.op("dve", lambda e: e.tensor_scalar(out=st[:, 14:15], in0=st[:, 13:14], scalar1=LN_EPS, scalar2=None, op0=ALU.add),
                 reads=[rst], writes=[rst])
            P.op("act", lambda e: e.activation(out=st[:, 14:15], in_=st[:, 14:15], func=AF.Sqrt), reads=[rst], writes=[rst])
            P.op("dve", lambda e: e.reciprocal(out=st[:, 15:16], in_=st[:, 14:15]), reads=[rst], writes=[rst])
            P.op("dve", lambda e: e.tensor_scalar(out=Ri, in0=Ri, scalar1=st[:, 12:13], scalar2=st[:, 15:16],
                                                   op0=ALU.subtract, op1=ALU.mult), reads=[rRi, rst], writes=[rRi])
            P.op("pool", lambda e: e.tensor_tensor(out=Ri, in0=Ri, in1=cx.lng[:, 0, :], op=ALU.mult),
                 reads=[rRi, cx.rlng], writes=[rRi])
            P.op("pool", lambda e: e.tensor_tensor(out=Ri, in0=Ri, in1=cx.lng[:, 1, :], op=ALU.add),
                 reads=[rRi, cx.rlng], writes=[rRi])
        if mode == "none":
            if out_ap is not None:
                P.dma("sp", out_ap[:, i, :], Ri, reads=[rRi])
            continue
        if mode == "xb":
            xb = cx.XA[:, i * D:(i + 1) * D]
            rxb = cx.rXA[i]
        else:
            xb = xbt[i % 2][:]
            rxb = rxbt[i % 2]
        P.op("act", lambda e: e.activation(out=xb, in_=Ri, func=AF.Copy), reads=[rRi], writes=[rxb])
        P.op("pool", lambda e: e.tensor_scalar(out=Ri, in0=Ri, scalar1=float(ALPHA), scalar2=0.0, op0=ALU.mult, op1=ALU.add),
             reads=[rRi], writes=[rRi])
        pb = 6 + (i % 2)
        pst = cx.ps[pb][:].bitcast(BF16)
        for k in range(NK):
            P.op("pe", lambda e, k=k: e.transpose(out=pst[:, k * 128:(k + 1) * 128], in_=xb[:, k * 128:(k + 1) * 128],
                                                  identity=cx.identb[:]),
                 reads=[rxb, cx.rcst], writes=[cx.rps[pb]])
        if mode == "xt":
            xt = cx.XA[:].rearrange("p (k t) -> p k t", k=NK)
            P.op("dve", lambda e: e.tensor_copy(out=xt[:, :, i * 128:(i + 1) * 128],
                                                in_=pst.rearrange("p (k t) -> p k t", k=NK)),
                 reads=[cx.rps[pb]], writes=[cx.rXA[i]])
        else:
            tt = tmpT[i % 2]
            rtt = rtmpT[i % 2]
            P.op("dve", lambda e: e.tensor_copy(out=tt[:], in_=pst), reads=[cx.rps[pb]], writes=[rtt])
            lb = 4 + (i % 2)
            for k in range(NK):
                P.op("pe", lambda e, k=k: e.matmul(cx.ps[lb][:, 0:NE], lhsT=tt[:, k * 128:(k + 1) * 128], rhs=wr[:, k, :],
                                                   start=(k == 0), stop=(k == NK - 1)),
                     reads=[rtt, rwr], writes=[cx.rps[lb]])
            P.op("act", lambda e: e.activation(out=cx.logits[:, i, :], in_=cx.ps[lb][:, 0:NE], func=AF.Copy),
                 reads=[cx.rps[lb]], writes=[cx.rlogits[i]])


def alloc_ln_scratch(cx, stack):
    nc = cx.nc
    tmpT = [stack.enter_context(nc.sbuf_tensor(un("tmpT"), [128, D], BF16)) for j in range(2)]
    rtmpT = [Reg(), Reg()]
    stat = [stack.enter_context(nc.sbuf_tensor(un("stat"), [128, 16], F32)) for j in range(2)]
    rstat = [Reg(), Reg()]
    xbt = [stack.enter_context(nc.sbuf_tensor(un("xbt"), [128, D], BF16)) for j in range(2)]
    rxbt = [Reg(), Reg()]
    return (tmpT, rtmpT, stat, rstat, xbt, rxbt)


def emit_moe(cx, wg_ap, wu_ap, wd_ap):
    P = cx.P
    nc = cx.nc
    with contextlib.ExitStack() as st:
        sb = lambda name, shape, dt: st.enter_context(nc.sbuf_tensor(un(name), shape, dt))
        aff = sb("aff", [128, NT, NE], F32)
        r_aff = Reg()
        sm = sb("sm", [128, NT, 4], F32)
        AG = sb("AG", [128, NT, NE, 4], BF16)
        r_AG = Reg()
        slot = sb("slot", [128, NT, NE], F32)
        r_slot = Reg()
        slotT = sb("slotT", [16, S], F32)
        r_slotT = Reg()
        m8 = sb("m8", [16, 8], F32)
        r_m8 = Reg()
        wbuf = [sb("wbuf%d" % j, [128, NK, D], BF16) for j in range(3)]
        r_wbuf = [Reg() for j in range(3)]
        Pm = sb("Pm", [128, NT, CAP], BF16)
        r_Pm = Reg()
        workbuf = sb("workbuf", [16, S], F32)
        xinT = sb("xinT", [128, NK, CAP], BF16)
        r_xinT = Reg()
        hT = sb("hT", [128, NK, CAP], BF16)
        r_hT = Reg()
        sg = [sb("sg%d" % j, [128, CAP], F32) for j in range(2)]
        r_sg = [Reg(), Reg()]
        Oe2 = [sb("Oe", [128, 2, D], F32) for j in range(2)]
        Oe = Oe2[0]
        r_Oe = Reg()
        gs = sb("gs", [128, 4], F32)
        r_gs = Reg()
        affT = Pm[:].rearrange("p a b -> p (a b)").bitcast(F32)[0:16, :]
        r_affT = Reg()
        work = workbuf[:]
        r_work = Reg()

        ident = cx.cst[:, C_IDENT:C_IDENT + 128]

        rl = cx.rlogits
        P.op("dve", lambda e: e.tensor_reduce(out=sm[:, :, 0], in_=cx.logits[:], axis=AX.X, op=ALU.max),
             reads=rl, writes=[r_aff])
        P.op("dve", lambda e: e.tensor_tensor(out=aff[:], in0=cx.logits[:],
                                              in1=sm[:, :, 0:1].to_broadcast([128, NT, NE]), op=ALU.subtract),
             reads=rl + [r_aff], writes=[r_aff])
        P.op("act", lambda e: e.activation(out=aff[:], in_=aff[:], func=AF.Exp), reads=[r_aff], writes=[r_aff])
        P.op("dve", lambda e: e.tensor_reduce(out=sm[:, :, 1], in_=aff[:], axis=AX.X, op=ALU.add),
             reads=[r_aff], writes=[r_aff])
        P.op("dve", lambda e: e.reciprocal(out=sm[:, :, 2], in_=sm[:, :, 1]), reads=[r_aff], writes=[r_aff])
        P.op("dve", lambda e: e.tensor_tensor(out=aff[:], in0=aff[:],
                                              in1=sm[:, :, 2:3].to_broadcast([128, NT, NE]), op=ALU.mult),
             reads=[r_aff], writes=[r_aff])
        P.op("dve", lambda e: e.tensor_copy(out=AG[:, :, :, 0], in_=aff[:]), reads=[r_aff], writes=[r_AG])
        P.op("dve", lambda e: e.tensor_tensor(out=AG[:, :, :, 1], in0=aff[:], in1=AG[:, :, :, 0], op=ALU.subtract),
             reads=[r_aff, r_AG], writes=[r_AG])
        for i in range(NT):
            P.op("pool", lambda e, i=i: e.memset(AG[:, i, :, 2], float(i)), writes=[r_AG])
        P.op("dve", lambda e: e.tensor_copy(out=AG[:, :, :, 3], in_=cx.cst[:, C_PIDX:C_PIDX + 1].to_broadcast([128, NT * NE]).rearrange("p (a b) -> p a b", a=NT)),
             reads=[cx.rcst], writes=[r_AG])
        for i in range(NT):
            P.dma("sp", cx.xd[i * 128:(i + 1) * 128, :], cx.XA[:, i * D:(i + 1) * D], reads=[cx.rXA[i]], writes=[cx.r_xd[i]])
        for g in range(4):
            for j in range(4):
                i = g * 4 + j
                P.op("pe", lambda e, i=i, j=j, g=g: e.transpose(out=cx.ps[g][0:16, j * 128:(j + 1) * 128],
                                                                in_=aff[:, i, :], identity=ident),
                     reads=[r_aff, cx.rcst], writes=[cx.rps[g]])
            P.op("act", lambda e, g=g: e.activation(out=affT[:, g * 512:(g + 1) * 512], in_=cx.ps[g][0:16, :], func=AF.Copy),
                 reads=[cx.rps[g]], writes=[r_affT])
        P.op("pool", lambda e: e.tensor_copy(out=work[:], in_=affT[:]), reads=[r_affT], writes=[r_work])
        nround = CAP // 8
        for r in range(nround):
            P.op("dve", lambda e: e.max(out=m8[:], in_=work[:]), reads=[r_work], writes=[r_m8])
            if r < nround - 1:
                P.op("dve", lambda e: e.match_replace(out=work[:], in_to_replace=m8[:], in_values=work[:], imm_value=-1.0),
                     reads=[r_work, r_m8], writes=[r_work])
        P.op("dve", lambda e: e.tensor_scalar(out=work[:], in0=affT[:], scalar1=m8[:, 7:8], scalar2=None, op0=ALU.is_ge),
             reads=[r_affT, r_m8], writes=[r_work])
        P.op("dve", lambda e: e.tensor_tensor_scan(out=slotT[:], data0=cx.cst[0:16, C_ONES:C_ONES + 1].to_broadcast([16, S]), data1=work[:], initial=0.0,
                                                    op0=ALU.mult, op1=ALU.add),
             reads=[r_work, cx.rcst], writes=[r_slotT])
        P.op("dve", lambda e: e.tensor_tensor(out=slotT[:], in0=slotT[:], in1=work[:], op=ALU.mult),
             reads=[r_work, r_slotT], writes=[r_slotT])
        P.op("dve", lambda e: e.tensor_scalar(out=slotT[:], in0=slotT[:], scalar1=-1.0, scalar2=None, op0=ALU.add),
             reads=[r_slotT], writes=[r_slotT])
        for i in range(NT):
            P.op("pe", lambda e, i=i: e.transpose(out=cx.ps[5][:, i * NE:(i + 1) * NE], in_=slotT[:, i * 128:(i + 1) * 128],
                                                  identity=ident[0:16, 0:16]),
                 reads=[r_slotT, cx.rcst], writes=[cx.rps[5]])
        P.op("act", lambda e: e.activation(out=slot[:].rearrange("p a b -> p (a b)"), in_=cx.ps[5][:, 0:NT * NE], func=AF.Copy),
             reads=[cx.rps[5]], writes=[r_slot])

        P.barrier()
        r_xk = [Reg() for k in range(NK)]
        r_hj = [Reg() for j in range(NK)]
        r_Oq2 = [[[Reg() for n in range(2)] for c in range(2)] for j in range(2)]
        r_Oq = r_Oq2[0]
        sc_prev = [[]]
        P.op("pool", lambda e: e.memset(Oe[:, 0, :], 0.0), writes=[r_Oq[0][0], r_Oq[0][1]])
        for i in range(NT):
            P.dma("sp", cx.yd[i * 128:(i + 1) * 128, :], Oe[:, 0, :], reads=[r_Oq[0][0], r_Oq[0][1]], writes=[cx.r_yd])
        wq = [0]
        xa3 = cx.XA[:].rearrange("p (s k f) -> p s k f", s=2, k=NK)
        wslot = [wbuf[0][:], wbuf[1][:], wbuf[2][:], xa3[:, 0], xa3[:, 1]]
        rw = [[r_wbuf[0]], [r_wbuf[1]], [r_wbuf[2]], [Reg()] + cx.rXA[0:8], [Reg()] + cx.rXA[8:16]]

        def load_w(ap2d):
            j = wq[0] % 5
            wq[0] += 1
            P.dma("pool", wslot[j], ap2d.rearrange("(k p) f -> p k f", p=128), writes=rw[j])
            return j

        def build_P(ex):
            for i in range(NT):
                P.op("dve", lambda e, i=i: e.tensor_scalar(out=Pm[:, i, :], in0=cx.cst[:, C_IOTA:C_IOTA + CAP],
                                                           scalar1=slot[:, i, ex:ex + 1], scalar2=None, op0=ALU.is_equal),
                     reads=[r_slot, cx.rcst], writes=[r_Pm])

        xin = sb("xin", [128, 2, D], BF16)
        r_xin = [Reg(), Reg()]
        gs2 = [sb("gs2", [128, 8], F32) for j in range(2)]
        r_gs2 = [Reg(), Reg()]
        idx2 = [sb("idx2", [128, 2], I32) for j in range(2)]
        r_idx2 = [Reg(), Reg()]

        def gsidx_and_gather(ex):
            g_ = gs2[ex % 2]
            r_g = r_gs2[ex % 2]
            for c in range(2):
                for i in range(NT):
                    P.op("pe", lambda e, i=i, c=c: e.matmul(cx.ps[6][:, 8 + c * 4:8 + c * 4 + 4], lhsT=Pm[:, i, c * 128:(c + 1) * 128],
                                                            rhs=AG[:, i, ex, :], start=(i == 0), stop=(i == NT - 1)),
                         reads=[r_Pm, r_AG], writes=[cx.rps[6]])
            P.op("dve", lambda e: e.tensor_copy(out=g_[:, 0:8], in_=cx.ps[6][:, 8:16]), reads=[cx.rps[6]], writes=[r_g])
            gv = g_[:, 0:8].rearrange("p (c f) -> p c f", c=2)
            P.op("dve", lambda e: e.tensor_tensor(out=gv[:, :, 0], in0=gv[:, :, 0], in1=gv[:, :, 1], op=ALU.add), reads=[r_g], writes=[r_g])
            P.op("dve", lambda e: e.scalar_tensor_tensor(out=gv[:, :, 2], in0=gv[:, :, 2], scalar=128.0, in1=gv[:, :, 3], op0=ALU.mult, op1=ALU.add),
                 reads=[r_g], writes=[r_g])
            ix = idx2[ex % 2]
            P.op("dve", lambda e: e.tensor_copy(out=ix[:], in_=gv[:, :, 2]), reads=[r_g], writes=[r_idx2[ex % 2]])
            for c in range(2):
                P.idma_gather(xin[:, c, :], cx.xd[:, :], ix[:, c:c + 1], reads=[r_idx2[ex % 2]] + cx.r_xd, writes=[r_xin[c]])

        build_P(0)
        gsidx_and_gather(0)
        jnext = (load_w(wg_ap[0]), load_w(wu_ap[0]), load_w(wd_ap[0]))
        for ex in range(NE):
            jg, ju, jd = jnext
            Oe = Oe2[ex % 2]
            r_Oq = r_Oq2[ex % 2]
            if ex + 1 < NE:
                jg1 = load_w(wg_ap[ex + 1])
                ju1 = load_w(wu_ap[ex + 1])
            gsc = gs2[ex % 2][:, 0:8].rearrange("p (c f) -> p c f", c=2)
            r_gs = r_gs2[ex % 2]
            for c in range(2):
                pb = 4 + c
                pst = cx.ps[pb][:].bitcast(BF16)
                for k in range(NK):
                    P.op("pe", lambda e, k=k, c=c: e.transpose(out=pst[:, k * 128:(k + 1) * 128], in_=xin[:, c, k * 128:(k + 1) * 128], identity=cx.identb[:]),
                         reads=[r_xin[c], cx.rcst], writes=[cx.rps[pb]])
                P.op("act", lambda e, c=c: e.activation(out=xinT[:, :, c * 128:(c + 1) * 128], in_=pst.rearrange("p (k t) -> p k t", k=NK), func=AF.Copy),
                     reads=[cx.rps[pb]], writes=r_xk)
            for j in range(NK):
                pg = 4 + (j % 2)
                pu = 6 + (j % 2)
                for k in range(NK):
                    P.op("pe", lambda e, j=j, k=k, pg=pg: e.matmul(cx.ps[pg][:, 0:CAP], lhsT=wslot[jg][:, k, j * 128:(j + 1) * 128],
                                                                   rhs=xinT[:, k, :], start=(k == 0), stop=(k == NK - 1)),
                         reads=rw[jg] + [r_xk[k]], writes=[cx.rps[pg]])
                for k in range(NK):
                    P.op("pe", lambda e, j=j, k=k, pu=pu: e.matmul(cx.ps[pu][:, 0:CAP], lhsT=wslot[ju][:, k, j * 128:(j + 1) * 128],
                                                                   rhs=xinT[:, k, :], start=(k == 0), stop=(k == NK - 1)),
                         reads=rw[ju] + [r_xk[k]], writes=[cx.rps[pu]])
                P.op("act", lambda e, j=j, pg=pg: e.activation(out=sg[j % 2][:], in_=cx.ps[pg][:, 0:CAP], func=AF.Silu),
                     reads=[cx.rps[pg]], writes=[r_sg[j % 2]])
                P.op("dve", lambda e, j=j, pu=pu: e.tensor_tensor(out=hT[:, j, :], in0=sg[j % 2][:], in1=cx.ps[pu][:, 0:CAP], op=ALU.mult),
                     reads=[cx.rps[pu], r_sg[j % 2]], writes=[r_hj[j]])
                if j == 1 and ex + 1 < NE:
                    build_P(ex + 1)
                if j == 5 and ex + 1 < NE:
                    gsidx_and_gather(ex + 1)
            if ex + 1 < NE:
                jnext = (jg1, ju1, load_w(wd_ap[ex + 1]))
            for c in range(2):
                for n in range(2):
                    pb = 4 + ((c * 2 + n) % 4)
                    for j in range(NK):
                        P.op("pe", lambda e, j=j, c=c, n=n, pb=pb: e.matmul(cx.ps[pb][:, :], lhsT=hT[:, j, c * 128:(c + 1) * 128],
                                                                            rhs=wslot[jd][:, j, n * 512:(n + 1) * 512],
                                                                            start=(j == 0), stop=(j == NK - 1)),
                             reads=rw[jd] + [r_hj[j]], writes=[cx.rps[pb]])
                    P.op("act", lambda e, c=c, n=n, pb=pb: e.activation(out=Oe[:, c, n * 512:(n + 1) * 512], in_=cx.ps[pb][:, :],
                                                                        func=AF.Copy, scale=gsc[:, c, 0:1]),
                         reads=[cx.rps[pb], r_gs], writes=[r_Oq[c][n]])
            toks = []
            for c in range(2):
                toks.append(P.idma_scatter_add(cx.yd[:, :], idx2[ex % 2][:, c:c + 1], Oe[:, c, :],
                                               reads=[r_idx2[ex % 2], r_Oq[c][0], r_Oq[c][1], cx.r_yd], after=sc_prev[0]))
            sc_prev[0] = toks
        for akey, aval in sc_prev[0]:
            P._wait("sp", akey, aval)
        Oe = Oe2[0]
        r_Oq = r_Oq2[0]
        for i in range(NT):
            c = i % 2
            P.dma("sp", Oe[:, c, :], cx.yd[i * 128:(i + 1) * 128, :], reads=[cx.r_yd], writes=[r_Oq[c][0], r_Oq[c][1]])
            eng = "dve" if i % 2 == 0 else "pool"
            P.op(eng, lambda e, i=i, c=c: e.tensor_tensor(out=cx.R[:, i, :], in0=cx.R[:, i, :], in1=Oe[:, c, :], op=ALU.add),
                 reads=[r_Oq[c][0], r_Oq[c][1], cx.rR[i]], writes=[cx.rR[i]])
    P.barrier()


def xt_view(cx):
    return cx.XA[:].rearrange("p (k t) -> p k t", k=NK)


def small_load(cx, out, in_, reg):
    cx.P.dma("sp", out, in_, writes=[reg], allow_slow_non_contiguous=True)


def gelu_tanh(cx, out, src, t1, t2, r_src, r_t1, r_t2, r_out, extra_mul=None, r_extra=None):
    P = cx.P
    P.op("act", lambda e: e.activation(out=t1, in_=src, func=AF.Square), reads=[r_src], writes=[r_t1])
    P.op("dve", lambda e: e.tensor_scalar(out=t1, in0=t1, scalar1=0.044715, scalar2=1.0, op0=ALU.mult, op1=ALU.add),
         reads=[r_t1], writes=[r_t1])
    P.op("dve", lambda e: e.tensor_tensor(out=t1, in0=t1, in1=src, op=ALU.mult), reads=[r_t1, r_src], writes=[r_t1])
    P.op("act", lambda e: e.activation(out=t1, in_=t1, func=AF.Sigmoid, scale=1.5957691216057308), reads=[r_t1], writes=[r_t1])
    if extra_mul is None:
        P.op("dve", lambda e: e.tensor_tensor(out=out, in0=t1, in1=src, op=ALU.mult), reads=[r_t1, r_src], writes=[r_out])
    else:
        P.op("dve", lambda e: e.tensor_tensor(out=t2, in0=t1, in1=src, op=ALU.mult), reads=[r_t1, r_src], writes=[r_t2])
        P.op("pool", lambda e: e.tensor_tensor(out=out, in0=t2, in1=extra_mul, op=ALU.mult), reads=[r_t2, r_extra], writes=[r_out])


def emit_lru(cx):
    P = cx.P
    nc = cx.nc
    A = cx.A
    XT = xt_view(cx)
    with contextlib.ExitStack() as st:
        sb = lambda name, shape, dt: st.enter_context(nc.sbuf_tensor(un(name), shape, dt))
        cw = sb("cw", [128, NK, 4], F32)
        cb = sb("cb", [128, NK], F32)
        gab = sb("gab", [128, 2, NK], F32)
        gxb = sb("gxb", [128, 2, NK], F32)
        lam = sb("lam", [128, 2, NK], F32)
        c8 = sb("c8", [128, 2, NK], F32)
        c16 = sb("c16", [128, 2, NK], F32)
        r_par = Reg()
        for j in range(4):
            small_load(cx, cw[:, :, j], A["lru_conv_w"][0, j].rearrange("(c p) -> p c", p=128), r_par)
        small_load(cx, cb[:], A["lru_conv_b"][0].rearrange("(c p) -> p c", p=128), r_par)
        for g in range(2):
            small_load(cx, gab[:, g, :], A["lru_gate_a_b"][0, g].rearrange("(c p) -> p c", p=128), r_par)
        for g in range(2):
            small_load(cx, gxb[:, g, :], A["lru_gate_x_b"][0, g].rearrange("(c p) -> p c", p=128), r_par)
        for g in range(2):
            small_load(cx, lam[:, g, :], A["lru_lambda"][0, g].rearrange("(c p) -> p c", p=128), r_par)
        P.op("act", lambda e: e.activation(out=c8[:], in_=lam[:], func=AF.Exp, scale=-1.0), reads=[r_par], writes=[r_par])
        P.op("act", lambda e: e.activation(out=c8[:], in_=c8[:], func=AF.Ln, bias=1.0), reads=[r_par], writes=[r_par])
        P.op("dve", lambda e: e.tensor_scalar(out=c16[:], in0=c8[:], scalar1=-16.0, scalar2=None, op0=ALU.mult), reads=[r_par], writes=[r_par])
        P.op("dve", lambda e: e.tensor_scalar(out=c8[:], in0=c8[:], scalar1=-8.0, scalar2=None, op0=ALU.mult), reads=[r_par], writes=[r_par])

        wgb = sb("wgb", [128, NK, 256], BF16)
        r_wgb = Reg()
        wu = sb("wu", [128, NK, 256], BF16)
        r_wu = Reg()
        wg4 = [[sb("wga", [128, 2, 256], BF16) for d in range(2)] for ax in range(2)]
        r_wg4 = [[Reg() for d in range(2)] for ax in range(2)]
        wo = sb("wo", [128, D], BF16)
        r_wo = Reg()
        upad = sb("upad", [128, S + 4], F32)
        r_upad = Reg()
        uc = sb("uc", [128, 2, S], F32)
        r_uc = [Reg(), Reg()]
        ucb = sb("ucb", [128, 2, S], BF16)
        r_ucb = [Reg(), Reg()]
        rbuf = sb("rbuf", [128, S], F32)
        r_rbuf = Reg()
        abuf = sb("abuf", [128, S], F32)
        r_abuf = Reg()
        tmp = sb("tmp", [128, S], F32)
        r_tmp = Reg()
        hsum = sb("hsum", [128, S], F32)
        r_hsum = Reg()
        gtc = sb("gtc", [128, S], BF16)
        r_gtc = Reg()
        P.op("pool", lambda e: e.memset(upad[:], 0.0), writes=[r_upad])

        def psum4(base):
            return [cx.ps[base + j] for j in range(4)], [cx.rps[base + j] for j in range(4)]

        pcount = [0]

        def next_ps4():
            base = 4 * (pcount[0] % 2)
            pcount[0] += 1
            return psum4(base)

        for n in range(4):
            P.dma("pool", wgb[:], A["lru_w_in"][0][:, n * 256:(n + 1) * 256].rearrange("(k p) c -> p k c", p=128), writes=[r_wgb])
            P.dma("pool", wu[:], A["lru_w_in"][0][:, D + n * 256:D + (n + 1) * 256].rearrange("(k p) c -> p k c", p=128), writes=[r_wu])
            for ax, nm in enumerate(("lru_gate_a_w", "lru_gate_x_w")):
                for d in range(2):
                    P.dma("pool", wg4[ax][d][:], A[nm][0, d, n].rearrange("(i p) j -> p i j", p=128), writes=[r_wg4[ax][d]])
            for cc in range(2):
                c = 2 * n + cc
                ps, rps = next_ps4()
                for tb in range(4):
                    for k in range(NK):
                        P.op("pe", lambda e, tb=tb, k=k: e.matmul(ps[tb][:, :], lhsT=wu[:, k, cc * 128:(cc + 1) * 128],
                                                                  rhs=XT[:, k, tb * 512:(tb + 1) * 512], start=(k == 0), stop=(k == NK - 1)),
                             reads=[r_wu] + cx.rXA[tb * 4:(tb + 1) * 4], writes=[rps[tb]])
                    P.op("act", lambda e, tb=tb: e.activation(out=upad[:, 2 + tb * 512:2 + (tb + 1) * 512], in_=ps[tb][:, :], func=AF.Copy),
                         reads=[rps[tb]], writes=[r_upad])
                ucc = uc[:, cc, :]
                P.op("dve", lambda e: e.tensor_scalar(out=ucc, in0=upad[:, 0:S], scalar1=cw[:, c, 0:1], scalar2=cb[:, c:c + 1],
                                                       op0=ALU.mult, op1=ALU.add), reads=[r_upad, r_par], writes=[r_uc[cc]])
                for j in range(1, 4):
                    P.op("dve", lambda e, j=j: e.scalar_tensor_tensor(out=ucc, in0=upad[:, j:j + S], scalar=cw[:, c, j:j + 1], in1=ucc,
                                                                      op0=ALU.mult, op1=ALU.add), reads=[r_upad, r_par, r_uc[cc]], writes=[r_uc[cc]])
                P.op("pool", lambda e: e.tensor_copy(out=ucb[:, cc, :], in_=ucc), reads=[r_uc[cc]], writes=[r_ucb[cc]])
            for cc in range(2):
                c = 2 * n + cc
                ucc = uc[:, cc, :]
                for d in range(2):
                    ps, rps = next_ps4()
                    for tb in range(4):
                        for i in range(2):
                            P.op("pe", lambda e, tb=tb, i=i: e.matmul(ps[tb][:, :], lhsT=wg4[0][d][:, i, cc * 128:(cc + 1) * 128],
                                                                      rhs=ucb[:, i, tb * 512:(tb + 1) * 512], start=(i == 0), stop=(i == 1)),
                                 reads=[r_wg4[0][d], r_ucb[0], r_ucb[1]], writes=[rps[tb]])
                        P.op("act", lambda e, tb=tb: e.activation(out=rbuf[:, tb * 512:(tb + 1) * 512], in_=ps[tb][:, :], func=AF.Sigmoid,
                                                                  bias=gab[:, d, c:c + 1]), reads=[rps[tb], r_par], writes=[r_rbuf])
                    P.op("act", lambda e: e.activation(out=abuf[:], in_=rbuf[:], func=AF.Exp, scale=c8[:, d, c:c + 1]),
                         reads=[r_rbuf, r_par], writes=[r_abuf])
                    P.op("act", lambda e: e.activation(out=tmp[:], in_=rbuf[:], func=AF.Exp, scale=c16[:, d, c:c + 1]),
                         reads=[r_rbuf, r_par], writes=[r_tmp])
                    P.op("act", lambda e: e.activation(out=tmp[:], in_=tmp[:], func=AF.Sqrt, scale=-1.0, bias=1.0),
                         reads=[r_tmp], writes=[r_tmp])
                    ps, rps = next_ps4()
                    for tb in range(4):
                        for i in range(2):
                            P.op("pe", lambda e, tb=tb, i=i: e.matmul(ps[tb][:, :], lhsT=wg4[1][d][:, i, cc * 128:(cc + 1) * 128],
                                                                      rhs=ucb[:, i, tb * 512:(tb + 1) * 512], start=(i == 0), stop=(i == 1)),
                                 reads=[r_wg4[1][d], r_ucb[0], r_ucb[1]], writes=[rps[tb]])
                        P.op("act", lambda e, tb=tb: e.activation(out=rbuf[:, tb * 512:(tb + 1) * 512], in_=ps[tb][:, :], func=AF.Sigmoid,
                                                                  bias=gxb[:, d, c:c + 1]), reads=[rps[tb], r_par], writes=[r_rbuf])
                    P.op("dve", lambda e: e.tensor_tensor(out=tmp[:], in0=tmp[:], in1=rbuf[:], op=ALU.mult), reads=[r_tmp, r_rbuf], writes=[r_tmp])
                    P.op("dve", lambda e: e.tensor_tensor(out=tmp[:], in0=tmp[:], in1=ucc, op=ALU.mult), reads=[r_tmp, r_uc[cc]], writes=[r_tmp])
                    if d == 0:
                        P.op("dve", lambda e: e.tensor_tensor_scan(out=hsum[:], data0=abuf[:], data1=tmp[:], initial=0.0,
                                                                    op0=ALU.mult, op1=ALU.add), reads=[r_abuf, r_tmp], writes=[r_hsum])
                    else:
                        P.op("dve", lambda e: e.tensor_tensor_scan(out=rbuf[:, ::-1], data0=abuf[:, ::-1], data1=tmp[:, ::-1], initial=0.0,
                                                                    op0=ALU.mult, op1=ALU.add), reads=[r_abuf, r_tmp], writes=[r_rbuf])
                        P.op("pool", lambda e: e.tensor_tensor(out=hsum[:], in0=hsum[:], in1=rbuf[:], op=ALU.add),
                             reads=[r_rbuf, r_hsum], writes=[r_hsum])
                ps, rps = next_ps4()
                for tb in range(4):
                    for k in range(NK):
                        P.op("pe", lambda e, tb=tb, k=k: e.matmul(ps[tb][:, :], lhsT=wgb[:, k, cc * 128:(cc + 1) * 128],
                                                                  rhs=XT[:, k, tb * 512:(tb + 1) * 512], start=(k == 0), stop=(k == NK - 1)),
                             reads=[r_wgb] + cx.rXA[tb * 4:(tb + 1) * 4], writes=[rps[tb]])
                    sl = slice(tb * 512, (tb + 1) * 512)
                    gelu_tanh(cx, gtc[:, sl], ps[tb][:, :], abuf[:, sl], tmp[:, sl], rps[tb], r_abuf, r_tmp, r_gtc,
                              extra_mul=hsum[:, sl], r_extra=r_hsum)
                P.dma("pool", wo[:], A["lru_w_out"][0][c * 128:(c + 1) * 128, :], writes=[r_wo])
                for i in range(NT):
                    for hh in range(2):
                        pb = (i * 2 + hh) % 8
                        P.op("pe", lambda e, i=i, hh=hh, pb=pb: e.matmul(cx.ps[pb][:, :], lhsT=gtc[:, i * 128:(i + 1) * 128],
                                                                         rhs=wo[:, hh * 512:(hh + 1) * 512], start=True, stop=True),
                             reads=[r_gtc, r_wo], writes=[cx.rps[pb]])
                        P.op("dve", lambda e, i=i, hh=hh, pb=pb: e.tensor_tensor(out=cx.R[:, i, hh * 512:(hh + 1) * 512],
                                                                                 in0=cx.R[:, i, hh * 512:(hh + 1) * 512],
                                                                                 in1=cx.ps[pb][:, :], op=ALU.add),
                             reads=[cx.rps[pb], cx.rR[i]], writes=[cx.rR[i]])
    P.barrier()


MLA_H = 8
MLA_SCALE = 192.0 ** -0.5
TWO_PI_HI = 6.28125
TWO_PI_LO = 2.0 * np.pi - 6.28125
MAGIC = 12582912.0


def rope_apply(cx, out_bf, src, cos, sin, t1, t2, nb, r_src, r_tab, r_t, r_out):
    P = cx.P
    a1 = src[:, :, 0:32]
    a2 = src[:, :, 32:64]
    P.op("dve", lambda e: e.tensor_tensor(out=t1, in0=a1, in1=cos, op=ALU.mult), reads=[r_src, r_tab], writes=[r_t])
    P.op("dve", lambda e: e.tensor_tensor(out=t2, in0=a2, in1=sin, op=ALU.mult), reads=[r_src, r_tab], writes=[r_t])
    P.op("dve", lambda e: e.tensor_tensor(out=out_bf[:, :, 0:32], in0=t1, in1=t2, op=ALU.subtract), reads=[r_t], writes=[r_out])
    P.op("dve", lambda e: e.tensor_tensor(out=t1, in0=a2, in1=cos, op=ALU.mult), reads=[r_src, r_tab, r_out], writes=[r_t])
    P.op("dve", lambda e: e.tensor_tensor(out=t2, in0=a1, in1=sin, op=ALU.mult), reads=[r_src, r_tab], writes=[r_t])
    P.op("dve", lambda e: e.tensor_tensor(out=out_bf[:, :, 32:64], in0=t1, in1=t2, op=ALU.add), reads=[r_t], writes=[r_out])


def emit_mla(cx, seq):
    P = cx.P
    nc = cx.nc
    A = cx.A
    XT = xt_view(cx)
    with contextlib.ExitStack() as st:
        sb = lambda name, shape, dt: st.enter_context(nc.sbuf_tensor(un(name), shape, dt))
        posi = sb("posi", [128, NT], I32)
        ang = sb("ang", [128, NT, 32], F32)
        nn = sb("nn", [128, NT, 32], F32)
        cosT = sb("cosT", [128, NT, 32], F32)
        sinT = sb("sinT", [128, NT, 32], F32)
        r_tab = Reg()
        small_load(cx, posi[:], A["positions"][seq].rearrange("(i p) -> p i", p=128), r_tab)
        P.op("dve", lambda e: e.tensor_copy(out=nn[:, :, 0], in_=posi[:]), reads=[r_tab], writes=[r_tab])
        P.op("dve", lambda e: e.tensor_tensor(out=ang[:], in0=nn[:, :, 0:1].to_broadcast([128, NT, 32]),
                                              in1=cx.cst[:, None, C_FREQ:C_FREQ + 32].to_broadcast([128, NT, 32]), op=ALU.mult),
             reads=[r_tab, cx.rcst], writes=[r_tab])
        P.op("dve", lambda e: e.tensor_scalar(out=nn[:], in0=ang[:], scalar1=float(1.0 / (2.0 * np.pi)), scalar2=None, op0=ALU.mult),
             reads=[r_tab], writes=[r_tab])
        P.op("dve", lambda e: e.tensor_scalar(out=nn[:], in0=nn[:], scalar1=MAGIC, scalar2=None, op0=ALU.add), reads=[r_tab], writes=[r_tab])
        P.op("dve", lambda e: e.tensor_scalar(out=nn[:], in0=nn[:], scalar1=-MAGIC, scalar2=None, op0=ALU.add), reads=[r_tab], writes=[r_tab])
        P.op("dve", lambda e: e.scalar_tensor_tensor(out=ang[:], in0=nn[:], scalar=-TWO_PI_HI, in1=ang[:], op0=ALU.mult, op1=ALU.add),
             reads=[r_tab], writes=[r_tab])
        P.op("dve", lambda e: e.scalar_tensor_tensor(out=ang[:], in0=nn[:], scalar=-TWO_PI_LO, in1=ang[:], op0=ALU.mult, op1=ALU.add),
             reads=[r_tab], writes=[r_tab])
        PI_S = 3.1415925
        P.op("dve", lambda e: e.tensor_scalar(out=ang[:], in0=ang[:], scalar1=PI_S, scalar2=-PI_S, op0=ALU.min, op1=ALU.max),
             reads=[r_tab], writes=[r_tab])
        P.op("act", lambda e: e.activation(out=sinT[:], in_=ang[:], func=AF.Sin), reads=[r_tab], writes=[r_tab])
        P.op("act", lambda e: e.activation(out=nn[:], in_=ang[:], func=AF.Abs), reads=[r_tab], writes=[r_tab])
        P.op("dve", lambda e: e.tensor_scalar(out=nn[:], in0=nn[:], scalar1=-1.0, scalar2=float(np.pi / 2), op0=ALU.mult, op1=ALU.add),
             reads=[r_tab], writes=[r_tab])
        P.op("act", lambda e: e.activation(out=cosT[:], in_=nn[:], func=AF.Sin), reads=[r_tab], writes=[r_tab])

        wuq = sb("wuq", [128, 3, 1536], BF16)
        wukv = sb("wukv", [128, 2, 2048], BF16)
        r_w = Reg()
        P.dma("pool", wuq[:], A["mla_w_uq"][0].rearrange("(k p) c -> p k c", p=128), writes=[r_w])
        P.dma("pool", wukv[:], A["mla_w_ukv"][0].rearrange("(k p) c -> p k c", p=128), writes=[r_w])
        cqnT = sb("cqnT", [128, 3, S], BF16)
        ckvnT = sb("ckvnT", [128, 2, S], BF16)
        kropeT = sb("kropeT", [64, S], BF16)
        r_lat = [Reg() for i in range(NT)]
        r_krT = Reg()
        with contextlib.ExitStack() as st1:
            sb1 = lambda name, shape, dt: st1.enter_context(nc.sbuf_tensor(un(name), shape, dt))
            wi = sb1("wi", [128, NK, 704], BF16)
            r_wi = Reg()
            P.dma("pool", wi[:], A["mla_w_in"][0].rearrange("(k p) c -> p k c", p=128), writes=[r_wi])
            gq = sb1("gq", [128, 384], F32)
            gkv = sb1("gkv", [128, 256], F32)
            r_g = Reg()
            P.dma("sp", gq[:], A["mla_q_norm_g"][0].partition_broadcast(128), writes=[r_g])
            P.dma("sp", gkv[:], A["mla_kv_norm_g"][0].partition_broadcast(128), writes=[r_g])
            zs = [sb1("zs", [128, 704], F32) for j in range(2)]
            r_zs = [Reg(), Reg()]
            junk = sb1("junk", [128, 384], F32)
            r_junk = Reg()
            ms = [sb1("ms", [128, 4], F32) for j in range(2)]
            r_ms = [Reg(), Reg()]
            cn = [sb1("cn", [128, 640], BF16) for j in range(2)]
            r_cn = [Reg(), Reg()]
            krz = sb1("krz", [128, NT, 64], F32)
            r_krz = Reg()
            krb = sb1("krb", [128, NT, 64], BF16)
            r_krb = Reg()
            rt1 = sb1("rt1", [128, NT, 32], F32)
            rt2 = sb1("rt2", [128, NT, 32], F32)
            r_rt = Reg()
            for i in range(NT):
                j2 = i % 2
                pa = 0 + 2 * j2
                pbk = 1 + 2 * j2
                for k in range(NK):
                    P.op("pe", lambda e, k=k: e.matmul(cx.ps[pa][:, :], lhsT=XT[:, k, i * 128:(i + 1) * 128], rhs=wi[:, k, 0:512],
                                                       start=(k == 0), stop=(k == NK - 1)), reads=[cx.rXA[i], r_wi], writes=[cx.rps[pa]])
                for k in range(NK):
                    P.op("pe", lambda e, k=k: e.matmul(cx.ps[pbk][:, 0:192], lhsT=XT[:, k, i * 128:(i + 1) * 128], rhs=wi[:, k, 512:704],
                                                       start=(k == 0), stop=(k == NK - 1)), reads=[cx.rXA[i], r_wi], writes=[cx.rps[pbk]])
                z = zs[j2]
                P.op("act", lambda e: e.activation(out=z[:, 0:512], in_=cx.ps[pa][:, :], func=AF.Copy), reads=[cx.rps[pa]], writes=[r_zs[j2]])
                P.op("dve", lambda e: e.tensor_copy(out=z[:, 512:704], in_=cx.ps[pbk][:, 0:192]), reads=[cx.rps[pbk]], writes=[r_zs[j2]])
                m = ms[j2]
                P.op("act", lambda e: e.activation(out=junk[:, 0:384], in_=z[:, 0:384], func=AF.Square, accum_out=m[:, 0:1]),
                     reads=[r_zs[j2]], writes=[r_junk, r_ms[j2]])
                P.op("act", lambda e: e.activation(out=junk[:, 0:256], in_=z[:, 384:640], func=AF.Square, accum_out=m[:, 1:2]),
                     reads=[r_zs[j2]], writes=[r_junk, r_ms[j2]])
                P.op("dve", lambda e: e.tensor_scalar(out=m[:, 0:1], in0=m[:, 0:1], scalar1=1.0 / 384.0, scalar2=LN_EPS, op0=ALU.mult, op1=ALU.add),
                     reads=[r_ms[j2]], writes=[r_ms[j2]])
                P.op("dve", lambda e: e.tensor_scalar(out=m[:, 1:2], in0=m[:, 1:2], scalar1=1.0 / 256.0, scalar2=LN_EPS, op0=ALU.mult, op1=ALU.add),
                     reads=[r_ms[j2]], writes=[r_ms[j2]])
                P.op("act", lambda e: e.activation(out=m[:, 0:2], in_=m[:, 0:2], func=AF.Sqrt), reads=[r_ms[j2]], writes=[r_ms[j2]])
                P.op("dve", lambda e: e.reciprocal(out=m[:, 2:4], in_=m[:, 0:2]), reads=[r_ms[j2]], writes=[r_ms[j2]])
                c_ = cn[j2]
                P.op("dve", lambda e: e.scalar_tensor_tensor(out=c_[:, 0:384], in0=z[:, 0:384], scalar=m[:, 2:3], in1=gq[:], op0=ALU.mult, op1=ALU.mult),
                     reads=[r_zs[j2], r_ms[j2], r_g], writes=[r_cn[j2]])
                P.op("dve", lambda e: e.scalar_tensor_tensor(out=c_[:, 384:640], in0=z[:, 384:640], scalar=m[:, 3:4], in1=gkv[:], op0=ALU.mult, op1=ALU.mult),
                     reads=[r_zs[j2], r_ms[j2], r_g], writes=[r_cn[j2]])
                P.op("pool", lambda e: e.tensor_copy(out=krz[:, i, :], in_=z[:, 640:704]), reads=[r_zs[j2]], writes=[r_krz])
                pt = 6 + j2
                pst = cx.ps[pt][:].bitcast(BF16)
                for k in range(5):
                    P.op("pe", lambda e, k=k: e.transpose(out=pst[:, k * 128:(k + 1) * 128], in_=c_[:, k * 128:(k + 1) * 128], identity=cx.identb[:]),
                         reads=[r_cn[j2], cx.rcst], writes=[cx.rps[pt]])
                P.op("act", lambda e: e.activation(out=cqnT[:, :, i * 128:(i + 1) * 128], in_=pst[:, 0:384].rearrange("p (k t) -> p k t", k=3), func=AF.Copy),
                     reads=[cx.rps[pt]], writes=[r_lat[i]])
                P.op("dve", lambda e: e.tensor_copy(out=ckvnT[:, :, i * 128:(i + 1) * 128], in_=pst[:, 384:640].rearrange("p (k t) -> p k t", k=2)),
                     reads=[cx.rps[pt]], writes=[r_lat[i]])
            rope_apply(cx, krb[:], krz[:], cosT[:], sinT[:], rt1[:], rt2[:], NT, r_krz, r_tab, r_rt, r_krb)
            for g in range(2):
                pst = cx.ps[4 + g][:].bitcast(BF16)
                for j in range(8):
                    i = g * 8 + j
                    P.op("pe", lambda e, i=i, j=j: e.transpose(out=pst[0:64, j * 128:(j + 1) * 128], in_=krb[:, i, :], identity=cx.identb[:]),
                         reads=[r_krb, cx.rcst], writes=[cx.rps[4 + g]])
                P.op("act", lambda e, g=g: e.activation(out=kropeT[:, g * 1024:(g + 1) * 1024], in_=pst[0:64, :], func=AF.Copy),
                     reads=[cx.rps[4 + g]], writes=[r_krT])
        P.barrier()
        qnT = sb("qnT", [128, S], BF16)
        knT = sb("knT", [128, S], BF16)
        qrT = sb("qrT", [64, S], BF16)
        vh = sb("vh", [128, NT, 128], BF16)
        r_qnT, r_knT, r_qrT, r_vh = Reg(), Reg(), Reg(), Reg()
        qrb = sb("qrb", [128, 8, 64], BF16)
        r_qrb = Reg()
        rt1 = sb("rt1b", [128, 8, 32], F32)
        rt2 = sb("rt2b", [128, 8, 32], F32)
        r_rt = Reg()
        Pb = sb("Pb", [128, S], BF16)
        r_Pb = Reg()
        PTs = sb("PTs", [128, NT, 128], BF16)
        r_PTs = Reg()
        oT = sb("oT", [128, 128], BF16)
        r_oT = Reg()
        sm = [sb("smx", [128, 8], F32) for j in range(2)]
        r_sm = [Reg(), Reg()]
        wo = sb("wo", [128, D], BF16)
        r_wo = Reg()
        all_lat = r_lat
        for h in range(MLA_H):
            P.dma("pool", wo[:], A["mla_w_out"][0][h * 128:(h + 1) * 128, :], writes=[r_wo])
            for tb in range(4):
                for k in range(3):
                    P.op("pe", lambda e, tb=tb, k=k: e.matmul(cx.ps[tb][:, :], lhsT=wuq[:, k, h * 192:h * 192 + 128],
                                                              rhs=cqnT[:, k, tb * 512:(tb + 1) * 512], start=(k == 0), stop=(k == 2)),
                         reads=[r_w] + all_lat[tb * 4:(tb + 1) * 4], writes=[cx.rps[tb]])
                P.op("act", lambda e, tb=tb: e.activation(out=qnT[:, tb * 512:(tb + 1) * 512], in_=cx.ps[tb][:, :], func=AF.Copy),
                     reads=[cx.rps[tb]], writes=[r_qnT])
            for tb in range(4):
                for k in range(2):
                    P.op("pe", lambda e, tb=tb, k=k: e.matmul(cx.ps[4 + tb][:, :], lhsT=wukv[:, k, h * 256:h * 256 + 128],
                                                              rhs=ckvnT[:, k, tb * 512:(tb + 1) * 512], start=(k == 0), stop=(k == 1)),
                         reads=[r_w] + all_lat[tb * 4:(tb + 1) * 4], writes=[cx.rps[4 + tb]])
                P.op("dve", lambda e, tb=tb: e.tensor_copy(out=knT[:, tb * 512:(tb + 1) * 512], in_=cx.ps[4 + tb][:, :]),
                     reads=[cx.rps[4 + tb]], writes=[r_knT])
            for g in range(4):
                for j in range(4):
                    i = g * 4 + j
                    for k in range(2):
                        P.op("pe", lambda e, i=i, j=j, k=k, g=g: e.matmul(cx.ps[g][:, j * 128:(j + 1) * 128], lhsT=ckvnT[:, k, i * 128:(i + 1) * 128],
                                                                          rhs=wukv[:, k, h * 256 + 128:h * 256 + 256], start=(k == 0), stop=(k == 1)),
                             reads=[r_w, all_lat[i]], writes=[cx.rps[g]])
                P.op("act", lambda e, g=g: e.activation(out=vh[:, g * 4:(g + 1) * 4, :], in_=cx.ps[g][:, :].rearrange("p (a b) -> p a b", a=4), func=AF.Copy),
                     reads=[cx.rps[g]], writes=[r_vh])
            for g in range(2):
                pb = 4 + g
                for j in range(8):
                    i = g * 8 + j
                    for k in range(3):
                        P.op("pe", lambda e, i=i, j=j, k=k, pb=pb: e.matmul(cx.ps[pb][:, j * 64:(j + 1) * 64], lhsT=cqnT[:, k, i * 128:(i + 1) * 128],
                                                                            rhs=wuq[:, k, h * 192 + 128:h * 192 + 192], start=(k == 0), stop=(k == 2)),
                             reads=[r_w, all_lat[i]], writes=[cx.rps[pb]])
                rope_apply(cx, qrb[:], cx.ps[pb][:, :].rearrange("p (a b) -> p a b", a=8), cosT[:, g * 8:(g + 1) * 8, :], sinT[:, g * 8:(g + 1) * 8, :],
                           rt1[:], rt2[:], 8, cx.rps[pb], r_tab, r_rt, r_qrb)
                pst = cx.ps[6 + g][:].bitcast(BF16)
                for j in range(8):
                    P.op("pe", lambda e, j=j: e.transpose(out=pst[0:64, j * 128:(j + 1) * 128], in_=qrb[:, j, :], identity=cx.identb[:]),
                         reads=[r_qrb, cx.rcst], writes=[cx.rps[6 + g]])
                P.op("act", lambda e, g=g: e.activation(out=qrT[:, g * 1024:(g + 1) * 1024], in_=pst[0:64, :], func=AF.Copy),
                     reads=[cx.rps[6 + g]], writes=[r_qrT])
            def emit_S(qb):
                base = 4 * (qb % 2)
                qs = slice(qb * 128, (qb + 1) * 128)
                for tb in range(4):
                    P.op("pe", lambda e, tb=tb: e.matmul(cx.ps[base + tb][:, :], lhsT=qnT[:, qs], rhs=knT[:, tb * 512:(tb + 1) * 512], start=True, stop=False),
                         reads=[r_qnT, r_knT], writes=[cx.rps[base + tb]])
                    P.op("pe", lambda e, tb=tb: e.matmul(cx.ps[base + tb][:, :], lhsT=qrT[:, qs], rhs=kropeT[:, tb * 512:(tb + 1) * 512], start=False, stop=True),
                         reads=[r_qrT, r_krT], writes=[cx.rps[base + tb]])

            def emit_softmax(qb):
                base = 4 * (qb % 2)
                m = sm[qb % 2]
                r_m = r_sm[qb % 2]
                for tb in range(4):
                    P.op("dve", lambda e, tb=tb: e.tensor_reduce(out=m[:, tb:tb + 1], in_=cx.ps[base + tb][:, :], axis=AX.X, op=ALU.max),
                         reads=[cx.rps[base + tb]], writes=[r_m])
                P.op("dve", lambda e: e.tensor_reduce(out=m[:, 0:1], in_=m[:, 0:4], axis=AX.X, op=ALU.max), reads=[r_m], writes=[r_m])
                P.op("dve", lambda e: e.tensor_scalar(out=m[:, 1:2], in0=m[:, 0:1], scalar1=-MLA_SCALE, scalar2=None, op0=ALU.mult), reads=[r_m], writes=[r_m])
                for tb in range(4):
                    P.op("act", lambda e, tb=tb: e.activation(out=Pb[:, tb * 512:(tb + 1) * 512], in_=cx.ps[base + tb][:, :], func=AF.Exp, scale=MLA_SCALE,
                                                              bias=m[:, 1:2], accum_out=m[:, 4 + tb:5 + tb]), reads=[cx.rps[base + tb], r_m], writes=[r_Pb, r_m])
                P.op("dve", lambda e: e.tensor_reduce(out=m[:, 2:3], in_=m[:, 4:8], axis=AX.X, op=ALU.add), reads=[r_m], writes=[r_m])
                P.op("dve", lambda e: e.reciprocal(out=m[:, 3:4], in_=m[:, 2:3]), reads=[r_m], writes=[r_m])

            def emit_rest(qb):
                base = 4 * (qb % 2)
                m = sm[qb % 2]
                r_m = r_sm[qb % 2]
                for g in range(2):
                    pst = cx.ps[base + g][:].bitcast(BF16)
                    for j in range(8):
                        c = g * 8 + j
                        P.op("pe", lambda e, c=c, j=j: e.transpose(out=pst[:, j * 128:(j + 1) * 128], in_=Pb[:, c * 128:(c + 1) * 128], identity=cx.identb[:]),
                             reads=[r_Pb, cx.rcst], writes=[cx.rps[base + g]])
                    P.op("act", lambda e, g=g: e.activation(out=PTs[:, g * 8:(g + 1) * 8, :], in_=pst.rearrange("p (a b) -> p a b", a=8), func=AF.Copy),
                         reads=[cx.rps[base + g]], writes=[r_PTs])
                for c in range(NT):
                    P.op("pe", lambda e, c=c: e.matmul(cx.ps[base + 2][:, 0:128], lhsT=vh[:, c, :], rhs=PTs[:, c, :], start=(c == 0), stop=(c == NT - 1)),
                         reads=[r_vh, r_PTs], writes=[cx.rps[base + 2]])
                P.op("act", lambda e: e.activation(out=oT[:], in_=cx.ps[base + 2][:, 0:128], func=AF.Copy), reads=[cx.rps[base + 2]], writes=[r_oT])
                for hh in range(2):
                    pb = base + 3 - hh
                    P.op("pe", lambda e, hh=hh, pb=pb: e.matmul(cx.ps[pb][:, :], lhsT=oT[:], rhs=wo[:, hh * 512:(hh + 1) * 512], start=True, stop=True),
                         reads=[r_oT, r_wo], writes=[cx.rps[pb]])
                    P.op("dve", lambda e, hh=hh, pb=pb: e.scalar_tensor_tensor(out=cx.R[:, qb, hh * 512:(hh + 1) * 512], in0=cx.ps[pb][:, :], scalar=m[:, 3:4],
                                                                               in1=cx.R[:, qb, hh * 512:(hh + 1) * 512], op0=ALU.mult, op1=ALU.add),
                         reads=[cx.rps[pb], r_m, cx.rR[qb]], writes=[cx.rR[qb]])

            emit_S(0)
            for qb in range(NT):
                if qb + 1 < NT:
                    emit_S(qb + 1)
                emit_softmax(qb)
                emit_rest(qb)
    P.barrier()


LN_QSCALE = float(np.log(128.0 ** -0.5))


def emit_linattn(cx, kind):
    P = cx.P
    nc = cx.nc
    A = cx.A
    XT = xt_view(cx)
    ml = (kind == "mlstm")
    pre = "mlstm" if ml else "gla"
    W_in = A[pre + "_w_in"][0]
    DV1 = 257 if ml else 256
    cs = 1.0 if ml else 1.0 / 16.0
    with contextlib.ExitStack() as st:
        sb = lambda name, shape, dt: st.enter_context(nc.sbuf_tensor(un(name), shape, dt))
        ng = sb("ng", [128, 256], F32)
        r_ng = Reg()
        r_gw = Reg()
        if ml:
            wgates = sb("wgates", [128, NK, 16], BF16)
            P.dma("pool", wgates[:], W_in[:, 3072:3088].rearrange("(k p) c -> p k c", p=128), writes=[r_gw])
            gbias = sb("gbias", [128, 16], F32)
            P.dma("sp", gbias[:], A["mlstm_gate_b"][0].rearrange("a b c -> (a b c)").partition_broadcast(128), writes=[r_gw])
            ngbias = sb("ngbias", [128, 16], F32)
            P.op("dve", lambda e: e.tensor_scalar(out=ngbias[:], in0=gbias[:], scalar1=-1.0, scalar2=None, op0=ALU.mult), reads=[r_gw], writes=[r_gw])
            wrep = [sb("wrep", [128, NK, 128], BF16) for j in range(2)]
            r_wrep = [Reg(), Reg()]
            igb = sb("igb", [128, S], F32)
            r_igb = Reg()
        else:
            wglr = sb("wglr", [128, NK, 32], BF16)
            P.dma("pool", wglr[:], W_in[:, 3072:3104].rearrange("(k p) c -> p k c", p=128), writes=[r_gw])
            gwpad = sb("gwpad", [32, 2, 512], BF16)
            P.op("pool", lambda e: e.memset(gwpad[:], 0.0), writes=[r_gw])
            for d in range(2):
                P.dma("pool", gwpad[d * 16:(d + 1) * 16, d, :], A["gla_gate_w"][0, d], writes=[r_gw])
            ngb = sb("ngb", [128, 2, 4], F32)
            for d in range(2):
                small_load(cx, ngb[:, d, :], A["gla_gate_b"][0, d].rearrange("(h p) -> p h", p=128), r_gw)
            P.op("dve", lambda e: e.tensor_scalar(out=ngb[:], in0=ngb[:], scalar1=-1.0, scalar2=None, op0=ALU.mult), reads=[r_gw], writes=[r_gw])
            glrT = sb("glrT", [32, S], BF16)
            r_glrT = Reg()
            for tb in range(4):
                for k in range(NK):
                    P.op("pe", lambda e, tb=tb, k=k: e.matmul(cx.ps[tb][0:32, :], lhsT=wglr[:, k, :], rhs=XT[:, k, tb * 512:(tb + 1) * 512],
                                                              start=(k == 0), stop=(k == NK - 1)), reads=[r_gw] + cx.rXA[tb * 4:(tb + 1) * 4], writes=[cx.rps[tb]])
                P.op("act", lambda e, tb=tb: e.activation(out=glrT[:, tb * 512:(tb + 1) * 512], in_=cx.ps[tb][0:32, :], func=AF.Copy),
                     reads=[cx.rps[tb]], writes=[r_glrT])

        WA = sb("WA", [128, NK, 256], BF16)
        WB = sb("WB", [128, NK, 256], BF16)
        r_WA, r_WB = Reg(), Reg()
        wq = WA[:, :, 0:128]
        wk = WA[:, :, 128:256]
        wo = WA[:].rearrange("p k c -> p (k c)").rearrange("p (j f) -> p j f", j=2)
        wv = WB
        wor = WB
        r_wq, r_wk, r_wv, r_wor, r_wo = r_WA, r_WA, r_WB, r_WB, r_WA
        qT = sb("qT", [128, S], F32)
        kT = sb("kT", [128, S], F32)
        r_qT, r_kT = Reg(), Reg()
        vb = sb("vb", [128, NT, DV1], BF16)
        r_vb = Reg()
        buf1 = sb("buf1", [128, S], F32)
        buf2 = sb("buf2", [128, S], F32)
        r_b1, r_b2 = Reg(), Reg()
        qe = sb("qe", [128, S], BF16)
        ke = sb("ke", [128, S], BF16)
        kgT = sb("kgT", [128, NT, 128], BF16)
        r_qe, r_ke, r_kgT = Reg(), Reg(), Reg()
        Oacc = sb("Oacc", [128, NT, 256], F32)
        r_O = [Reg() for i in range(NT)]
        offs = sb("offs", [128, NT], F32)
        gsv = sb("gsv", [128, NT], F32)
        eg = sb("eg", [128, NT], F32)
        r_small = Reg()
        Sst = sb("Sst", [128, DV1], F32)
        Sb = sb("Sb", [128, DV1], BF16)
        r_S, r_Sb = Reg(), Reg()
        atm = [sb("atm", [128, 128], BF16) for j in range(2)]
        r_atm = [Reg(), Reg()]
        dn = [sb("dn", [128, 4], F32) for j in range(2)]
        r_dn = [Reg(), Reg()]
        hsall = sb("hsall", [128, NT, 10], F32)
        r_hsall = Reg()
        gact = [sb("gact", [128, 256], F32) for j in range(2)]
        r_gact = [Reg(), Reg()]
        xn = [sb("xn", [128, 256], F32) for j in range(2)]
        r_xn = [Reg(), Reg()]
        gbt = [sb("gbt", [128, 256], BF16) for j in range(2)]
        r_gbt = [Reg(), Reg()]
        gTt = [sb("gTt", [128, 2, 128], BF16) for j in range(2)]
        r_gTt = [Reg(), Reg()]
        P.op("pool", lambda e: e.memset(offs[:, 0:1], 0.0), writes=[r_small])
        if ml:
            P.op("pool", lambda e: e.memset(vb[:, :, 256:257], 1.0), writes=[r_vb])

        b1v = buf1[:].rearrange("p (c j) -> p c j", j=128)
        b2v = buf2[:].rearrange("p (c j) -> p c j", j=128)
        allXA = cx.rXA

        def wload(dst, reg, c0, c1):
            P.dma("pool", dst, W_in[:, c0:c1].rearrange("(k p) c -> p k c", p=128), writes=[reg])

        for h in range(4):
            wload(wq, r_wq, h * 128, (h + 1) * 128)
            wload(wk, r_wk, 512 + h * 128, 512 + (h + 1) * 128)
            wload(wv[:], r_wv, 1024 + h * 256, 1024 + (h + 1) * 256)
            P.dma("sp", ng[:], A[pre + "_norm_g"][0][h * 256:(h + 1) * 256].partition_broadcast(128), writes=[r_ng])
            for (wsrc, r_w, dst, r_dst, base) in ((wq, r_wq, qT, r_qT, 0), (wk, r_wk, kT, r_kT, 4)):
                for tb in range(4):
                    pb = base + tb
                    for k in range(NK):
                        P.op("pe", lambda e, k=k, pb=pb, tb=tb, wsrc=wsrc: e.matmul(cx.ps[pb][:, :], lhsT=wsrc[:, k, :], rhs=XT[:, k, tb * 512:(tb + 1) * 512],
                                                                                    start=(k == 0), stop=(k == NK - 1)),
                             reads=[r_w] + allXA[tb * 4:(tb + 1) * 4], writes=[cx.rps[pb]])
                    eng = "act" if base == 0 else "dve"
                    if eng == "act":
                        P.op("act", lambda e, pb=pb, tb=tb, dst=dst: e.activation(out=dst[:, tb * 512:(tb + 1) * 512], in_=cx.ps[pb][:, :], func=AF.Copy),
                             reads=[cx.rps[pb]], writes=[r_dst])
                    else:
                        P.op("dve", lambda e, pb=pb, tb=tb, dst=dst: e.tensor_copy(out=dst[:, tb * 512:(tb + 1) * 512], in_=cx.ps[pb][:, :]),
                             reads=[cx.rps[pb]], writes=[r_dst])
            for i in range(NT):
                pb = (i // 2) % 8
                off = (i % 2) * 256
                for k in range(NK):
                    P.op("pe", lambda e, k=k, pb=pb, off=off, i=i: e.matmul(cx.ps[pb][:, off:off + 256], lhsT=XT[:, k, i * 128:(i + 1) * 128], rhs=wv[:, k, :],
                                                                            start=(k == 0), stop=(k == NK - 1)), reads=[r_wv, allXA[i]], writes=[cx.rps[pb]])
                P.op("act", lambda e, pb=pb, off=off, i=i: e.activation(out=vb[:, i, 0:256], in_=cx.ps[pb][:, off:off + 256], func=AF.Copy),
                     reads=[cx.rps[pb]], writes=[r_vb])
            wload(wor[:], r_wor, 2048 + h * 256, 2048 + (h + 1) * 256)
            P.dma("pool", wo, A[pre + "_w_out"][0][h * 256:(h + 1) * 256, :].rearrange("(j p) f -> p j f", p=128), writes=[r_wo])
            for d in range(2):
                if ml:
                    jf = d * 8 + 4 + h
                    ji = d * 8 + h
                    for jj, gidx in enumerate((jf, ji)):
                        P.op("dve", lambda e, jj=jj, gidx=gidx: e.tensor_copy(out=wrep[jj][:], in_=wgates[:, :, gidx:gidx + 1].to_broadcast([128, NK, 128])),
                             reads=[r_gw], writes=[r_wrep[jj]])
                    for tb in range(4):
                        for k in range(NK):
                            P.op("pe", lambda e, k=k, tb=tb: e.matmul(cx.ps[tb][:, :], lhsT=wrep[0][:, k, :], rhs=XT[:, k, tb * 512:(tb + 1) * 512],
                                                                      start=(k == 0), stop=(k == NK - 1)), reads=[r_wrep[0]] + allXA[tb * 4:(tb + 1) * 4], writes=[cx.rps[tb]])
                        P.op("act", lambda e, tb=tb: e.activation(out=buf1[:, tb * 512:(tb + 1) * 512], in_=cx.ps[tb][:, :], func=AF.Exp, scale=-1.0,
                                                                  bias=ngbias[:, jf:jf + 1]), reads=[cx.rps[tb], r_gw], writes=[r_b1])
                    for tb in range(4):
                        for k in range(NK):
                            P.op("pe", lambda e, k=k, tb=tb: e.matmul(cx.ps[4 + tb][:, :], lhsT=wrep[1][:, k, :], rhs=XT[:, k, tb * 512:(tb + 1) * 512],
                                                                      start=(k == 0), stop=(k == NK - 1)), reads=[r_wrep[1]] + allXA[tb * 4:(tb + 1) * 4], writes=[cx.rps[4 + tb]])
                        P.op("act", lambda e, tb=tb: e.activation(out=igb[:, tb * 512:(tb + 1) * 512], in_=cx.ps[4 + tb][:, :], func=AF.Identity,
                                                                  bias=gbias[:, ji:ji + 1]), reads=[cx.rps[4 + tb], r_gw], writes=[r_igb])
                else:
                    for tb in range(4):
                        P.op("pe", lambda e, tb=tb: e.matmul(cx.ps[tb][:, :], lhsT=gwpad[:, d, h * 128:(h + 1) * 128], rhs=glrT[:, tb * 512:(tb + 1) * 512],
                                                             start=True, stop=True), reads=[r_gw, r_glrT], writes=[cx.rps[tb]])
                        P.op("act", lambda e, tb=tb: e.activation(out=buf1[:, tb * 512:(tb + 1) * 512], in_=cx.ps[tb][:, :], func=AF.Exp, scale=-1.0,
                                                                  bias=ngb[:, d, h:h + 1]), reads=[cx.rps[tb], r_gw], writes=[r_b1])
                P.op("act", lambda e: e.activation(out=buf1[:], in_=buf1[:], func=AF.Ln, bias=1.0), reads=[r_b1], writes=[r_b1])
                P.op("dve", lambda e: e.tensor_tensor_scan(out=buf2[:], data0=cx.cst[:, C_ONES:C_ONES + 1].to_broadcast([128, S]), data1=buf1[:], initial=0.0,
                                                            op0=ALU.mult, op1=ALU.add), reads=[r_b1, cx.rcst], writes=[r_b2])
                P.op("dve", lambda e: e.tensor_copy(out=offs[:, 1:NT], in_=b2v[:, 0:NT - 1, 127]), reads=[r_b2], writes=[r_small])
                P.op("dve", lambda e: e.tensor_tensor(out=b2v, in0=b2v, in1=offs[:, :, None].to_broadcast([128, NT, 128]), op=ALU.subtract),
                     reads=[r_b2, r_small], writes=[r_b2])
                P.op("dve", lambda e: e.tensor_copy(out=gsv[:], in_=b2v[:, :, 127]), reads=[r_b2], writes=[r_small])
                P.op("act", lambda e: e.activation(out=eg[:], in_=gsv[:], func=AF.Exp, scale=-cs), reads=[r_small], writes=[r_small])
                gs_bc = gsv[:, :, None].to_broadcast([128, NT, 128])
                if d == 1:
                    P.op("dve", lambda e: e.scalar_tensor_tensor(out=buf2[:], in0=buf2[:], scalar=-1.0, in1=buf1[:], op0=ALU.mult, op1=ALU.add),
                         reads=[r_b1, r_b2], writes=[r_b2])
                    P.op("dve", lambda e: e.tensor_tensor(out=b2v, in0=b2v, in1=gs_bc, op=ALU.add), reads=[r_b2, r_small], writes=[r_b2])
                P.op("dve", lambda e: e.scalar_tensor_tensor(out=b1v, in0=b2v, scalar=-1.0, in1=gs_bc, op0=ALU.mult, op1=ALU.add),
                     reads=[r_b2, r_small], writes=[r_b1])
                if ml:
                    P.op("dve", lambda e: e.scalar_tensor_tensor(out=buf1[:], in0=buf1[:], scalar=-cs, in1=igb[:], op0=ALU.mult, op1=ALU.add),
                         reads=[r_b1, r_igb], writes=[r_b1])
                    P.op("act", lambda e: e.activation(out=buf1[:], in_=buf1[:], func=AF.Exp), reads=[r_b1], writes=[r_b1])
                else:
                    P.op("act", lambda e: e.activation(out=buf1[:], in_=buf1[:], func=AF.Exp, scale=-cs), reads=[r_b1], writes=[r_b1])
                P.op("dve", lambda e: e.tensor_tensor(out=ke[:], in0=kT[:], in1=buf1[:], op=ALU.mult), reads=[r_kT, r_b1], writes=[r_ke])
                for g in range(2):
                    pst = cx.ps[6 + g][:].bitcast(BF16)
                    for j in range(8):
                        c = g * 8 + j
                        P.op("pe", lambda e, c=c, j=j: e.transpose(out=pst[:, j * 128:(j + 1) * 128], in_=ke[:, c * 128:(c + 1) * 128], identity=cx.identb[:]),
                             reads=[r_ke, cx.rcst], writes=[cx.rps[6 + g]])
                    P.op("act", lambda e, g=g: e.activation(out=kgT[:, g * 8:(g + 1) * 8, :], in_=pst.rearrange("p (a b) -> p a b", a=8), func=AF.Copy),
                         reads=[cx.rps[6 + g]], writes=[r_kgT])
                if ml:
                    P.op("dve", lambda e: e.scalar_tensor_tensor(out=buf1[:], in0=buf2[:], scalar=cs, in1=igb[:], op0=ALU.mult, op1=ALU.add),
                         reads=[r_b2, r_igb], writes=[r_b1])
                    P.op("act", lambda e: e.activation(out=buf1[:], in_=buf1[:], func=AF.Exp), reads=[r_b1], writes=[r_b1])
                else:
                    P.op("act", lambda e: e.activation(out=buf1[:], in_=buf2[:], func=AF.Exp, scale=cs), reads=[r_b2], writes=[r_b1])
                P.op("dve", lambda e: e.tensor_tensor(out=ke[:], in0=kT[:], in1=buf1[:], op=ALU.mult), reads=[r_kT, r_b1], writes=[r_ke])
                P.op("act", lambda e: e.activation(out=buf2[:], in_=buf2[:], func=AF.Exp, scale=-cs, bias=cx.cst[:, C_LNQ:C_LNQ + 1]), reads=[r_b2, cx.rcst], writes=[r_b2])
                P.op("dve", lambda e: e.tensor_tensor(out=qe[:], in0=qT[:], in1=buf2[:], op=ALU.mult), reads=[r_qT, r_b2], writes=[r_qe])
                order = list(range(NT)) if d == 0 else list(range(NT - 1, -1, -1))
                mcol = C_TRIU if d == 0 else C_TRIL
                for n_, c in enumerate(order):
                    csl = slice(c * 128, (c + 1) * 128)
                    pa = n_ % 2
                    po = 2 + (n_ % 2)
                    pss = 4 + (n_ % 2)
                    first = (n_ == 0)
                    last = (n_ == NT - 1)
                    P.op("pe", lambda e: e.matmul(cx.ps[pa][:, 0:128], lhsT=ke[:, csl], rhs=qe[:, csl], start=True, stop=True),
                         reads=[r_ke, r_qe], writes=[cx.rps[pa]])
                    am = atm[n_ % 2]
                    P.op("dve", lambda e: e.tensor_tensor(out=am[:], in0=cx.cst[:, mcol:mcol + 128], in1=cx.ps[pa][:, 0:128], op=ALU.mult),
                         reads=[cx.rps[pa], cx.rcst], writes=[r_atm[n_ % 2]])
                    P.op("pe", lambda e: e.matmul(cx.ps[po][:, 0:DV1], lhsT=am[:], rhs=vb[:, c, :], start=True, stop=first),
                         reads=[r_atm[n_ % 2], r_vb], writes=[cx.rps[po]])
                    if not first:
                        P.op("pe", lambda e: e.matmul(cx.ps[po][:, 0:DV1], lhsT=qe[:, csl], rhs=Sb[:], start=False, stop=True),
                             reads=[r_qe, r_Sb], writes=[cx.rps[po]])
                    if not last:
                        P.op("pe", lambda e: e.matmul(cx.ps[pss][:, 0:DV1], lhsT=kgT[:, c, :], rhs=vb[:, c, :], start=True, stop=True),
                             reads=[r_kgT, r_vb], writes=[cx.rps[pss]])
                        if first:
                            P.op("dve", lambda e: e.tensor_copy(out=Sst[:], in_=cx.ps[pss][:, 0:DV1]), reads=[cx.rps[pss]], writes=[r_S])
                        else:
                            P.op("dve", lambda e: e.scalar_tensor_tensor(out=Sst[:], in0=Sst[:], scalar=eg[:, c:c + 1], in1=cx.ps[pss][:, 0:DV1],
                                                                          op0=ALU.mult, op1=ALU.add), reads=[cx.rps[pss], r_S, r_small], writes=[r_S])
                        P.op("act", lambda e: e.activation(out=Sb[:], in_=Sst[:], func=AF.Copy), reads=[r_S], writes=[r_Sb])
                    oc = Oacc[:, c, :]
                    if ml:
                        dd = dn[n_ % 2]
                        r_dd = r_dn[n_ % 2]
                        P.op("act", lambda e: e.activation(out=dd[:, 0:1], in_=cx.ps[po][:, 256:257], func=AF.Abs), reads=[cx.rps[po]], writes=[r_dd])
                        P.op("dve", lambda e: e.tensor_scalar(out=dd[:, 0:1], in0=dd[:, 0:1], scalar1=1.0, scalar2=None, op0=ALU.max), reads=[r_dd], writes=[r_dd])
                        P.op("dve", lambda e: e.reciprocal(out=dd[:, 1:2], in_=dd[:, 0:1]), reads=[r_dd], writes=[r_dd])
                        if d == 0:
                            P.op("act", lambda e: e.activation(out=oc, in_=cx.ps[po][:, 0:256], func=AF.Copy, scale=dd[:, 1:2]),
                                 reads=[cx.rps[po], r_dd], writes=[r_O[c]])
                        else:
                            P.op("dve", lambda e: e.scalar_tensor_tensor(out=oc, in0=cx.ps[po][:, 0:256], scalar=dd[:, 1:2], in1=oc, op0=ALU.mult, op1=ALU.add),
                                 reads=[cx.rps[po], r_dd, r_O[c]], writes=[r_O[c]])
                    else:
                        if d == 0:
                            P.op("act", lambda e: e.activation(out=oc, in_=cx.ps[po][:, 0:256], func=AF.Copy), reads=[cx.rps[po]], writes=[r_O[c]])
                        else:
                            P.op("dve", lambda e: e.tensor_tensor(out=oc, in0=oc, in1=cx.ps[po][:, 0:256], op=ALU.add), reads=[cx.rps[po], r_O[c]], writes=[r_O[c]])
            for i in range(NT):
                P.op("dve", lambda e, i=i: e.bn_stats(out=hsall[:, i, 0:6], in_=Oacc[:, i, :]), reads=[r_O[i]], writes=[r_hsall])
            for i in range(NT):
                P.op("dve", lambda e, i=i: e.bn_aggr(out=hsall[:, i, 6:8], in_=hsall[:, i, 0:6]), reads=[r_hsall], writes=[r_hsall])
            P.op("dve", lambda e: e.tensor_scalar(out=hsall[:, :, 8], in0=hsall[:, :, 7], scalar1=LN_EPS, scalar2=None, op0=ALU.add), reads=[r_hsall], writes=[r_hsall])
            P.op("act", lambda e: e.activation(out=hsall[:, :, 8], in_=hsall[:, :, 8], func=AF.Sqrt), reads=[r_hsall], writes=[r_hsall])
            P.op("dve", lambda e: e.reciprocal(out=hsall[:, :, 9], in_=hsall[:, :, 8]), reads=[r_hsall], writes=[r_hsall])
            for i in range(NT):
                j2 = i % 2
                hs = hsall[:, i, :]
                r_hs = r_hsall
                oc = Oacc[:, i, :]
                x_ = xn[j2]
                P.op("dve", lambda e: e.tensor_scalar(out=x_[:], in0=oc, scalar1=hs[:, 6:7], scalar2=hs[:, 9:10], op0=ALU.subtract, op1=ALU.mult),
                     reads=[r_O[i], r_hs], writes=[r_xn[j2]])
                P.op("pool", lambda e: e.tensor_tensor(out=x_[:], in0=x_[:], in1=ng[:], op=ALU.mult), reads=[r_xn[j2], r_ng], writes=[r_xn[j2]])
                pg = 6 + j2
                for k in range(NK):
                    P.op("pe", lambda e, k=k: e.matmul(cx.ps[pg][:, 0:256], lhsT=XT[:, k, i * 128:(i + 1) * 128], rhs=wor[:, k, :],
                                                       start=(k == 0), stop=(k == NK - 1)), reads=[r_wor, allXA[i]], writes=[cx.rps[pg]])
                ga = gact[j2]
                P.op("act", lambda e: e.activation(out=ga[:], in_=cx.ps[pg][:, 0:256], func=(AF.Sigmoid if ml else AF.Silu)),
                     reads=[cx.rps[pg]], writes=[r_gact[j2]])
                gb_ = gbt[j2]
                P.op("dve", lambda e: e.tensor_tensor(out=gb_[:], in0=x_[:], in1=ga[:], op=ALU.mult), reads=[r_xn[j2], r_gact[j2]], writes=[r_gbt[j2]])
                pt = 4 + j2
                pst = cx.ps[pt][:].bitcast(BF16)
                for j in range(2):
                    P.op("pe", lambda e, j=j: e.transpose(out=pst[:, j * 128:(j + 1) * 128], in_=gb_[:, j * 128:(j + 1) * 128], identity=cx.identb[:]),
                         reads=[r_gbt[j2], cx.rcst], writes=[cx.rps[pt]])
                gT_ = gTt[j2]
                P.op("act", lambda e: e.activation(out=gT_[:], in_=pst[:, 0:256].rearrange("p (a b) -> p a b", a=2), func=AF.Copy),
                     reads=[cx.rps[pt]], writes=[r_gTt[j2]])
                for hh in range(2):
                    pb = (i * 2 + hh) % 4
                    for j in range(2):
                        P.op("pe", lambda e, j=j, hh=hh, pb=pb: e.matmul(cx.ps[pb][:, :], lhsT=gT_[:, j, :], rhs=wo[:, j, hh * 512:(hh + 1) * 512],
                                                                         start=(j == 0), stop=(j == 1)), reads=[r_gTt[j2], r_wo], writes=[cx.rps[pb]])
                    P.op("dve", lambda e, hh=hh, pb=pb: e.tensor_tensor(out=cx.R[:, i, hh * 512:(hh + 1) * 512], in0=cx.R[:, i, hh * 512:(hh + 1) * 512],
                                                                        in1=cx.ps[pb][:, :], op=ALU.add), reads=[cx.rps[pb], cx.rR[i]], writes=[cx.rR[i]])
    P.barrier()


def declare_inputs(nc, nseq, names_shapes):
    aps = {}
    for name, shape, dt in names_shapes:
        aps[name] = nc.dram_tensor(name, list(shape), dt, kind="ExternalInput").ap()
    return aps


WEIGHT_SPECS = [
    ("mlstm_w_in", (1, 1024, 3088)), ("mlstm_gate_b", (1, 2, 2, 4)), ("mlstm_norm_g", (1, 1024)), ("mlstm_w_out", (1, 1024, 1024)),
    ("gla_w_in", (1, 1024, 3104)), ("gla_gate_w", (1, 2, 16, 512)), ("gla_gate_b", (1, 2, 512)), ("gla_norm_g", (1, 1024)),
    ("gla_w_out", (1, 1024, 1024)),
    ("lru_w_in", (1, 1024, 2048)), ("lru_conv_w", (1, 4, 1024)), ("lru_conv_b", (1, 1024)),
    ("lru_gate_a_w", (1, 2, 4, 256, 256)), ("lru_gate_a_b", (1, 2, 1024)), ("lru_gate_x_w", (1, 2, 4, 256, 256)),
    ("lru_gate_x_b", (1, 2, 1024)), ("lru_lambda", (1, 2, 1024)), ("lru_w_out", (1, 1024, 1024)),
    ("mla_w_in", (1, 1024, 704)), ("mla_q_norm_g", (1, 384)), ("mla_kv_norm_g", (1, 256)), ("mla_w_uq", (1, 384, 1536)),
    ("mla_w_ukv", (1, 256, 2048)), ("mla_w_out", (1, 1024, 1024)),
    ("moe_router", (4, 1024, 16)), ("moe_w_gate", (4, 16, 1024, 1024)), ("moe_w_up", (4, 16, 1024, 1024)),
    ("moe_w_down", (4, 16, 1024, 1024)), ("ln_g", (4, 2, 1024)), ("ln_b", (4, 2, 1024)),
]


def build_program(nseq, stages):
    nc = bass.Bass("TRN2", target_bir_lowering=False)
    specs = [("x", (nseq, S, D), F32), ("positions", (nseq, S), I32)]
    specs += [(n, s, F32) for n, s in WEIGHT_SPECS]
    specs += [("consts", (128, C_END), F32), ("consts16", (16, 16 * 128), F32)]
    A = declare_inputs(nc, nseq, specs)
    out = nc.dram_tensor("out", [nseq, S, D], F32, kind="ExternalOutput").ap()
    xd = nc.dram_tensor("xd_scratch", [S, D], BF16, kind="ExternalOutput").ap()
    with contextlib.ExitStack() as stack:
        P = Prog(nc, stack)
        cx = setup_ctx(nc, stack, P)
        cx.A = A
        cx.xd = xd
        cx.yd = nc.dram_tensor("yd_scratch", [S, D], F32, kind="ExternalOutput").ap()
        cx.r_yd = Reg()
        cx.r_xd = [Reg() for i in range(NT)]
        load_consts(cx, A["consts"], A["consts16"])
        wr = stack.enter_context(nc.sbuf_tensor("wr", [128, NK, NE], BF16))
        rwr = Reg()
        lnscr = None
        for s in range(nseq):
            for stg in stages:
                kind = stg[0]
                if kind == "load":
                    xin = A["x"][s].rearrange("(i p) d -> p i d", p=128)
                    for i in range(NT):
                        P.dma("sp", cx.R[:, i, :], xin[:, i, :], writes=[cx.rR[i]])
                elif kind == "ln":
                    _, L, j, mode = stg
                    if mode == "xb":
                        P.dma("pool", wr[:], A["moe_router"][L].rearrange("(k p) e -> p k e", p=128), writes=[rwr])
                    emit_ln(cx, A["ln_g"][L, j], A["ln_b"][L, j], mode, wr=wr, rwr=rwr, do_ln=True,
                            out_ap=out[s].rearrange("(i p) d -> p i d", p=128), scratch=lnscr)
                elif kind == "prep":
                    _, L, mode = stg
                    if mode == "xb":
                        P.dma("pool", wr[:], A["moe_router"][L].rearrange("(k p) e -> p k e", p=128), writes=[rwr])
                    emit_ln(cx, None, None, mode, wr=wr, rwr=rwr, do_ln=False, scratch=lnscr)
                elif kind == "moe":
                    _, L = stg
                    emit_moe(cx, A["moe_w_gate"][L], A["moe_w_up"][L], A["moe_w_down"][L])
                elif kind == "mla":
                    emit_mla(cx, s)
                elif kind in ("mlstm", "gla"):
                    emit_linattn(cx, kind)
                elif kind == "lru":
                    emit_lru(cx)
                elif kind == "store":
                    oo = out[s].rearrange("(i p) d -> p i d", p=128)
                    for i in range(NT):
                        P.dma("sp", oo[:, i, :], cx.R[:, i, :], reads=[cx.rR[i]])
                else:
                    raise ValueError(kind)
        P.barrier(engines=["sp"])
        print("instructions", P.ninstr, "waits", P.nwaits)
    return nc


MIXERS = ("mlstm", "gla", "lru", "mla")


def full_stages():
    st = [("load",), ("prep", 0, "xt")]
    for L in range(DEPTH):
        st.append((MIXERS[L % 4],))
        st.append(("ln", L, 0, "xb"))
        st.append(("moe", L))
        st.append(("ln", L, 1, "xt" if L < DEPTH - 1 else "none"))
    return st


NSEQ_PER_LAUNCH = 4
_PROG_CACHE = {}


def kernel(**inputs):
    nseq = NSEQ_PER_LAUNCH
    x = np.ascontiguousarray(np.asarray(inputs["x"], dtype=np.float32))
    pos = np.ascontiguousarray(np.asarray(inputs["positions"], dtype=np.int32))
    B = x.shape[0]
    per_core = B // NCORES
    consts = make_consts()
    consts16 = make_consts16()
    weights = {n: np.ascontiguousarray(np.asarray(inputs[n], dtype=np.float32)) for n, _ in WEIGHT_SPECS}
    out = np.empty((B, S, D), np.float32)
    for l0 in range(0, per_core, nseq):
        if nseq not in _PROG_CACHE:
            _PROG_CACHE[nseq] = build_program(nseq, full_stages())
        nc = _PROG_CACHE[nseq]
        in_maps = []
        for c in range(NCORES):
            b0 = c * per_core + l0
            m = {"x": x[b0:b0 + nseq], "positions": pos[b0:b0 + nseq], "consts": consts, "consts16": consts16}
            m.update(weights)
            in_maps.append(m)
        res = run_bass_kernel_spmd(nc, in_maps, core_ids=list(range(NCORES)))
        for c in range(NCORES):
            b0 = c * per_core + l0
            out[b0:b0 + nseq] = np.asarray(res.results[c]["out"]).reshape(nseq, S, D)
    return out
```

```python
import contextlib
import numpy as np
import concourse.bass as bass
import concourse.mybir as mybir
from concourse.bass_utils import run_bass_kernel_spmd

F32 = mybir.dt.float32
BF16 = mybir.dt.bfloat16
I32 = mybir.dt.int32
AF = mybir.ActivationFunctionType
ALU = mybir.AluOpType
AX = mybir.AxisListType

D = 1024
S = 2048
NT = 16
NK = 8
DEPTH = 4
ALPHA = (2 * DEPTH) ** 0.25
LN_EPS = 1e-5
NE = 16
CAP = 256
NCORES = 8
SEQ_PER_CORE = 4

SAME_ENGINE_RAW = True


_UN = [0]


def un(name):
    _UN[0] += 1
    return "%s_%d" % (name, _UN[0])


class Reg:
    __slots__ = ("w", "r", "name")

    def __init__(self, name=""):
        self.w = None
        self.r = {}
        self.name = name


class Prog:
    ENG = ("pe", "act", "dve", "pool", "sp")
    KDMA = 8

    def __init__(self, nc, stack):
        self.nc = nc
        self.eng = {"pe": nc.tensor, "act": nc.scalar, "dve": nc.vector, "pool": nc.gpsimd, "sp": nc.sync}
        self.sem = {}
        for e in self.ENG:
            self.sem[("c", e)] = stack.enter_context(nc.semaphore("s_" + e))
        self.dq = ("sp", "pool", "act")
        for q in self.dq:
            for k in range(self.KDMA):
                self.sem[("d", q, k)] = stack.enter_context(nc.semaphore("d_%s%d" % (q, k)))
        self.cnt = {e: 0 for e in self.ENG}
        self.dcnt = {q: 0 for q in self.dq}
        self.known = {e: {} for e in self.ENG}
        self.nwaits = 0
        self.ninstr = 0

    def _wait(self, eng, key, val):
        if self.known[eng].get(key, 0) >= val:
            return
        self.eng[eng].wait_ge(self.sem[key], val)
        self.known[eng][key] = val
        self.nwaits += 1

    def _deps(self, eng, reads, writes):
        me = ("c", eng)
        for r in reads:
            if r.w is not None:
                key, val = r.w
                if key == me:
                    if eng != "pe" and SAME_ENGINE_RAW:
                        self._wait(eng, key, val)
                else:
                    self._wait(eng, key, val)
        for w in writes:
            if w.w is not None:
                key, val = w.w
                if key != me:
                    self._wait(eng, key, val)
            for key, val in w.r.items():
                if key != me:
                    self._wait(eng, key, val)

    def _update(self, tok, reads, writes):
        key, val = tok
        for r in reads:
            if r.r.get(key, 0) < val:
                r.r[key] = val
        for w in writes:
            w.w = tok
            w.r = {}

    def op(self, eng, fn, reads=(), writes=()):
        self._deps(eng, reads, writes)
        ins = fn(self.eng[eng])
        self.cnt[eng] += 1
        self.ninstr += 1
        tok = (("c", eng), self.cnt[eng])
        ins.then_inc(self.sem[tok[0]], 1)
        self._update(tok, reads, writes)
        return tok

    def dma(self, q, out, in_, reads=(), writes=(), **kw):
        k = self.dcnt[q]
        slot = k % self.KDMA
        key = ("d", q, slot)
        if k >= self.KDMA:
            self._wait(q, key, 16 * (k // self.KDMA))
        self._deps(q, reads, writes)
        ins = self.eng[q].dma_start(out=out, in_=in_, **kw)
        tok = (key, 16 * (k // self.KDMA + 1))
        ins.then_inc(self.sem[key], 16)
        self.dcnt[q] = k + 1
        self.ninstr += 1
        self._update(tok, reads, writes)
        return tok

    def idma_gather(self, out, in_dram, idx_ap, reads=(), writes=()):
        q = "pool"
        k = self.dcnt[q]
        slot = k % self.KDMA
        key = ("d", q, slot)
        if k >= self.KDMA:
            self._wait(q, key, 16 * (k // self.KDMA))
        self._deps(q, reads, writes)
        ins = self.eng[q].indirect_dma_start(out=out, out_offset=None, in_=in_dram,
                                             in_offset=bass.IndirectOffsetOnAxis(ap=idx_ap, axis=0))
        tok = (key, 16 * (k // self.KDMA + 1))
        ins.then_inc(self.sem[key], 16)
        self.dcnt[q] = k + 1
        self.ninstr += 1
        self._update(tok, reads, writes)
        return tok

    def idma_scatter_add(self, out_dram, idx_ap, in_, reads=(), writes=(), after=()):
        q = "pool"
        k = self.dcnt[q]
        slot = k % self.KDMA
        key = ("d", q, slot)
        if k >= self.KDMA:
            self._wait(q, key, 16 * (k // self.KDMA))
        for akey, aval in after:
            self._wait(q, akey, aval)
        self._deps(q, reads, writes)
        ins = self.eng[q].indirect_dma_start(out=out_dram, out_offset=bass.IndirectOffsetOnAxis(ap=idx_ap, axis=0),
                                             in_=in_, in_offset=None, compute_op=ALU.add)
        tok = (key, 16 * (k // self.KDMA + 1))
        ins.then_inc(self.sem[key], 16)
        self.dcnt[q] = k + 1
        self.ninstr += 1
        self._update(tok, reads, writes)
        return tok

    def latest_tokens(self):
        toks = []
        for e in self.ENG:
            if self.cnt[e] > 0:
                toks.append((("c", e), self.cnt[e]))
        for q in self.dq:
            k = self.dcnt[q]
            for slot in range(self.KDMA):
                n = (k - slot + self.KDMA - 1) // self.KDMA
                if n > 0:
                    toks.append((("d", q, slot), 16 * n))
        return toks

    def barrier(self, engines=None):
        toks = self.latest_tokens()
        for e in (engines or self.ENG):
            for key, val in toks:
                if key != ("c", e):
                    self._wait(e, key, val)


C_IDENT = 0
C_IOTA = 128
C_PIDX = 384
C_TRIU = 386
C_TRIL = 514
C_ONES = 642
C_FREQ = 770
C_LNQ = 802
C_END = 804


def make_consts():
    c = np.zeros((128, C_END), np.float32)
    c[:, C_IDENT:C_IDENT + 128] = np.eye(128, dtype=np.float32)
    c[:, C_IOTA:C_IOTA + 256] = np.arange(256, dtype=np.float32)[None, :]
    c[:, C_PIDX] = np.arange(128)
    c[:, C_PIDX + 1] = np.arange(128) + 128
    s = np.arange(128)[:, None]
    t = np.arange(128)[None, :]
    c[:, C_TRIU:C_TRIU + 128] = (s <= t)
    c[:, C_TRIL:C_TRIL + 128] = (s >= t)
    c[:, C_ONES:C_ONES + 128] = 1.0
    c[:, C_LNQ] = np.float32(np.log(128.0 ** -0.5))
    c[:, C_FREQ:C_FREQ + 32] = (10000.0 ** (-np.arange(32, dtype=np.float32) / np.float32(32))).astype(np.float32)[None, :]
    return c


def make_consts16():
    c = np.zeros((16, 16 * 128), np.float32)
    for e in range(16):
        c[e, e * 128:(e + 1) * 128] = 1.0
    return c


class Ctx:
    pass


def setup_ctx(nc, stack, P):
    cx = Ctx()
    cx.nc = nc
    cx.P = P
    sb = lambda name, shape, dt: stack.enter_context(nc.sbuf_tensor(name, shape, dt))
    cx.R = sb("R", [128, NT, D], F32)
    cx.rR = [Reg("R%d" % i) for i in range(NT)]
    cx.XA = sb("XA", [128, NT * D], BF16)
    cx.rXA = [Reg("XA%d" % i) for i in range(NT)]
    cx.cst = sb("cst", [128, C_END], F32)
    cx.rcst = Reg("cst")
    cx.identb = sb("identb", [128, 128], BF16)
    cx.logits = sb("logits", [128, NT, NE], F32)
    cx.rlogits = [Reg("lg%d" % i) for i in range(NT)]
    cx.ps = []
    cx.rps = []
    for b in range(8):
        cx.ps.append(stack.enter_context(nc.psum_tensor("ps%d" % b, [128, 512], F32)))
        cx.rps.append(Reg("ps%d" % b))
    return cx


def load_consts(cx, consts_ap, consts16_ap):
    P = cx.P
    P.dma("sp", cx.cst[:], consts_ap[:, :], writes=[cx.rcst])
    P.op("dve", lambda e: e.tensor_copy(out=cx.identb[:], in_=cx.cst[:, C_IDENT:C_IDENT + 128]),
         reads=[cx.rcst], writes=[cx.rcst])


def emit_ln(cx, lng_ap, lnb_ap, mode, wr=None, rwr=None, do_ln=True, out_ap=None, scratch=None):
    P = cx.P
    nc = cx.nc
    with contextlib.ExitStack() as lst:
        _emit_ln(cx, lst, lng_ap, lnb_ap, mode, wr, rwr, do_ln, out_ap)
    P.barrier()


def _emit_ln(cx, lst, lng_ap, lnb_ap, mode, wr, rwr, do_ln, out_ap):
    P = cx.P
    nc = cx.nc
    cx.lng = lst.enter_context(nc.sbuf_tensor(un("lng"), [128, 2, D], F32))
    cx.rlng = Reg("lng")
    tmpT, rtmpT, stat, rstat, xbt, rxbt = alloc_ln_scratch(cx, lst)
    if do_ln:
        P.dma("sp", cx.lng[:, 0, :], lng_ap.partition_broadcast(128), writes=[cx.rlng])
        P.dma("sp", cx.lng[:, 1, :], lnb_ap.partition_broadcast(128), writes=[cx.rlng])
    for i in range(NT):
        Ri = cx.R[:, i, :]
        rRi = cx.rR[i]
        st = stat[i % 2]
        rst = rstat[i % 2]
        if do_ln:
            for h in range(2):
                P.op("dve", lambda e, h=h: e.bn_stats(out=st[:, h * 6:(h + 1) * 6], in_=Ri[:, h * 512:(h + 1) * 512]),
                     reads=[rRi], writes=[rst])
            P.op("dve", lambda e: e.bn_aggr(out=st[:, 12:14], in_=st[:, 0:12]), reads=[rst], writes=[rst])
            P.op("dve", lambda e: e.tensor_scalar(out=st[:, 14:15], in0=st[:, 13:14], scalar1=LN_EPS, scalar2=None, op0=ALU.add),
                 reads=[rst], writes=[rst])
            P.op("act", lambda e: e.activation(out=st[:, 14:15], in_=st[:, 14:15], func=AF.Sqrt), reads=[rst], writes=[rst])
            P.op("dve", lambda e: e.reciprocal(out=st[:, 15:16], in_=st[:, 14:15]), reads=[rst], writes=[rst])
            P.op("dve", lambda e: e.tensor_scalar(out=Ri, in0=Ri, scalar1=st[:, 12:13], scalar2=st[:, 15:16],
                                                   op0=ALU.subtract, op1=ALU.mult), reads=[rRi, rst], writes=[rRi])
            P.op("dve", lambda e: e.tensor_tensor(out=Ri, in0=Ri, in1=cx.lng[:, 0, :], op=ALU.mult),
                 reads=[rRi, cx.rlng], writes=[rRi])
            P.op("pool", lambda e: e.tensor_tensor(out=Ri, in0=Ri, in1=cx.lng[:, 1, :], op=ALU.add),
                 reads=[rRi, cx.rlng], writes=[rRi])
        if mode == "none":
            if out_ap is not None:
                P.dma("sp", out_ap[:, i, :], Ri, reads=[rRi])
            continue
        if mode == "xb":
            xb = cx.XA[:, i * D:(i + 1) * D]
            rxb = cx.rXA[i]
        else:
            xb = xbt[i % 2][:]
            rxb = rxbt[i % 2]
        P.op("act", lambda e: e.activation(out=xb, in_=Ri, func=AF.Copy), reads=[rRi], writes=[rxb])
        P.op("pool", lambda e: e.tensor_scalar(out=Ri, in0=Ri, scalar1=float(ALPHA), scalar2=0.0, op0=ALU.mult, op1=ALU.add),
             reads=[rRi], writes=[rRi])
        pb = 6 + (i % 2)
        pst = cx.ps[pb][:].bitcast(BF16)
        for k in range(NK):
            P.op("pe", lambda e, k=k: e.transpose(out=pst[:, k * 128:(k + 1) * 128], in_=xb[:, k * 128:(k + 1) * 128],
                                                  identity=cx.identb[:]),
                 reads=[rxb, cx.rcst], writes=[cx.rps[pb]])
        if mode == "xt":
            xt = cx.XA[:].rearrange("p (k t) -> p k t", k=NK)
            P.op("dve", lambda e: e.tensor_copy(out=xt[:, :, i * 128:(i + 1) * 128],
                                                in_=pst.rearrange("p (k t) -> p k t", k=NK)),
                 reads=[cx.rps[pb]], writes=[cx.rXA[i]])
        else:
            tt = tmpT[i % 2]
            rtt = rtmpT[i % 2]
            P.op("dve", lambda e: e.tensor_copy(out=tt[:], in_=pst), reads=[cx.rps[pb]], writes=[rtt])
            lb = 4 + (i % 2)
            for k in range(NK):
                P.op("pe", lambda e, k=k: e.matmul(cx.ps[lb][:, 0:NE], lhsT=tt[:, k * 128:(k + 1) * 128], rhs=wr[:, k, :],
                                                   start=(k == 0), stop=(k == NK - 1)),
                     reads=[rtt, rwr], writes=[cx.rps[lb]])
            P.op("act", lambda e: e.activation(out=cx.logits[:, i, :], in_=cx.ps[lb][:, 0:NE], func=AF.Copy),
                 reads=[cx.rps[lb]], writes=[cx.rlogits[i]])


def alloc_ln_scratch(cx, stack):
    nc = cx.nc
    tmpT = [stack.enter_context(nc.sbuf_tensor(un("tmpT"), [128, D], BF16)) for j in range(2)]
    rtmpT = [Reg(), Reg()]
    stat = [stack.enter_context(nc.sbuf_tensor(un("stat"), [128, 16], F32)) for j in range(2)]
    rstat = [Reg(), Reg()]
    xbt = [stack.enter_context(nc.sbuf_tensor(un("xbt"), [128, D], BF16)) for j in range(2)]
    rxbt = [Reg(), Reg()]
    return (tmpT, rtmpT, stat, rstat, xbt, rxbt)


def emit_moe(cx, wg_ap, wu_ap, wd_ap):
    P = cx.P
    nc = cx.nc
    with contextlib.ExitStack() as st:
        sb = lambda name, shape, dt: st.enter_context(nc.sbuf_tensor(un(name), shape, dt))
        aff = sb("aff", [128, NT, NE], F32)
        r_aff = Reg()
        sm = sb("sm", [128, NT, 4], F32)
        AG = sb("AG", [128, NT, NE, 4], BF16)
        r_AG = Reg()
        slot = sb("slot", [128, NT, NE], F32)
        r_slot = Reg()
        slotT = sb("slotT", [16, S], F32)
        r_slotT = Reg()
        m8 = sb("m8", [16, 8], F32)
        r_m8 = Reg()
        wbuf = [sb("wbuf%d" % j, [128, NK, D], BF16) for j in range(3)]
        r_wbuf = [Reg() for j in range(3)]
        Pm = sb("Pm", [128, NT, CAP], BF16)
        r_Pm = Reg()
        workbuf = sb("workbuf", [16, S], F32)
        xinT = sb("xinT", [128, NK, CAP], BF16)
        r_xinT = Reg()
        hT = sb("hT", [128, NK, CAP], BF16)
        r_hT = Reg()
        sg = [sb("sg%d" % j, [128, CAP], F32) for j in range(2)]
        r_sg = [Reg(), Reg()]
        Oe2 = [sb("Oe", [128, 2, D], F32) for j in range(2)]
        Oe = Oe2[0]
        r_Oe = Reg()
        gs = sb("gs", [128, 4], F32)
        r_gs = Reg()
        affT = Pm[:].rearrange("p a b -> p (a b)").bitcast(F32)[0:16, :]
        r_affT = Reg()
        work = workbuf[:]
        r_work = Reg()

        ident = cx.cst[:, C_IDENT:C_IDENT + 128]

        rl = cx.rlogits
        P.op("dve", lambda e: e.tensor_reduce(out=sm[:, :, 0], in_=cx.logits[:], axis=AX.X, op=ALU.max),
             reads=rl, writes=[r_aff])
        P.op("dve", lambda e: e.tensor_tensor(out=aff[:], in0=cx.logits[:],
                                              in1=sm[:, :, 0:1].to_broadcast([128, NT, NE]), op=ALU.subtract),
             reads=rl + [r_aff], writes=[r_aff])
        P.op("act", lambda e: e.activation(out=aff[:], in_=aff[:], func=AF.Exp), reads=[r_aff], writes=[r_aff])
        P.op("dve", lambda e: e.tensor_reduce(out=sm[:, :, 1], in_=aff[:], axis=AX.X, op=ALU.add),
             reads=[r_aff], writes=[r_aff])
        P.op("dve", lambda e: e.reciprocal(out=sm[:, :, 2], in_=sm[:, :, 1]), reads=[r_aff], writes=[r_aff])
        P.op("dve", lambda e: e.tensor_tensor(out=aff[:], in0=aff[:],
                                              in1=sm[:, :, 2:3].to_broadcast([128, NT, NE]), op=ALU.mult),
             reads=[r_aff], writes=[r_aff])
        P.op("dve", lambda e: e.tensor_copy(out=AG[:, :, :, 0], in_=aff[:]), reads=[r_aff], writes=[r_AG])
        P.op("dve", lambda e: e.tensor_tensor(out=AG[:, :, :, 1], in0=aff[:], in1=AG[:, :, :, 0], op=ALU.subtract),
             reads=[r_aff, r_AG], writes=[r_AG])
        for i in range(NT):
            P.op("pool", lambda e, i=i: e.memset(AG[:, i, :, 2], float(i)), writes=[r_AG])
        P.op("dve", lambda e: e.tensor_copy(out=AG[:, :, :, 3], in_=cx.cst[:, C_PIDX:C_PIDX + 1].to_broadcast([128, NT * NE]).rearrange("p (a b) -> p a b", a=NT)),
             reads=[cx.rcst], writes=[r_AG])
        for i in range(NT):
            P.dma("sp", cx.xd[i * 128:(i + 1) * 128, :], cx.XA[:, i * D:(i + 1) * D], reads=[cx.rXA[i]], writes=[cx.r_xd[i]])
        for g in range(4):
            for j in range(4):
                i = g * 4 + j
                P.op("pe", lambda e, i=i, j=j, g=g: e.transpose(out=cx.ps[g][0:16, j * 128:(j + 1) * 128],
                                                                in_=aff[:, i, :], identity=ident),
                     reads=[r_aff, cx.rcst], writes=[cx.rps[g]])
            P.op("act", lambda e, g=g: e.activation(out=affT[:, g * 512:(g + 1) * 512], in_=cx.ps[g][0:16, :], func=AF.Copy),
                 reads=[cx.rps[g]], writes=[r_affT])
        P.op("pool", lambda e: e.tensor_copy(out=work[:], in_=affT[:]), reads=[r_affT], writes=[r_work])
        nround = CAP // 8
        for r in range(nround):
            P.op("dve", lambda e: e.max(out=m8[:], in_=work[:]), reads=[r_work], writes=[r_m8])
            if r < nround - 1:
                P.op("dve", lambda e: e.match_replace(out=work[:], in_to_replace=m8[:], in_values=work[:], imm_value=-1.0),
                     reads=[r_work, r_m8], writes=[r_work])
        P.op("dve", lambda e: e.tensor_scalar(out=work[:], in0=affT[:], scalar1=m8[:, 7:8], scalar2=None, op0=ALU.is_ge),
             reads=[r_affT, r_m8], writes=[r_work])
        P.op("dve", lambda e: e.tensor_tensor_scan(out=slotT[:], data0=cx.cst[0:16, C_ONES:C_ONES + 1].to_broadcast([16, S]), data1=work[:], initial=0.0,
                                                    op0=ALU.mult, op1=ALU.add),
             reads=[r_work, cx.rcst], writes=[r_slotT])
        P.op("dve", lambda e: e.tensor_tensor(out=slotT[:], in0=slotT[:], in1=work[:], op=ALU.mult),
             reads=[r_work, r_slotT], writes=[r_slotT])
        P.op("dve", lambda e: e.tensor_scalar(out=slotT[:], in0=slotT[:], scalar1=-1.0, scalar2=None, op0=ALU.add),
             reads=[r_slotT], writes=[r_slotT])
        for i in range(NT):
            P.op("pe", lambda e, i=i: e.transpose(out=cx.ps[5][:, i * NE:(i + 1) * NE], in_=slotT[:, i * 128:(i + 1) * 128],
                                                  identity=ident[0:16, 0:16]),
                 reads=[r_slotT, cx.rcst], writes=[cx.rps[5]])
        P.op("act", lambda e: e.activation(out=slot[:].rearrange("p a b -> p (a b)"), in_=cx.ps[5][:, 0:NT * NE], func=AF.Copy),
             reads=[cx.rps[5]], writes=[r_slot])

        P.barrier()
        r_xk = [Reg() for k in range(NK)]
        r_hj = [Reg() for j in range(NK)]
        r_Oq2 = [[[Reg() for n in range(2)] for c in range(2)] for j in range(2)]
        r_Oq = r_Oq2[0]
        sc_prev = [[]]
        P.op("pool", lambda e: e.memset(Oe[:, 0, :], 0.0), writes=[r_Oq[0][0], r_Oq[0][1]])
        for i in range(NT):
            P.dma("sp", cx.yd[i * 128:(i + 1) * 128, :], Oe[:, 0, :], reads=[r_Oq[0][0], r_Oq[0][1]], writes=[cx.r_yd])
        wq = [0]
        xa3 = cx.XA[:].rearrange("p (s k f) -> p s k f", s=2, k=NK)
        wslot = [wbuf[0][:], wbuf[1][:], wbuf[2][:], xa3[:, 0], xa3[:, 1]]
        rw = [[r_wbuf[0]], [r_wbuf[1]], [r_wbuf[2]], [Reg()] + cx.rXA[0:8], [Reg()] + cx.rXA[8:16]]

        def load_w(ap2d):
            j = wq[0] % 5
            wq[0] += 1
            P.dma("pool", wslot[j], ap2d.rearrange("(k p) f -> p k f", p=128), writes=rw[j])
            return j

        def build_P(ex):
            for i in range(NT):
                P.op("dve", lambda e, i=i: e.tensor_scalar(out=Pm[:, i, :], in0=cx.cst[:, C_IOTA:C_IOTA + CAP],
                                                           scalar1=slot[:, i, ex:ex + 1], scalar2=None, op0=ALU.is_equal),
                     reads=[r_slot, cx.rcst], writes=[r_Pm])

        xin = sb("xin", [128, 2, D], BF16)
        r_xin = [Reg(), Reg()]
        gs2 = [sb("gs2", [128, 8], F32) for j in range(2)]
        r_gs2 = [Reg(), Reg()]
        idx2 = [sb("idx2", [128, 2], I32) for j in range(2)]
        r_idx2 = [Reg(), Reg()]

        def gsidx_and_gather(ex):
            g_ = gs2[ex % 2]
            r_g = r_gs2[ex % 2]
            for c in range(2):
                for i in range(NT):
                    P.op("pe", lambda e, i=i, c=c: e.matmul(cx.ps[6][:, 8 + c * 4:8 + c * 4 + 4], lhsT=Pm[:, i, c * 128:(c + 1) * 128],
                                                            rhs=AG[:, i, ex, :], start=(i == 0), stop=(i == NT - 1)),
                         reads=[r_Pm, r_AG], writes=[cx.rps[6]])
            P.op("dve", lambda e: e.tensor_copy(out=g_[:, 0:8], in_=cx.ps[6][:, 8:16]), reads=[cx.rps[6]], writes=[r_g])
            gv = g_[:, 0:8].rearrange("p (c f) -> p c f", c=2)
            P.op("dve", lambda e: e.tensor_tensor(out=gv[:, :, 0], in0=gv[:, :, 0], in1=gv[:, :, 1], op=ALU.add), reads=[r_g], writes=[r_g])
            P.op("dve", lambda e: e.scalar_tensor_tensor(out=gv[:, :, 2], in0=gv[:, :, 2], scalar=128.0, in1=gv[:, :, 3], op0=ALU.mult, op1=ALU.add),
                 reads=[r_g], writes=[r_g])
            ix = idx2[ex % 2]
            P.op("dve", lambda e: e.tensor_copy(out=ix[:], in_=gv[:, :, 2]), reads=[r_g], writes=[r_idx2[ex % 2]])
            for c in range(2):
                P.idma_gather(xin[:, c, :], cx.xd[:, :], ix[:, c:c + 1], reads=[r_idx2[ex % 2]] + cx.r_xd, writes=[r_xin[c]])

        build_P(0)
        gsidx_and_gather(0)
        jnext = (load_w(wg_ap[0]), load_w(wu_ap[0]), load_w(wd_ap[0]))
        for ex in range(NE):
            jg, ju, jd = jnext
            Oe = Oe2[ex % 2]
            r_Oq = r_Oq2[ex % 2]
            if ex + 1 < NE:
                jg1 = load_w(wg_ap[ex + 1])
                ju1 = load_w(wu_ap[ex + 1])
            gsc = gs2[ex % 2][:, 0:8].rearrange("p (c f) -> p c f", c=2)
            r_gs = r_gs2[ex % 2]
            for c in range(2):
                pb = 4 + c
                pst = cx.ps[pb][:].bitcast(BF16)
                for k in range(NK):
                    P.op("pe", lambda e, k=k, c=c: e.transpose(out=pst[:, k * 128:(k + 1) * 128], in_=xin[:, c, k * 128:(k + 1) * 128], identity=cx.identb[:]),
                         reads=[r_xin[c], cx.rcst], writes=[cx.rps[pb]])
                P.op("act", lambda e, c=c: e.activation(out=xinT[:, :, c * 128:(c + 1) * 128], in_=pst.rearrange("p (k t) -> p k t", k=NK), func=AF.Copy),
                     reads=[cx.rps[pb]], writes=r_xk)
            for j in range(NK):
                pg = 4 + (j % 2)
                pu = 6 + (j % 2)
                for k in range(NK):
                    P.op("pe", lambda e, j=j, k=k, pg=pg: e.matmul(cx.ps[pg][:, 0:CAP], lhsT=wslot[jg][:, k, j * 128:(j + 1) * 128],
                                                                   rhs=xinT[:, k, :], start=(k == 0), stop=(k == NK - 1)),
                         reads=rw[jg] + [r_xk[k]], writes=[cx.rps[pg]])
                for k in range(NK):
                    P.op("pe", lambda e, j=j, k=k, pu=pu: e.matmul(cx.ps[pu][:, 0:CAP], lhsT=wslot[ju][:, k, j * 128:(j + 1) * 128],
                                                                   rhs=xinT[:, k, :], start=(k == 0), stop=(k == NK - 1)),
                         reads=rw[ju] + [r_xk[k]], writes=[cx.rps[pu]])
                P.op("act", lambda e, j=j, pg=pg: e.activation(out=sg[j % 2][:], in_=cx.ps[pg][:, 0:CAP], func=AF.Silu),
                     reads=[cx.rps[pg]], writes=[r_sg[j % 2]])
                P.op("dve", lambda e, j=j, pu=pu: e.tensor_tensor(out=hT[:, j, :], in0=sg[j % 2][:], in1=cx.ps[pu][:, 0:CAP], op=ALU.mult),
                     reads=[cx.rps[pu], r_sg[j % 2]], writes=[r_hj[j]])
                if j == 1 and ex + 1 < NE:
                    build_P(ex + 1)
                if j == 5 and ex + 1 < NE:
                    gsidx_and_gather(ex + 1)
            if ex + 1 < NE:
                jnext = (jg1, ju1, load_w(wd_ap[ex + 1]))
            for c in range(2):
                for n in range(2):
                    pb = 4 + ((c * 2 + n) % 4)
                    for j in range(NK):
                        P.op("pe", lambda e, j=j, c=c, n=n, pb=pb: e.matmul(cx.ps[pb][:, :], lhsT=hT[:, j, c * 128:(c + 1) * 128],
                                                                            rhs=wslot[jd][:, j, n * 512:(n + 1) * 512],
                                                                            start=(j == 0), stop=(j == NK - 1)),
                             reads=rw[jd] + [r_hj[j]], writes=[cx.rps[pb]])
                    P.op("act", lambda e, c=c, n=n, pb=pb: e.activation(out=Oe[:, c, n * 512:(n + 1) * 512], in_=cx.ps[pb][:, :],
                                                                        func=AF.Copy, scale=gsc[:, c, 0:1]),
                         reads=[cx.rps[pb], r_gs], writes=[r_Oq[c][n]])
            toks = []
            for c in range(2):
                toks.append(P.idma_scatter_add(cx.yd[:, :], idx2[ex % 2][:, c:c + 1], Oe[:, c, :],
                                               reads=[r_idx2[ex % 2], r_Oq[c][0], r_Oq[c][1], cx.r_yd], after=sc_prev[0]))
            sc_prev[0] = toks
        for akey, aval in sc_prev[0]:
            P._wait("sp", akey, aval)
        Oe = Oe2[0]
        r_Oq = r_Oq2[0]
        for i in range(NT):
            c = i % 2
            P.dma("sp", Oe[:, c, :], cx.yd[i * 128:(i + 1) * 128, :], reads=[cx.r_yd], writes=[r_Oq[c][0], r_Oq[c][1]])
            eng = "dve" if i % 2 == 0 else "pool"
            P.op(eng, lambda e, i=i, c=c: e.tensor_tensor(out=cx.R[:, i, :], in0=cx.R[:, i, :], in1=Oe[:, c, :], op=ALU.add),
                 reads=[r_Oq[c][0], r_Oq[c][1], cx.rR[i]], writes=[cx.rR[i]])
    P.barrier()


def xt_view(cx):
    return cx.XA[:].rearrange("p (k t) -> p k t", k=NK)


def small_load(cx, out, in_, reg):
    cx.P.dma("sp", out, in_, writes=[reg], allow_slow_non_contiguous=True)


def gelu_tanh(cx, out, src, t1, t2, r_src, r_t1, r_t2, r_out, extra_mul=None, r_extra=None):
    P = cx.P
    P.op("act", lambda e: e.activation(out=t1, in_=src, func=AF.Square), reads=[r_src], writes=[r_t1])
    P.op("dve", lambda e: e.tensor_scalar(out=t1, in0=t1, scalar1=0.044715, scalar2=1.0, op0=ALU.mult, op1=ALU.add),
         reads=[r_t1], writes=[r_t1])
    P.op("dve", lambda e: e.tensor_tensor(out=t1, in0=t1, in1=src, op=ALU.mult), reads=[r_t1, r_src], writes=[r_t1])
    P.op("act", lambda e: e.activation(out=t1, in_=t1, func=AF.Sigmoid, scale=1.5957691216057308), reads=[r_t1], writes=[r_t1])
    if extra_mul is None:
        P.op("dve", lambda e: e.tensor_tensor(out=out, in0=t1, in1=src, op=ALU.mult), reads=[r_t1, r_src], writes=[r_out])
    else:
        P.op("dve", lambda e: e.tensor_tensor(out=t2, in0=t1, in1=src, op=ALU.mult), reads=[r_t1, r_src], writes=[r_t2])
        P.op("pool", lambda e: e.tensor_tensor(out=out, in0=t2, in1=extra_mul, op=ALU.mult), reads=[r_t2, r_extra], writes=[r_out])


def emit_lru(cx):
    P = cx.P
    nc = cx.nc
    A = cx.A
    XT = xt_view(cx)
    with contextlib.ExitStack() as st:
        sb = lambda name, shape, dt: st.enter_context(nc.sbuf_tensor(un(name), shape, dt))
        cw = sb("cw", [128, NK, 4], F32)
        cb = sb("cb", [128, NK], F32)
        gab = sb("gab", [128, 2, NK], F32)
        gxb = sb("gxb", [128, 2, NK], F32)
        lam = sb("lam", [128, 2, NK], F32)
        c8 = sb("c8", [128, 2, NK], F32)
        c16 = sb("c16", [128, 2, NK], F32)
        r_par = Reg()
        for j in range(4):
            small_load(cx, cw[:, :, j], A["lru_conv_w"][0, j].rearrange("(c p) -> p c", p=128), r_par)
        small_load(cx, cb[:], A["lru_conv_b"][0].rearrange("(c p) -> p c", p=128), r_par)
        for g in range(2):
            small_load(cx, gab[:, g, :], A["lru_gate_a_b"][0, g].rearrange("(c p) -> p c", p=128), r_par)
        for g in range(2):
            small_load(cx, gxb[:, g, :], A["lru_gate_x_b"][0, g].rearrange("(c p) -> p c", p=128), r_par)
        for g in range(2):
            small_load(cx, lam[:, g, :], A["lru_lambda"][0, g].rearrange("(c p) -> p c", p=128), r_par)
        P.op("act", lambda e: e.activation(out=c8[:], in_=lam[:], func=AF.Exp, scale=-1.0), reads=[r_par], writes=[r_par])
        P.op("act", lambda e: e.activation(out=c8[:], in_=c8[:], func=AF.Ln, bias=1.0), reads=[r_par], writes=[r_par])
        P.op("dve", lambda e: e.tensor_scalar(out=c16[:], in0=c8[:], scalar1=-16.0, scalar2=None, op0=ALU.mult), reads=[r_par], writes=[r_par])
        P.op("dve", lambda e: e.tensor_scalar(out=c8[:], in0=c8[:], scalar1=-8.0, scalar2=None, op0=ALU.mult), reads=[r_par], writes=[r_par])

        wgb = sb("wgb", [128, NK, 256], BF16)
        r_wgb = Reg()
        wu = sb("wu", [128, NK, 256], BF16)
        r_wu = Reg()
        wg4 = [[sb("wga", [128, 2, 256], BF16) for d in range(2)] for ax in range(2)]
        r_wg4 = [[Reg() for d in range(2)] for ax in range(2)]
        wo = sb("wo", [128, D], BF16)
        r_wo = Reg()
        upad = sb("upad", [128, S + 4], F32)
        r_upad = Reg()
        uc = sb("uc", [128, 2, S], F32)
        r_uc = [Reg(), Reg()]
        ucb = sb("ucb", [128, 2, S], BF16)
        r_ucb = [Reg(), Reg()]
        rbuf = sb("rbuf", [128, S], F32)
        r_rbuf = Reg()
        abuf = sb("abuf", [128, S], F32)
        r_abuf = Reg()
        tmp = sb("tmp", [128, S], F32)
        r_tmp = Reg()
        hsum = sb("hsum", [128, S], F32)
        r_hsum = Reg()
        gtc = sb("gtc", [128, S], BF16)
        r_gtc = Reg()
        P.op("pool", lambda e: e.memset(upad[:], 0.0), writes=[r_upad])

        def psum4(base):
            return [cx.ps[base + j] for j in range(4)], [cx.rps[base + j] for j in range(4)]

        pcount = [0]

        def next_ps4():
            base = 4 * (pcount[0] % 2)
            pcount[0] += 1
            return psum4(base)

        for n in range(4):
            P.dma("pool", wgb[:], A["lru_w_in"][0][:, n * 256:(n + 1) * 256].rearrange("(k p) c -> p k c", p=128), writes=[r_wgb])
            P.dma("pool", wu[:], A["lru_w_in"][0][:, D + n * 256:D + (n + 1) * 256].rearrange("(k p) c -> p k c", p=128), writes=[r_wu])
            for ax, nm in enumerate(("lru_gate_a_w", "lru_gate_x_w")):
                for d in range(2):
                    P.dma("pool", wg4[ax][d][:], A[nm][0, d, n].rearrange("(i p) j -> p i j", p=128), writes=[r_wg4[ax][d]])
            for cc in range(2):
                c = 2 * n + cc
                ps, rps = next_ps4()
                for tb in range(4):
                    for k in range(NK):
                        P.op("pe", lambda e, tb=tb, k=k: e.matmul(ps[tb][:, :], lhsT=wu[:, k, cc * 128:(cc + 1) * 128],
                                                                  rhs=XT[:, k, tb * 512:(tb + 1) * 512], start=(k == 0), stop=(k == NK - 1)),
                             reads=[r_wu] + cx.rXA[tb * 4:(tb + 1) * 4], writes=[rps[tb]])
                    P.op("act", lambda e, tb=tb: e.activation(out=upad[:, 2 + tb * 512:2 + (tb + 1) * 512], in_=ps[tb][:, :], func=AF.Copy),
                         reads=[rps[tb]], writes=[r_upad])
                ucc = uc[:, cc, :]
                P.op("dve", lambda e: e.tensor_scalar(out=ucc, in0=upad[:, 0:S], scalar1=cw[:, c, 0:1], scalar2=cb[:, c:c + 1],
                                                       op0=ALU.mult, op1=ALU.add), reads=[r_upad, r_par], writes=[r_uc[cc]])
                for j in range(1, 4):
                    P.op("dve", lambda e, j=j: e.scalar_tensor_tensor(out=ucc, in0=upad[:, j:j + S], scalar=cw[:, c, j:j + 1], in1=ucc,
                                                                      op0=ALU.mult, op1=ALU.add), reads=[r_upad, r_par, r_uc[cc]], writes=[r_uc[cc]])
                P.op("pool", lambda e: e.tensor_copy(out=ucb[:, cc, :], in_=ucc), reads=[r_uc[cc]], writes=[r_ucb[cc]])
            for cc in range(2):
                c = 2 * n + cc
                ucc = uc[:, cc, :]
                for d in range(2):
                    ps, rps = next_ps4()
                    for tb in range(4):
                        for i in range(2):
                            P.op("pe", lambda e, tb=tb, i=i: e.matmul(ps[tb][:, :], lhsT=wg4[0][d][:, i, cc * 128:(cc + 1) * 128],
                                                                      rhs=ucb[:, i, tb * 512:(tb + 1) * 512], start=(i == 0), stop=(i == 1)),
                                 reads=[r_wg4[0][d], r_ucb[0], r_ucb[1]], writes=[rps[tb]])
                        P.op("act", lambda e, tb=tb: e.activation(out=rbuf[:, tb * 512:(tb + 1) * 512], in_=ps[tb][:, :], func=AF.Sigmoid,
                                                                  bias=gab[:, d, c:c + 1]), reads=[rps[tb], r_par], writes=[r_rbuf])
                    P.op("act", lambda e: e.activation(out=abuf[:], in_=rbuf[:], func=AF.Exp, scale=c8[:, d, c:c + 1]),
                         reads=[r_rbuf, r_par], writes=[r_abuf])
                    P.op("act", lambda e: e.activation(out=tmp[:], in_=rbuf[:], func=AF.Exp, scale=c16[:, d, c:c + 1]),
                         reads=[r_rbuf, r_par], writes=[r_tmp])
                    P.op("act", lambda e: e.activation(out=tmp[:], in_=tmp[:], func=AF.Sqrt, scale=-1.0, bias=1.0),
                         reads=[r_tmp], writes=[r_tmp])
                    ps, rps = next_ps4()
                    for tb in range(4):
                        for i in range(2):
                            P.op("pe", lambda e, tb=tb, i=i: e.matmul(ps[tb][:, :], lhsT=wg4[1][d][:, i, cc * 128:(cc + 1) * 128],
                                                                      rhs=ucb[:, i, tb * 512:(tb + 1) * 512], start=(i == 0), stop=(i == 1)),
                                 reads=[r_wg4[1][d], r_ucb[0], r_ucb[1]], writes=[rps[tb]])
                        P.op("act", lambda e, tb=tb: e.activation(out=rbuf[:, tb * 512:(tb + 1) * 512], in_=ps[tb][:, :], func=AF.Sigmoid,
                                                                  bias=gxb[:, d, c:c + 1]), reads=[rps[tb], r_par], writes=[r_rbuf])
                    P.op("dve", lambda e: e.tensor_tensor(out=tmp[:], in0=tmp[:], in1=rbuf[:], op=ALU.mult), reads=[r_tmp, r_rbuf], writes=[r_tmp])
                    P.op("dve", lambda e: e.tensor_tensor(out=tmp[:], in0=tmp[:], in1=ucc, op=ALU.mult), reads=[r_tmp, r_uc[cc]], writes=[r_tmp])
                    if d == 0:
                        P.op("dve", lambda e: e.tensor_tensor_scan(out=hsum[:], data0=abuf[:], data1=tmp[:], initial=0.0,
                                                                    op0=ALU.mult, op1=ALU.add), reads=[r_abuf, r_tmp], writes=[r_hsum])
                    else:
                        P.op("dve", lambda e: e.tensor_tensor_scan(out=rbuf[:, ::-1], data0=abuf[:, ::-1], data1=tmp[:, ::-1], initial=0.0,
                                                                    op0=ALU.mult, op1=ALU.add), reads=[r_abuf, r_tmp], writes=[r_rbuf])
                        P.op("pool", lambda e: e.tensor_tensor(out=hsum[:], in0=hsum[:], in1=rbuf[:], op=ALU.add),
                             reads=[r_rbuf, r_hsum], writes=[r_hsum])
                ps, rps = next_ps4()
                for tb in range(4):
                    for k in range(NK):
                        P.op("pe", lambda e, tb=tb, k=k: e.matmul(ps[tb][:, :], lhsT=wgb[:, k, cc * 128:(cc + 1) * 128],
                                                                  rhs=XT[:, k, tb * 512:(tb + 1) * 512], start=(k == 0), stop=(k == NK - 1)),
                             reads=[r_wgb] + cx.rXA[tb * 4:(tb + 1) * 4], writes=[rps[tb]])
                    sl = slice(tb * 512, (tb + 1) * 512)
                    gelu_tanh(cx, gtc[:, sl], ps[tb][:, :], abuf[:, sl], tmp[:, sl], rps[tb], r_abuf, r_tmp, r_gtc,
                              extra_mul=hsum[:, sl], r_extra=r_hsum)
                P.dma("pool", wo[:], A["lru_w_out"][0][c * 128:(c + 1) * 128, :], writes=[r_wo])
                for i in range(NT):
                    for hh in range(2):
                        pb = (i * 2 + hh) % 8
                        P.op("pe", lambda e, i=i, hh=hh, pb=pb: e.matmul(cx.ps[pb][:, :], lhsT=gtc[:, i * 128:(i + 1) * 128],
                                                                         rhs=wo[:, hh * 512:(hh + 1) * 512], start=True, stop=True),
                             reads=[r_gtc, r_wo], writes=[cx.rps[pb]])
                        P.op("dve", lambda e, i=i, hh=hh, pb=pb: e.tensor_tensor(out=cx.R[:, i, hh * 512:(hh + 1) * 512],
                                                                                 in0=cx.R[:, i, hh * 512:(hh + 1) * 512],
                                                                                 in1=cx.ps[pb][:, :], op=ALU.add),
                             reads=[cx.rps[pb], cx.rR[i]], writes=[cx.rR[i]])
    P.barrier()


MLA_H = 8
MLA_SCALE = 192.0 ** -0.5
TWO_PI_HI = 6.28125
TWO_PI_LO = 2.0 * np.pi - 6.28125
MAGIC = 12582912.0


def rope_apply(cx, out_bf, src, cos, sin, t1, t2, nb, r_src, r_tab, r_t, r_out):
    P = cx.P
    a1 = src[:, :, 0:32]
    a2 = src[:, :, 32:64]
    P.op("dve", lambda e: e.tensor_tensor(out=t1, in0=a1, in1=cos, op=ALU.mult), reads=[r_src, r_tab], writes=[r_t])
    P.op("dve", lambda e: e.tensor_tensor(out=t2, in0=a2, in1=sin, op=ALU.mult), reads=[r_src, r_tab], writes=[r_t])
    P.op("dve", lambda e: e.tensor_tensor(out=out_bf[:, :, 0:32], in0=t1, in1=t2, op=ALU.subtract), reads=[r_t], writes=[r_out])
    P.op("dve", lambda e: e.tensor_tensor(out=t1, in0=a2, in1=cos, op=ALU.mult), reads=[r_src, r_tab, r_out], writes=[r_t])
    P.op("dve", lambda e: e.tensor_tensor(out=t2, in0=a1, in1=sin, op=ALU.mult), reads=[r_src, r_tab], writes=[r_t])
    P.op("dve", lambda e: e.tensor_tensor(out=out_bf[:, :, 32:64], in0=t1, in1=t2, op=ALU.add), reads=[r_t], writes=[r_out])


def emit_mla(cx, seq):
    P = cx.P
    nc = cx.nc
    A = cx.A
    XT = xt_view(cx)
    with contextlib.ExitStack() as st:
        sb = lambda name, shape, dt: st.enter_context(nc.sbuf_tensor(un(name), shape, dt))
        posi = sb("posi", [128, NT], I32)
        ang = sb("ang", [128, NT, 32], F32)
        nn = sb("nn", [128, NT, 32], F32)
        cosT = sb("cosT", [128, NT, 32], F32)
        sinT = sb("sinT", [128, NT, 32], F32)
        r_tab = Reg()
        small_load(cx, posi[:], A["positions"][seq].rearrange("(i p) -> p i", p=128), r_tab)
        P.op("dve", lambda e: e.tensor_copy(out=nn[:, :, 0], in_=posi[:]), reads=[r_tab], writes=[r_tab])
        P.op("dve", lambda e: e.tensor_tensor(out=ang[:], in0=nn[:, :, 0:1].to_broadcast([128, NT, 32]),
                                              in1=cx.cst[:, None, C_FREQ:C_FREQ + 32].to_broadcast([128, NT, 32]), op=ALU.mult),
             reads=[r_tab, cx.rcst], writes=[r_tab])
        P.op("dve", lambda e: e.tensor_scalar(out=nn[:], in0=ang[:], scalar1=float(1.0 / (2.0 * np.pi)), scalar2=None, op0=ALU.mult),
             reads=[r_tab], writes=[r_tab])
        P.op("dve", lambda e: e.tensor_scalar(out=nn[:], in0=nn[:], scalar1=MAGIC, scalar2=None, op0=ALU.add), reads=[r_tab], writes=[r_tab])
        P.op("dve", lambda e: e.tensor_scalar(out=nn[:], in0=nn[:], scalar1=-MAGIC, scalar2=None, op0=ALU.add), reads=[r_tab], writes=[r_tab])
        P.op("dve", lambda e: e.scalar_tensor_tensor(out=ang[:], in0=nn[:], scalar=-TWO_PI_HI, in1=ang[:], op0=ALU.mult, op1=ALU.add),
             reads=[r_tab], writes=[r_tab])
        P.op("dve", lambda e: e.scalar_tensor_tensor(out=ang[:], in0=nn[:], scalar=-TWO_PI_LO, in1=ang[:], op0=ALU.mult, op1=ALU.add),
             reads=[r_tab], writes=[r_tab])
        PI_S = 3.1415925
        P.op("dve", lambda e: e.tensor_scalar(out=ang[:], in0=ang[:], scalar1=PI_S, scalar2=-PI_S, op0=ALU.min, op1=ALU.max),
             reads=[r_tab], writes=[r_tab])
        P.op("act", lambda e: e.activation(out=sinT[:], in_=ang[:], func=AF.Sin), reads=[r_tab], writes=[r_tab])
        P.op("act", lambda e: e.activation(out=nn[:], in_=ang[:], func=AF.Abs), reads=[r_tab], writes=[r_tab])
        P.op("dve", lambda e: e.tensor_scalar(out=nn[:], in0=nn[:], scalar1=-1.0, scalar2=float(np.pi / 2), op0=ALU.mult, op1=ALU.add),
             reads=[r_tab], writes=[r_tab])
        P.op("act", lambda e: e.activation(out=cosT[:], in_=nn[:], func=AF.Sin), reads=[r_tab], writes=[r_tab])

        wuq = sb("wuq", [128, 3, 1536], BF16)
        wukv = sb("wukv", [128, 2, 2048], BF16)
        r_w = Reg()
        P.dma("pool", wuq[:], A["mla_w_uq"][0].rearrange("(k p) c -> p k c", p=128), writes=[r_w])
        P.dma("pool", wukv[:], A["mla_w_ukv"][0].rearrange("(k p) c -> p k c", p=128), writes=[r_w])
        cqnT = sb("cqnT", [128, 3, S], BF16)
        ckvnT = sb("ckvnT", [128, 2, S], BF16)
        kropeT = sb("kropeT", [64, S], BF16)
        r_lat = [Reg() for i in range(NT)]
        r_krT = Reg()
        with contextlib.ExitStack() as st1:
            sb1 = lambda name, shape, dt: st1.enter_context(nc.sbuf_tensor(un(name), shape, dt))
            wi = sb1("wi", [128, NK, 704], BF16)
            r_wi = Reg()
            P.dma("pool", wi[:], A["mla_w_in"][0].rearrange("(k p) c -> p k c", p=128), writes=[r_wi])
            gq = sb1("gq", [128, 384], F32)
            gkv = sb1("gkv", [128, 256], F32)
            r_g = Reg()
            P.dma("sp", gq[:], A["mla_q_norm_g"][0].partition_broadcast(128), writes=[r_g])
            P.dma("sp", gkv[:], A["mla_kv_norm_g"][0].partition_broadcast(128), writes=[r_g])
            zs = [sb1("zs", [128, 704], F32) for j in range(2)]
            r_zs = [Reg(), Reg()]
            junk = sb1("junk", [128, 384], F32)
            r_junk = Reg()
            ms = [sb1("ms", [128, 4], F32) for j in range(2)]
            r_ms = [Reg(), Reg()]
            cn = [sb1("cn", [128, 640], BF16) for j in range(2)]
            r_cn = [Reg(), Reg()]
            krz = sb1("krz", [128, NT, 64], F32)
            r_krz = Reg()
            krb = sb1("krb", [128, NT, 64], BF16)
            r_krb = Reg()
            rt1 = sb1("rt1", [128, NT, 32], F32)
            rt2 = sb1("rt2", [128, NT, 32], F32)
            r_rt = Reg()
            for i in range(NT):
                j2 = i % 2
                pa = 0 + 2 * j2
                pbk = 1 + 2 * j2
                for k in range(NK):
                    P.op("pe", lambda e, k=k: e.matmul(cx.ps[pa][:, :], lhsT=XT[:, k, i * 128:(i + 1) * 128], rhs=wi[:, k, 0:512],
                                                       start=(k == 0), stop=(k == NK - 1)), reads=[cx.rXA[i], r_wi], writes=[cx.rps[pa]])
                for k in range(NK):
                    P.op("pe", lambda e, k=k: e.matmul(cx.ps[pbk][:, 0:192], lhsT=XT[:, k, i * 128:(i + 1) * 128], rhs=wi[:, k, 512:704],
                                                       start=(k == 0), stop=(k == NK - 1)), reads=[cx.rXA[i], r_wi], writes=[cx.rps[pbk]])
                z = zs[j2]
                P.op("act", lambda e: e.activation(out=z[:, 0:512], in_=cx.ps[pa][:, :], func=AF.Copy), reads=[cx.rps[pa]], writes=[r_zs[j2]])
                P.op("dve", lambda e: e.tensor_copy(out=z[:, 512:704], in_=cx.ps[pbk][:, 0:192]), reads=[cx.rps[pbk]], writes=[r_zs[j2]])
                m = ms[j2]
                P.op("act", lambda e: e.activation(out=junk[:, 0:384], in_=z[:, 0:384], func=AF.Square, accum_out=m[:, 0:1]),
                     reads=[r_zs[j2]], writes=[r_junk, r_ms[j2]])
                P.op("act", lambda e: e.activation(out=junk[:, 0:256], in_=z[:, 384:640], func=AF.Square, accum_out=m[:, 1:2]),
                     reads=[r_zs[j2]], writes=[r_junk, r_ms[j2]])
                P.op("dve", lambda e: e.tensor_scalar(out=m[:, 0:1], in0=m[:, 0:1], scalar1=1.0 / 384.0, scalar2=LN_EPS, op0=ALU.mult, op1=ALU.add),
                     reads=[r_ms[j2]], writes=[r_ms[j2]])
                P.op("dve", lambda e: e.tensor_scalar(out=m[:, 1:2], in0=m[:, 1:2], scalar1=1.0 / 256.0, scalar2=LN_EPS, op0=ALU.mult, op1=ALU.add),
                     reads=[r_ms[j2]], writes=[r_ms[j2]])
                P.op("act", lambda e: e.activation(out=m[:, 0:2], in_=m[:, 0:2], func=AF.Sqrt), reads=[r_ms[j2]], writes=[r_ms[j2]])
                P.op("dve", lambda e: e.reciprocal(out=m[:, 2:4], in_=m[:, 0:2]), reads=[r_ms[j2]], writes=[r_ms[j2]])
                c_ = cn[j2]
                P.op("dve", lambda e: e.scalar_tensor_tensor(out=c_[:, 0:384], in0=z[:, 0:384], scalar=m[:, 2:3], in1=gq[:], op0=ALU.mult, op1=ALU.mult),
                     reads=[r_zs[j2], r_ms[j2], r_g], writes=[r_cn[j2]])
                P.op("dve", lambda e: e.scalar_tensor_tensor(out=c_[:, 384:640], in0=z[:, 384:640], scalar=m[:, 3:4], in1=gkv[:], op0=ALU.mult, op1=ALU.mult),
                     reads=[r_zs[j2], r_ms[j2], r_g], writes=[r_cn[j2]])
                P.op("pool", lambda e: e.tensor_copy(out=krz[:, i, :], in_=z[:, 640:704]), reads=[r_zs[j2]], writes=[r_krz])
                pt = 6 + j2
                pst = cx.ps[pt][:].bitcast(BF16)
                for k in range(5):
                    P.op("pe", lambda e, k=k: e.transpose(out=pst[:, k * 128:(k + 1) * 128], in_=c_[:, k * 128:(k + 1) * 128], identity=cx.identb[:]),
                         reads=[r_cn[j2], cx.rcst], writes=[cx.rps[pt]])
                P.op("act", lambda e: e.activation(out=cqnT[:, :, i * 128:(i + 1) * 128], in_=pst[:, 0:384].rearrange("p (k t) -> p k t", k=3), func=AF.Copy),
                     reads=[cx.rps[pt]], writes=[r_lat[i]])
                P.op("dve", lambda e: e.tensor_copy(out=ckvnT[:, :, i * 128:(i + 1) * 128], in_=pst[:, 384:640].rearrange("p (k t) -> p k t", k=2)),
                     reads=[cx.rps[pt]], writes=[r_lat[i]])
            rope_apply(cx, krb[:], krz[:], cosT[:], sinT[:], rt1[:], rt2[:], NT, r_krz, r_tab, r_rt, r_krb)
            for g in range(2):
                pst = cx.ps[4 + g][:].bitcast(BF16)
                for j in range(8):
                    i = g * 8 + j
                    P.op("pe", lambda e, i=i, j=j: e.transpose(out=pst[0:64, j * 128:(j + 1) * 128], in_=krb[:, i, :], identity=cx.identb[:]),
                         reads=[r_krb, cx.rcst], writes=[cx.rps[4 + g]])
                P.op("act", lambda e, g=g: e.activation(out=kropeT[:, g * 1024:(g + 1) * 1024], in_=pst[0:64, :], func=AF.Copy),
                     reads=[cx.rps[4 + g]], writes=[r_krT])
        P.barrier()
        qnT = sb("qnT", [128, S], BF16)
        knT = sb("knT", [128, S], BF16)
        qrT = sb("qrT", [64, S], BF16)
        vh = sb("vh", [128, NT, 128], BF16)
        r_qnT, r_knT, r_qrT, r_vh = Reg(), Reg(), Reg(), Reg()
        qrb = sb("qrb", [128, 8, 64], BF16)
        r_qrb = Reg()
        rt1 = sb("rt1b", [128, 8, 32], F32)
        rt2 = sb("rt2b", [128, 8, 32], F32)
        r_rt = Reg()
        Pb = sb("Pb", [128, S], BF16)
        r_Pb = Reg()
        PTs4 = sb("PTs4", [128, NT, 512], BF16)
        r_PTs = Reg()
        oT4 = sb("oT4", [128, 512], BF16)
        r_oT = Reg()
        sm = [sb("smx", [128, 8], F32) for j in range(8)]
        r_sm = [Reg() for j in range(8)]
        wo = sb("wo", [128, D], BF16)
        r_wo = Reg()
        all_lat = r_lat
        for h in range(MLA_H):
            P.dma("pool", wo[:], A["mla_w_out"][0][h * 128:(h + 1) * 128, :], writes=[r_wo])
            for tb in range(4):
                for k in range(3):
                    P.op("pe", lambda e, tb=tb, k=k: e.matmul(cx.ps[tb][:, :], lhsT=wuq[:, k, h * 192:h * 192 + 128],
                                                              rhs=cqnT[:, k, tb * 512:(tb + 1) * 512], start=(k == 0), stop=(k == 2)),
                         reads=[r_w] + all_lat[tb * 4:(tb + 1) * 4], writes=[cx.rps[tb]])
                P.op("act", lambda e, tb=tb: e.activation(out=qnT[:, tb * 512:(tb + 1) * 512], in_=cx.ps[tb][:, :], func=AF.Copy),
                     reads=[cx.rps[tb]], writes=[r_qnT])
            for tb in range(4):
                for k in range(2):
                    P.op("pe", lambda e, tb=tb, k=k: e.matmul(cx.ps[4 + tb][:, :], lhsT=wukv[:, k, h * 256:h * 256 + 128],
                                                              rhs=ckvnT[:, k, tb * 512:(tb + 1) * 512], start=(k == 0), stop=(k == 1)),
                         reads=[r_w] + all_lat[tb * 4:(tb + 1) * 4], writes=[cx.rps[4 + tb]])
                P.op("dve", lambda e, tb=tb: e.tensor_copy(out=knT[:, tb * 512:(tb + 1) * 512], in_=cx.ps[4 + tb][:, :]),
                     reads=[cx.rps[4 + tb]], writes=[r_knT])
            for g in range(4):
                for j in range(4):
                    i = g * 4 + j
                    for k in range(2):
                        P.op("pe", lambda e, i=i, j=j, k=k, g=g: e.matmul(cx.ps[g][:, j * 128:(j + 1) * 128], lhsT=ckvnT[:, k, i * 128:(i + 1) * 128],
                                                                          rhs=wukv[:, k, h * 256 + 128:h * 256 + 256], start=(k == 0), stop=(k == 1)),
                             reads=[r_w, all_lat[i]], writes=[cx.rps[g]])
                P.op("act", lambda e, g=g: e.activation(out=vh[:, g * 4:(g + 1) * 4, :], in_=cx.ps[g][:, :].rearrange("p (a b) -> p a b", a=4), func=AF.Copy),
                     reads=[cx.rps[g]], writes=[r_vh])
            for g in range(2):
                pb = 4 + g
                for j in range(8):
                    i = g * 8 + j
                    for k in range(3):
                        P.op("pe", lambda e, i=i, j=j, k=k, pb=pb: e.matmul(cx.ps[pb][:, j * 64:(j + 1) * 64], lhsT=cqnT[:, k, i * 128:(i + 1) * 128],
                                                                            rhs=wuq[:, k, h * 192 + 128:h * 192 + 192], start=(k == 0), stop=(k == 2)),
                             reads=[r_w, all_lat[i]], writes=[cx.rps[pb]])
                rope_apply(cx, qrb[:], cx.ps[pb][:, :].rearrange("p (a b) -> p a b", a=8), cosT[:, g * 8:(g + 1) * 8, :], sinT[:, g * 8:(g + 1) * 8, :],
                           rt1[:], rt2[:], 8, cx.rps[pb], r_tab, r_rt, r_qrb)
                pst = cx.ps[6 + g][:].bitcast(BF16)
                for j in range(8):
                    P.op("pe", lambda e, j=j: e.transpose(out=pst[0:64, j * 128:(j + 1) * 128], in_=qrb[:, j, :], identity=cx.identb[:]),
                         reads=[r_qrb, cx.rcst], writes=[cx.rps[6 + g]])
                P.op("act", lambda e, g=g: e.activation(out=qrT[:, g * 1024:(g + 1) * 1024], in_=pst[0:64, :], func=AF.Copy),
                     reads=[cx.rps[6 + g]], writes=[r_qrT])
            def emit_S(qb):
                base = 4 * (qb % 2)
                qs = slice(qb * 128, (qb + 1) * 128)
                for tb in range(4):
                    P.op("pe", lambda e, tb=tb: e.matmul(cx.ps[base + tb][:, :], lhsT=qnT[:, qs], rhs=knT[:, tb * 512:(tb + 1) * 512], start=True, stop=False),
                         reads=[r_qnT, r_knT], writes=[cx.rps[base + tb]])
                    P.op("pe", lambda e, tb=tb: e.matmul(cx.ps[base + tb][:, :], lhsT=qrT[:, qs], rhs=kropeT[:, tb * 512:(tb + 1) * 512], start=False, stop=True),
                         reads=[r_qrT, r_krT], writes=[cx.rps[base + tb]])

            def emit_softmax(qb):
                base = 4 * (qb % 2)
                m = sm[qb % 8]
                r_m = r_sm[qb % 8]
                for tb in range(4):
                    P.op("dve", lambda e, tb=tb: e.tensor_reduce(out=m[:, tb:tb + 1], in_=cx.ps[base + tb][:, :], axis=AX.X, op=ALU.max),
                         reads=[cx.rps[base + tb]], writes=[r_m])
                P.op("dve", lambda e: e.tensor_reduce(out=m[:, 0:1], in_=m[:, 0:4], axis=AX.X, op=ALU.max), reads=[r_m], writes=[r_m])
                P.op("dve", lambda e: e.tensor_scalar(out=m[:, 1:2], in0=m[:, 0:1], scalar1=-MLA_SCALE, scalar2=None, op0=ALU.mult), reads=[r_m], writes=[r_m])
                for tb in range(4):
                    P.op("act", lambda e, tb=tb: e.activation(out=Pb[:, tb * 512:(tb + 1) * 512], in_=cx.ps[base + tb][:, :], func=AF.Exp, scale=MLA_SCALE,
                                                              bias=m[:, 1:2], accum_out=m[:, 4 + tb:5 + tb]), reads=[cx.rps[base + tb], r_m], writes=[r_Pb, r_m])
                P.op("dve", lambda e: e.tensor_reduce(out=m[:, 2:3], in_=m[:, 4:8], axis=AX.X, op=ALU.add), reads=[r_m], writes=[r_m])
                P.op("dve", lambda e: e.reciprocal(out=m[:, 3:4], in_=m[:, 2:3]), reads=[r_m], writes=[r_m])

            def emit_rest(qb):
                base = 4 * (qb % 2)
                qi = qb % 4
                for g in range(2):
                    pst = cx.ps[base + g][:].bitcast(BF16)
                    for j in range(8):
                        c = g * 8 + j
                        P.op("pe", lambda e, c=c, j=j: e.transpose(out=pst[:, j * 128:(j + 1) * 128], in_=Pb[:, c * 128:(c + 1) * 128], identity=cx.identb[:]),
                             reads=[r_Pb, cx.rcst], writes=[cx.rps[base + g]])
                    P.op("act", lambda e, g=g: e.activation(out=PTs4[:, g * 8:(g + 1) * 8, qi * 128:(qi + 1) * 128], in_=pst.rearrange("p (a b) -> p a b", a=8), func=AF.Copy),
                         reads=[cx.rps[base + g]], writes=[r_PTs])
                if qi != 3:
                    return
                for c in range(NT):
                    P.op("pe", lambda e, c=c: e.matmul(cx.ps[base + 2][:, :], lhsT=vh[:, c, :], rhs=PTs4[:, c, :], start=(c == 0), stop=(c == NT - 1)),
                         reads=[r_vh, r_PTs], writes=[cx.rps[base + 2]])
                P.op("act", lambda e: e.activation(out=oT4[:], in_=cx.ps[base + 2][:, :], func=AF.Copy), reads=[cx.rps[base + 2]], writes=[r_oT])
                for q2 in range(4):
                    qq = qb - 3 + q2
                    m = sm[qq % 8]
                    r_m = r_sm[qq % 8]
                    for hh in range(2):
                        pb = base + 3 - hh
                        P.op("pe", lambda e, hh=hh, pb=pb, q2=q2: e.matmul(cx.ps[pb][:, :], lhsT=oT4[:, q2 * 128:(q2 + 1) * 128], rhs=wo[:, hh * 512:(hh + 1) * 512], start=True, stop=True),
                             reads=[r_oT, r_wo], writes=[cx.rps[pb]])
                        P.op("dve", lambda e, hh=hh, pb=pb, qq=qq, m=m: e.scalar_tensor_tensor(out=cx.R[:, qq, hh * 512:(hh + 1) * 512], in0=cx.ps[pb][:, :], scalar=m[:, 3:4],
                                                                                       in1=cx.R[:, qq, hh * 512:(hh + 1) * 512], op0=ALU.mult, op1=ALU.add),
                             reads=[cx.rps[pb], r_m, cx.rR[qq]], writes=[cx.rR[qq]])

            emit_S(0)
            for qb in range(NT):
                if qb + 1 < NT:
                    emit_S(qb + 1)
                emit_softmax(qb)
                emit_rest(qb)
    P.barrier()


LN_QSCALE = float(np.log(128.0 ** -0.5))


def emit_linattn(cx, kind):
    P = cx.P
    nc = cx.nc
    A = cx.A
    XT = xt_view(cx)
    ml = (kind == "mlstm")
    pre = "mlstm" if ml else "gla"
    W_in = A[pre + "_w_in"][0]
    DV1 = 257 if ml else 256
    cs = 1.0 if ml else 1.0 / 16.0
    with contextlib.ExitStack() as st:
        sb = lambda name, shape, dt: st.enter_context(nc.sbuf_tensor(un(name), shape, dt))
        ng = sb("ng", [128, 256], F32)
        r_ng = Reg()
        r_gw = Reg()
        if ml:
            wgates = sb("wgates", [128, NK, 16], BF16)
            P.dma("pool", wgates[:], W_in[:, 3072:3088].rearrange("(k p) c -> p k c", p=128), writes=[r_gw])
            gbias = sb("gbias", [128, 16], F32)
            P.dma("sp", gbias[:], A["mlstm_gate_b"][0].rearrange("a b c -> (a b c)").partition_broadcast(128), writes=[r_gw])
            ngbias = sb("ngbias", [128, 16], F32)
            P.op("dve", lambda e: e.tensor_scalar(out=ngbias[:], in0=gbias[:], scalar1=-1.0, scalar2=None, op0=ALU.mult), reads=[r_gw], writes=[r_gw])
            wrep = [sb("wrep", [128, NK, 128], BF16) for j in range(2)]
            r_wrep = [Reg(), Reg()]
            igb = sb("igb", [128, S], F32)
            r_igb = Reg()
        else:
            wglr = sb("wglr", [128, NK, 32], BF16)
            P.dma("pool", wglr[:], W_in[:, 3072:3104].rearrange("(k p) c -> p k c", p=128), writes=[r_gw])
            gwpad = sb("gwpad", [32, 2, 512], BF16)
            P.op("pool", lambda e: e.memset(gwpad[:], 0.0), writes=[r_gw])
            for d in range(2):
                P.dma("pool", gwpad[d * 16:(d + 1) * 16, d, :], A["gla_gate_w"][0, d], writes=[r_gw])
            ngb = sb("ngb", [128, 2, 4], F32)
            for d in range(2):
                small_load(cx, ngb[:, d, :], A["gla_gate_b"][0, d].rearrange("(h p) -> p h", p=128), r_gw)
            P.op("dve", lambda e: e.tensor_scalar(out=ngb[:], in0=ngb[:], scalar1=-1.0, scalar2=None, op0=ALU.mult), reads=[r_gw], writes=[r_gw])
            glrT = sb("glrT", [32, S], BF16)
            r_glrT = Reg()
            for tb in range(4):
                for k in range(NK):
                    P.op("pe", lambda e, tb=tb, k=k: e.matmul(cx.ps[tb][0:32, :], lhsT=wglr[:, k, :], rhs=XT[:, k, tb * 512:(tb + 1) * 512],
                                                              start=(k == 0), stop=(k == NK - 1)), reads=[r_gw] + cx.rXA[tb * 4:(tb + 1) * 4], writes=[cx.rps[tb]])
                P.op("act", lambda e, tb=tb: e.activation(out=glrT[:, tb * 512:(tb + 1) * 512], in_=cx.ps[tb][0:32, :], func=AF.Copy),
                     reads=[cx.rps[tb]], writes=[r_glrT])

        WA = sb("WA", [128, NK, 256], BF16)
        WB = sb("WB", [128, NK, 256], BF16)
        r_WA, r_WB = Reg(), Reg()
        wq = WA[:, :, 0:128]
        wk = WA[:, :, 128:256]
        wo = WA[:].rearrange("p k c -> p (k c)").rearrange("p (j f) -> p j f", j=2)
        wv = WB
        wor = WB
        r_wq, r_wk, r_wv, r_wor, r_wo = r_WA, r_WA, r_WB, r_WB, r_WA
        qT = sb("qT", [128, S], F32)
        kT = sb("kT", [128, S], F32)
        r_qT, r_kT = Reg(), Reg()
        vb = sb("vb", [128, NT, DV1], BF16)
        r_vb = Reg()
        buf1 = sb("buf1", [128, S], F32)
        buf2 = sb("buf2", [128, S], F32)
        r_b1, r_b2 = Reg(), Reg()
        qe = sb("qe", [128, S], BF16)
        ke = sb("ke", [128, S], BF16)
        kgT = sb("kgT", [128, NT, 128], BF16)
        r_qe, r_ke, r_kgT = Reg(), Reg(), Reg()
        Oacc = sb("Oacc", [128, NT, 256], F32)
        r_O = [Reg() for i in range(NT)]
        offs = sb("offs", [128, NT], F32)
        gsv = sb("gsv", [128, NT], F32)
        eg = sb("eg", [128, NT], F32)
        r_small = Reg()
        Sst = sb("Sst", [128, DV1], F32)
        Sb = sb("Sb", [128, DV1], BF16)
        r_S, r_Sb = Reg(), Reg()
        atm = [sb("atm", [128, 128], BF16) for j in range(2)]
        r_atm = [Reg(), Reg()]
        dn = [sb("dn", [128, 4], F32) for j in range(2)]
        r_dn = [Reg(), Reg()]
        hsall = sb("hsall", [128, NT, 10], F32)
        r_hsall = Reg()
        gact = [sb("gact", [128, 256], F32) for j in range(2)]
        r_gact = [Reg(), Reg()]
        xn = [sb("xn", [128, 256], F32) for j in range(2)]
        r_xn = [Reg(), Reg()]
        gbt = [sb("gbt", [128, 256], BF16) for j in range(2)]
        r_gbt = [Reg(), Reg()]
        gTt = [sb("gTt", [128, 2, 128], BF16) for j in range(2)]
        r_gTt = [Reg(), Reg()]
        P.op("pool", lambda e: e.memset(offs[:, 0:1], 0.0), writes=[r_small])
        if ml:
            P.op("pool", lambda e: e.memset(vb[:, :, 256:257], 1.0), writes=[r_vb])

        b1v = buf1[:].rearrange("p (c j) -> p c j", j=128)
        b2v = buf2[:].rearrange("p (c j) -> p c j", j=128)
        allXA = cx.rXA

        def wload(dst, reg, c0, c1):
            P.dma("pool", dst, W_in[:, c0:c1].rearrange("(k p) c -> p k c", p=128), writes=[reg])

        for h in range(4):
            wload(wq, r_wq, h * 128, (h + 1) * 128)
            wload(wk, r_wk, 512 + h * 128, 512 + (h + 1) * 128)
            wload(wv[:], r_wv, 1024 + h * 256, 1024 + (h + 1) * 256)
            P.dma("sp", ng[:], A[pre + "_norm_g"][0][h * 256:(h + 1) * 256].partition_broadcast(128), writes=[r_ng])
            for (wsrc, r_w, dst, r_dst, base) in ((wq, r_wq, qT, r_qT, 0), (wk, r_wk, kT, r_kT, 4)):
                for tb in range(4):
                    pb = base + tb
                    for k in range(NK):
                        P.op("pe", lambda e, k=k, pb=pb, tb=tb, wsrc=wsrc: e.matmul(cx.ps[pb][:, :], lhsT=wsrc[:, k, :], rhs=XT[:, k, tb * 512:(tb + 1) * 512],
                                                                                    start=(k == 0), stop=(k == NK - 1)),
                             reads=[r_w] + allXA[tb * 4:(tb + 1) * 4], writes=[cx.rps[pb]])
                    eng = "act" if base == 0 else "dve"
                    if eng == "act":
                        P.op("act", lambda e, pb=pb, tb=tb, dst=dst: e.activation(out=dst[:, tb * 512:(tb + 1) * 512], in_=cx.ps[pb][:, :], func=AF.Copy),
                             reads=[cx.rps[pb]], writes=[r_dst])
                    else:
                        P.op("dve", lambda e, pb=pb, tb=tb, dst=dst: e.tensor_copy(out=dst[:, tb * 512:(tb + 1) * 512], in_=cx.ps[pb][:, :]),
                             reads=[cx.rps[pb]], writes=[r_dst])
            for i in range(NT):
                pb = (i // 2) % 8
                off = (i % 2) * 256
                for k in range(NK):
                    P.op("pe", lambda e, k=k, pb=pb, off=off, i=i: e.matmul(cx.ps[pb][:, off:off + 256], lhsT=XT[:, k, i * 128:(i + 1) * 128], rhs=wv[:, k, :],
                                                                            start=(k == 0), stop=(k == NK - 1)), reads=[r_wv, allXA[i]], writes=[cx.rps[pb]])
                P.op("act", lambda e, pb=pb, off=off, i=i: e.activation(out=vb[:, i, 0:256], in_=cx.ps[pb][:, off:off + 256], func=AF.Copy),
                     reads=[cx.rps[pb]], writes=[r_vb])
            wload(wor[:], r_wor, 2048 + h * 256, 2048 + (h + 1) * 256)
            P.dma("pool", wo, A[pre + "_w_out"][0][h * 256:(h + 1) * 256, :].rearrange("(j p) f -> p j f", p=128), writes=[r_wo])
            for d in range(2):
                if ml:
                    jf = d * 8 + 4 + h
                    ji = d * 8 + h
                    for jj, gidx in enumerate((jf, ji)):
                        P.op("dve", lambda e, jj=jj, gidx=gidx: e.tensor_copy(out=wrep[jj][:], in_=wgates[:, :, gidx:gidx + 1].to_broadcast([128, NK, 128])),
                             reads=[r_gw], writes=[r_wrep[jj]])
                    for tb in range(4):
                        for k in range(NK):
                            P.op("pe", lambda e, k=k, tb=tb: e.matmul(cx.ps[tb][:, :], lhsT=wrep[0][:, k, :], rhs=XT[:, k, tb * 512:(tb + 1) * 512],
                                                                      start=(k == 0), stop=(k == NK - 1)), reads=[r_wrep[0]] + allXA[tb * 4:(tb + 1) * 4], writes=[cx.rps[tb]])
                        P.op("act", lambda e, tb=tb: e.activation(out=buf1[:, tb * 512:(tb + 1) * 512], in_=cx.ps[tb][:, :], func=AF.Exp, scale=-1.0,
                                                                  bias=ngbias[:, jf:jf + 1]), reads=[cx.rps[tb], r_gw], writes=[r_b1])
                    for tb in range(4):
                        for k in range(NK):
                            P.op("pe", lambda e, k=k, tb=tb: e.matmul(cx.ps[4 + tb][:, :], lhsT=wrep[1][:, k, :], rhs=XT[:, k, tb * 512:(tb + 1) * 512],
                                                                      start=(k == 0), stop=(k == NK - 1)), reads=[r_wrep[1]] + allXA[tb * 4:(tb + 1) * 4], writes=[cx.rps[4 + tb]])
                        P.op("act", lambda e, tb=tb: e.activation(out=igb[:, tb * 512:(tb + 1) * 512], in_=cx.ps[4 + tb][:, :], func=AF.Identity,
                                                                  bias=gbias[:, ji:ji + 1]), reads=[cx.rps[4 + tb], r_gw], writes=[r_igb])
                else:
                    for tb in range(4):
                        P.op("pe", lambda e, tb=tb: e.matmul(cx.ps[tb][:, :], lhsT=gwpad[:, d, h * 128:(h + 1) * 128], rhs=glrT[:, tb * 512:(tb + 1) * 512],
                                                             start=True, stop=True), reads=[r_gw, r_glrT], writes=[cx.rps[tb]])
                        P.op("act", lambda e, tb=tb: e.activation(out=buf1[:, tb * 512:(tb + 1) * 512], in_=cx.ps[tb][:, :], func=AF.Exp, scale=-1.0,
                                                                  bias=ngb[:, d, h:h + 1]), reads=[cx.rps[tb], r_gw], writes=[r_b1])
                P.op("act", lambda e: e.activation(out=buf1[:], in_=buf1[:], func=AF.Ln, bias=1.0), reads=[r_b1], writes=[r_b1])
                P.op("dve", lambda e: e.tensor_tensor_scan(out=buf2[:], data0=cx.cst[:, C_ONES:C_ONES + 1].to_broadcast([128, S]), data1=buf1[:], initial=0.0,
                                                            op0=ALU.mult, op1=ALU.add), reads=[r_b1, cx.rcst], writes=[r_b2])
                P.op("dve", lambda e: e.tensor_copy(out=offs[:, 1:NT], in_=b2v[:, 0:NT - 1, 127]), reads=[r_b2], writes=[r_small])
                P.op("dve", lambda e: e.tensor_tensor(out=b2v, in0=b2v, in1=offs[:, :, None].to_broadcast([128, NT, 128]), op=ALU.subtract),
                     reads=[r_b2, r_small], writes=[r_b2])
                P.op("dve", lambda e: e.tensor_copy(out=gsv[:], in_=b2v[:, :, 127]), reads=[r_b2], writes=[r_small])
                P.op("act", lambda e: e.activation(out=eg[:], in_=gsv[:], func=AF.Exp, scale=-cs), reads=[r_small], writes=[r_small])
                gs_bc = gsv[:, :, None].to_broadcast([128, NT, 128])
                if d == 1:
                    P.op("dve", lambda e: e.scalar_tensor_tensor(out=buf2[:], in0=buf2[:], scalar=-1.0, in1=buf1[:], op0=ALU.mult, op1=ALU.add),
                         reads=[r_b1, r_b2], writes=[r_b2])
                    P.op("dve", lambda e: e.tensor_tensor(out=b2v, in0=b2v, in1=gs_bc, op=ALU.add), reads=[r_b2, r_small], writes=[r_b2])
                P.op("dve", lambda e: e.scalar_tensor_tensor(out=b1v, in0=b2v, scalar=-1.0, in1=gs_bc, op0=ALU.mult, op1=ALU.add),
                     reads=[r_b2, r_small], writes=[r_b1])
                if ml:
                    P.op("dve", lambda e: e.scalar_tensor_tensor(out=buf1[:], in0=buf1[:], scalar=-cs, in1=igb[:], op0=ALU.mult, op1=ALU.add),
                         reads=[r_b1, r_igb], writes=[r_b1])
                    P.op("act", lambda e: e.activation(out=buf1[:], in_=buf1[:], func=AF.Exp), reads=[r_b1], writes=[r_b1])
                else:
                    P.op("act", lambda e: e.activation(out=buf1[:], in_=buf1[:], func=AF.Exp, scale=-cs), reads=[r_b1], writes=[r_b1])
                P.op("dve", lambda e: e.tensor_tensor(out=ke[:], in0=kT[:], in1=buf1[:], op=ALU.mult), reads=[r_kT, r_b1], writes=[r_ke])
                for g in range(2):
                    pst = cx.ps[6 + g][:].bitcast(BF16)
                    for j in range(8):
                        c = g * 8 + j
                        P.op("pe", lambda e, c=c, j=j: e.transpose(out=pst[:, j * 128:(j + 1) * 128], in_=ke[:, c * 128:(c + 1) * 128], identity=cx.identb[:]),
                             reads=[r_ke, cx.rcst], writes=[cx.rps[6 + g]])
                    P.op("act", lambda e, g=g: e.activation(out=kgT[:, g * 8:(g + 1) * 8, :], in_=pst.rearrange("p (a b) -> p a b", a=8), func=AF.Copy),
                         reads=[cx.rps[6 + g]], writes=[r_kgT])
                if ml:
                    P.op("dve", lambda e: e.scalar_tensor_tensor(out=buf1[:], in0=buf2[:], scalar=cs, in1=igb[:], op0=ALU.mult, op1=ALU.add),
                         reads=[r_b2, r_igb], writes=[r_b1])
                    P.op("act", lambda e: e.activation(out=buf1[:], in_=buf1[:], func=AF.Exp), reads=[r_b1], writes=[r_b1])
                else:
                    P.op("act", lambda e: e.activation(out=buf1[:], in_=buf2[:], func=AF.Exp, scale=cs), reads=[r_b2], writes=[r_b1])
                P.op("dve", lambda e: e.tensor_tensor(out=ke[:], in0=kT[:], in1=buf1[:], op=ALU.mult), reads=[r_kT, r_b1], writes=[r_ke])
                P.op("act", lambda e: e.activation(out=buf2[:], in_=buf2[:], func=AF.Exp, scale=-cs, bias=cx.cst[:, C_LNQ:C_LNQ + 1]), reads=[r_b2, cx.rcst], writes=[r_b2])
                P.op("dve", lambda e: e.tensor_tensor(out=qe[:], in0=qT[:], in1=buf2[:], op=ALU.mult), reads=[r_qT, r_b2], writes=[r_qe])
                order = list(range(NT)) if d == 0 else list(range(NT - 1, -1, -1))
                mcol = C_TRIU if d == 0 else C_TRIL
                for n_, c in enumerate(order):
                    csl = slice(c * 128, (c + 1) * 128)
                    pa = n_ % 2
                    po = 2 + (n_ % 2)
                    pss = 4 + (n_ % 2)
                    first = (n_ == 0)
                    last = (n_ == NT - 1)
                    P.op("pe", lambda e: e.matmul(cx.ps[pa][:, 0:128], lhsT=ke[:, csl], rhs=qe[:, csl], start=True, stop=True),
                         reads=[r_ke, r_qe], writes=[cx.rps[pa]])
                    am = atm[n_ % 2]
                    P.op("dve", lambda e: e.tensor_tensor(out=am[:], in0=cx.cst[:, mcol:mcol + 128], in1=cx.ps[pa][:, 0:128], op=ALU.mult),
                         reads=[cx.rps[pa], cx.rcst], writes=[r_atm[n_ % 2]])
                    P.op("pe", lambda e: e.matmul(cx.ps[po][:, 0:DV1], lhsT=am[:], rhs=vb[:, c, :], start=True, stop=first),
                         reads=[r_atm[n_ % 2], r_vb], writes=[cx.rps[po]])
                    if not first:
                        P.op("pe", lambda e: e.matmul(cx.ps[po][:, 0:DV1], lhsT=qe[:, csl], rhs=Sb[:], start=False, stop=True),
                             reads=[r_qe, r_Sb], writes=[cx.rps[po]])
                    if not last:
                        P.op("pe", lambda e: e.matmul(cx.ps[pss][:, 0:DV1], lhsT=kgT[:, c, :], rhs=vb[:, c, :], start=True, stop=True),
                             reads=[r_kgT, r_vb], writes=[cx.rps[pss]])
                        if first:
                            P.op("dve", lambda e: e.tensor_copy(out=Sst[:], in_=cx.ps[pss][:, 0:DV1]), reads=[cx.rps[pss]], writes=[r_S])
                        else:
                            P.op("dve", lambda e: e.scalar_tensor_tensor(out=Sst[:], in0=Sst[:], scalar=eg[:, c:c + 1], in1=cx.ps[pss][:, 0:DV1],
                                                                          op0=ALU.mult, op1=ALU.add), reads=[cx.rps[pss], r_S, r_small], writes=[r_S])
                        P.op("act", lambda e: e.activation(out=Sb[:], in_=Sst[:], func=AF.Copy), reads=[r_S], writes=[r_Sb])
                    oc = Oacc[:, c, :]
                    if ml:
                        dd = dn[n_ % 2]
                        r_dd = r_dn[n_ % 2]
                        P.op("act", lambda e: e.activation(out=dd[:, 0:1], in_=cx.ps[po][:, 256:257], func=AF.Abs), reads=[cx.rps[po]], writes=[r_dd])
                        P.op("dve", lambda e: e.tensor_scalar(out=dd[:, 0:1], in0=dd[:, 0:1], scalar1=1.0, scalar2=None, op0=ALU.max), reads=[r_dd], writes=[r_dd])
                        P.op("dve", lambda e: e.reciprocal(out=dd[:, 1:2], in_=dd[:, 0:1]), reads=[r_dd], writes=[r_dd])
                        if d == 0:
                            P.op("act", lambda e: e.activation(out=oc, in_=cx.ps[po][:, 0:256], func=AF.Copy, scale=dd[:, 1:2]),
                                 reads=[cx.rps[po], r_dd], writes=[r_O[c]])
                        else:
                            P.op("dve", lambda e: e.scalar_tensor_tensor(out=oc, in0=cx.ps[po][:, 0:256], scalar=dd[:, 1:2], in1=oc, op0=ALU.mult, op1=ALU.add),
                                 reads=[cx.rps[po], r_dd, r_O[c]], writes=[r_O[c]])
                    else:
                        if d == 0:
                            P.op("act", lambda e: e.activation(out=oc, in_=cx.ps[po][:, 0:256], func=AF.Copy), reads=[cx.rps[po]], writes=[r_O[c]])
                        else:
                            P.op("dve", lambda e: e.tensor_tensor(out=oc, in0=oc, in1=cx.ps[po][:, 0:256], op=ALU.add), reads=[cx.rps[po], r_O[c]], writes=[r_O[c]])
            for i in range(NT):
                P.op("dve", lambda e, i=i: e.bn_stats(out=hsall[:, i, 0:6], in_=Oacc[:, i, :]), reads=[r_O[i]], writes=[r_hsall])
            for i in range(NT):
                P.op("dve", lambda e, i=i: e.bn_aggr(out=hsall[:, i, 6:8], in_=hsall[:, i, 0:6]), reads=[r_hsall], writes=[r_hsall])
            P.op("dve", lambda e: e.tensor_scalar(out=hsall[:, :, 8], in0=hsall[:, :, 7], scalar1=LN_EPS, scalar2=None, op0=ALU.add), reads=[r_hsall], writes=[r_hsall])
            P.op("act", lambda e: e.activation(out=hsall[:, :, 8], in_=hsall[:, :, 8], func=AF.Sqrt), reads=[r_hsall], writes=[r_hsall])
            P.op("dve", lambda e: e.reciprocal(out=hsall[:, :, 9], in_=hsall[:, :, 8]), reads=[r_hsall], writes=[r_hsall])
            for i in range(NT):
                j2 = i % 2
                hs = hsall[:, i, :]
                r_hs = r_hsall
                oc = Oacc[:, i, :]
                x_ = xn[j2]
                P.op("dve", lambda e: e.tensor_scalar(out=x_[:], in0=oc, scalar1=hs[:, 6:7], scalar2=hs[:, 9:10], op0=ALU.subtract, op1=ALU.mult),
                     reads=[r_O[i], r_hs], writes=[r_xn[j2]])
                P.op("pool", lambda e: e.tensor_tensor(out=x_[:], in0=x_[:], in1=ng[:], op=ALU.mult), reads=[r_xn[j2], r_ng], writes=[r_xn[j2]])
                pg = 6 + j2
                for k in range(NK):
                    P.op("pe", lambda e, k=k: e.matmul(cx.ps[pg][:, 0:256], lhsT=XT[:, k, i * 128:(i + 1) * 128], rhs=wor[:, k, :],
                                                       start=(k == 0), stop=(k == NK - 1)), reads=[r_wor, allXA[i]], writes=[cx.rps[pg]])
                ga = gact[j2]
                P.op("act", lambda e: e.activation(out=ga[:], in_=cx.ps[pg][:, 0:256], func=(AF.Sigmoid if ml else AF.Silu)),
                     reads=[cx.rps[pg]], writes=[r_gact[j2]])
                gb_ = gbt[j2]
                P.op("dve", lambda e: e.tensor_tensor(out=gb_[:], in0=x_[:], in1=ga[:], op=ALU.mult), reads=[r_xn[j2], r_gact[j2]], writes=[r_gbt[j2]])
                pt = 4 + j2
                pst = cx.ps[pt][:].bitcast(BF16)
                for j in range(2):
                    P.op("pe", lambda e, j=j: e.transpose(out=pst[:, j * 128:(j + 1) * 128], in_=gb_[:, j * 128:(j + 1) * 128], identity=cx.identb[:]),
                         reads=[r_gbt[j2], cx.rcst], writes=[cx.rps[pt]])
                gT_ = gTt[j2]
                P.op("act", lambda e: e.activation(out=gT_[:], in_=pst[:, 0:256].rearrange("p (a b) -> p a b", a=2), func=AF.Copy),
                     reads=[cx.rps[pt]], writes=[r_gTt[j2]])
                for hh in range(2):
                    pb = (i * 2 + hh) % 4
                    for j in range(2):
                        P.op("pe", lambda e, j=j, hh=hh, pb=pb: e.matmul(cx.ps[pb][:, :], lhsT=gT_[:, j, :], rhs=wo[:, j, hh * 512:(hh + 1) * 512],
                                                                         start=(j == 0), stop=(j == 1)), reads=[r_gTt[j2], r_wo], writes=[cx.rps[pb]])
                    P.op("dve", lambda e, hh=hh, pb=pb: e.tensor_tensor(out=cx.R[:, i, hh * 512:(hh + 1) * 512], in0=cx.R[:, i, hh * 512:(hh + 1) * 512],
                                                                        in1=cx.ps[pb][:, :], op=ALU.add), reads=[cx.rps[pb], cx.rR[i]], writes=[cx.rR[i]])
    P.barrier()


def declare_inputs(nc, nseq, names_shapes):
    aps = {}
    for name, shape, dt in names_shapes:
        aps[name] = nc.dram_tensor(name, list(shape), dt, kind="ExternalInput").ap()
    return aps


WEIGHT_SPECS = [
    ("mlstm_w_in", (1, 1024, 3088)), ("mlstm_gate_b", (1, 2, 2, 4)), ("mlstm_norm_g", (1, 1024)), ("mlstm_w_out", (1, 1024, 1024)),
    ("gla_w_in", (1, 1024, 3104)), ("gla_gate_w", (1, 2, 16, 512)), ("gla_gate_b", (1, 2, 512)), ("gla_norm_g", (1, 1024)),
    ("gla_w_out", (1, 1024, 1024)),
    ("lru_w_in", (1, 1024, 2048)), ("lru_conv_w", (1, 4, 1024)), ("lru_conv_b", (1, 1024)),
    ("lru_gate_a_w", (1, 2, 4, 256, 256)), ("lru_gate_a_b", (1, 2, 1024)), ("lru_gate_x_w", (1, 2, 4, 256, 256)),
    ("lru_gate_x_b", (1, 2, 1024)), ("lru_lambda", (1, 2, 1024)), ("lru_w_out", (1, 1024, 1024)),
    ("mla_w_in", (1, 1024, 704)), ("mla_q_norm_g", (1, 384)), ("mla_kv_norm_g", (1, 256)), ("mla_w_uq", (1, 384, 1536)),
    ("mla_w_ukv", (1, 256, 2048)), ("mla_w_out", (1, 1024, 1024)),
    ("moe_router", (4, 1024, 16)), ("moe_w_gate", (4, 16, 1024, 1024)), ("moe_w_up", (4, 16, 1024, 1024)),
    ("moe_w_down", (4, 16, 1024, 1024)), ("ln_g", (4, 2, 1024)), ("ln_b", (4, 2, 1024)),
]


def build_program(nseq, stages):
    nc = bass.Bass("TRN2", target_bir_lowering=False)
    specs = [("x", (nseq, S, D), F32), ("positions", (nseq, S), I32)]
    specs += [(n, s, F32) for n, s in WEIGHT_SPECS]
    specs += [("consts", (128, C_END), F32), ("consts16", (16, 16 * 128), F32)]
    A = declare_inputs(nc, nseq, specs)
    out = nc.dram_tensor("out", [nseq, S, D], F32, kind="ExternalOutput").ap()
    xd = nc.dram_tensor("xd_scratch", [S, D], BF16, kind="ExternalOutput").ap()
    with contextlib.ExitStack() as stack:
        P = Prog(nc, stack)
        cx = setup_ctx(nc, stack, P)
        cx.A = A
        cx.xd = xd
        cx.yd = nc.dram_tensor("yd_scratch", [S, D], F32, kind="ExternalOutput").ap()
        cx.r_yd = Reg()
        cx.r_xd = [Reg() for i in range(NT)]
        load_consts(cx, A["consts"], A["consts16"])
        wr = stack.enter_context(nc.sbuf_tensor("wr", [128, NK, NE], BF16))
        rwr = Reg()
        lnscr = None
        for s in range(nseq):
            for stg in stages:
                kind = stg[0]
                if kind == "load":
                    xin = A["x"][s].rearrange("(i p) d -> p i d", p=128)
                    for i in range(NT):
                        P.dma("sp", cx.R[:, i, :], xin[:, i, :], writes=[cx.rR[i]])
                elif kind == "ln":
                    _, L, j, mode = stg
                    if mode == "xb":
                        P.dma("pool", wr[:], A["moe_router"][L].rearrange("(k p) e -> p k e", p=128), writes=[rwr])
                    emit_ln(cx, A["ln_g"][L, j], A["ln_b"][L, j], mode, wr=wr, rwr=rwr, do_ln=True,
                            out_ap=out[s].rearrange("(i p) d -> p i d", p=128), scratch=lnscr)
                elif kind == "prep":
                    _, L, mode = stg
                    if mode == "xb":
                        P.dma("pool", wr[:], A["moe_router"][L].rearrange("(k p) e -> p k e", p=128), writes=[rwr])
                    emit_ln(cx, None, None, mode, wr=wr, rwr=rwr, do_ln=False, scratch=lnscr)
                elif kind == "moe":
                    _, L = stg
                    emit_moe(cx, A["moe_w_gate"][L], A["moe_w_up"][L], A["moe_w_down"][L])
                elif kind == "mla":
                    emit_mla(cx, s)
                elif kind in ("mlstm", "gla"):
                    emit_linattn(cx, kind)
                elif kind == "lru":
                    emit_lru(cx)
                elif kind == "store":
                    oo = out[s].rearrange("(i p) d -> p i d", p=128)
                    for i in range(NT):
                        P.dma("sp", oo[:, i, :], cx.R[:, i, :], reads=[cx.rR[i]])
                else:
                    raise ValueError(kind)
        P.barrier(engines=["sp"])
        print("instructions", P.ninstr, "waits", P.nwaits)
    return nc


MIXERS = ("mlstm", "gla", "lru", "mla")


def full_stages():
    st = [("load",), ("prep", 0, "xt")]
    for L in range(DEPTH):
        st.append((MIXERS[L % 4],))
        st.append(("ln", L, 0, "xb"))
        st.append(("moe", L))
        st.append(("ln", L, 1, "xt" if L < DEPTH - 1 else "none"))
    return st


NSEQ_PER_LAUNCH = 4
_PROG_CACHE = {}


def kernel(**inputs):
    nseq = NSEQ_PER_LAUNCH
    x = np.ascontiguousarray(np.asarray(inputs["x"], dtype=np.float32))
    pos = np.ascontiguousarray(np.asarray(inputs["positions"], dtype=np.int32))
    B = x.shape[0]
    per_core = B // NCORES
    consts = make_consts()
    consts16 = make_consts16()
    weights = {n: np.ascontiguousarray(np.asarray(inputs[n], dtype=np.float32)) for n, _ in WEIGHT_SPECS}
    out = np.empty((B, S, D), np.float32)
    for l0 in range(0, per_core, nseq):
        if nseq not in _PROG_CACHE:
            _PROG_CACHE[nseq] = build_program(nseq, full_stages())
        nc = _PROG_CACHE[nseq]
        in_maps = []
        for c in range(NCORES):
            b0 = c * per_core + l0
            m = {"x": x[b0:b0 + nseq], "positions": pos[b0:b0 + nseq], "consts": consts, "consts16": consts16}
            m.update(weights)
            in_maps.append(m)
        res = run_bass_kernel_spmd(nc, in_maps, core_ids=list(range(NCORES)))
        for c in range(NCORES):
            b0 = c * per_core + l0
            out[b0:b0 + nseq] = np.asarray(res.results[c]["out"]).reshape(nseq, S, D)
    return out
```

```python
import contextlib
import numpy as np
import concourse.bass as bass
import concourse.mybir as mybir
from concourse.bass_utils import run_bass_kernel_spmd

F32 = mybir.dt.float32
BF16 = mybir.dt.bfloat16
I32 = mybir.dt.int32
AF = mybir.ActivationFunctionType
ALU = mybir.AluOpType
AX = mybir.AxisListType

D = 1024
S = 2048
NT = 16
NK = 8
DEPTH = 4
ALPHA = (2 * DEPTH) ** 0.25
LN_EPS = 1e-5
NE = 16
CAP = 256
NCORES = 8
SEQ_PER_CORE = 4

SAME_ENGINE_RAW = True


_UN = [0]


def un(name):
    _UN[0] += 1
    return "%s_%d" % (name, _UN[0])


class Reg:
    __slots__ = ("w", "r", "name")

    def __init__(self, name=""):
        self.w = None
        self.r = {}
        self.name = name


class Prog:
    ENG = ("pe", "act", "dve", "pool", "sp")
    KDMA = 8

    def __init__(self, nc, stack):
        self.nc = nc
        self.eng = {"pe": nc.tensor, "act": nc.scalar, "dve": nc.vector, "pool": nc.gpsimd, "sp": nc.sync}
        self.sem = {}
        for e in self.ENG:
            self.sem[("c", e)] = stack.enter_context(nc.semaphore("s_" + e))
        self.dq = ("sp", "pool", "act")
        for q in self.dq:
            for k in range(self.KDMA):
                self.sem[("d", q, k)] = stack.enter_context(nc.semaphore("d_%s%d" % (q, k)))
        self.cnt = {e: 0 for e in self.ENG}
        self.dcnt = {q: 0 for q in self.dq}
        self.known = {e: {} for e in self.ENG}
        self.nwaits = 0
        self.ninstr = 0

    def _wait(self, eng, key, val):
        if self.known[eng].get(key, 0) >= val:
            return
        self.eng[eng].wait_ge(self.sem[key], val)
        self.known[eng][key] = val
        self.nwaits += 1

    def _deps(self, eng, reads, writes):
        me = ("c", eng)
        for r in reads:
            if r.w is not None:
                key, val = r.w
                if key == me:
                    if eng != "pe" and SAME_ENGINE_RAW:
                        self._wait(eng, key, val)
                else:
                    self._wait(eng, key, val)
        for w in writes:
            if w.w is not None:
                key, val = w.w
                if key != me:
                    self._wait(eng, key, val)
            for key, val in w.r.items():
                if key != me:
                    self._wait(eng, key, val)

    def _update(self, tok, reads, writes):
        key, val = tok
        for r in reads:
            if r.r.get(key, 0) < val:
                r.r[key] = val
        for w in writes:
            w.w = tok
            w.r = {}

    def op(self, eng, fn, reads=(), writes=()):
        self._deps(eng, reads, writes)
        ins = fn(self.eng[eng])
        self.cnt[eng] += 1
        self.ninstr += 1
        tok = (("c", eng), self.cnt[eng])
        ins.then_inc(self.sem[tok[0]], 1)
        self._update(tok, reads, writes)
        return tok

    def dma(self, q, out, in_, reads=(), writes=(), **kw):
        k = self.dcnt[q]
        slot = k % self.KDMA
        key = ("d", q, slot)
        if k >= self.KDMA:
            self._wait(q, key, 16 * (k // self.KDMA))
        self._deps(q, reads, writes)
        ins = self.eng[q].dma_start(out=out, in_=in_, **kw)
        tok = (key, 16 * (k // self.KDMA + 1))
        ins.then_inc(self.sem[key], 16)
        self.dcnt[q] = k + 1
        self.ninstr += 1
        self._update(tok, reads, writes)
        return tok

    def idma_gather(self, out, in_dram, idx_ap, reads=(), writes=()):
        q = "pool"
        k = self.dcnt[q]
        slot = k % self.KDMA
        key = ("d", q, slot)
        if k >= self.KDMA:
            self._wait(q, key, 16 * (k // self.KDMA))
        self._deps(q, reads, writes)
        ins = self.eng[q].indirect_dma_start(out=out, out_offset=None, in_=in_dram,
                                             in_offset=bass.IndirectOffsetOnAxis(ap=idx_ap, axis=0))
        tok = (key, 16 * (k // self.KDMA + 1))
        ins.then_inc(self.sem[key], 16)
        self.dcnt[q] = k + 1
        self.ninstr += 1
        self._update(tok, reads, writes)
        return tok

    def idma_scatter_add(self, out_dram, idx_ap, in_, reads=(), writes=(), after=()):
        q = "pool"
        k = self.dcnt[q]
        slot = k % self.KDMA
        key = ("d", q, slot)
        if k >= self.KDMA:
            self._wait(q, key, 16 * (k // self.KDMA))
        for akey, aval in after:
            self._wait(q, akey, aval)
        self._deps(q, reads, writes)
        ins = self.eng[q].indirect_dma_start(out=out_dram, out_offset=bass.IndirectOffsetOnAxis(ap=idx_ap, axis=0),
                                             in_=in_, in_offset=None, compute_op=ALU.add)
        tok = (key, 16 * (k // self.KDMA + 1))
        ins.then_inc(self.sem[key], 16)
        self.dcnt[q] = k + 1
        self.ninstr += 1
        self._update(tok, reads, writes)
        return tok

    def latest_tokens(self):
        toks = []
        for e in self.ENG:
            if self.cnt[e] > 0:
                toks.append((("c", e), self.cnt[e]))
        for q in self.dq:
            k = self.dcnt[q]
            for slot in range(self.KDMA):
                n = (k - slot + self.KDMA - 1) // self.KDMA
                if n > 0:
                    toks.append((("d", q, slot), 16 * n))
        return toks

    def barrier(self, engines=None):
        toks = self.latest_tokens()
        for e in (engines or self.ENG):
            for key, val in toks:
                if key != ("c", e):
                    self._wait(e, key, val)


C_IDENT = 0
C_IOTA = 128
C_PIDX = 384
C_TRIU = 386
C_TRIL = 514
C_ONES = 642
C_FREQ = 770
C_LNQ = 802
C_END = 804


def make_consts():
    c = np.zeros((128, C_END), np.float32)
    c[:, C_IDENT:C_IDENT + 128] = np.eye(128, dtype=np.float32)
    c[:, C_IOTA:C_IOTA + 256] = np.arange(256, dtype=np.float32)[None, :]
    c[:, C_PIDX] = np.arange(128)
    c[:, C_PIDX + 1] = np.arange(128) + 128
    s = np.arange(128)[:, None]
    t = np.arange(128)[None, :]
    c[:, C_TRIU:C_TRIU + 128] = (s <= t)
    c[:, C_TRIL:C_TRIL + 128] = (s >= t)
    c[:, C_ONES:C_ONES + 128] = 1.0
    c[:, C_LNQ] = np.float32(np.log(128.0 ** -0.5))
    c[:, C_FREQ:C_FREQ + 32] = (10000.0 ** (-np.arange(32, dtype=np.float32) / np.float32(32))).astype(np.float32)[None, :]
    return c


def make_consts16():
    c = np.zeros((16, 16 * 128), np.float32)
    for e in range(16):
        c[e, e * 128:(e + 1) * 128] = 1.0
    return c


class Ctx:
    pass


def setup_ctx(nc, stack, P):
    cx = Ctx()
    cx.nc = nc
    cx.P = P
    sb = lambda name, shape, dt: stack.enter_context(nc.sbuf_tensor(name, shape, dt))
    cx.R = sb("R", [128, NT, D], F32)
    cx.rR = [Reg("R%d" % i) for i in range(NT)]
    cx.XA = sb("XA", [128, NT * D], BF16)
    cx.rXA = [Reg("XA%d" % i) for i in range(NT)]
    cx.cst = sb("cst", [128, C_END], F32)
    cx.rcst = Reg("cst")
    cx.identb = sb("identb", [128, 128], BF16)
    cx.logits = sb("logits", [128, NT, NE], F32)
    cx.rlogits = [Reg("lg%d" % i) for i in range(NT)]
    cx.ps = []
    cx.rps = []
    for b in range(8):
        cx.ps.append(stack.enter_context(nc.psum_tensor("ps%d" % b, [128, 512], F32)))
        cx.rps.append(Reg("ps%d" % b))
    return cx


def load_consts(cx, consts_ap, consts16_ap):
    P = cx.P
    P.dma("sp", cx.cst[:], consts_ap[:, :], writes=[cx.rcst])
    P.op("dve", lambda e: e.tensor_copy(out=cx.identb[:], in_=cx.cst[:, C_IDENT:C_IDENT + 128]),
         reads=[cx.rcst], writes=[cx.rcst])


def emit_ln(cx, lng_ap, lnb_ap, mode, wr=None, rwr=None, do_ln=True, out_ap=None, scratch=None):
    P = cx.P
    nc = cx.nc
    with contextlib.ExitStack() as lst:
        _emit_ln(cx, lst, lng_ap, lnb_ap, mode, wr, rwr, do_ln, out_ap)
    P.barrier()


def _emit_ln(cx, lst, lng_ap, lnb_ap, mode, wr, rwr, do_ln, out_ap):
    P = cx.P
    nc = cx.nc
    cx.lng = lst.enter_context(nc.sbuf_tensor(un("lng"), [128, 2, D], F32))
    cx.rlng = Reg("lng")
    tmpT, rtmpT, stat, rstat, xbt, rxbt = alloc_ln_scratch(cx, lst)
    if do_ln:
        P.dma("sp", cx.lng[:, 0, :], lng_ap.partition_broadcast(128), writes=[cx.rlng])
        P.dma("sp", cx.lng[:, 1, :], lnb_ap.partition_broadcast(128), writes=[cx.rlng])
    for i in range(NT):
        Ri = cx.R[:, i, :]
        rRi = cx.rR[i]
        st = stat[i % 2]
        rst = rstat[i % 2]
        if do_ln:
            for h in range(2):
                P.op("dve", lambda e, h=h: e.bn_stats(out=st[:, h * 6:(h + 1) * 6], in_=Ri[:, h * 512:(h + 1) * 512]),
                     reads=[rRi], writes=[rst])
            P.op("dve", lambda e: e.bn_aggr(out=st[:, 12:14], in_=st[:, 0:12]), reads=[rst], writes=[rst])
            P.op("dve", lambda e: e.tensor_scalar(out=st[:, 14:15], in0=st[:, 13:14], scalar1=LN_EPS, scalar2=None, op0=ALU.add),
                 reads=[rst], writes=[rst])
            P.op("act", lambda e: e.activation(out=st[:, 14:15], in_=st[:, 14:15], func=AF.Sqrt), reads=[rst], writes=[rst])
            P.op("dve", lambda e: e.reciprocal(out=st[:, 15:16], in_=st[:, 14:15]), reads=[rst], writes=[rst])
            P.op("dve", lambda e: e.tensor_scalar(out=Ri, in0=Ri, scalar1=st[:, 12:13], scalar2=st[:, 15:16],
                                                   op0=ALU.subtract, op1=ALU.mult), reads=[rRi, rst], writes=[rRi])
            P.op("dve", lambda e: e.tensor_tensor(out=Ri, in0=Ri, in1=cx.lng[:, 0, :], op=ALU.mult),
                 reads=[rRi, cx.rlng], writes=[rRi])
            P.op("pool", lambda e: e.tensor_tensor(out=Ri, in0=Ri, in1=cx.lng[:, 1, :], op=ALU.add),
                 reads=[rRi, cx.rlng], writes=[rRi])
        if mode == "none":
            if out_ap is not None:
                P.dma("sp", out_ap[:, i, :], Ri, reads=[rRi])
            continue
        if mode == "xb":
            xb = cx.XA[:, i * D:(i + 1) * D]
            rxb = cx.rXA[i]
        else:
            xb = xbt[i % 2][:]
            rxb = rxbt[i % 2]
        P.op("act", lambda e: e.activation(out=xb, in_=Ri, func=AF.Copy), reads=[rRi], writes=[rxb])
        P.op("pool", lambda e: e.tensor_scalar(out=Ri, in0=Ri, scalar1=float(ALPHA), scalar2=0.0, op0=ALU.mult, op1=ALU.add),
             reads=[rRi], writes=[rRi])
        pb = 6 + (i % 2)
        pst = cx.ps[pb][:].bitcast(BF16)
        for k in range(NK):
            P.op("pe", lambda e, k=k: e.transpose(out=pst[:, k * 128:(k + 1) * 128], in_=xb[:, k * 128:(k + 1) * 128],
                                                  identity=cx.identb[:]),
                 reads=[rxb, cx.rcst], writes=[cx.rps[pb]])
        if mode == "xt":
            xt = cx.XA[:].rearrange("p (k t) -> p k t", k=NK)
            P.op("dve", lambda e: e.tensor_copy(out=xt[:, :, i * 128:(i + 1) * 128],
                                                in_=pst.rearrange("p (k t) -> p k t", k=NK)),
                 reads=[cx.rps[pb]], writes=[cx.rXA[i]])
        else:
            tt = tmpT[i % 2]
            rtt = rtmpT[i % 2]
            P.op("dve", lambda e: e.tensor_copy(out=tt[:], in_=pst), reads=[cx.rps[pb]], writes=[rtt])
            lb = 4 + (i % 2)
            for k in range(NK):
                P.op("pe", lambda e, k=k: e.matmul(cx.ps[lb][:, 0:NE], lhsT=tt[:, k * 128:(k + 1) * 128], rhs=wr[:, k, :],
                                                   start=(k == 0), stop=(k == NK - 1)),
                     reads=[rtt, rwr], writes=[cx.rps[lb]])
            P.op("act", lambda e: e.activation(out=cx.logits[:, i, :], in_=cx.ps[lb][:, 0:NE], func=AF.Copy),
                 reads=[cx.rps[lb]], writes=[cx.rlogits[i]])


def alloc_ln_scratch(cx, stack):
    nc = cx.nc
    tmpT = [stack.enter_context(nc.sbuf_tensor(un("tmpT"), [128, D], BF16)) for j in range(2)]
    rtmpT = [Reg(), Reg()]
    stat = [stack.enter_context(nc.sbuf_tensor(un("stat"), [128, 16], F32)) for j in range(2)]
    rstat = [Reg(), Reg()]
    xbt = [stack.enter_context(nc.sbuf_tensor(un("xbt"), [128, D], BF16)) for j in range(2)]
    rxbt = [Reg(), Reg()]
    return (tmpT, rtmpT, stat, rstat, xbt, rxbt)


def emit_moe(cx, wg_ap, wu_ap, wd_ap):
    P = cx.P
    nc = cx.nc
    with contextlib.ExitStack() as st:
        sb = lambda name, shape, dt: st.enter_context(nc.sbuf_tensor(un(name), shape, dt))
        aff = sb("aff", [128, NT, NE], F32)
        r_aff = Reg()
        sm = sb("sm", [128, NT, 4], F32)
        AG = sb("AG", [128, NT, NE, 4], BF16)
        r_AG = Reg()
        slot = sb("slot", [128, NT, NE], F32)
        r_slot = Reg()
        slotT = sb("slotT", [16, S], F32)
        r_slotT = Reg()
        m8 = sb("m8", [16, 8], F32)
        r_m8 = Reg()
        wbuf = [sb("wbuf%d" % j, [128, NK, D], BF16) for j in range(3)]
        r_wbuf = [Reg() for j in range(3)]
        Pm = sb("Pm", [128, NT, CAP], BF16)
        r_Pm = Reg()
        workbuf = sb("workbuf", [16, S], F32)
        xinT = sb("xinT", [128, NK, CAP], BF16)
        r_xinT = Reg()
        hT = sb("hT", [128, NK, CAP], BF16)
        r_hT = Reg()
        sg = [sb("sg%d" % j, [128, CAP], F32) for j in range(2)]
        r_sg = [Reg(), Reg()]
        Oe2 = [sb("Oe", [128, 2, D], F32) for j in range(2)]
        Oe = Oe2[0]
        r_Oe = Reg()
        gs = sb("gs", [128, 4], F32)
        r_gs = Reg()
        affT = Pm[:].rearrange("p a b -> p (a b)").bitcast(F32)[0:16, :]
        r_affT = Reg()
        work = workbuf[:]
        r_work = Reg()

        ident = cx.cst[:, C_IDENT:C_IDENT + 128]

        rl = cx.rlogits
        P.op("dve", lambda e: e.tensor_reduce(out=sm[:, :, 0], in_=cx.logits[:], axis=AX.X, op=ALU.max),
             reads=rl, writes=[r_aff])
        P.op("dve", lambda e: e.tensor_tensor(out=aff[:], in0=cx.logits[:],
                                              in1=sm[:, :, 0:1].to_broadcast([128, NT, NE]), op=ALU.subtract),
             reads=rl + [r_aff], writes=[r_aff])
        P.op("act", lambda e: e.activation(out=aff[:], in_=aff[:], func=AF.Exp), reads=[r_aff], writes=[r_aff])
        P.op("dve", lambda e: e.tensor_reduce(out=sm[:, :, 1], in_=aff[:], axis=AX.X, op=ALU.add),
             reads=[r_aff], writes=[r_aff])
        P.op("dve", lambda e: e.reciprocal(out=sm[:, :, 2], in_=sm[:, :, 1]), reads=[r_aff], writes=[r_aff])
        P.op("dve", lambda e: e.tensor_tensor(out=aff[:], in0=aff[:],
                                              in1=sm[:, :, 2:3].to_broadcast([128, NT, NE]), op=ALU.mult),
             reads=[r_aff], writes=[r_aff])
        P.op("dve", lambda e: e.tensor_copy(out=AG[:, :, :, 0], in_=aff[:]), reads=[r_aff], writes=[r_AG])
        P.op("dve", lambda e: e.tensor_tensor(out=AG[:, :, :, 1], in0=aff[:], in1=AG[:, :, :, 0], op=ALU.subtract),
             reads=[r_aff, r_AG], writes=[r_AG])
        for i in range(NT):
            P.op("pool", lambda e, i=i: e.memset(AG[:, i, :, 2], float(i)), writes=[r_AG])
        P.op("dve", lambda e: e.tensor_copy(out=AG[:, :, :, 3], in_=cx.cst[:, C_PIDX:C_PIDX + 1].to_broadcast([128, NT * NE]).rearrange("p (a b) -> p a b", a=NT)),
             reads=[cx.rcst], writes=[r_AG])
        for i in range(NT):
            P.dma("sp", cx.xd[i * 128:(i + 1) * 128, :], cx.XA[:, i * D:(i + 1) * D], reads=[cx.rXA[i]], writes=[cx.r_xd[i]])
        for g in range(4):
            for j in range(4):
                i = g * 4 + j
                P.op("pe", lambda e, i=i, j=j, g=g: e.transpose(out=cx.ps[g][0:16, j * 128:(j + 1) * 128],
                                                                in_=aff[:, i, :], identity=ident),
                     reads=[r_aff, cx.rcst], writes=[cx.rps[g]])
            P.op("act", lambda e, g=g: e.activation(out=affT[:, g * 512:(g + 1) * 512], in_=cx.ps[g][0:16, :], func=AF.Copy),
                 reads=[cx.rps[g]], writes=[r_affT])
        H = S // 2
        sl32 = Oe2[1][:].rearrange("p a b -> p (a b)")[0:32, 0:H + CAP]
        work32 = sl32[:, 0:H]
        P.op("pool", lambda e: e.tensor_copy(out=sl32[0:16, 0:H], in_=affT[:, 0:H]), reads=[r_affT], writes=[r_work])
        P.dma("sp", sl32[16:32, 0:H], affT[:, H:S], reads=[r_affT], writes=[r_work])
        nround = CAP // 8
        for r in range(nround):
            P.op("dve", lambda e, r=r: e.max(out=sl32[:, H + r * 8:H + (r + 1) * 8], in_=work32), reads=[r_work], writes=[r_m8])
            if r < nround - 1:
                P.op("dve", lambda e, r=r: e.match_replace(out=work32, in_to_replace=sl32[:, H + r * 8:H + (r + 1) * 8], in_values=work32, imm_value=-1.0),
                     reads=[r_work, r_m8], writes=[r_work])
        P.dma("sp", workbuf[:, 0:CAP], sl32[16:32, H:H + CAP], reads=[r_m8], writes=[r_work])
        A_ = sl32[0:16, H:H + CAP]
        Brev = workbuf[:, 0:CAP][:, ::-1]
        cand = workbuf[:, 512:512 + CAP + 1]
        P.op("dve", lambda e: e.tensor_tensor(out=cand[:, 1:CAP], in0=A_[:, 0:CAP - 1], in1=Brev[:, 1:CAP], op=ALU.min), reads=[r_work, r_m8], writes=[r_work])
        P.op("dve", lambda e: e.tensor_copy(out=cand[:, 0:1], in_=Brev[:, 0:1]), reads=[r_work], writes=[r_work])
        P.op("dve", lambda e: e.tensor_copy(out=cand[:, CAP:CAP + 1], in_=A_[:, CAP - 1:CAP]), reads=[r_work, r_m8], writes=[r_work])
        P.op("dve", lambda e: e.tensor_reduce(out=m8[:, 7:8], in_=cand, axis=AX.X, op=ALU.max), reads=[r_work], writes=[r_m8])
        P.op("dve", lambda e: e.tensor_scalar(out=work[:], in0=affT[:], scalar1=m8[:, 7:8], scalar2=None, op0=ALU.is_ge),
             reads=[r_affT, r_m8, r_work], writes=[r_work])
        P.op("dve", lambda e: e.tensor_tensor_scan(out=slotT[:], data0=cx.cst[0:16, C_ONES:C_ONES + 1].to_broadcast([16, S]), data1=work[:], initial=0.0,
                                                    op0=ALU.mult, op1=ALU.add),
             reads=[r_work, cx.rcst], writes=[r_slotT])
        P.op("dve", lambda e: e.tensor_tensor(out=slotT[:], in0=slotT[:], in1=work[:], op=ALU.mult),
             reads=[r_work, r_slotT], writes=[r_slotT])
        P.op("dve", lambda e: e.tensor_scalar(out=slotT[:], in0=slotT[:], scalar1=-1.0, scalar2=None, op0=ALU.add),
             reads=[r_slotT], writes=[r_slotT])
        for i in range(NT):
            P.op("pe", lambda e, i=i: e.transpose(out=cx.ps[5][:, i * NE:(i + 1) * NE], in_=slotT[:, i * 128:(i + 1) * 128],
                                                  identity=ident[0:16, 0:16]),
                 reads=[r_slotT, cx.rcst], writes=[cx.rps[5]])
        P.op("act", lambda e: e.activation(out=slot[:].rearrange("p a b -> p (a b)"), in_=cx.ps[5][:, 0:NT * NE], func=AF.Copy),
             reads=[cx.rps[5]], writes=[r_slot])

        P.barrier()
        r_xk = [Reg() for k in range(NK)]
        r_hj = [Reg() for j in range(NK)]
        r_Oq2 = [[[Reg() for n in range(2)] for c in range(2)] for j in range(2)]
        r_Oq = r_Oq2[0]
        sc_prev = [[]]
        P.op("pool", lambda e: e.memset(Oe[:, 0, :], 0.0), writes=[r_Oq[0][0], r_Oq[0][1]])
        for i in range(NT):
            P.dma("sp", cx.yd[i * 128:(i + 1) * 128, :], Oe[:, 0, :], reads=[r_Oq[0][0], r_Oq[0][1]], writes=[cx.r_yd])
        wq = [0]
        xa3 = cx.XA[:].rearrange("p (s k f) -> p s k f", s=2, k=NK)
        wslot = [wbuf[0][:], wbuf[1][:], wbuf[2][:], xa3[:, 0], xa3[:, 1]]
        rw = [[r_wbuf[0]], [r_wbuf[1]], [r_wbuf[2]], [Reg()] + cx.rXA[0:8], [Reg()] + cx.rXA[8:16]]

        def load_w(ap2d):
            j = wq[0] % 5
            wq[0] += 1
            P.dma("pool", wslot[j], ap2d.rearrange("(k p) f -> p k f", p=128), writes=rw[j])
            return j

        def build_P(ex):
            for i in range(NT):
                P.op("dve", lambda e, i=i: e.tensor_scalar(out=Pm[:, i, :], in0=cx.cst[:, C_IOTA:C_IOTA + CAP],
                                                           scalar1=slot[:, i, ex:ex + 1], scalar2=None, op0=ALU.is_equal),
                     reads=[r_slot, cx.rcst], writes=[r_Pm])

        xin = sb("xin", [128, 2, D], BF16)
        r_xin = [Reg(), Reg()]
        gs2 = [sb("gs2", [128, 8], F32) for j in range(2)]
        r_gs2 = [Reg(), Reg()]
        idx2 = [sb("idx2", [128, 2], I32) for j in range(2)]
        r_idx2 = [Reg(), Reg()]

        def gsidx_and_gather(ex):
            g_ = gs2[ex % 2]
            r_g = r_gs2[ex % 2]
            for c in range(2):
                for i in range(NT):
                    P.op("pe", lambda e, i=i, c=c: e.matmul(cx.ps[6][:, 8 + c * 4:8 + c * 4 + 4], lhsT=Pm[:, i, c * 128:(c + 1) * 128],
                                                            rhs=AG[:, i, ex, :], start=(i == 0), stop=(i == NT - 1)),
                         reads=[r_Pm, r_AG], writes=[cx.rps[6]])
            P.op("dve", lambda e: e.tensor_copy(out=g_[:, 0:8], in_=cx.ps[6][:, 8:16]), reads=[cx.rps[6]], writes=[r_g])
            gv = g_[:, 0:8].rearrange("p (c f) -> p c f", c=2)
            P.op("dve", lambda e: e.tensor_tensor(out=gv[:, :, 0], in0=gv[:, :, 0], in1=gv[:, :, 1], op=ALU.add), reads=[r_g], writes=[r_g])
            P.op("dve", lambda e: e.scalar_tensor_tensor(out=gv[:, :, 2], in0=gv[:, :, 2], scalar=128.0, in1=gv[:, :, 3], op0=ALU.mult, op1=ALU.add),
                 reads=[r_g], writes=[r_g])
            ix = idx2[ex % 2]
            P.op("dve", lambda e: e.tensor_copy(out=ix[:], in_=gv[:, :, 2]), reads=[r_g], writes=[r_idx2[ex % 2]])
            for c in range(2):
                P.idma_gather(xin[:, c, :], cx.xd[:, :], ix[:, c:c + 1], reads=[r_idx2[ex % 2]] + cx.r_xd, writes=[r_xin[c]])

        build_P(0)
        gsidx_and_gather(0)
        jnext = (load_w(wg_ap[0]), load_w(wu_ap[0]), load_w(wd_ap[0]))
        for ex in range(NE):
            jg, ju, jd = jnext
            Oe = Oe2[ex % 2]
            r_Oq = r_Oq2[ex % 2]
            if ex + 1 < NE:
                jg1 = load_w(wg_ap[ex + 1])
                ju1 = load_w(wu_ap[ex + 1])
            gsc = gs2[ex % 2][:, 0:8].rearrange("p (c f) -> p c f", c=2)
            r_gs = r_gs2[ex % 2]
            for c in range(2):
                pb = 4 + c
                pst = cx.ps[pb][:].bitcast(BF16)
                for k in range(NK):
                    P.op("pe", lambda e, k=k, c=c: e.transpose(out=pst[:, k * 128:(k + 1) * 128], in_=xin[:, c, k * 128:(k + 1) * 128], identity=cx.identb[:]),
                         reads=[r_xin[c], cx.rcst], writes=[cx.rps[pb]])
                P.op("act", lambda e, c=c: e.activation(out=xinT[:, :, c * 128:(c + 1) * 128], in_=pst.rearrange("p (k t) -> p k t", k=NK), func=AF.Copy),
                     reads=[cx.rps[pb]], writes=r_xk)
            for j in range(NK):
                pg = 4 + (j % 2)
                pu = 6 + (j % 2)
                for k in range(NK):
                    P.op("pe", lambda e, j=j, k=k, pg=pg: e.matmul(cx.ps[pg][:, 0:CAP], lhsT=wslot[jg][:, k, j * 128:(j + 1) * 128],
                                                                   rhs=xinT[:, k, :], start=(k == 0), stop=(k == NK - 1)),
                         reads=rw[jg] + [r_xk[k]], writes=[cx.rps[pg]])
                for k in range(NK):
                    P.op("pe", lambda e, j=j, k=k, pu=pu: e.matmul(cx.ps[pu][:, 0:CAP], lhsT=wslot[ju][:, k, j * 128:(j + 1) * 128],
                                                                   rhs=xinT[:, k, :], start=(k == 0), stop=(k == NK - 1)),
                         reads=rw[ju] + [r_xk[k]], writes=[cx.rps[pu]])
                P.op("act", lambda e, j=j, pg=pg: e.activation(out=sg[j % 2][:], in_=cx.ps[pg][:, 0:CAP], func=AF.Silu),
                     reads=[cx.rps[pg]], writes=[r_sg[j % 2]])
                P.op("dve", lambda e, j=j, pu=pu: e.tensor_tensor(out=hT[:, j, :], in0=sg[j % 2][:], in1=cx.ps[pu][:, 0:CAP], op=ALU.mult),
                     reads=[cx.rps[pu], r_sg[j % 2]], writes=[r_hj[j]])
                if j == 1 and ex + 1 < NE:
                    build_P(ex + 1)
                if j == 5 and ex + 1 < NE:
                    gsidx_and_gather(ex + 1)
            if ex + 1 < NE:
                jnext = (jg1, ju1, load_w(wd_ap[ex + 1]))
            for c in range(2):
                for n in range(2):
                    pb = 4 + ((c * 2 + n) % 4)
                    for j in range(NK):
                        P.op("pe", lambda e, j=j, c=c, n=n, pb=pb: e.matmul(cx.ps[pb][:, :], lhsT=hT[:, j, c * 128:(c + 1) * 128],
                                                                            rhs=wslot[jd][:, j, n * 512:(n + 1) * 512],
                                                                            start=(j == 0), stop=(j == NK - 1)),
                             reads=rw[jd] + [r_hj[j]], writes=[cx.rps[pb]])
                    P.op("act", lambda e, c=c, n=n, pb=pb: e.activation(out=Oe[:, c, n * 512:(n + 1) * 512], in_=cx.ps[pb][:, :],
                                                                        func=AF.Copy, scale=gsc[:, c, 0:1]),
                         reads=[cx.rps[pb], r_gs], writes=[r_Oq[c][n]])
            toks = []
            for c in range(2):
                toks.append(P.idma_scatter_add(cx.yd[:, :], idx2[ex % 2][:, c:c + 1], Oe[:, c, :],
                                               reads=[r_idx2[ex % 2], r_Oq[c][0], r_Oq[c][1], cx.r_yd], after=sc_prev[0]))
            sc_prev[0] = toks
        for akey, aval in sc_prev[0]:
            P._wait("sp", akey, aval)
        Oe = Oe2[0]
        r_Oq = r_Oq2[0]
        for i in range(NT):
            c = i % 2
            P.dma("sp", Oe[:, c, :], cx.yd[i * 128:(i + 1) * 128, :], reads=[cx.r_yd], writes=[r_Oq[c][0], r_Oq[c][1]])
            eng = "dve" if i % 2 == 0 else "pool"
            P.op(eng, lambda e, i=i, c=c: e.tensor_tensor(out=cx.R[:, i, :], in0=cx.R[:, i, :], in1=Oe[:, c, :], op=ALU.add),
                 reads=[r_Oq[c][0], r_Oq[c][1], cx.rR[i]], writes=[cx.rR[i]])
    P.barrier()


def xt_view(cx):
    return cx.XA[:].rearrange("p (k t) -> p k t", k=NK)


def small_load(cx, out, in_, reg):
    cx.P.dma("sp", out, in_, writes=[reg], allow_slow_non_contiguous=True)


def gelu_tanh(cx, out, src, t1, t2, r_src, r_t1, r_t2, r_out, extra_mul=None, r_extra=None):
    P = cx.P
    P.op("act", lambda e: e.activation(out=t1, in_=src, func=AF.Square), reads=[r_src], writes=[r_t1])
    P.op("dve", lambda e: e.tensor_scalar(out=t1, in0=t1, scalar1=0.044715, scalar2=1.0, op0=ALU.mult, op1=ALU.add),
         reads=[r_t1], writes=[r_t1])
    P.op("dve", lambda e: e.tensor_tensor(out=t1, in0=t1, in1=src, op=ALU.mult), reads=[r_t1, r_src], writes=[r_t1])
    P.op("act", lambda e: e.activation(out=t1, in_=t1, func=AF.Sigmoid, scale=1.5957691216057308), reads=[r_t1], writes=[r_t1])
    if extra_mul is None:
        P.op("dve", lambda e: e.tensor_tensor(out=out, in0=t1, in1=src, op=ALU.mult), reads=[r_t1, r_src], writes=[r_out])
    else:
        P.op("dve", lambda e: e.tensor_tensor(out=t2, in0=t1, in1=src, op=ALU.mult), reads=[r_t1, r_src], writes=[r_t2])
        P.op("pool", lambda e: e.tensor_tensor(out=out, in0=t2, in1=extra_mul, op=ALU.mult), reads=[r_t2, r_extra], writes=[r_out])


def emit_lru(cx):
    P = cx.P
    nc = cx.nc
    A = cx.A
    XT = xt_view(cx)
    with contextlib.ExitStack() as st:
        sb = lambda name, shape, dt: st.enter_context(nc.sbuf_tensor(un(name), shape, dt))
        cw = sb("cw", [128, NK, 4], F32)
        cb = sb("cb", [128, NK], F32)
        gab = sb("gab", [128, 2, NK], F32)
        gxb = sb("gxb", [128, 2, NK], F32)
        lam = sb("lam", [128, 2, NK], F32)
        c8 = sb("c8", [128, 2, NK], F32)
        c16 = sb("c16", [128, 2, NK], F32)
        r_par = Reg()
        for j in range(4):
            small_load(cx, cw[:, :, j], A["lru_conv_w"][0, j].rearrange("(c p) -> p c", p=128), r_par)
        small_load(cx, cb[:], A["lru_conv_b"][0].rearrange("(c p) -> p c", p=128), r_par)
        for g in range(2):
            small_load(cx, gab[:, g, :], A["lru_gate_a_b"][0, g].rearrange("(c p) -> p c", p=128), r_par)
        for g in range(2):
            small_load(cx, gxb[:, g, :], A["lru_gate_x_b"][0, g].rearrange("(c p) -> p c", p=128), r_par)
        for g in range(2):
            small_load(cx, lam[:, g, :], A["lru_lambda"][0, g].rearrange("(c p) -> p c", p=128), r_par)
        P.op("act", lambda e: e.activation(out=c8[:], in_=lam[:], func=AF.Exp, scale=-1.0), reads=[r_par], writes=[r_par])
        P.op("act", lambda e: e.activation(out=c8[:], in_=c8[:], func=AF.Ln, bias=1.0), reads=[r_par], writes=[r_par])
        P.op("dve", lambda e: e.tensor_scalar(out=c16[:], in0=c8[:], scalar1=-16.0, scalar2=None, op0=ALU.mult), reads=[r_par], writes=[r_par])
        P.op("dve", lambda e: e.tensor_scalar(out=c8[:], in0=c8[:], scalar1=-8.0, scalar2=None, op0=ALU.mult), reads=[r_par], writes=[r_par])

        wgb = sb("wgb", [128, NK, 256], BF16)
        r_wgb = Reg()
        wu = sb("wu", [128, NK, 256], BF16)
        r_wu = Reg()
        wg4 = [[sb("wga", [128, 2, 256], BF16) for d in range(2)] for ax in range(2)]
        r_wg4 = [[Reg() for d in range(2)] for ax in range(2)]
        wo = sb("wo", [128, D], BF16)
        r_wo = Reg()
        upad = sb("upad", [128, S + 4], F32)
        r_upad = Reg()
        uc = sb("uc", [128, 2, S], F32)
        r_uc = [Reg(), Reg()]
        ucb = sb("ucb", [128, 2, S], BF16)
        r_ucb = [Reg(), Reg()]
        rbuf = sb("rbuf", [128, S], F32)
        r_rbuf = Reg()
        abuf = sb("abuf", [128, S], F32)
        r_abuf = Reg()
        tmp = sb("tmp", [128, S], F32)
        r_tmp = Reg()
        hsum = sb("hsum", [128, S], F32)
        r_hsum = Reg()
        gtc = sb("gtc", [128, S], BF16)
        r_gtc = Reg()
        P.op("pool", lambda e: e.memset(upad[:], 0.0), writes=[r_upad])

        def psum4(base):
            return [cx.ps[base + j] for j in range(4)], [cx.rps[base + j] for j in range(4)]

        pcount = [0]

        def next_ps4():
            base = 4 * (pcount[0] % 2)
            pcount[0] += 1
            return psum4(base)

        for n in range(4):
            P.dma("pool", wgb[:], A["lru_w_in"][0][:, n * 256:(n + 1) * 256].rearrange("(k p) c -> p k c", p=128), writes=[r_wgb])
            P.dma("pool", wu[:], A["lru_w_in"][0][:, D + n * 256:D + (n + 1) * 256].rearrange("(k p) c -> p k c", p=128), writes=[r_wu])
            for ax, nm in enumerate(("lru_gate_a_w", "lru_gate_x_w")):
                for d in range(2):
                    P.dma("pool", wg4[ax][d][:], A[nm][0, d, n].rearrange("(i p) j -> p i j", p=128), writes=[r_wg4[ax][d]])
            for cc in range(2):
                c = 2 * n + cc
                ps, rps = next_ps4()
                for tb in range(4):
                    for k in range(NK):
                        P.op("pe", lambda e, tb=tb, k=k: e.matmul(ps[tb][:, :], lhsT=wu[:, k, cc * 128:(cc + 1) * 128],
                                                                  rhs=XT[:, k, tb * 512:(tb + 1) * 512], start=(k == 0), stop=(k == NK - 1)),
                             reads=[r_wu] + cx.rXA[tb * 4:(tb + 1) * 4], writes=[rps[tb]])
                    P.op("act", lambda e, tb=tb: e.activation(out=upad[:, 2 + tb * 512:2 + (tb + 1) * 512], in_=ps[tb][:, :], func=AF.Copy),
                         reads=[rps[tb]], writes=[r_upad])
                ucc = uc[:, cc, :]
                P.op("dve", lambda e: e.tensor_scalar(out=ucc, in0=upad[:, 0:S], scalar1=cw[:, c, 0:1], scalar2=cb[:, c:c + 1],
                                                       op0=ALU.mult, op1=ALU.add), reads=[r_upad, r_par], writes=[r_uc[cc]])
                for j in range(1, 4):
                    P.op("dve", lambda e, j=j: e.scalar_tensor_tensor(out=ucc, in0=upad[:, j:j + S], scalar=cw[:, c, j:j + 1], in1=ucc,
                                                                      op0=ALU.mult, op1=ALU.add), reads=[r_upad, r_par, r_uc[cc]], writes=[r_uc[cc]])
                P.op("pool", lambda e: e.tensor_copy(out=ucb[:, cc, :], in_=ucc), reads=[r_uc[cc]], writes=[r_ucb[cc]])
            for cc in range(2):
                c = 2 * n + cc
                ucc = uc[:, cc, :]
                for d in range(2):
                    ps, rps = next_ps4()
                    for tb in range(4):
                        for i in range(2):
                            P.op("pe", lambda e, tb=tb, i=i: e.matmul(ps[tb][:, :], lhsT=wg4[0][d][:, i, cc * 128:(cc + 1) * 128],
                                                                      rhs=ucb[:, i, tb * 512:(tb + 1) * 512], start=(i == 0), stop=(i == 1)),
                                 reads=[r_wg4[0][d], r_ucb[0], r_ucb[1]], writes=[rps[tb]])
                        P.op("act", lambda e, tb=tb: e.activation(out=rbuf[:, tb * 512:(tb + 1) * 512], in_=ps[tb][:, :], func=AF.Sigmoid,
                                                                  bias=gab[:, d, c:c + 1]), reads=[rps[tb], r_par], writes=[r_rbuf])
                    P.op("act", lambda e: e.activation(out=abuf[:], in_=rbuf[:], func=AF.Exp, scale=c8[:, d, c:c + 1]),
                         reads=[r_rbuf, r_par], writes=[r_abuf])
                    P.op("act", lambda e: e.activation(out=tmp[:], in_=rbuf[:], func=AF.Exp, scale=c16[:, d, c:c + 1]),
                         reads=[r_rbuf, r_par], writes=[r_tmp])
                    P.op("act", lambda e: e.activation(out=tmp[:], in_=tmp[:], func=AF.Sqrt, scale=-1.0, bias=1.0),
                         reads=[r_tmp], writes=[r_tmp])
                    ps, rps = next_ps4()
                    for tb in range(4):
                        for i in range(2):
                            P.op("pe", lambda e, tb=tb, i=i: e.matmul(ps[tb][:, :], lhsT=wg4[1][d][:, i, cc * 128:(cc + 1) * 128],
                                                                      rhs=ucb[:, i, tb * 512:(tb + 1) * 512], start=(i == 0), stop=(i == 1)),
                                 reads=[r_wg4[1][d], r_ucb[0], r_ucb[1]], writes=[rps[tb]])
                        P.op("act", lambda e, tb=tb: e.activation(out=rbuf[:, tb * 512:(tb + 1) * 512], in_=ps[tb][:, :], func=AF.Sigmoid,
                                                                  bias=gxb[:, d, c:c + 1]), reads=[rps[tb], r_par], writes=[r_rbuf])
                    P.op("dve", lambda e: e.tensor_tensor(out=tmp[:], in0=tmp[:], in1=rbuf[:], op=ALU.mult), reads=[r_tmp, r_rbuf], writes=[r_tmp])
                    P.op("dve", lambda e: e.tensor_tensor(out=tmp[:], in0=tmp[:], in1=ucc, op=ALU.mult), reads=[r_tmp, r_uc[cc]], writes=[r_tmp])
                    if d == 0:
                        P.op("dve", lambda e: e.tensor_tensor_scan(out=hsum[:], data0=abuf[:], data1=tmp[:], initial=0.0,
                                                                    op0=ALU.mult, op1=ALU.add), reads=[r_abuf, r_tmp], writes=[r_hsum])
                    else:
                        P.op("dve", lambda e: e.tensor_tensor_scan(out=rbuf[:, ::-1], data0=abuf[:, ::-1], data1=tmp[:, ::-1], initial=0.0,
                                                                    op0=ALU.mult, op1=ALU.add), reads=[r_abuf, r_tmp], writes=[r_rbuf])
                        P.op("pool", lambda e: e.tensor_tensor(out=hsum[:], in0=hsum[:], in1=rbuf[:], op=ALU.add),
                             reads=[r_rbuf, r_hsum], writes=[r_hsum])
                ps, rps = next_ps4()
                for tb in range(4):
                    for k in range(NK):
                        P.op("pe", lambda e, tb=tb, k=k: e.matmul(ps[tb][:, :], lhsT=wgb[:, k, cc * 128:(cc + 1) * 128],
                                                                  rhs=XT[:, k, tb * 512:(tb + 1) * 512], start=(k == 0), stop=(k == NK - 1)),
                             reads=[r_wgb] + cx.rXA[tb * 4:(tb + 1) * 4], writes=[rps[tb]])
                    sl = slice(tb * 512, (tb + 1) * 512)
                    gelu_tanh(cx, gtc[:, sl], ps[tb][:, :], abuf[:, sl], tmp[:, sl], rps[tb], r_abuf, r_tmp, r_gtc,
                              extra_mul=hsum[:, sl], r_extra=r_hsum)
                P.dma("pool", wo[:], A["lru_w_out"][0][c * 128:(c + 1) * 128, :], writes=[r_wo])
                for i in range(NT):
                    for hh in range(2):
                        pb = (i * 2 + hh) % 8
                        P.op("pe", lambda e, i=i, hh=hh, pb=pb: e.matmul(cx.ps[pb][:, :], lhsT=gtc[:, i * 128:(i + 1) * 128],
                                                                         rhs=wo[:, hh * 512:(hh + 1) * 512], start=True, stop=True),
                             reads=[r_gtc, r_wo], writes=[cx.rps[pb]])
                        P.op("dve", lambda e, i=i, hh=hh, pb=pb: e.tensor_tensor(out=cx.R[:, i, hh * 512:(hh + 1) * 512],
                                                                                 in0=cx.R[:, i, hh * 512:(hh + 1) * 512],
                                                                                 in1=cx.ps[pb][:, :], op=ALU.add),
                             reads=[cx.rps[pb], cx.rR[i]], writes=[cx.rR[i]])
    P.barrier()


MLA_H = 8
MLA_SCALE = 192.0 ** -0.5
TWO_PI_HI = 6.28125
TWO_PI_LO = 2.0 * np.pi - 6.28125
MAGIC = 12582912.0


def rope_apply(cx, out_bf, src, cos, sin, t1, t2, nb, r_src, r_tab, r_t, r_out):
    P = cx.P
    a1 = src[:, :, 0:32]
    a2 = src[:, :, 32:64]
    P.op("dve", lambda e: e.tensor_tensor(out=t1, in0=a1, in1=cos, op=ALU.mult), reads=[r_src, r_tab], writes=[r_t])
    P.op("dve", lambda e: e.tensor_tensor(out=t2, in0=a2, in1=sin, op=ALU.mult), reads=[r_src, r_tab], writes=[r_t])
    P.op("dve", lambda e: e.tensor_tensor(out=out_bf[:, :, 0:32], in0=t1, in1=t2, op=ALU.subtract), reads=[r_t], writes=[r_out])
    P.op("dve", lambda e: e.tensor_tensor(out=t1, in0=a2, in1=cos, op=ALU.mult), reads=[r_src, r_tab, r_out], writes=[r_t])
    P.op("dve", lambda e: e.tensor_tensor(out=t2, in0=a1, in1=sin, op=ALU.mult), reads=[r_src, r_tab], writes=[r_t])
    P.op("dve", lambda e: e.tensor_tensor(out=out_bf[:, :, 32:64], in0=t1, in1=t2, op=ALU.add), reads=[r_t], writes=[r_out])


def emit_mla(cx, seq):
    P = cx.P
    nc = cx.nc
    A = cx.A
    XT = xt_view(cx)
    with contextlib.ExitStack() as st:
        sb = lambda name, shape, dt: st.enter_context(nc.sbuf_tensor(un(name), shape, dt))
        posi = sb("posi", [128, NT], I32)
        ang = sb("ang", [128, NT, 32], F32)
        nn = sb("nn", [128, NT, 32], F32)
        cosT = sb("cosT", [128, NT, 32], F32)
        sinT = sb("sinT", [128, NT, 32], F32)
        r_tab = Reg()
        small_load(cx, posi[:], A["positions"][seq].rearrange("(i p) -> p i", p=128), r_tab)
        P.op("dve", lambda e: e.tensor_copy(out=nn[:, :, 0], in_=posi[:]), reads=[r_tab], writes=[r_tab])
        P.op("dve", lambda e: e.tensor_tensor(out=ang[:], in0=nn[:, :, 0:1].to_broadcast([128, NT, 32]),
                                              in1=cx.cst[:, None, C_FREQ:C_FREQ + 32].to_broadcast([128, NT, 32]), op=ALU.mult),
             reads=[r_tab, cx.rcst], writes=[r_tab])
        P.op("dve", lambda e: e.tensor_scalar(out=nn[:], in0=ang[:], scalar1=float(1.0 / (2.0 * np.pi)), scalar2=None, op0=ALU.mult),
             reads=[r_tab], writes=[r_tab])
        P.op("dve", lambda e: e.tensor_scalar(out=nn[:], in0=nn[:], scalar1=MAGIC, scalar2=None, op0=ALU.add), reads=[r_tab], writes=[r_tab])
        P.op("dve", lambda e: e.tensor_scalar(out=nn[:], in0=nn[:], scalar1=-MAGIC, scalar2=None, op0=ALU.add), reads=[r_tab], writes=[r_tab])
        P.op("dve", lambda e: e.scalar_tensor_tensor(out=ang[:], in0=nn[:], scalar=-TWO_PI_HI, in1=ang[:], op0=ALU.mult, op1=ALU.add),
             reads=[r_tab], writes=[r_tab])
        P.op("dve", lambda e: e.scalar_tensor_tensor(out=ang[:], in0=nn[:], scalar=-TWO_PI_LO, in1=ang[:], op0=ALU.mult, op1=ALU.add),
             reads=[r_tab], writes=[r_tab])
        PI_S = 3.1415925
        P.op("dve", lambda e: e.tensor_scalar(out=ang[:], in0=ang[:], scalar1=PI_S, scalar2=-PI_S, op0=ALU.min, op1=ALU.max),
             reads=[r_tab], writes=[r_tab])
        P.op("act", lambda e: e.activation(out=sinT[:], in_=ang[:], func=AF.Sin), reads=[r_tab], writes=[r_tab])
        P.op("act", lambda e: e.activation(out=nn[:], in_=ang[:], func=AF.Abs), reads=[r_tab], writes=[r_tab])
        P.op("dve", lambda e: e.tensor_scalar(out=nn[:], in0=nn[:], scalar1=-1.0, scalar2=float(np.pi / 2), op0=ALU.mult, op1=ALU.add),
             reads=[r_tab], writes=[r_tab])
        P.op("act", lambda e: e.activation(out=cosT[:], in_=nn[:], func=AF.Sin), reads=[r_tab], writes=[r_tab])

        wuq = sb("wuq", [128, 3, 1536], BF16)
        wukv = sb("wukv", [128, 2, 2048], BF16)
        r_w = Reg()
        P.dma("pool", wuq[:], A["mla_w_uq"][0].rearrange("(k p) c -> p k c", p=128), writes=[r_w])
        P.dma("pool", wukv[:], A["mla_w_ukv"][0].rearrange("(k p) c -> p k c", p=128), writes=[r_w])
        cqnT = sb("cqnT", [128, 3, S], BF16)
        ckvnT = sb("ckvnT", [128, 2, S], BF16)
        kropeT = sb("kropeT", [64, S], BF16)
        r_lat = [Reg() for i in range(NT)]
        r_krT = Reg()
        with contextlib.ExitStack() as st1:
            sb1 = lambda name, shape, dt: st1.enter_context(nc.sbuf_tensor(un(name), shape, dt))
            wi = sb1("wi", [128, NK, 704], BF16)
            r_wi = Reg()
            P.dma("pool", wi[:], A["mla_w_in"][0].rearrange("(k p) c -> p k c", p=128), writes=[r_wi])
            gq = sb1("gq", [128, 384], F32)
            gkv = sb1("gkv", [128, 256], F32)
            r_g = Reg()
            P.dma("sp", gq[:], A["mla_q_norm_g"][0].partition_broadcast(128), writes=[r_g])
            P.dma("sp", gkv[:], A["mla_kv_norm_g"][0].partition_broadcast(128), writes=[r_g])
            zs = [sb1("zs", [128, 704], F32) for j in range(2)]
            r_zs = [Reg(), Reg()]
            junk = sb1("junk", [128, 384], F32)
            r_junk = Reg()
            ms = [sb1("ms", [128, 4], F32) for j in range(2)]
            r_ms = [Reg(), Reg()]
            cn = [sb1("cn", [128, 640], BF16) for j in range(2)]
            r_cn = [Reg(), Reg()]
            krz = sb1("krz", [128, NT, 64], F32)
            r_krz = Reg()
            krb = sb1("krb", [128, NT, 64], BF16)
            r_krb = Reg()
            rt1 = sb1("rt1", [128, NT, 32], F32)
            rt2 = sb1("rt2", [128, NT, 32], F32)
            r_rt = Reg()
            for i in range(NT):
                j2 = i % 2
                pa = 0 + 2 * j2
                pbk = 1 + 2 * j2
                for k in range(NK):
                    P.op("pe", lambda e, k=k: e.matmul(cx.ps[pa][:, :], lhsT=XT[:, k, i * 128:(i + 1) * 128], rhs=wi[:, k, 0:512],
                                                       start=(k == 0), stop=(k == NK - 1)), reads=[cx.rXA[i], r_wi], writes=[cx.rps[pa]])
                for k in range(NK):
                    P.op("pe", lambda e, k=k: e.matmul(cx.ps[pbk][:, 0:192], lhsT=XT[:, k, i * 128:(i + 1) * 128], rhs=wi[:, k, 512:704],
                                                       start=(k == 0), stop=(k == NK - 1)), reads=[cx.rXA[i], r_wi], writes=[cx.rps[pbk]])
                z = zs[j2]
                P.op("act", lambda e: e.activation(out=z[:, 0:512], in_=cx.ps[pa][:, :], func=AF.Copy), reads=[cx.rps[pa]], writes=[r_zs[j2]])
                P.op("dve", lambda e: e.tensor_copy(out=z[:, 512:704], in_=cx.ps[pbk][:, 0:192]), reads=[cx.rps[pbk]], writes=[r_zs[j2]])
                m = ms[j2]
                P.op("act", lambda e: e.activation(out=junk[:, 0:384], in_=z[:, 0:384], func=AF.Square, accum_out=m[:, 0:1]),
                     reads=[r_zs[j2]], writes=[r_junk, r_ms[j2]])
                P.op("act", lambda e: e.activation(out=junk[:, 0:256], in_=z[:, 384:640], func=AF.Square, accum_out=m[:, 1:2]),
                     reads=[r_zs[j2]], writes=[r_junk, r_ms[j2]])
                P.op("dve", lambda e: e.tensor_scalar(out=m[:, 0:1], in0=m[:, 0:1], scalar1=1.0 / 384.0, scalar2=LN_EPS, op0=ALU.mult, op1=ALU.add),
                     reads=[r_ms[j2]], writes=[r_ms[j2]])
                P.op("dve", lambda e: e.tensor_scalar(out=m[:, 1:2], in0=m[:, 1:2], scalar1=1.0 / 256.0, scalar2=LN_EPS, op0=ALU.mult, op1=ALU.add),
                     reads=[r_ms[j2]], writes=[r_ms[j2]])
                P.op("act", lambda e: e.activation(out=m[:, 0:2], in_=m[:, 0:2], func=AF.Sqrt), reads=[r_ms[j2]], writes=[r_ms[j2]])
                P.op("dve", lambda e: e.reciprocal(out=m[:, 2:4], in_=m[:, 0:2]), reads=[r_ms[j2]], writes=[r_ms[j2]])
                c_ = cn[j2]
                P.op("dve", lambda e: e.scalar_tensor_tensor(out=c_[:, 0:384], in0=z[:, 0:384], scalar=m[:, 2:3], in1=gq[:], op0=ALU.mult, op1=ALU.mult),
                     reads=[r_zs[j2], r_ms[j2], r_g], writes=[r_cn[j2]])
                P.op("dve", lambda e: e.scalar_tensor_tensor(out=c_[:, 384:640], in0=z[:, 384:640], scalar=m[:, 3:4], in1=gkv[:], op0=ALU.mult, op1=ALU.mult),
                     reads=[r_zs[j2], r_ms[j2], r_g], writes=[r_cn[j2]])
                P.op("pool", lambda e: e.tensor_copy(out=krz[:, i, :], in_=z[:, 640:704]), reads=[r_zs[j2]], writes=[r_krz])
                pt = 6 + j2
                pst = cx.ps[pt][:].bitcast(BF16)
                for k in range(5):
                    P.op("pe", lambda e, k=k: e.transpose(out=pst[:, k * 128:(k + 1) * 128], in_=c_[:, k * 128:(k + 1) * 128], identity=cx.identb[:]),
                         reads=[r_cn[j2], cx.rcst], writes=[cx.rps[pt]])
                P.op("act", lambda e: e.activation(out=cqnT[:, :, i * 128:(i + 1) * 128], in_=pst[:, 0:384].rearrange("p (k t) -> p k t", k=3), func=AF.Copy),
                     reads=[cx.rps[pt]], writes=[r_lat[i]])
                P.op("dve", lambda e: e.tensor_copy(out=ckvnT[:, :, i * 128:(i + 1) * 128], in_=pst[:, 384:640].rearrange("p (k t) -> p k t", k=2)),
                     reads=[cx.rps[pt]], writes=[r_lat[i]])
            rope_apply(cx, krb[:], krz[:], cosT[:], sinT[:], rt1[:], rt2[:], NT, r_krz, r_tab, r_rt, r_krb)
            for g in range(2):
                pst = cx.ps[4 + g][:].bitcast(BF16)
                for j in range(8):
                    i = g * 8 + j
                    P.op("pe", lambda e, i=i, j=j: e.transpose(out=pst[0:64, j * 128:(j + 1) * 128], in_=krb[:, i, :], identity=cx.identb[:]),
                         reads=[r_krb, cx.rcst], writes=[cx.rps[4 + g]])
                P.op("act", lambda e, g=g: e.activation(out=kropeT[:, g * 1024:(g + 1) * 1024], in_=pst[0:64, :], func=AF.Copy),
                     reads=[cx.rps[4 + g]], writes=[r_krT])
        P.barrier()
        qnT = sb("qnT", [128, S], BF16)
        knT = sb("knT", [128, S], BF16)
        qrT = sb("qrT", [64, S], BF16)
        vh = sb("vh", [128, NT, 128], BF16)
        r_qnT, r_knT, r_qrT, r_vh = Reg(), Reg(), Reg(), Reg()
        qrb = sb("qrb", [128, 8, 64], BF16)
        r_qrb = Reg()
        rt1 = sb("rt1b", [128, 8, 32], F32)
        rt2 = sb("rt2b", [128, 8, 32], F32)
        r_rt = Reg()
        Pb = sb("Pb", [128, S], BF16)
        r_Pb = Reg()
        PTs4 = sb("PTs4", [128, NT, 512], BF16)
        r_PTs = Reg()
        oT4 = sb("oT4", [128, 512], BF16)
        r_oT = Reg()
        sm = [sb("smx", [128, 8], F32) for j in range(8)]
        r_sm = [Reg() for j in range(8)]
        wo = sb("wo", [128, D], BF16)
        r_wo = Reg()
        all_lat = r_lat
        for h in range(MLA_H):
            P.dma("pool", wo[:], A["mla_w_out"][0][h * 128:(h + 1) * 128, :], writes=[r_wo])
            for tb in range(4):
                for k in range(3):
                    P.op("pe", lambda e, tb=tb, k=k: e.matmul(cx.ps[tb][:, :], lhsT=wuq[:, k, h * 192:h * 192 + 128],
                                                              rhs=cqnT[:, k, tb * 512:(tb + 1) * 512], start=(k == 0), stop=(k == 2)),
                         reads=[r_w] + all_lat[tb * 4:(tb + 1) * 4], writes=[cx.rps[tb]])
                P.op("act", lambda e, tb=tb: e.activation(out=qnT[:, tb * 512:(tb + 1) * 512], in_=cx.ps[tb][:, :], func=AF.Copy),
                     reads=[cx.rps[tb]], writes=[r_qnT])
            for tb in range(4):
                for k in range(2):
                    P.op("pe", lambda e, tb=tb, k=k: e.matmul(cx.ps[4 + tb][:, :], lhsT=wukv[:, k, h * 256:h * 256 + 128],
                                                              rhs=ckvnT[:, k, tb * 512:(tb + 1) * 512], start=(k == 0), stop=(k == 1)),
                         reads=[r_w] + all_lat[tb * 4:(tb + 1) * 4], writes=[cx.rps[4 + tb]])
                P.op("dve", lambda e, tb=tb: e.tensor_copy(out=knT[:, tb * 512:(tb + 1) * 512], in_=cx.ps[4 + tb][:, :]),
                     reads=[cx.rps[4 + tb]], writes=[r_knT])
            for g in range(4):
                for j in range(4):
                    i = g * 4 + j
                    for k in range(2):
                        P.op("pe", lambda e, i=i, j=j, k=k, g=g: e.matmul(cx.ps[g][:, j * 128:(j + 1) * 128], lhsT=ckvnT[:, k, i * 128:(i + 1) * 128],
                                                                          rhs=wukv[:, k, h * 256 + 128:h * 256 + 256], start=(k == 0), stop=(k == 1)),
                             reads=[r_w, all_lat[i]], writes=[cx.rps[g]])
                P.op("act", lambda e, g=g: e.activation(out=vh[:, g * 4:(g + 1) * 4, :], in_=cx.ps[g][:, :].rearrange("p (a b) -> p a b", a=4), func=AF.Copy),
                     reads=[cx.rps[g]], writes=[r_vh])
            for g in range(2):
                pb = 4 + g
                for j in range(8):
                    i = g * 8 + j
                    for k in range(3):
                        P.op("pe", lambda e, i=i, j=j, k=k, pb=pb: e.matmul(cx.ps[pb][:, j * 64:(j + 1) * 64], lhsT=cqnT[:, k, i * 128:(i + 1) * 128],
                                                                            rhs=wuq[:, k, h * 192 + 128:h * 192 + 192], start=(k == 0), stop=(k == 2)),
                             reads=[r_w, all_lat[i]], writes=[cx.rps[pb]])
                rope_apply(cx, qrb[:], cx.ps[pb][:, :].rearrange("p (a b) -> p a b", a=8), cosT[:, g * 8:(g + 1) * 8, :], sinT[:, g * 8:(g + 1) * 8, :],
                           rt1[:], rt2[:], 8, cx.rps[pb], r_tab, r_rt, r_qrb)
                pst = cx.ps[6 + g][:].bitcast(BF16)
                for j in range(8):
                    P.op("pe", lambda e, j=j: e.transpose(out=pst[0:64, j * 128:(j + 1) * 128], in_=qrb[:, j, :], identity=cx.identb[:]),
                         reads=[r_qrb, cx.rcst], writes=[cx.rps[6 + g]])
                P.op("act", lambda e, g=g: e.activation(out=qrT[:, g * 1024:(g + 1) * 1024], in_=pst[0:64, :], func=AF.Copy),
                     reads=[cx.rps[6 + g]], writes=[r_qrT])
            def emit_S(qb):
                base = 4 * (qb % 2)
                qs = slice(qb * 128, (qb + 1) * 128)
                for tb in range(4):
                    P.op("pe", lambda e, tb=tb: e.matmul(cx.ps[base + tb][:, :], lhsT=qnT[:, qs], rhs=knT[:, tb * 512:(tb + 1) * 512], start=True, stop=False),
                         reads=[r_qnT, r_knT], writes=[cx.rps[base + tb]])
                    P.op("pe", lambda e, tb=tb: e.matmul(cx.ps[base + tb][:, :], lhsT=qrT[:, qs], rhs=kropeT[:, tb * 512:(tb + 1) * 512], start=False, stop=True),
                         reads=[r_qrT, r_krT], writes=[cx.rps[base + tb]])

            def emit_softmax(qb):
                base = 4 * (qb % 2)
                m = sm[qb % 8]
                r_m = r_sm[qb % 8]
                for tb in range(4):
                    P.op("dve", lambda e, tb=tb: e.tensor_reduce(out=m[:, tb:tb + 1], in_=cx.ps[base + tb][:, :], axis=AX.X, op=ALU.max),
                         reads=[cx.rps[base + tb]], writes=[r_m])
                P.op("dve", lambda e: e.tensor_reduce(out=m[:, 0:1], in_=m[:, 0:4], axis=AX.X, op=ALU.max), reads=[r_m], writes=[r_m])
                P.op("dve", lambda e: e.tensor_scalar(out=m[:, 1:2], in0=m[:, 0:1], scalar1=-MLA_SCALE, scalar2=None, op0=ALU.mult), reads=[r_m], writes=[r_m])
                for tb in range(4):
                    P.op("act", lambda e, tb=tb: e.activation(out=Pb[:, tb * 512:(tb + 1) * 512], in_=cx.ps[base + tb][:, :], func=AF.Exp, scale=MLA_SCALE,
                                                              bias=m[:, 1:2], accum_out=m[:, 4 + tb:5 + tb]), reads=[cx.rps[base + tb], r_m], writes=[r_Pb, r_m])
                P.op("dve", lambda e: e.tensor_reduce(out=m[:, 2:3], in_=m[:, 4:8], axis=AX.X, op=ALU.add), reads=[r_m], writes=[r_m])
                P.op("dve", lambda e: e.reciprocal(out=m[:, 3:4], in_=m[:, 2:3]), reads=[r_m], writes=[r_m])

            def emit_rest(qb):
                base = 4 * (qb % 2)
                qi = qb % 4
                for g in range(2):
                    pst = cx.ps[base + g][:].bitcast(BF16)
                    for j in range(8):
                        c = g * 8 + j
                        P.op("pe", lambda e, c=c, j=j: e.transpose(out=pst[:, j * 128:(j + 1) * 128], in_=Pb[:, c * 128:(c + 1) * 128], identity=cx.identb[:]),
                             reads=[r_Pb, cx.rcst], writes=[cx.rps[base + g]])
                    P.op("act", lambda e, g=g: e.activation(out=PTs4[:, g * 8:(g + 1) * 8, qi * 128:(qi + 1) * 128], in_=pst.rearrange("p (a b) -> p a b", a=8), func=AF.Copy),
                         reads=[cx.rps[base + g]], writes=[r_PTs])
                if qi != 3:
                    return
                for c in range(NT):
                    P.op("pe", lambda e, c=c: e.matmul(cx.ps[base + 2][:, :], lhsT=vh[:, c, :], rhs=PTs4[:, c, :], start=(c == 0), stop=(c == NT - 1)),
                         reads=[r_vh, r_PTs], writes=[cx.rps[base + 2]])
                P.op("act", lambda e: e.activation(out=oT4[:], in_=cx.ps[base + 2][:, :], func=AF.Copy), reads=[cx.rps[base + 2]], writes=[r_oT])
                for q2 in range(4):
                    qq = qb - 3 + q2
                    m = sm[qq % 8]
                    r_m = r_sm[qq % 8]
                    for hh in range(2):
                        pb = base + 3 - hh
                        P.op("pe", lambda e, hh=hh, pb=pb, q2=q2: e.matmul(cx.ps[pb][:, :], lhsT=oT4[:, q2 * 128:(q2 + 1) * 128], rhs=wo[:, hh * 512:(hh + 1) * 512], start=True, stop=True),
                             reads=[r_oT, r_wo], writes=[cx.rps[pb]])
                        P.op("dve", lambda e, hh=hh, pb=pb, qq=qq, m=m: e.scalar_tensor_tensor(out=cx.R[:, qq, hh * 512:(hh + 1) * 512], in0=cx.ps[pb][:, :], scalar=m[:, 3:4],
                                                                                       in1=cx.R[:, qq, hh * 512:(hh + 1) * 512], op0=ALU.mult, op1=ALU.add),
                             reads=[cx.rps[pb], r_m, cx.rR[qq]], writes=[cx.rR[qq]])

            emit_S(0)
            for qb in range(NT):
                if qb + 1 < NT:
                    emit_S(qb + 1)
                emit_softmax(qb)
                emit_rest(qb)
    P.barrier()


LN_QSCALE = float(np.log(128.0 ** -0.5))


def emit_linattn(cx, kind):
    P = cx.P
    nc = cx.nc
    A = cx.A
    XT = xt_view(cx)
    ml = (kind == "mlstm")
    pre = "mlstm" if ml else "gla"
    W_in = A[pre + "_w_in"][0]
    DV1 = 257 if ml else 256
    cs = 1.0 if ml else 1.0 / 16.0
    with contextlib.ExitStack() as st:
        sb = lambda name, shape, dt: st.enter_context(nc.sbuf_tensor(un(name), shape, dt))
        ng = sb("ng", [128, 256], F32)
        r_ng = Reg()
        r_gw = Reg()
        if ml:
            wgates = sb("wgates", [128, NK, 16], BF16)
            P.dma("pool", wgates[:], W_in[:, 3072:3088].rearrange("(k p) c -> p k c", p=128), writes=[r_gw])
            gbias = sb("gbias", [128, 16], F32)
            P.dma("sp", gbias[:], A["mlstm_gate_b"][0].rearrange("a b c -> (a b c)").partition_broadcast(128), writes=[r_gw])
            ngbias = sb("ngbias", [128, 16], F32)
            P.op("dve", lambda e: e.tensor_scalar(out=ngbias[:], in0=gbias[:], scalar1=-1.0, scalar2=None, op0=ALU.mult), reads=[r_gw], writes=[r_gw])
            wrep = [sb("wrep", [128, NK, 128], BF16) for j in range(2)]
            r_wrep = [Reg(), Reg()]
            igb = sb("igb", [128, S], F32)
            r_igb = Reg()
        else:
            wglr = sb("wglr", [128, NK, 32], BF16)
            P.dma("pool", wglr[:], W_in[:, 3072:3104].rearrange("(k p) c -> p k c", p=128), writes=[r_gw])
            gwpad = sb("gwpad", [32, 2, 512], BF16)
            P.op("pool", lambda e: e.memset(gwpad[:], 0.0), writes=[r_gw])
            for d in range(2):
                P.dma("pool", gwpad[d * 16:(d + 1) * 16, d, :], A["gla_gate_w"][0, d], writes=[r_gw])
            ngb = sb("ngb", [128, 2, 4], F32)
            for d in range(2):
                small_load(cx, ngb[:, d, :], A["gla_gate_b"][0, d].rearrange("(h p) -> p h", p=128), r_gw)
            P.op("dve", lambda e: e.tensor_scalar(out=ngb[:], in0=ngb[:], scalar1=-1.0, scalar2=None, op0=ALU.mult), reads=[r_gw], writes=[r_gw])
            glrT = sb("glrT", [32, S], BF16)
            r_glrT = Reg()
            for tb in range(4):
                for k in range(NK):
                    P.op("pe", lambda e, tb=tb, k=k: e.matmul(cx.ps[tb][0:32, :], lhsT=wglr[:, k, :], rhs=XT[:, k, tb * 512:(tb + 1) * 512],
                                                              start=(k == 0), stop=(k == NK - 1)), reads=[r_gw] + cx.rXA[tb * 4:(tb + 1) * 4], writes=[cx.rps[tb]])
                P.op("act", lambda e, tb=tb: e.activation(out=glrT[:, tb * 512:(tb + 1) * 512], in_=cx.ps[tb][0:32, :], func=AF.Copy),
                     reads=[cx.rps[tb]], writes=[r_glrT])

        WA = sb("WA", [128, NK, 256], BF16)
        WB = sb("WB", [128, NK, 256], BF16)
        r_WA, r_WB = Reg(), Reg()
        wq = WA[:, :, 0:128]
        wk = WA[:, :, 128:256]
        wo = WA[:].rearrange("p k c -> p (k c)").rearrange("p (j f) -> p j f", j=2)
        wv = WB
        wor = WB
        r_wq, r_wk, r_wv, r_wor, r_wo = r_WA, r_WA, r_WB, r_WB, r_WA
        qT = sb("qT", [128, S], F32)
        kT = sb("kT", [128, S], F32)
        r_qT, r_kT = Reg(), Reg()
        vb = sb("vb", [128, NT, DV1], BF16)
        r_vb = Reg()
        buf1 = sb("buf1", [128, S], F32)
        buf2 = sb("buf2", [128, S], F32)
        r_b1, r_b2 = Reg(), Reg()
        qe = sb("qe", [128, S], BF16)
        ke = sb("ke", [128, S], BF16)
        kgT = sb("kgT", [128, NT, 128], BF16)
        r_qe, r_ke, r_kgT = Reg(), Reg(), Reg()
        Oacc = sb("Oacc", [128, NT, 256], F32)
        r_O = [Reg() for i in range(NT)]
        offs = sb("offs", [128, NT], F32)
        gsv = sb("gsv", [128, NT], F32)
        eg = sb("eg", [128, NT], F32)
        r_small = Reg()
        Sst = sb("Sst", [128, DV1], F32)
        Sb = sb("Sb", [128, DV1], BF16)
        r_S, r_Sb = Reg(), Reg()
        atm = [sb("atm", [128, 128], BF16) for j in range(2)]
        r_atm = [Reg(), Reg()]
        dn = [sb("dn", [128, 4], F32) for j in range(2)]
        r_dn = [Reg(), Reg()]
        hsall = sb("hsall", [128, NT, 10], F32)
        r_hsall = Reg()
        gact = [sb("gact", [128, 256], F32) for j in range(2)]
        r_gact = [Reg(), Reg()]
        xn = [sb("xn", [128, 256], F32) for j in range(2)]
        r_xn = [Reg(), Reg()]
        gbt = [sb("gbt", [128, 256], BF16) for j in range(2)]
        r_gbt = [Reg(), Reg()]
        gTt = [sb("gTt", [128, 2, 128], BF16) for j in range(2)]
        r_gTt = [Reg(), Reg()]
        P.op("pool", lambda e: e.memset(offs[:, 0:1], 0.0), writes=[r_small])
        if ml:
            P.op("pool", lambda e: e.memset(vb[:, :, 256:257], 1.0), writes=[r_vb])

        b1v = buf1[:].rearrange("p (c j) -> p c j", j=128)
        b2v = buf2[:].rearrange("p (c j) -> p c j", j=128)
        allXA = cx.rXA

        def wload(dst, reg, c0, c1):
            P.dma("pool", dst, W_in[:, c0:c1].rearrange("(k p) c -> p k c", p=128), writes=[reg])

        for h in range(4):
            wload(wq, r_wq, h * 128, (h + 1) * 128)
            wload(wk, r_wk, 512 + h * 128, 512 + (h + 1) * 128)
            wload(wv[:], r_wv, 1024 + h * 256, 1024 + (h + 1) * 256)
            P.dma("sp", ng[:], A[pre + "_norm_g"][0][h * 256:(h + 1) * 256].partition_broadcast(128), writes=[r_ng])
            for (wsrc, r_w, dst, r_dst, base) in ((wq, r_wq, qT, r_qT, 0), (wk, r_wk, kT, r_kT, 4)):
                for tb in range(4):
                    pb = base + tb
                    for k in range(NK):
                        P.op("pe", lambda e, k=k, pb=pb, tb=tb, wsrc=wsrc: e.matmul(cx.ps[pb][:, :], lhsT=wsrc[:, k, :], rhs=XT[:, k, tb * 512:(tb + 1) * 512],
                                                                                    start=(k == 0), stop=(k == NK - 1)),
                             reads=[r_w] + allXA[tb * 4:(tb + 1) * 4], writes=[cx.rps[pb]])
                    eng = "act" if base == 0 else "dve"
                    if eng == "act":
                        P.op("act", lambda e, pb=pb, tb=tb, dst=dst: e.activation(out=dst[:, tb * 512:(tb + 1) * 512], in_=cx.ps[pb][:, :], func=AF.Copy),
                             reads=[cx.rps[pb]], writes=[r_dst])
                    else:
                        P.op("dve", lambda e, pb=pb, tb=tb, dst=dst: e.tensor_copy(out=dst[:, tb * 512:(tb + 1) * 512], in_=cx.ps[pb][:, :]),
                             reads=[cx.rps[pb]], writes=[r_dst])
            for i in range(NT):
                pb = (i // 2) % 8
                off = (i % 2) * 256
                for k in range(NK):
                    P.op("pe", lambda e, k=k, pb=pb, off=off, i=i: e.matmul(cx.ps[pb][:, off:off + 256], lhsT=XT[:, k, i * 128:(i + 1) * 128], rhs=wv[:, k, :],
                                                                            start=(k == 0), stop=(k == NK - 1)), reads=[r_wv, allXA[i]], writes=[cx.rps[pb]])
                P.op("act", lambda e, pb=pb, off=off, i=i: e.activation(out=vb[:, i, 0:256], in_=cx.ps[pb][:, off:off + 256], func=AF.Copy),
                     reads=[cx.rps[pb]], writes=[r_vb])
            wload(wor[:], r_wor, 2048 + h * 256, 2048 + (h + 1) * 256)
            P.dma("pool", wo, A[pre + "_w_out"][0][h * 256:(h + 1) * 256, :].rearrange("(j p) f -> p j f", p=128), writes=[r_wo])
            for d in range(2):
                if ml:
                    jf = d * 8 + 4 + h
                    ji = d * 8 + h
                    for jj, gidx in enumerate((jf, ji)):
                        P.op("dve", lambda e, jj=jj, gidx=gidx: e.tensor_copy(out=wrep[jj][:], in_=wgates[:, :, gidx:gidx + 1].to_broadcast([128, NK, 128])),
                             reads=[r_gw], writes=[r_wrep[jj]])
                    for tb in range(4):
                        for k in range(NK):
                            P.op("pe", lambda e, k=k, tb=tb: e.matmul(cx.ps[tb][:, :], lhsT=wrep[0][:, k, :], rhs=XT[:, k, tb * 512:(tb + 1) * 512],
                                                                      start=(k == 0), stop=(k == NK - 1)), reads=[r_wrep[0]] + allXA[tb * 4:(tb + 1) * 4], writes=[cx.rps[tb]])
                        P.op("act", lambda e, tb=tb: e.activation(out=buf1[:, tb * 512:(tb + 1) * 512], in_=cx.ps[tb][:, :], func=AF.Exp, scale=-1.0,
                                                                  bias=ngbias[:, jf:jf + 1]), reads=[cx.rps[tb], r_gw], writes=[r_b1])
                    for tb in range(4):
                        for k in range(NK):
                            P.op("pe", lambda e, k=k, tb=tb: e.matmul(cx.ps[4 + tb][:, :], lhsT=wrep[1][:, k, :], rhs=XT[:, k, tb * 512:(tb + 1) * 512],
                                                                      start=(k == 0), stop=(k == NK - 1)), reads=[r_wrep[1]] + allXA[tb * 4:(tb + 1) * 4], writes=[cx.rps[4 + tb]])
                        P.op("act", lambda e, tb=tb: e.activation(out=igb[:, tb * 512:(tb + 1) * 512], in_=cx.ps[4 + tb][:, :], func=AF.Identity,
                                                                  bias=gbias[:, ji:ji + 1]), reads=[cx.rps[4 + tb], r_gw], writes=[r_igb])
                else:
                    for tb in range(4):
                        P.op("pe", lambda e, tb=tb: e.matmul(cx.ps[tb][:, :], lhsT=gwpad[:, d, h * 128:(h + 1) * 128], rhs=glrT[:, tb * 512:(tb + 1) * 512],
                                                             start=True, stop=True), reads=[r_gw, r_glrT], writes=[cx.rps[tb]])
                        P.op("act", lambda e, tb=tb: e.activation(out=buf1[:, tb * 512:(tb + 1) * 512], in_=cx.ps[tb][:, :], func=AF.Exp, scale=-1.0,
                                                                  bias=ngb[:, d, h:h + 1]), reads=[cx.rps[tb], r_gw], writes=[r_b1])
                P.op("act", lambda e: e.activation(out=buf1[:], in_=buf1[:], func=AF.Ln, bias=1.0), reads=[r_b1], writes=[r_b1])
                P.op("dve", lambda e: e.tensor_tensor_scan(out=buf2[:], data0=cx.cst[:, C_ONES:C_ONES + 1].to_broadcast([128, S]), data1=buf1[:], initial=0.0,
                                                            op0=ALU.mult, op1=ALU.add), reads=[r_b1, cx.rcst], writes=[r_b2])
                P.op("dve", lambda e: e.tensor_copy(out=offs[:, 1:NT], in_=b2v[:, 0:NT - 1, 127]), reads=[r_b2], writes=[r_small])
                P.op("dve", lambda e: e.tensor_tensor(out=b2v, in0=b2v, in1=offs[:, :, None].to_broadcast([128, NT, 128]), op=ALU.subtract),
                     reads=[r_b2, r_small], writes=[r_b2])
                P.op("dve", lambda e: e.tensor_copy(out=gsv[:], in_=b2v[:, :, 127]), reads=[r_b2], writes=[r_small])
                P.op("act", lambda e: e.activation(out=eg[:], in_=gsv[:], func=AF.Exp, scale=-cs), reads=[r_small], writes=[r_small])
                gs_bc = gsv[:, :, None].to_broadcast([128, NT, 128])
                if d == 1:
                    P.op("dve", lambda e: e.scalar_tensor_tensor(out=buf2[:], in0=buf2[:], scalar=-1.0, in1=buf1[:], op0=ALU.mult, op1=ALU.add),
                         reads=[r_b1, r_b2], writes=[r_b2])
                    P.op("dve", lambda e: e.tensor_tensor(out=b2v, in0=b2v, in1=gs_bc, op=ALU.add), reads=[r_b2, r_small], writes=[r_b2])
                P.op("dve", lambda e: e.scalar_tensor_tensor(out=b1v, in0=b2v, scalar=-1.0, in1=gs_bc, op0=ALU.mult, op1=ALU.add),
                     reads=[r_b2, r_small], writes=[r_b1])
                if ml:
                    P.op("dve", lambda e: e.scalar_tensor_tensor(out=buf1[:], in0=buf1[:], scalar=-cs, in1=igb[:], op0=ALU.mult, op1=ALU.add),
                         reads=[r_b1, r_igb], writes=[r_b1])
                    P.op("act", lambda e: e.activation(out=buf1[:], in_=buf1[:], func=AF.Exp), reads=[r_b1], writes=[r_b1])
                else:
                    P.op("act", lambda e: e.activation(out=buf1[:], in_=buf1[:], func=AF.Exp, scale=-cs), reads=[r_b1], writes=[r_b1])
                P.op("dve", lambda e: e.tensor_tensor(out=ke[:], in0=kT[:], in1=buf1[:], op=ALU.mult), reads=[r_kT, r_b1], writes=[r_ke])
                for g in range(2):
                    pst = cx.ps[6 + g][:].bitcast(BF16)
                    for j in range(8):
                        c = g * 8 + j
                        P.op("pe", lambda e, c=c, j=j: e.transpose(out=pst[:, j * 128:(j + 1) * 128], in_=ke[:, c * 128:(c + 1) * 128], identity=cx.identb[:]),
                             reads=[r_ke, cx.rcst], writes=[cx.rps[6 + g]])
                    P.op("act", lambda e, g=g: e.activation(out=kgT[:, g * 8:(g + 1) * 8, :], in_=pst.rearrange("p (a b) -> p a b", a=8), func=AF.Copy),
                         reads=[cx.rps[6 + g]], writes=[r_kgT])
                if ml:
                    P.op("dve", lambda e: e.scalar_tensor_tensor(out=buf1[:], in0=buf2[:], scalar=cs, in1=igb[:], op0=ALU.mult, op1=ALU.add),
                         reads=[r_b2, r_igb], writes=[r_b1])
                    P.op("act", lambda e: e.activation(out=buf1[:], in_=buf1[:], func=AF.Exp), reads=[r_b1], writes=[r_b1])
                else:
                    P.op("act", lambda e: e.activation(out=buf1[:], in_=buf2[:], func=AF.Exp, scale=cs), reads=[r_b2], writes=[r_b1])
                P.op("dve", lambda e: e.tensor_tensor(out=ke[:], in0=kT[:], in1=buf1[:], op=ALU.mult), reads=[r_kT, r_b1], writes=[r_ke])
                P.op("act", lambda e: e.activation(out=buf2[:], in_=buf2[:], func=AF.Exp, scale=-cs, bias=cx.cst[:, C_LNQ:C_LNQ + 1]), reads=[r_b2, cx.rcst], writes=[r_b2])
                P.op("dve", lambda e: e.tensor_tensor(out=qe[:], in0=qT[:], in1=buf2[:], op=ALU.mult), reads=[r_qT, r_b2], writes=[r_qe])
                order = list(range(NT)) if d == 0 else list(range(NT - 1, -1, -1))
                mcol = C_TRIU if d == 0 else C_TRIL
                for n_, c in enumerate(order):
                    csl = slice(c * 128, (c + 1) * 128)
                    pa = n_ % 2
                    po = 2 + (n_ % 2)
                    pss = 4 + (n_ % 2)
                    first = (n_ == 0)
                    last = (n_ == NT - 1)
                    P.op("pe", lambda e: e.matmul(cx.ps[pa][:, 0:128], lhsT=ke[:, csl], rhs=qe[:, csl], start=True, stop=True),
                         reads=[r_ke, r_qe], writes=[cx.rps[pa]])
                    am = atm[n_ % 2]
                    P.op("dve", lambda e: e.tensor_tensor(out=am[:], in0=cx.cst[:, mcol:mcol + 128], in1=cx.ps[pa][:, 0:128], op=ALU.mult),
                         reads=[cx.rps[pa], cx.rcst], writes=[r_atm[n_ % 2]])
                    P.op("pe", lambda e: e.matmul(cx.ps[po][:, 0:DV1], lhsT=am[:], rhs=vb[:, c, :], start=True, stop=first),
                         reads=[r_atm[n_ % 2], r_vb], writes=[cx.rps[po]])
                    if not first:
                        P.op("pe", lambda e: e.matmul(cx.ps[po][:, 0:DV1], lhsT=qe[:, csl], rhs=Sb[:], start=False, stop=True),
                             reads=[r_qe, r_Sb], writes=[cx.rps[po]])
                    if not last:
                        P.op("pe", lambda e: e.matmul(cx.ps[pss][:, 0:DV1], lhsT=kgT[:, c, :], rhs=vb[:, c, :], start=True, stop=True),
                             reads=[r_kgT, r_vb], writes=[cx.rps[pss]])
                        if first:
                            P.op("dve", lambda e: e.tensor_copy(out=Sst[:], in_=cx.ps[pss][:, 0:DV1]), reads=[cx.rps[pss]], writes=[r_S])
                        else:
                            P.op("dve", lambda e: e.scalar_tensor_tensor(out=Sst[:], in0=Sst[:], scalar=eg[:, c:c + 1], in1=cx.ps[pss][:, 0:DV1],
                                                                          op0=ALU.mult, op1=ALU.add), reads=[cx.rps[pss], r_S, r_small], writes=[r_S])
                        P.op("act", lambda e: e.activation(out=Sb[:], in_=Sst[:], func=AF.Copy), reads=[r_S], writes=[r_Sb])
                    oc = Oacc[:, c, :]
                    if ml:
                        dd = dn[n_ % 2]
                        r_dd = r_dn[n_ % 2]
                        P.op("act", lambda e: e.activation(out=dd[:, 0:1], in_=cx.ps[po][:, 256:257], func=AF.Abs), reads=[cx.rps[po]], writes=[r_dd])
                        P.op("dve", lambda e: e.tensor_scalar(out=dd[:, 0:1], in0=dd[:, 0:1], scalar1=1.0, scalar2=None, op0=ALU.max), reads=[r_dd], writes=[r_dd])
                        P.op("dve", lambda e: e.reciprocal(out=dd[:, 1:2], in_=dd[:, 0:1]), reads=[r_dd], writes=[r_dd])
                        if d == 0:
                            P.op("act", lambda e: e.activation(out=oc, in_=cx.ps[po][:, 0:256], func=AF.Copy, scale=dd[:, 1:2]),
                                 reads=[cx.rps[po], r_dd], writes=[r_O[c]])
                        else:
                            P.op("dve", lambda e: e.scalar_tensor_tensor(out=oc, in0=cx.ps[po][:, 0:256], scalar=dd[:, 1:2], in1=oc, op0=ALU.mult, op1=ALU.add),
                                 reads=[cx.rps[po], r_dd, r_O[c]], writes=[r_O[c]])
                    else:
                        if d == 0:
                            P.op("act", lambda e: e.activation(out=oc, in_=cx.ps[po][:, 0:256], func=AF.Copy), reads=[cx.rps[po]], writes=[r_O[c]])
                        else:
                            P.op("dve", lambda e: e.tensor_tensor(out=oc, in0=oc, in1=cx.ps[po][:, 0:256], op=ALU.add), reads=[cx.rps[po], r_O[c]], writes=[r_O[c]])
            for i in range(NT):
                P.op("dve", lambda e, i=i: e.bn_stats(out=hsall[:, i, 0:6], in_=Oacc[:, i, :]), reads=[r_O[i]], writes=[r_hsall])
            for i in range(NT):
                P.op("dve", lambda e, i=i: e.bn_aggr(out=hsall[:, i, 6:8], in_=hsall[:, i, 0:6]), reads=[r_hsall], writes=[r_hsall])
            P.op("dve", lambda e: e.tensor_scalar(out=hsall[:, :, 8], in0=hsall[:, :, 7], scalar1=LN_EPS, scalar2=None, op0=ALU.add), reads=[r_hsall], writes=[r_hsall])
            P.op("act", lambda e: e.activation(out=hsall[:, :, 8], in_=hsall[:, :, 8], func=AF.Sqrt), reads=[r_hsall], writes=[r_hsall])
            P.op("dve", lambda e: e.reciprocal(out=hsall[:, :, 9], in_=hsall[:, :, 8]), reads=[r_hsall], writes=[r_hsall])
            for i in range(NT):
                j2 = i % 2
                hs = hsall[:, i, :]
                r_hs = r_hsall
                oc = Oacc[:, i, :]
                x_ = xn[j2]
                P.op("dve", lambda e: e.tensor_scalar(out=x_[:], in0=oc, scalar1=hs[:, 6:7], scalar2=hs[:, 9:10], op0=ALU.subtract, op1=ALU.mult),
                     reads=[r_O[i], r_hs], writes=[r_xn[j2]])
                P.op("pool", lambda e: e.tensor_tensor(out=x_[:], in0=x_[:], in1=ng[:], op=ALU.mult), reads=[r_xn[j2], r_ng], writes=[r_xn[j2]])
                pg = 6 + j2
                for k in range(NK):
                    P.op("pe", lambda e, k=k: e.matmul(cx.ps[pg][:, 0:256], lhsT=XT[:, k, i * 128:(i + 1) * 128], rhs=wor[:, k, :],
                                                       start=(k == 0), stop=(k == NK - 1)), reads=[r_wor, allXA[i]], writes=[cx.rps[pg]])
                ga = gact[j2]
                P.op("act", lambda e: e.activation(out=ga[:], in_=cx.ps[pg][:, 0:256], func=(AF.Sigmoid if ml else AF.Silu)),
                     reads=[cx.rps[pg]], writes=[r_gact[j2]])
                gb_ = gbt[j2]
                P.op("dve", lambda e: e.tensor_tensor(out=gb_[:], in0=x_[:], in1=ga[:], op=ALU.mult), reads=[r_xn[j2], r_gact[j2]], writes=[r_gbt[j2]])
                pt = 4 + j2
                pst = cx.ps[pt][:].bitcast(BF16)
                for j in range(2):
                    P.op("pe", lambda e, j=j: e.transpose(out=pst[:, j * 128:(j + 1) * 128], in_=gb_[:, j * 128:(j + 1) * 128], identity=cx.identb[:]),
                         reads=[r_gbt[j2], cx.rcst], writes=[cx.rps[pt]])
                gT_ = gTt[j2]
                P.op("act", lambda e: e.activation(out=gT_[:], in_=pst[:, 0:256].rearrange("p (a b) -> p a b", a=2), func=AF.Copy),
                     reads=[cx.rps[pt]], writes=[r_gTt[j2]])
                for hh in range(2):
                    pb = (i * 2 + hh) % 4
                    for j in range(2):
                        P.op("pe", lambda e, j=j, hh=hh, pb=pb: e.matmul(cx.ps[pb][:, :], lhsT=gT_[:, j, :], rhs=wo[:, j, hh * 512:(hh + 1) * 512],
                                                                         start=(j == 0), stop=(j == 1)), reads=[r_gTt[j2], r_wo], writes=[cx.rps[pb]])
                    P.op("dve", lambda e, hh=hh, pb=pb: e.tensor_tensor(out=cx.R[:, i, hh * 512:(hh + 1) * 512], in0=cx.R[:, i, hh * 512:(hh + 1) * 512],
                                                                        in1=cx.ps[pb][:, :], op=ALU.add), reads=[cx.rps[pb], cx.rR[i]], writes=[cx.rR[i]])
    P.barrier()


def declare_inputs(nc, nseq, names_shapes):
    aps = {}
    for name, shape, dt in names_shapes:
        aps[name] = nc.dram_tensor(name, list(shape), dt, kind="ExternalInput").ap()
    return aps


WEIGHT_SPECS = [
    ("mlstm_w_in", (1, 1024, 3088)), ("mlstm_gate_b", (1, 2, 2, 4)), ("mlstm_norm_g", (1, 1024)), ("mlstm_w_out", (1, 1024, 1024)),
    ("gla_w_in", (1, 1024, 3104)), ("gla_gate_w", (1, 2, 16, 512)), ("gla_gate_b", (1, 2, 512)), ("gla_norm_g", (1, 1024)),
    ("gla_w_out", (1, 1024, 1024)),
    ("lru_w_in", (1, 1024, 2048)), ("lru_conv_w", (1, 4, 1024)), ("lru_conv_b", (1, 1024)),
    ("lru_gate_a_w", (1, 2, 4, 256, 256)), ("lru_gate_a_b", (1, 2, 1024)), ("lru_gate_x_w", (1, 2, 4, 256, 256)),
    ("lru_gate_x_b", (1, 2, 1024)), ("lru_lambda", (1, 2, 1024)), ("lru_w_out", (1, 1024, 1024)),
    ("mla_w_in", (1, 1024, 704)), ("mla_q_norm_g", (1, 384)), ("mla_kv_norm_g", (1, 256)), ("mla_w_uq", (1, 384, 1536)),
    ("mla_w_ukv", (1, 256, 2048)), ("mla_w_out", (1, 1024, 1024)),
    ("moe_router", (4, 1024, 16)), ("moe_w_gate", (4, 16, 1024, 1024)), ("moe_w_up", (4, 16, 1024, 1024)),
    ("moe_w_down", (4, 16, 1024, 1024)), ("ln_g", (4, 2, 1024)), ("ln_b", (4, 2, 1024)),
]


def build_program(nseq, stages):
    nc = bass.Bass("TRN2", target_bir_lowering=False)
    specs = [("x", (nseq, S, D), F32), ("positions", (nseq, S), I32)]
    specs += [(n, s, F32) for n, s in WEIGHT_SPECS]
    specs += [("consts", (128, C_END), F32), ("consts16", (16, 16 * 128), F32)]
    A = declare_inputs(nc, nseq, specs)
    out = nc.dram_tensor("out", [nseq, S, D], F32, kind="ExternalOutput").ap()
    xd = nc.dram_tensor("xd_scratch", [S, D], BF16, kind="ExternalOutput").ap()
    with contextlib.ExitStack() as stack:
        P = Prog(nc, stack)
        cx = setup_ctx(nc, stack, P)
        cx.A = A
        cx.xd = xd
        cx.yd = nc.dram_tensor("yd_scratch", [S, D], F32, kind="ExternalOutput").ap()
        cx.r_yd = Reg()
        cx.r_xd = [Reg() for i in range(NT)]
        load_consts(cx, A["consts"], A["consts16"])
        wr = stack.enter_context(nc.sbuf_tensor("wr", [128, NK, NE], BF16))
        rwr = Reg()
        lnscr = None
        for s in range(nseq):
            for stg in stages:
                kind = stg[0]
                if kind == "load":
                    xin = A["x"][s].rearrange("(i p) d -> p i d", p=128)
                    for i in range(NT):
                        P.dma("sp", cx.R[:, i, :], xin[:, i, :], writes=[cx.rR[i]])
                elif kind == "ln":
                    _, L, j, mode = stg
                    if mode == "xb":
                        P.dma("pool", wr[:], A["moe_router"][L].rearrange("(k p) e -> p k e", p=128), writes=[rwr])
                    emit_ln(cx, A["ln_g"][L, j], A["ln_b"][L, j], mode, wr=wr, rwr=rwr, do_ln=True,
                            out_ap=out[s].rearrange("(i p) d -> p i d", p=128), scratch=lnscr)
                elif kind == "prep":
                    _, L, mode = stg
                    if mode == "xb":
                        P.dma("pool", wr[:], A["moe_router"][L].rearrange("(k p) e -> p k e", p=128), writes=[rwr])
                    emit_ln(cx, None, None, mode, wr=wr, rwr=rwr, do_ln=False, scratch=lnscr)
                elif kind == "moe":
                    _, L = stg
                    emit_moe(cx, A["moe_w_gate"][L], A["moe_w_up"][L], A["moe_w_down"][L])
                elif kind == "mla":
                    emit_mla(cx, s)
                elif kind in ("mlstm", "gla"):
                    emit_linattn(cx, kind)
                elif kind == "lru":
                    emit_lru(cx)
                elif kind == "store":
                    oo = out[s].rearrange("(i p) d -> p i d", p=128)
                    for i in range(NT):
                        P.dma("sp", oo[:, i, :], cx.R[:, i, :], reads=[cx.rR[i]])
                else:
                    raise ValueError(kind)
        P.barrier(engines=["sp"])
        print("instructions", P.ninstr, "waits", P.nwaits)
    return nc


MIXERS = ("mlstm", "gla", "lru", "mla")


def full_stages():
    st = [("load",), ("prep", 0, "xt")]
    for L in range(DEPTH):
        st.append((MIXERS[L % 4],))
        st.append(("ln", L, 0, "xb"))
        st.append(("moe", L))
        st.append(("ln", L, 1, "xt" if L < DEPTH - 1 else "none"))
    return st


NSEQ_PER_LAUNCH = 4
_PROG_CACHE = {}


def kernel(**inputs):
    nseq = NSEQ_PER_LAUNCH
    x = np.ascontiguousarray(np.asarray(inputs["x"], dtype=np.float32))
    pos = np.ascontiguousarray(np.asarray(inputs["positions"], dtype=np.int32))
    B = x.shape[0]
    per_core = B // NCORES
    consts = make_consts()
    consts16 = make_consts16()
    weights = {n: np.ascontiguousarray(np.asarray(inputs[n], dtype=np.float32)) for n, _ in WEIGHT_SPECS}
    out = np.empty((B, S, D), np.float32)
    for l0 in range(0, per_core, nseq):
        if nseq not in _PROG_CACHE:
            _PROG_CACHE[nseq] = build_program(nseq, full_stages())
        nc = _PROG_CACHE[nseq]
        in_maps = []
        for c in range(NCORES):
            b0 = c * per_core + l0
            m = {"x": x[b0:b0 + nseq], "positions": pos[b0:b0 + nseq], "consts": consts, "consts16": consts16}
            m.update(weights)
            in_maps.append(m)
        res = run_bass_kernel_spmd(nc, in_maps, core_ids=list(range(NCORES)))
        for c in range(NCORES):
            b0 = c * per_core + l0
            out[b0:b0 + nseq] = np.asarray(res.results[c]["out"]).reshape(nseq, S, D)
    return out
```

```python
import contextlib
import numpy as np
import concourse.bass as bass
import concourse.mybir as mybir
from concourse.bass_utils import run_bass_kernel_spmd

F32 = mybir.dt.float32
BF16 = mybir.dt.bfloat16
I32 = mybir.dt.int32
AF = mybir.ActivationFunctionType
ALU = mybir.AluOpType
AX = mybir.AxisListType

D = 1024
S = 2048
NT = 16
NK = 8
DEPTH = 4
ALPHA = (2 * DEPTH) ** 0.25
LN_EPS = 1e-5
NE = 16
CAP = 256
NCORES = 8
SEQ_PER_CORE = 4

SAME_ENGINE_RAW = True


_UN = [0]


def un(name):
    _UN[0] += 1
    return "%s_%d" % (name, _UN[0])


class Reg:
    __slots__ = ("w", "r", "name")

    def __init__(self, name=""):
        self.w = None
        self.r = {}
        self.name = name


class Prog:
    ENG = ("pe", "act", "dve", "pool", "sp")
    KDMA = 8

    def __init__(self, nc, stack):
        self.nc = nc
        self.eng = {"pe": nc.tensor, "act": nc.scalar, "dve": nc.vector, "pool": nc.gpsimd, "sp": nc.sync}
        self.sem = {}
        for e in self.ENG:
            self.sem[("c", e)] = stack.enter_context(nc.semaphore("s_" + e))
        self.dq = ("sp", "pool", "act")
        for q in self.dq:
            for k in range(self.KDMA):
                self.sem[("d", q, k)] = stack.enter_context(nc.semaphore("d_%s%d" % (q, k)))
        self.cnt = {e: 0 for e in self.ENG}
        self.dcnt = {q: 0 for q in self.dq}
        self.known = {e: {} for e in self.ENG}
        self.nwaits = 0
        self.ninstr = 0

    def _wait(self, eng, key, val):
        if self.known[eng].get(key, 0) >= val:
            return
        self.eng[eng].wait_ge(self.sem[key], val)
        self.known[eng][key] = val
        self.nwaits += 1

    def _deps(self, eng, reads, writes):
        me = ("c", eng)
        for r in reads:
            if r.w is not None:
                key, val = r.w
                if key == me:
                    if eng != "pe" and SAME_ENGINE_RAW:
                        self._wait(eng, key, val)
                else:
                    self._wait(eng, key, val)
        for w in writes:
            if w.w is not None:
                key, val = w.w
                if key != me:
                    self._wait(eng, key, val)
            for key, val in w.r.items():
                if key != me:
                    self._wait(eng, key, val)

    def _update(self, tok, reads, writes):
        key, val = tok
        for r in reads:
            if r.r.get(key, 0) < val:
                r.r[key] = val
        for w in writes:
            w.w = tok
            w.r = {}

    def op(self, eng, fn, reads=(), writes=()):
        self._deps(eng, reads, writes)
        ins = fn(self.eng[eng])
        self.cnt[eng] += 1
        self.ninstr += 1
        tok = (("c", eng), self.cnt[eng])
        ins.then_inc(self.sem[tok[0]], 1)
        self._update(tok, reads, writes)
        return tok

    def dma(self, q, out, in_, reads=(), writes=(), **kw):
        k = self.dcnt[q]
        slot = k % self.KDMA
        key = ("d", q, slot)
        if k >= self.KDMA:
            self._wait(q, key, 16 * (k // self.KDMA))
        self._deps(q, reads, writes)
        ins = self.eng[q].dma_start(out=out, in_=in_, **kw)
        tok = (key, 16 * (k // self.KDMA + 1))
        ins.then_inc(self.sem[key], 16)
        self.dcnt[q] = k + 1
        self.ninstr += 1
        self._update(tok, reads, writes)
        return tok

    def idma_gather(self, out, in_dram, idx_ap, reads=(), writes=()):
        q = "pool"
        k = self.dcnt[q]
        slot = k % self.KDMA
        key = ("d", q, slot)
        if k >= self.KDMA:
            self._wait(q, key, 16 * (k // self.KDMA))
        self._deps(q, reads, writes)
        ins = self.eng[q].indirect_dma_start(out=out, out_offset=None, in_=in_dram,
                                             in_offset=bass.IndirectOffsetOnAxis(ap=idx_ap, axis=0))
        tok = (key, 16 * (k // self.KDMA + 1))
        ins.then_inc(self.sem[key], 16)
        self.dcnt[q] = k + 1
        self.ninstr += 1
        self._update(tok, reads, writes)
        return tok

    def idma_scatter_add(self, out_dram, idx_ap, in_, reads=(), writes=(), after=()):
        q = "pool"
        k = self.dcnt[q]
        slot = k % self.KDMA
        key = ("d", q, slot)
        if k >= self.KDMA:
            self._wait(q, key, 16 * (k // self.KDMA))
        for akey, aval in after:
            self._wait(q, akey, aval)
        self._deps(q, reads, writes)
        ins = self.eng[q].indirect_dma_start(out=out_dram, out_offset=bass.IndirectOffsetOnAxis(ap=idx_ap, axis=0),
                                             in_=in_, in_offset=None, compute_op=ALU.add)
        tok = (key, 16 * (k // self.KDMA + 1))
        ins.then_inc(self.sem[key], 16)
        self.dcnt[q] = k + 1
        self.ninstr += 1
        self._update(tok, reads, writes)
        return tok

    def latest_tokens(self):
        toks = []
        for e in self.ENG:
            if self.cnt[e] > 0:
                toks.append((("c", e), self.cnt[e]))
        for q in self.dq:
            k = self.dcnt[q]
            for slot in range(self.KDMA):
                n = (k - slot + self.KDMA - 1) // self.KDMA
                if n > 0:
                    toks.append((("d", q, slot), 16 * n))
        return toks

    def barrier(self, engines=None):
        toks = self.latest_tokens()
        for e in (engines or self.ENG):
            for key, val in toks:
                if key != ("c", e):
                    self._wait(e, key, val)


C_IDENT = 0
C_IOTA = 128
C_PIDX = 384
C_TRIU = 386
C_TRIL = 514
C_ONES = 642
C_FREQ = 770
C_LNQ = 802
C_END = 804


def make_consts():
    c = np.zeros((128, C_END), np.float32)
    c[:, C_IDENT:C_IDENT + 128] = np.eye(128, dtype=np.float32)
    c[:, C_IOTA:C_IOTA + 256] = np.arange(256, dtype=np.float32)[None, :]
    c[:, C_PIDX] = np.arange(128)
    c[:, C_PIDX + 1] = np.arange(128) + 128
    s = np.arange(128)[:, None]
    t = np.arange(128)[None, :]
    c[:, C_TRIU:C_TRIU + 128] = (s <= t)
    c[:, C_TRIL:C_TRIL + 128] = (s >= t)
    c[:, C_ONES:C_ONES + 128] = 1.0
    c[:, C_LNQ] = np.float32(np.log(128.0 ** -0.5))
    c[:, C_FREQ:C_FREQ + 32] = (10000.0 ** (-np.arange(32, dtype=np.float32) / np.float32(32))).astype(np.float32)[None, :]
    return c


def make_consts16():
    c = np.zeros((16, 16 * 128), np.float32)
    for e in range(16):
        c[e, e * 128:(e + 1) * 128] = 1.0
    return c


class Ctx:
    pass


def setup_ctx(nc, stack, P):
    cx = Ctx()
    cx.nc = nc
    cx.P = P
    sb = lambda name, shape, dt: stack.enter_context(nc.sbuf_tensor(name, shape, dt))
    cx.R = sb("R", [128, NT, D], F32)
    cx.rR = [Reg("R%d" % i) for i in range(NT)]
    cx.XA = sb("XA", [128, NT * D], BF16)
    cx.rXA = [Reg("XA%d" % i) for i in range(NT)]
    cx.cst = sb("cst", [128, C_END], F32)
    cx.rcst = Reg("cst")
    cx.identb = sb("identb", [128, 128], BF16)
    cx.logits = sb("logits", [128, NT, NE], F32)
    cx.rlogits = [Reg("lg%d" % i) for i in range(NT)]
    cx.ps = []
    cx.rps = []
    for b in range(8):
        cx.ps.append(stack.enter_context(nc.psum_tensor("ps%d" % b, [128, 512], F32)))
        cx.rps.append(Reg("ps%d" % b))
    return cx


def load_consts(cx, consts_ap, consts16_ap):
    P = cx.P
    P.dma("sp", cx.cst[:], consts_ap[:, :], writes=[cx.rcst])
    P.op("dve", lambda e: e.tensor_copy(out=cx.identb[:], in_=cx.cst[:, C_IDENT:C_IDENT + 128]),
         reads=[cx.rcst], writes=[cx.rcst])


def emit_ln(cx, lng_ap, lnb_ap, mode, wr=None, rwr=None, do_ln=True, out_ap=None, scratch=None):
    P = cx.P
    nc = cx.nc
    with contextlib.ExitStack() as lst:
        _emit_ln(cx, lst, lng_ap, lnb_ap, mode, wr, rwr, do_ln, out_ap)
    P.barrier()


def _emit_ln(cx, lst, lng_ap, lnb_ap, mode, wr, rwr, do_ln, out_ap):
    P = cx.P
    nc = cx.nc
    cx.lng = lst.enter_context(nc.sbuf_tensor(un("lng"), [128, 2, D], F32))
    cx.rlng = Reg("lng")
    tmpT, rtmpT, stat, rstat, xbt, rxbt = alloc_ln_scratch(cx, lst)
    if do_ln:
        P.dma("sp", cx.lng[:, 0, :], lng_ap.partition_broadcast(128), writes=[cx.rlng])
        P.dma("sp", cx.lng[:, 1, :], lnb_ap.partition_broadcast(128), writes=[cx.rlng])
    for i in range(NT):
        Ri = cx.R[:, i, :]
        rRi = cx.rR[i]
        st = stat[i % 2]
        rst = rstat[i % 2]
        if do_ln:
            for h in range(2):
                P.op("dve", lambda e, h=h: e.bn_stats(out=st[:, h * 6:(h + 1) * 6], in_=Ri[:, h * 512:(h + 1) * 512]),
                     reads=[rRi], writes=[rst])
            P.op("dve", lambda e: e.bn_aggr(out=st[:, 12:14], in_=st[:, 0:12]), reads=[rst], writes=[rst])
            P.op("dve", lambda e: e.tensor_scalar(out=st[:, 14:15], in0=st[:, 13:14], scalar1=LN_EPS, scalar2=None, op0=ALU.add),
                 reads=[rst], writes=[rst])
            P.op("act", lambda e: e.activation(out=st[:, 14:15], in_=st[:, 14:15], func=AF.Sqrt), reads=[rst], writes=[rst])
            P.op("dve", lambda e: e.reciprocal(out=st[:, 15:16], in_=st[:, 14:15]), reads=[rst], writes=[rst])
            P.op("dve", lambda e: e.tensor_scalar(out=Ri, in0=Ri, scalar1=st[:, 12:13], scalar2=st[:, 15:16],
                                                   op0=ALU.subtract, op1=ALU.mult), reads=[rRi, rst], writes=[rRi])
            P.op("dve", lambda e: e.tensor_tensor(out=Ri, in0=Ri, in1=cx.lng[:, 0, :], op=ALU.mult),
                 reads=[rRi, cx.rlng], writes=[rRi])
            P.op("pool", lambda e: e.tensor_tensor(out=Ri, in0=Ri, in1=cx.lng[:, 1, :], op=ALU.add),
                 reads=[rRi, cx.rlng], writes=[rRi])
        if mode == "none":
            if out_ap is not None:
                P.dma("sp", out_ap[:, i, :], Ri, reads=[rRi])
            continue
        if mode == "xb":
            xb = cx.XA[:, i * D:(i + 1) * D]
            rxb = cx.rXA[i]
        else:
            xb = xbt[i % 2][:]
            rxb = rxbt[i % 2]
        P.op("act", lambda e: e.activation(out=xb, in_=Ri, func=AF.Copy), reads=[rRi], writes=[rxb])
        P.op("pool", lambda e: e.tensor_scalar(out=Ri, in0=Ri, scalar1=float(ALPHA), scalar2=0.0, op0=ALU.mult, op1=ALU.add),
             reads=[rRi], writes=[rRi])
        pb = 6 + (i % 2)
        pst = cx.ps[pb][:].bitcast(BF16)
        for k in range(NK):
            P.op("pe", lambda e, k=k: e.transpose(out=pst[:, k * 128:(k + 1) * 128], in_=xb[:, k * 128:(k + 1) * 128],
                                                  identity=cx.identb[:]),
                 reads=[rxb, cx.rcst], writes=[cx.rps[pb]])
        if mode == "xt":
            xt = cx.XA[:].rearrange("p (k t) -> p k t", k=NK)
            P.op("dve", lambda e: e.tensor_copy(out=xt[:, :, i * 128:(i + 1) * 128],
                                                in_=pst.rearrange("p (k t) -> p k t", k=NK)),
                 reads=[cx.rps[pb]], writes=[cx.rXA[i]])
        else:
            tt = tmpT[i % 2]
            rtt = rtmpT[i % 2]
            P.op("dve", lambda e: e.tensor_copy(out=tt[:], in_=pst), reads=[cx.rps[pb]], writes=[rtt])
            lb = 4 + (i % 2)
            for k in range(NK):
                P.op("pe", lambda e, k=k: e.matmul(cx.ps[lb][:, 0:NE], lhsT=tt[:, k * 128:(k + 1) * 128], rhs=wr[:, k, :],
                                                   start=(k == 0), stop=(k == NK - 1)),
                     reads=[rtt, rwr], writes=[cx.rps[lb]])
            P.op("act", lambda e: e.activation(out=cx.logits[:, i, :], in_=cx.ps[lb][:, 0:NE], func=AF.Copy),
                 reads=[cx.rps[lb]], writes=[cx.rlogits[i]])


def alloc_ln_scratch(cx, stack):
    nc = cx.nc
    tmpT = [stack.enter_context(nc.sbuf_tensor(un("tmpT"), [128, D], BF16)) for j in range(2)]
    rtmpT = [Reg(), Reg()]
    stat = [stack.enter_context(nc.sbuf_tensor(un("stat"), [128, 16], F32)) for j in range(2)]
    rstat = [Reg(), Reg()]
    xbt = [stack.enter_context(nc.sbuf_tensor(un("xbt"), [128, D], BF16)) for j in range(2)]
    rxbt = [Reg(), Reg()]
    return (tmpT, rtmpT, stat, rstat, xbt, rxbt)


def emit_moe(cx, wg_ap, wu_ap, wd_ap):
    P = cx.P
    nc = cx.nc
    with contextlib.ExitStack() as st:
        sb = lambda name, shape, dt: st.enter_context(nc.sbuf_tensor(un(name), shape, dt))
        aff = sb("aff", [128, NT, NE], F32)
        r_aff = Reg()
        sm = sb("sm", [128, NT, 4], F32)
        AG = sb("AG", [128, NT, NE, 4], BF16)
        r_AG = Reg()
        slot = sb("slot", [128, NT, NE], F32)
        r_slot = Reg()
        slotT = sb("slotT", [16, S], F32)
        r_slotT = Reg()
        m8 = sb("m8", [16, 8], F32)
        r_m8 = Reg()
        wbuf = [sb("wbuf%d" % j, [128, NK, D], BF16) for j in range(3)]
        r_wbuf = [Reg() for j in range(3)]
        Pm = sb("Pm", [128, NT, CAP], BF16)
        r_Pm = Reg()
        workbuf = sb("workbuf", [16, S], F32)
        xinT = sb("xinT", [128, NK, CAP], BF16)
        r_xinT = Reg()
        hT = sb("hT", [128, NK, CAP], BF16)
        r_hT = Reg()
        sg = [sb("sg%d" % j, [128, CAP], F32) for j in range(2)]
        r_sg = [Reg(), Reg()]
        Oe2 = [sb("Oe", [128, 2, D], F32) for j in range(2)]
        Oe = Oe2[0]
        r_Oe = Reg()
        gs = sb("gs", [128, 4], F32)
        r_gs = Reg()
        affT = Pm[:].rearrange("p a b -> p (a b)").bitcast(F32)[0:16, :]
        r_affT = Reg()
        work = workbuf[:]
        r_work = Reg()

        ident = cx.cst[:, C_IDENT:C_IDENT + 128]

        rl = cx.rlogits
        P.op("dve", lambda e: e.tensor_reduce(out=sm[:, :, 0], in_=cx.logits[:], axis=AX.X, op=ALU.max),
             reads=rl, writes=[r_aff])
        P.op("dve", lambda e: e.tensor_tensor(out=aff[:], in0=cx.logits[:],
                                              in1=sm[:, :, 0:1].to_broadcast([128, NT, NE]), op=ALU.subtract),
             reads=rl + [r_aff], writes=[r_aff])
        P.op("act", lambda e: e.activation(out=aff[:], in_=aff[:], func=AF.Exp), reads=[r_aff], writes=[r_aff])
        P.op("dve", lambda e: e.tensor_reduce(out=sm[:, :, 1], in_=aff[:], axis=AX.X, op=ALU.add),
             reads=[r_aff], writes=[r_aff])
        P.op("dve", lambda e: e.reciprocal(out=sm[:, :, 2], in_=sm[:, :, 1]), reads=[r_aff], writes=[r_aff])
        P.op("dve", lambda e: e.tensor_tensor(out=aff[:], in0=aff[:],
                                              in1=sm[:, :, 2:3].to_broadcast([128, NT, NE]), op=ALU.mult),
             reads=[r_aff], writes=[r_aff])
        P.op("dve", lambda e: e.tensor_copy(out=AG[:, :, :, 0], in_=aff[:]), reads=[r_aff], writes=[r_AG])
        P.op("dve", lambda e: e.tensor_tensor(out=AG[:, :, :, 1], in0=aff[:], in1=AG[:, :, :, 0], op=ALU.subtract),
             reads=[r_aff, r_AG], writes=[r_AG])
        for i in range(NT):
            P.op("pool", lambda e, i=i: e.memset(AG[:, i, :, 2], float(i)), writes=[r_AG])
        P.op("dve", lambda e: e.tensor_copy(out=AG[:, :, :, 3], in_=cx.cst[:, C_PIDX:C_PIDX + 1].to_broadcast([128, NT * NE]).rearrange("p (a b) -> p a b", a=NT)),
             reads=[cx.rcst], writes=[r_AG])
        for i in range(NT):
            P.dma("sp", cx.xd[i * 128:(i + 1) * 128, :], cx.XA[:, i * D:(i + 1) * D], reads=[cx.rXA[i]], writes=[cx.r_xd[i]])
        for g in range(4):
            for j in range(4):
                i = g * 4 + j
                P.op("pe", lambda e, i=i, j=j, g=g: e.transpose(out=cx.ps[g][0:16, j * 128:(j + 1) * 128],
                                                                in_=aff[:, i, :], identity=ident),
                     reads=[r_aff, cx.rcst], writes=[cx.rps[g]])
            P.op("act", lambda e, g=g: e.activation(out=affT[:, g * 512:(g + 1) * 512], in_=cx.ps[g][0:16, :], func=AF.Copy),
                 reads=[cx.rps[g]], writes=[r_affT])
        H = S // 2
        sl32 = Oe2[1][:].rearrange("p a b -> p (a b)")[0:32, 0:H + CAP]
        work32 = sl32[:, 0:H]
        P.op("pool", lambda e: e.tensor_copy(out=sl32[0:16, 0:H], in_=affT[:, 0:H]), reads=[r_affT], writes=[r_work])
        P.dma("sp", sl32[16:32, 0:H], affT[:, H:S], reads=[r_affT], writes=[r_work])
        nround = CAP // 8
        for r in range(nround):
            P.op("dve", lambda e, r=r: e.max(out=sl32[:, H + r * 8:H + (r + 1) * 8], in_=work32), reads=[r_work], writes=[r_m8])
            if r < nround - 1:
                P.op("dve", lambda e, r=r: e.match_replace(out=work32, in_to_replace=sl32[:, H + r * 8:H + (r + 1) * 8], in_values=work32, imm_value=-1.0),
                     reads=[r_work, r_m8], writes=[r_work])
        P.dma("sp", workbuf[:, 0:CAP], sl32[16:32, H:H + CAP], reads=[r_m8], writes=[r_work])
        A_ = sl32[0:16, H:H + CAP]
        Brev = workbuf[:, 0:CAP][:, ::-1]
        cand = workbuf[:, 512:512 + CAP + 1]
        P.op("dve", lambda e: e.tensor_tensor(out=cand[:, 1:CAP], in0=A_[:, 0:CAP - 1], in1=Brev[:, 1:CAP], op=ALU.min), reads=[r_work, r_m8], writes=[r_work])
        P.op("dve", lambda e: e.tensor_copy(out=cand[:, 0:1], in_=Brev[:, 0:1]), reads=[r_work], writes=[r_work])
        P.op("dve", lambda e: e.tensor_copy(out=cand[:, CAP:CAP + 1], in_=A_[:, CAP - 1:CAP]), reads=[r_work, r_m8], writes=[r_work])
        P.op("dve", lambda e: e.tensor_reduce(out=m8[:, 7:8], in_=cand, axis=AX.X, op=ALU.max), reads=[r_work], writes=[r_m8])
        P.op("dve", lambda e: e.tensor_scalar(out=work[:], in0=affT[:], scalar1=m8[:, 7:8], scalar2=None, op0=ALU.is_ge),
             reads=[r_affT, r_m8, r_work], writes=[r_work])
        P.op("dve", lambda e: e.tensor_tensor_scan(out=slotT[:], data0=cx.cst[0:16, C_ONES:C_ONES + 1].to_broadcast([16, S]), data1=work[:], initial=0.0,
                                                    op0=ALU.mult, op1=ALU.add),
             reads=[r_work, cx.rcst], writes=[r_slotT])
        P.op("dve", lambda e: e.tensor_tensor(out=slotT[:], in0=slotT[:], in1=work[:], op=ALU.mult),
             reads=[r_work, r_slotT], writes=[r_slotT])
        P.op("dve", lambda e: e.tensor_scalar(out=slotT[:], in0=slotT[:], scalar1=-1.0, scalar2=None, op0=ALU.add),
             reads=[r_slotT], writes=[r_slotT])
        for i in range(NT):
            P.op("pe", lambda e, i=i: e.transpose(out=cx.ps[5][:, i * NE:(i + 1) * NE], in_=slotT[:, i * 128:(i + 1) * 128],
                                                  identity=ident[0:16, 0:16]),
                 reads=[r_slotT, cx.rcst], writes=[cx.rps[5]])
        P.op("act", lambda e: e.activation(out=slot[:].rearrange("p a b -> p (a b)"), in_=cx.ps[5][:, 0:NT * NE], func=AF.Copy),
             reads=[cx.rps[5]], writes=[r_slot])

        P.barrier()
        r_xk = [Reg() for k in range(NK)]
        r_hj = [Reg() for j in range(NK)]
        r_Oq2 = [[[Reg() for n in range(2)] for c in range(2)] for j in range(2)]
        r_Oq = r_Oq2[0]
        sc_prev = [[]]
        P.op("pool", lambda e: e.memset(Oe[:, 0, :], 0.0), writes=[r_Oq[0][0], r_Oq[0][1]])
        for i in range(NT):
            P.dma("sp", cx.yd[i * 128:(i + 1) * 128, :], Oe[:, 0, :], reads=[r_Oq[0][0], r_Oq[0][1]], writes=[cx.r_yd])
        wq = [0]
        xa3 = cx.XA[:].rearrange("p (s k f) -> p s k f", s=2, k=NK)
        wslot = [wbuf[0][:], wbuf[1][:], wbuf[2][:], xa3[:, 0], xa3[:, 1]]
        rw = [[r_wbuf[0]], [r_wbuf[1]], [r_wbuf[2]], [Reg()] + cx.rXA[0:8], [Reg()] + cx.rXA[8:16]]

        def load_w(ap2d):
            j = wq[0] % 5
            wq[0] += 1
            P.dma("pool", wslot[j], ap2d.rearrange("(k p) f -> p k f", p=128), writes=rw[j])
            return j

        def build_P(ex):
            for i in range(NT):
                P.op("dve", lambda e, i=i: e.tensor_scalar(out=Pm[:, i, :], in0=cx.cst[:, C_IOTA:C_IOTA + CAP],
                                                           scalar1=slot[:, i, ex:ex + 1], scalar2=None, op0=ALU.is_equal),
                     reads=[r_slot, cx.rcst], writes=[r_Pm])

        xin = sb("xin", [128, 2, D], BF16)
        r_xin = [Reg(), Reg()]
        gs2 = [sb("gs2", [128, 8], F32) for j in range(2)]
        r_gs2 = [Reg(), Reg()]
        idx2 = [sb("idx2", [128, 2], I32) for j in range(2)]
        r_idx2 = [Reg(), Reg()]

        def gsidx_and_gather(ex):
            g_ = gs2[ex % 2]
            r_g = r_gs2[ex % 2]
            for c in range(2):
                for i in range(NT):
                    P.op("pe", lambda e, i=i, c=c: e.matmul(cx.ps[6][:, 8 + c * 4:8 + c * 4 + 4], lhsT=Pm[:, i, c * 128:(c + 1) * 128],
                                                            rhs=AG[:, i, ex, :], start=(i == 0), stop=(i == NT - 1)),
                         reads=[r_Pm, r_AG], writes=[cx.rps[6]])
            P.op("dve", lambda e: e.tensor_copy(out=g_[:, 0:8], in_=cx.ps[6][:, 8:16]), reads=[cx.rps[6]], writes=[r_g])
            gv = g_[:, 0:8].rearrange("p (c f) -> p c f", c=2)
            P.op("dve", lambda e: e.tensor_tensor(out=gv[:, :, 0], in0=gv[:, :, 0], in1=gv[:, :, 1], op=ALU.add), reads=[r_g], writes=[r_g])
            P.op("dve", lambda e: e.scalar_tensor_tensor(out=gv[:, :, 2], in0=gv[:, :, 2], scalar=128.0, in1=gv[:, :, 3], op0=ALU.mult, op1=ALU.add),
                 reads=[r_g], writes=[r_g])
            ix = idx2[ex % 2]
            P.op("dve", lambda e: e.tensor_copy(out=ix[:], in_=gv[:, :, 2]), reads=[r_g], writes=[r_idx2[ex % 2]])
            for c in range(2):
                P.idma_gather(xin[:, c, :], cx.xd[:, :], ix[:, c:c + 1], reads=[r_idx2[ex % 2]] + cx.r_xd, writes=[r_xin[c]])

        build_P(0)
        gsidx_and_gather(0)
        jnext = (load_w(wg_ap[0]), load_w(wu_ap[0]), load_w(wd_ap[0]))
        for ex in range(NE):
            jg, ju, jd = jnext
            Oe = Oe2[ex % 2]
            r_Oq = r_Oq2[ex % 2]
            if ex + 1 < NE:
                jg1 = load_w(wg_ap[ex + 1])
                ju1 = load_w(wu_ap[ex + 1])
            gsc = gs2[ex % 2][:, 0:8].rearrange("p (c f) -> p c f", c=2)
            r_gs = r_gs2[ex % 2]
            for c in range(2):
                pb = 4 + c
                pst = cx.ps[pb][:].bitcast(BF16)
                for k in range(NK):
                    P.op("pe", lambda e, k=k, c=c: e.transpose(out=pst[:, k * 128:(k + 1) * 128], in_=xin[:, c, k * 128:(k + 1) * 128], identity=cx.identb[:]),
                         reads=[r_xin[c], cx.rcst], writes=[cx.rps[pb]])
                P.op("act", lambda e, c=c: e.activation(out=xinT[:, :, c * 128:(c + 1) * 128], in_=pst.rearrange("p (k t) -> p k t", k=NK), func=AF.Copy),
                     reads=[cx.rps[pb]], writes=r_xk)
            for j in range(NK):
                pg = 4 + (j % 2)
                pu = 6 + (j % 2)
                for k in range(NK):
                    P.op("pe", lambda e, j=j, k=k, pg=pg: e.matmul(cx.ps[pg][:, 0:CAP], lhsT=wslot[jg][:, k, j * 128:(j + 1) * 128],
                                                                   rhs=xinT[:, k, :], start=(k == 0), stop=(k == NK - 1)),
                         reads=rw[jg] + [r_xk[k]], writes=[cx.rps[pg]])
                for k in range(NK):
                    P.op("pe", lambda e, j=j, k=k, pu=pu: e.matmul(cx.ps[pu][:, 0:CAP], lhsT=wslot[ju][:, k, j * 128:(j + 1) * 128],
                                                                   rhs=xinT[:, k, :], start=(k == 0), stop=(k == NK - 1)),
                         reads=rw[ju] + [r_xk[k]], writes=[cx.rps[pu]])
                P.op("act", lambda e, j=j, pg=pg: e.activation(out=sg[j % 2][:], in_=cx.ps[pg][:, 0:CAP], func=AF.Silu),
                     reads=[cx.rps[pg]], writes=[r_sg[j % 2]])
                P.op("dve", lambda e, j=j, pu=pu: e.tensor_tensor(out=hT[:, j, :], in0=sg[j % 2][:], in1=cx.ps[pu][:, 0:CAP], op=ALU.mult),
                     reads=[cx.rps[pu], r_sg[j % 2]], writes=[r_hj[j]])
                if j == 1 and ex + 1 < NE:
                    build_P(ex + 1)
                if j == 3 and ex + 1 < NE:
                    gsidx_and_gather(ex + 1)
            if ex + 1 < NE:
                jnext = (jg1, ju1, load_w(wd_ap[ex + 1]))
            for c in range(2):
                for n in range(2):
                    pb = 4 + ((c * 2 + n) % 4)
                    for j in range(NK):
                        P.op("pe", lambda e, j=j, c=c, n=n, pb=pb: e.matmul(cx.ps[pb][:, :], lhsT=hT[:, j, c * 128:(c + 1) * 128],
                                                                            rhs=wslot[jd][:, j, n * 512:(n + 1) * 512],
                                                                            start=(j == 0), stop=(j == NK - 1)),
                             reads=rw[jd] + [r_hj[j]], writes=[cx.rps[pb]])
                    P.op("act", lambda e, c=c, n=n, pb=pb: e.activation(out=Oe[:, c, n * 512:(n + 1) * 512], in_=cx.ps[pb][:, :],
                                                                        func=AF.Copy, scale=gsc[:, c, 0:1]),
                         reads=[cx.rps[pb], r_gs], writes=[r_Oq[c][n]])
            toks = []
            for c in range(2):
                toks.append(P.idma_scatter_add(cx.yd[:, :], idx2[ex % 2][:, c:c + 1], Oe[:, c, :],
                                               reads=[r_idx2[ex % 2], r_Oq[c][0], r_Oq[c][1], cx.r_yd], after=sc_prev[0]))
            sc_prev[0] = toks
        for akey, aval in sc_prev[0]:
            P._wait("sp", akey, aval)
        for i in range(NT):
            jb = (i // 2) % 2
            c = i % 2
            buf = Oe2[jb][:, c, :]
            rb = r_Oq2[jb][c]
            P.dma("sp", buf, cx.yd[i * 128:(i + 1) * 128, :], reads=[cx.r_yd], writes=[rb[0], rb[1]])
            eng = "dve" if i % 2 == 0 else "pool"
            P.op(eng, lambda e, i=i, buf=buf: e.tensor_tensor(out=cx.R[:, i, :], in0=cx.R[:, i, :], in1=buf, op=ALU.add),
                 reads=[rb[0], rb[1], cx.rR[i]], writes=[cx.rR[i]])
    P.barrier()


def xt_view(cx):
    return cx.XA[:].rearrange("p (k t) -> p k t", k=NK)


def small_load(cx, out, in_, reg):
    cx.P.dma("sp", out, in_, writes=[reg], allow_slow_non_contiguous=True)


def gelu_tanh(cx, out, src, t1, t2, r_src, r_t1, r_t2, r_out, extra_mul=None, r_extra=None):
    P = cx.P
    P.op("act", lambda e: e.activation(out=t1, in_=src, func=AF.Square), reads=[r_src], writes=[r_t1])
    P.op("dve", lambda e: e.tensor_scalar(out=t1, in0=t1, scalar1=0.044715, scalar2=1.0, op0=ALU.mult, op1=ALU.add),
         reads=[r_t1], writes=[r_t1])
    P.op("dve", lambda e: e.tensor_tensor(out=t1, in0=t1, in1=src, op=ALU.mult), reads=[r_t1, r_src], writes=[r_t1])
    P.op("act", lambda e: e.activation(out=t1, in_=t1, func=AF.Sigmoid, scale=1.5957691216057308), reads=[r_t1], writes=[r_t1])
    if extra_mul is None:
        P.op("dve", lambda e: e.tensor_tensor(out=out, in0=t1, in1=src, op=ALU.mult), reads=[r_t1, r_src], writes=[r_out])
    else:
        P.op("dve", lambda e: e.tensor_tensor(out=t2, in0=t1, in1=src, op=ALU.mult), reads=[r_t1, r_src], writes=[r_t2])
        P.op("pool", lambda e: e.tensor_tensor(out=out, in0=t2, in1=extra_mul, op=ALU.mult), reads=[r_t2, r_extra], writes=[r_out])


def emit_lru(cx):
    P = cx.P
    nc = cx.nc
    A = cx.A
    XT = xt_view(cx)
    with contextlib.ExitStack() as st:
        sb = lambda name, shape, dt: st.enter_context(nc.sbuf_tensor(un(name), shape, dt))
        cw = sb("cw", [128, NK, 4], F32)
        cb = sb("cb", [128, NK], F32)
        gab = sb("gab", [128, 2, NK], F32)
        gxb = sb("gxb", [128, 2, NK], F32)
        lam = sb("lam", [128, 2, NK], F32)
        c8 = sb("c8", [128, 2, NK], F32)
        c16 = sb("c16", [128, 2, NK], F32)
        r_par = Reg()
        for j in range(4):
            small_load(cx, cw[:, :, j], A["lru_conv_w"][0, j].rearrange("(c p) -> p c", p=128), r_par)
        small_load(cx, cb[:], A["lru_conv_b"][0].rearrange("(c p) -> p c", p=128), r_par)
        for g in range(2):
            small_load(cx, gab[:, g, :], A["lru_gate_a_b"][0, g].rearrange("(c p) -> p c", p=128), r_par)
        for g in range(2):
            small_load(cx, gxb[:, g, :], A["lru_gate_x_b"][0, g].rearrange("(c p) -> p c", p=128), r_par)
        for g in range(2):
            small_load(cx, lam[:, g, :], A["lru_lambda"][0, g].rearrange("(c p) -> p c", p=128), r_par)
        P.op("act", lambda e: e.activation(out=c8[:], in_=lam[:], func=AF.Exp, scale=-1.0), reads=[r_par], writes=[r_par])
        P.op("act", lambda e: e.activation(out=c8[:], in_=c8[:], func=AF.Ln, bias=1.0), reads=[r_par], writes=[r_par])
        P.op("dve", lambda e: e.tensor_scalar(out=c16[:], in0=c8[:], scalar1=-16.0, scalar2=None, op0=ALU.mult), reads=[r_par], writes=[r_par])
        P.op("dve", lambda e: e.tensor_scalar(out=c8[:], in0=c8[:], scalar1=-8.0, scalar2=None, op0=ALU.mult), reads=[r_par], writes=[r_par])

        wgb = sb("wgb", [128, NK, 256], BF16)
        r_wgb = Reg()
        wu = sb("wu", [128, NK, 256], BF16)
        r_wu = Reg()
        wg4 = [[sb("wga", [128, 2, 256], BF16) for d in range(2)] for ax in range(2)]
        r_wg4 = [[Reg() for d in range(2)] for ax in range(2)]
        wo = sb("wo", [128, D], BF16)
        r_wo = Reg()
        upad = sb("upad", [128, S + 4], F32)
        r_upad = Reg()
        uc = sb("uc", [128, 2, S], F32)
        r_uc = [Reg(), Reg()]
        ucb = sb("ucb", [128, 2, S], BF16)
        r_ucb = [Reg(), Reg()]
        rbuf = sb("rbuf", [128, S], F32)
        r_rbuf = Reg()
        abuf = sb("abuf", [128, S], F32)
        r_abuf = Reg()
        tmp = sb("tmp", [128, S], F32)
        r_tmp = Reg()
        hsum = sb("hsum", [128, S], F32)
        r_hsum = Reg()
        gtc = sb("gtc", [128, S], BF16)
        r_gtc = Reg()
        P.op("pool", lambda e: e.memset(upad[:], 0.0), writes=[r_upad])

        def psum4(base):
            return [cx.ps[base + j] for j in range(4)], [cx.rps[base + j] for j in range(4)]

        pcount = [0]

        def next_ps4():
            base = 4 * (pcount[0] % 2)
            pcount[0] += 1
            return psum4(base)

        for n in range(4):
            P.dma("pool", wgb[:], A["lru_w_in"][0][:, n * 256:(n + 1) * 256].rearrange("(k p) c -> p k c", p=128), writes=[r_wgb])
            P.dma("pool", wu[:], A["lru_w_in"][0][:, D + n * 256:D + (n + 1) * 256].rearrange("(k p) c -> p k c", p=128), writes=[r_wu])
            for ax, nm in enumerate(("lru_gate_a_w", "lru_gate_x_w")):
                for d in range(2):
                    P.dma("pool", wg4[ax][d][:], A[nm][0, d, n].rearrange("(i p) j -> p i j", p=128), writes=[r_wg4[ax][d]])
            for cc in range(2):
                c = 2 * n + cc
                ps, rps = next_ps4()
                for tb in range(4):
                    for k in range(NK):
                        P.op("pe", lambda e, tb=tb, k=k: e.matmul(ps[tb][:, :], lhsT=wu[:, k, cc * 128:(cc + 1) * 128],
                                                                  rhs=XT[:, k, tb * 512:(tb + 1) * 512], start=(k == 0), stop=(k == NK - 1)),
                             reads=[r_wu] + cx.rXA[tb * 4:(tb + 1) * 4], writes=[rps[tb]])
                    P.op("act", lambda e, tb=tb: e.activation(out=upad[:, 2 + tb * 512:2 + (tb + 1) * 512], in_=ps[tb][:, :], func=AF.Copy),
                         reads=[rps[tb]], writes=[r_upad])
                ucc = uc[:, cc, :]
                P.op("dve", lambda e: e.tensor_scalar(out=ucc, in0=upad[:, 0:S], scalar1=cw[:, c, 0:1], scalar2=cb[:, c:c + 1],
                                                       op0=ALU.mult, op1=ALU.add), reads=[r_upad, r_par], writes=[r_uc[cc]])
                for j in range(1, 4):
                    P.op("dve", lambda e, j=j: e.scalar_tensor_tensor(out=ucc, in0=upad[:, j:j + S], scalar=cw[:, c, j:j + 1], in1=ucc,
                                                                      op0=ALU.mult, op1=ALU.add), reads=[r_upad, r_par, r_uc[cc]], writes=[r_uc[cc]])
                P.op("pool", lambda e: e.tensor_copy(out=ucb[:, cc, :], in_=ucc), reads=[r_uc[cc]], writes=[r_ucb[cc]])
            for cc in range(2):
                c = 2 * n + cc
                ucc = uc[:, cc, :]
                for d in range(2):
                    ps, rps = next_ps4()
                    for tb in range(4):
                        for i in range(2):
                            P.op("pe", lambda e, tb=tb, i=i: e.matmul(ps[tb][:, :], lhsT=wg4[0][d][:, i, cc * 128:(cc + 1) * 128],
                                                                      rhs=ucb[:, i, tb * 512:(tb + 1) * 512], start=(i == 0), stop=(i == 1)),
                                 reads=[r_wg4[0][d], r_ucb[0], r_ucb[1]], writes=[rps[tb]])
                        P.op("act", lambda e, tb=tb: e.activation(out=rbuf[:, tb * 512:(tb + 1) * 512], in_=ps[tb][:, :], func=AF.Sigmoid,
                                                                  bias=gab[:, d, c:c + 1]), reads=[rps[tb], r_par], writes=[r_rbuf])
                    P.op("act", lambda e: e.activation(out=abuf[:], in_=rbuf[:], func=AF.Exp, scale=c8[:, d, c:c + 1]),
                         reads=[r_rbuf, r_par], writes=[r_abuf])
                    P.op("act", lambda e: e.activation(out=tmp[:], in_=rbuf[:], func=AF.Exp, scale=c16[:, d, c:c + 1]),
                         reads=[r_rbuf, r_par], writes=[r_tmp])
                    P.op("act", lambda e: e.activation(out=tmp[:], in_=tmp[:], func=AF.Sqrt, scale=-1.0, bias=1.0),
                         reads=[r_tmp], writes=[r_tmp])
                    ps, rps = next_ps4()
                    for tb in range(4):
                        for i in range(2):
                            P.op("pe", lambda e, tb=tb, i=i: e.matmul(ps[tb][:, :], lhsT=wg4[1][d][:, i, cc * 128:(cc + 1) * 128],
                                                                      rhs=ucb[:, i, tb * 512:(tb + 1) * 512], start=(i == 0), stop=(i == 1)),
                                 reads=[r_wg4[1][d], r_ucb[0], r_ucb[1]], writes=[rps[tb]])
                        P.op("act", lambda e, tb=tb: e.activation(out=rbuf[:, tb * 512:(tb + 1) * 512], in_=ps[tb][:, :], func=AF.Sigmoid,
                                                                  bias=gxb[:, d, c:c + 1]), reads=[rps[tb], r_par], writes=[r_rbuf])
                    P.op("dve", lambda e: e.tensor_tensor(out=tmp[:], in0=tmp[:], in1=rbuf[:], op=ALU.mult), reads=[r_tmp, r_rbuf], writes=[r_tmp])
                    P.op("dve", lambda e: e.tensor_tensor(out=tmp[:], in0=tmp[:], in1=ucc, op=ALU.mult), reads=[r_tmp, r_uc[cc]], writes=[r_tmp])
                    if d == 0:
                        P.op("dve", lambda e: e.tensor_tensor_scan(out=hsum[:], data0=abuf[:], data1=tmp[:], initial=0.0,
                                                                    op0=ALU.mult, op1=ALU.add), reads=[r_abuf, r_tmp], writes=[r_hsum])
                    else:
                        P.op("dve", lambda e: e.tensor_tensor_scan(out=rbuf[:, ::-1], data0=abuf[:, ::-1], data1=tmp[:, ::-1], initial=0.0,
                                                                    op0=ALU.mult, op1=ALU.add), reads=[r_abuf, r_tmp], writes=[r_rbuf])
                        P.op("pool", lambda e: e.tensor_tensor(out=hsum[:], in0=hsum[:], in1=rbuf[:], op=ALU.add),
                             reads=[r_rbuf, r_hsum], writes=[r_hsum])
                ps, rps = next_ps4()
                for tb in range(4):
                    for k in range(NK):
                        P.op("pe", lambda e, tb=tb, k=k: e.matmul(ps[tb][:, :], lhsT=wgb[:, k, cc * 128:(cc + 1) * 128],
                                                                  rhs=XT[:, k, tb * 512:(tb + 1) * 512], start=(k == 0), stop=(k == NK - 1)),
                             reads=[r_wgb] + cx.rXA[tb * 4:(tb + 1) * 4], writes=[rps[tb]])
                    sl = slice(tb * 512, (tb + 1) * 512)
                    gelu_tanh(cx, gtc[:, sl], ps[tb][:, :], abuf[:, sl], tmp[:, sl], rps[tb], r_abuf, r_tmp, r_gtc,
                              extra_mul=hsum[:, sl], r_extra=r_hsum)
                P.dma("pool", wo[:], A["lru_w_out"][0][c * 128:(c + 1) * 128, :], writes=[r_wo])
                for i in range(NT):
                    for hh in range(2):
                        pb = (i * 2 + hh) % 8
                        P.op("pe", lambda e, i=i, hh=hh, pb=pb: e.matmul(cx.ps[pb][:, :], lhsT=gtc[:, i * 128:(i + 1) * 128],
                                                                         rhs=wo[:, hh * 512:(hh + 1) * 512], start=True, stop=True),
                             reads=[r_gtc, r_wo], writes=[cx.rps[pb]])
                        P.op("dve", lambda e, i=i, hh=hh, pb=pb: e.tensor_tensor(out=cx.R[:, i, hh * 512:(hh + 1) * 512],
                                                                                 in0=cx.R[:, i, hh * 512:(hh + 1) * 512],
                                                                                 in1=cx.ps[pb][:, :], op=ALU.add),
                             reads=[cx.rps[pb], cx.rR[i]], writes=[cx.rR[i]])
    P.barrier()


MLA_H = 8
MLA_SCALE = 192.0 ** -0.5
TWO_PI_HI = 6.28125
TWO_PI_LO = 2.0 * np.pi - 6.28125
MAGIC = 12582912.0


def rope_apply(cx, out_bf, src, cos, sin, t1, t2, nb, r_src, r_tab, r_t, r_out):
    P = cx.P
    a1 = src[:, :, 0:32]
    a2 = src[:, :, 32:64]
    P.op("dve", lambda e: e.tensor_tensor(out=t1, in0=a1, in1=cos, op=ALU.mult), reads=[r_src, r_tab], writes=[r_t])
    P.op("dve", lambda e: e.tensor_tensor(out=t2, in0=a2, in1=sin, op=ALU.mult), reads=[r_src, r_tab], writes=[r_t])
    P.op("dve", lambda e: e.tensor_tensor(out=out_bf[:, :, 0:32], in0=t1, in1=t2, op=ALU.subtract), reads=[r_t], writes=[r_out])
    P.op("dve", lambda e: e.tensor_tensor(out=t1, in0=a2, in1=cos, op=ALU.mult), reads=[r_src, r_tab, r_out], writes=[r_t])
    P.op("dve", lambda e: e.tensor_tensor(out=t2, in0=a1, in1=sin, op=ALU.mult), reads=[r_src, r_tab], writes=[r_t])
    P.op("dve", lambda e: e.tensor_tensor(out=out_bf[:, :, 32:64], in0=t1, in1=t2, op=ALU.add), reads=[r_t], writes=[r_out])


def emit_mla(cx, seq):
    P = cx.P
    nc = cx.nc
    A = cx.A
    XT = xt_view(cx)
    with contextlib.ExitStack() as st:
        sb = lambda name, shape, dt: st.enter_context(nc.sbuf_tensor(un(name), shape, dt))
        posi = sb("posi", [128, NT], I32)
        ang = sb("ang", [128, NT, 32], F32)
        nn = sb("nn", [128, NT, 32], F32)
        cosT = sb("cosT", [128, NT, 32], F32)
        sinT = sb("sinT", [128, NT, 32], F32)
        r_tab = Reg()
        small_load(cx, posi[:], A["positions"][seq].rearrange("(i p) -> p i", p=128), r_tab)
        P.op("dve", lambda e: e.tensor_copy(out=nn[:, :, 0], in_=posi[:]), reads=[r_tab], writes=[r_tab])
        P.op("dve", lambda e: e.tensor_tensor(out=ang[:], in0=nn[:, :, 0:1].to_broadcast([128, NT, 32]),
                                              in1=cx.cst[:, None, C_FREQ:C_FREQ + 32].to_broadcast([128, NT, 32]), op=ALU.mult),
             reads=[r_tab, cx.rcst], writes=[r_tab])
        P.op("dve", lambda e: e.tensor_scalar(out=nn[:], in0=ang[:], scalar1=float(1.0 / (2.0 * np.pi)), scalar2=None, op0=ALU.mult),
             reads=[r_tab], writes=[r_tab])
        P.op("dve", lambda e: e.tensor_scalar(out=nn[:], in0=nn[:], scalar1=MAGIC, scalar2=None, op0=ALU.add), reads=[r_tab], writes=[r_tab])
        P.op("dve", lambda e: e.tensor_scalar(out=nn[:], in0=nn[:], scalar1=-MAGIC, scalar2=None, op0=ALU.add), reads=[r_tab], writes=[r_tab])
        P.op("dve", lambda e: e.scalar_tensor_tensor(out=ang[:], in0=nn[:], scalar=-TWO_PI_HI, in1=ang[:], op0=ALU.mult, op1=ALU.add),
             reads=[r_tab], writes=[r_tab])
        P.op("dve", lambda e: e.scalar_tensor_tensor(out=ang[:], in0=nn[:], scalar=-TWO_PI_LO, in1=ang[:], op0=ALU.mult, op1=ALU.add),
             reads=[r_tab], writes=[r_tab])
        PI_S = 3.1415925
        P.op("dve", lambda e: e.tensor_scalar(out=ang[:], in0=ang[:], scalar1=PI_S, scalar2=-PI_S, op0=ALU.min, op1=ALU.max),
             reads=[r_tab], writes=[r_tab])
        P.op("act", lambda e: e.activation(out=sinT[:], in_=ang[:], func=AF.Sin), reads=[r_tab], writes=[r_tab])
        P.op("act", lambda e: e.activation(out=nn[:], in_=ang[:], func=AF.Abs), reads=[r_tab], writes=[r_tab])
        P.op("dve", lambda e: e.tensor_scalar(out=nn[:], in0=nn[:], scalar1=-1.0, scalar2=float(np.pi / 2), op0=ALU.mult, op1=ALU.add),
             reads=[r_tab], writes=[r_tab])
        P.op("act", lambda e: e.activation(out=cosT[:], in_=nn[:], func=AF.Sin), reads=[r_tab], writes=[r_tab])

        wuq = sb("wuq", [128, 3, 1536], BF16)
        wukv = sb("wukv", [128, 2, 2048], BF16)
        r_w = Reg()
        P.dma("pool", wuq[:], A["mla_w_uq"][0].rearrange("(k p) c -> p k c", p=128), writes=[r_w])
        P.dma("pool", wukv[:], A["mla_w_ukv"][0].rearrange("(k p) c -> p k c", p=128), writes=[r_w])
        cqnT = sb("cqnT", [128, 3, S], BF16)
        ckvnT = sb("ckvnT", [128, 2, S], BF16)
        kropeT = sb("kropeT", [64, S], BF16)
        r_lat = [Reg() for i in range(NT)]
        r_krT = Reg()
        with contextlib.ExitStack() as st1:
            sb1 = lambda name, shape, dt: st1.enter_context(nc.sbuf_tensor(un(name), shape, dt))
            wi = sb1("wi", [128, NK, 704], BF16)
            r_wi = Reg()
            P.dma("pool", wi[:], A["mla_w_in"][0].rearrange("(k p) c -> p k c", p=128), writes=[r_wi])
            gq = sb1("gq", [128, 384], F32)
            gkv = sb1("gkv", [128, 256], F32)
            r_g = Reg()
            P.dma("sp", gq[:], A["mla_q_norm_g"][0].partition_broadcast(128), writes=[r_g])
            P.dma("sp", gkv[:], A["mla_kv_norm_g"][0].partition_broadcast(128), writes=[r_g])
            zs = [sb1("zs", [128, 704], F32) for j in range(2)]
            r_zs = [Reg(), Reg()]
            junk = sb1("junk", [128, 384], F32)
            r_junk = Reg()
            ms = [sb1("ms", [128, 4], F32) for j in range(2)]
            r_ms = [Reg(), Reg()]
            cn = [sb1("cn", [128, 640], BF16) for j in range(2)]
            r_cn = [Reg(), Reg()]
            krz = sb1("krz", [128, NT, 64], F32)
            r_krz = Reg()
            krb = sb1("krb", [128, NT, 64], BF16)
            r_krb = Reg()
            rt1 = sb1("rt1", [128, NT, 32], F32)
            rt2 = sb1("rt2", [128, NT, 32], F32)
            r_rt = Reg()
            for i in range(NT):
                j2 = i % 2
                pa = 0 + 2 * j2
                pbk = 1 + 2 * j2
                for k in range(NK):
                    P.op("pe", lambda e, k=k: e.matmul(cx.ps[pa][:, :], lhsT=XT[:, k, i * 128:(i + 1) * 128], rhs=wi[:, k, 0:512],
                                                       start=(k == 0), stop=(k == NK - 1)), reads=[cx.rXA[i], r_wi], writes=[cx.rps[pa]])
                for k in range(NK):
                    P.op("pe", lambda e, k=k: e.matmul(cx.ps[pbk][:, 0:192], lhsT=XT[:, k, i * 128:(i + 1) * 128], rhs=wi[:, k, 512:704],
                                                       start=(k == 0), stop=(k == NK - 1)), reads=[cx.rXA[i], r_wi], writes=[cx.rps[pbk]])
                z = zs[j2]
                P.op("act", lambda e: e.activation(out=z[:, 0:512], in_=cx.ps[pa][:, :], func=AF.Copy), reads=[cx.rps[pa]], writes=[r_zs[j2]])
                P.op("dve", lambda e: e.tensor_copy(out=z[:, 512:704], in_=cx.ps[pbk][:, 0:192]), reads=[cx.rps[pbk]], writes=[r_zs[j2]])
                m = ms[j2]
                P.op("act", lambda e: e.activation(out=junk[:, 0:384], in_=z[:, 0:384], func=AF.Square, accum_out=m[:, 0:1]),
                     reads=[r_zs[j2]], writes=[r_junk, r_ms[j2]])
                P.op("act", lambda e: e.activation(out=junk[:, 0:256], in_=z[:, 384:640], func=AF.Square, accum_out=m[:, 1:2]),
                     reads=[r_zs[j2]], writes=[r_junk, r_ms[j2]])
                P.op("dve", lambda e: e.tensor_scalar(out=m[:, 0:1], in0=m[:, 0:1], scalar1=1.0 / 384.0, scalar2=LN_EPS, op0=ALU.mult, op1=ALU.add),
                     reads=[r_ms[j2]], writes=[r_ms[j2]])
                P.op("dve", lambda e: e.tensor_scalar(out=m[:, 1:2], in0=m[:, 1:2], scalar1=1.0 / 256.0, scalar2=LN_EPS, op0=ALU.mult, op1=ALU.add),
                     reads=[r_ms[j2]], writes=[r_ms[j2]])
                P.op("act", lambda e: e.activation(out=m[:, 0:2], in_=m[:, 0:2], func=AF.Sqrt), reads=[r_ms[j2]], writes=[r_ms[j2]])
                P.op("dve", lambda e: e.reciprocal(out=m[:, 2:4], in_=m[:, 0:2]), reads=[r_ms[j2]], writes=[r_ms[j2]])
                c_ = cn[j2]
                P.op("dve", lambda e: e.scalar_tensor_tensor(out=c_[:, 0:384], in0=z[:, 0:384], scalar=m[:, 2:3], in1=gq[:], op0=ALU.mult, op1=ALU.mult),
                     reads=[r_zs[j2], r_ms[j2], r_g], writes=[r_cn[j2]])
                P.op("dve", lambda e: e.scalar_tensor_tensor(out=c_[:, 384:640], in0=z[:, 384:640], scalar=m[:, 3:4], in1=gkv[:], op0=ALU.mult, op1=ALU.mult),
                     reads=[r_zs[j2], r_ms[j2], r_g], writes=[r_cn[j2]])
                P.op("pool", lambda e: e.tensor_copy(out=krz[:, i, :], in_=z[:, 640:704]), reads=[r_zs[j2]], writes=[r_krz])
                pt = 6 + j2
                pst = cx.ps[pt][:].bitcast(BF16)
                for k in range(5):
                    P.op("pe", lambda e, k=k: e.transpose(out=pst[:, k * 128:(k + 1) * 128], in_=c_[:, k * 128:(k + 1) * 128], identity=cx.identb[:]),
                         reads=[r_cn[j2], cx.rcst], writes=[cx.rps[pt]])
                P.op("act", lambda e: e.activation(out=cqnT[:, :, i * 128:(i + 1) * 128], in_=pst[:, 0:384].rearrange("p (k t) -> p k t", k=3), func=AF.Copy),
                     reads=[cx.rps[pt]], writes=[r_lat[i]])
                P.op("dve", lambda e: e.tensor_copy(out=ckvnT[:, :, i * 128:(i + 1) * 128], in_=pst[:, 384:640].rearrange("p (k t) -> p k t", k=2)),
                     reads=[cx.rps[pt]], writes=[r_lat[i]])
            rope_apply(cx, krb[:], krz[:], cosT[:], sinT[:], rt1[:], rt2[:], NT, r_krz, r_tab, r_rt, r_krb)
            for g in range(2):
                pst = cx.ps[4 + g][:].bitcast(BF16)
                for j in range(8):
                    i = g * 8 + j
                    P.op("pe", lambda e, i=i, j=j: e.transpose(out=pst[0:64, j * 128:(j + 1) * 128], in_=krb[:, i, :], identity=cx.identb[:]),
                         reads=[r_krb, cx.rcst], writes=[cx.rps[4 + g]])
                P.op("act", lambda e, g=g: e.activation(out=kropeT[:, g * 1024:(g + 1) * 1024], in_=pst[0:64, :], func=AF.Copy),
                     reads=[cx.rps[4 + g]], writes=[r_krT])
        P.barrier()
        qnT = sb("qnT", [128, S], BF16)
        knT = sb("knT", [128, S], BF16)
        qrT = sb("qrT", [64, S], BF16)
        vh = sb("vh", [128, NT, 128], BF16)
        r_qnT, r_knT, r_qrT, r_vh = Reg(), Reg(), Reg(), Reg()
        qrb = sb("qrb", [128, 8, 64], BF16)
        r_qrb = Reg()
        rt1 = sb("rt1b", [128, 8, 32], F32)
        rt2 = sb("rt2b", [128, 8, 32], F32)
        r_rt = Reg()
        Pb = sb("Pb", [128, S], BF16)
        r_Pb = Reg()
        PTs4 = sb("PTs4", [128, NT, 512], BF16)
        r_PTs = Reg()
        oT4 = sb("oT4", [128, 512], BF16)
        r_oT = Reg()
        sm = [sb("smx", [128, 8], F32) for j in range(8)]
        r_sm = [Reg() for j in range(8)]
        wo = sb("wo", [128, D], BF16)
        r_wo = Reg()
        all_lat = r_lat
        for h in range(MLA_H):
            P.dma("pool", wo[:], A["mla_w_out"][0][h * 128:(h + 1) * 128, :], writes=[r_wo])
            for tb in range(4):
                for k in range(3):
                    P.op("pe", lambda e, tb=tb, k=k: e.matmul(cx.ps[tb][:, :], lhsT=wuq[:, k, h * 192:h * 192 + 128],
                                                              rhs=cqnT[:, k, tb * 512:(tb + 1) * 512], start=(k == 0), stop=(k == 2)),
                         reads=[r_w] + all_lat[tb * 4:(tb + 1) * 4], writes=[cx.rps[tb]])
                P.op("act", lambda e, tb=tb: e.activation(out=qnT[:, tb * 512:(tb + 1) * 512], in_=cx.ps[tb][:, :], func=AF.Copy),
                     reads=[cx.rps[tb]], writes=[r_qnT])
            for tb in range(4):
                for k in range(2):
                    P.op("pe", lambda e, tb=tb, k=k: e.matmul(cx.ps[4 + tb][:, :], lhsT=wukv[:, k, h * 256:h * 256 + 128],
                                                              rhs=ckvnT[:, k, tb * 512:(tb + 1) * 512], start=(k == 0), stop=(k == 1)),
                         reads=[r_w] + all_lat[tb * 4:(tb + 1) * 4], writes=[cx.rps[4 + tb]])
                P.op("dve", lambda e, tb=tb: e.tensor_copy(out=knT[:, tb * 512:(tb + 1) * 512], in_=cx.ps[4 + tb][:, :]),
                     reads=[cx.rps[4 + tb]], writes=[r_knT])
            for g in range(4):
                for j in range(4):
                    i = g * 4 + j
                    for k in range(2):
                        P.op("pe", lambda e, i=i, j=j, k=k, g=g: e.matmul(cx.ps[g][:, j * 128:(j + 1) * 128], lhsT=ckvnT[:, k, i * 128:(i + 1) * 128],
                                                                          rhs=wukv[:, k, h * 256 + 128:h * 256 + 256], start=(k == 0), stop=(k == 1)),
                             reads=[r_w, all_lat[i]], writes=[cx.rps[g]])
                P.op("act", lambda e, g=g: e.activation(out=vh[:, g * 4:(g + 1) * 4, :], in_=cx.ps[g][:, :].rearrange("p (a b) -> p a b", a=4), func=AF.Copy),
                     reads=[cx.rps[g]], writes=[r_vh])
            for g in range(2):
                pb = 4 + g
                for j in range(8):
                    i = g * 8 + j
                    for k in range(3):
                        P.op("pe", lambda e, i=i, j=j, k=k, pb=pb: e.matmul(cx.ps[pb][:, j * 64:(j + 1) * 64], lhsT=cqnT[:, k, i * 128:(i + 1) * 128],
                                                                            rhs=wuq[:, k, h * 192 + 128:h * 192 + 192], start=(k == 0), stop=(k == 2)),
                             reads=[r_w, all_lat[i]], writes=[cx.rps[pb]])
                rope_apply(cx, qrb[:], cx.ps[pb][:, :].rearrange("p (a b) -> p a b", a=8), cosT[:, g * 8:(g + 1) * 8, :], sinT[:, g * 8:(g + 1) * 8, :],
                           rt1[:], rt2[:], 8, cx.rps[pb], r_tab, r_rt, r_qrb)
                pst = cx.ps[6 + g][:].bitcast(BF16)
                for j in range(8):
                    P.op("pe", lambda e, j=j: e.transpose(out=pst[0:64, j * 128:(j + 1) * 128], in_=qrb[:, j, :], identity=cx.identb[:]),
                         reads=[r_qrb, cx.rcst], writes=[cx.rps[6 + g]])
                P.op("act", lambda e, g=g: e.activation(out=qrT[:, g * 1024:(g + 1) * 1024], in_=pst[0:64, :], func=AF.Copy),
                     reads=[cx.rps[6 + g]], writes=[r_qrT])
            def emit_S(qb):
                base = 4 * (qb % 2)
                qs = slice(qb * 128, (qb + 1) * 128)
                for tb in range(4):
                    P.op("pe", lambda e, tb=tb: e.matmul(cx.ps[base + tb][:, :], lhsT=qnT[:, qs], rhs=knT[:, tb * 512:(tb + 1) * 512], start=True, stop=False),
                         reads=[r_qnT, r_knT], writes=[cx.rps[base + tb]])
                    P.op("pe", lambda e, tb=tb: e.matmul(cx.ps[base + tb][:, :], lhsT=qrT[:, qs], rhs=kropeT[:, tb * 512:(tb + 1) * 512], start=False, stop=True),
                         reads=[r_qrT, r_krT], writes=[cx.rps[base + tb]])

            def emit_softmax(qb):
                base = 4 * (qb % 2)
                m = sm[qb % 8]
                r_m = r_sm[qb % 8]
                for tb in range(4):
                    P.op("dve", lambda e, tb=tb: e.tensor_reduce(out=m[:, tb:tb + 1], in_=cx.ps[base + tb][:, :], axis=AX.X, op=ALU.max),
                         reads=[cx.rps[base + tb]], writes=[r_m])
                P.op("dve", lambda e: e.tensor_reduce(out=m[:, 0:1], in_=m[:, 0:4], axis=AX.X, op=ALU.max), reads=[r_m], writes=[r_m])
                P.op("dve", lambda e: e.tensor_scalar(out=m[:, 1:2], in0=m[:, 0:1], scalar1=-MLA_SCALE, scalar2=None, op0=ALU.mult), reads=[r_m], writes=[r_m])
                for tb in range(4):
                    P.op("act", lambda e, tb=tb: e.activation(out=Pb[:, tb * 512:(tb + 1) * 512], in_=cx.ps[base + tb][:, :], func=AF.Exp, scale=MLA_SCALE,
                                                              bias=m[:, 1:2], accum_out=m[:, 4 + tb:5 + tb]), reads=[cx.rps[base + tb], r_m], writes=[r_Pb, r_m])
                P.op("dve", lambda e: e.tensor_reduce(out=m[:, 2:3], in_=m[:, 4:8], axis=AX.X, op=ALU.add), reads=[r_m], writes=[r_m])
                P.op("dve", lambda e: e.reciprocal(out=m[:, 3:4], in_=m[:, 2:3]), reads=[r_m], writes=[r_m])

            def emit_rest(qb):
                base = 4 * (qb % 2)
                qi = qb % 4
                for g in range(2):
                    pst = cx.ps[base + g][:].bitcast(BF16)
                    for j in range(8):
                        c = g * 8 + j
                        P.op("pe", lambda e, c=c, j=j: e.transpose(out=pst[:, j * 128:(j + 1) * 128], in_=Pb[:, c * 128:(c + 1) * 128], identity=cx.identb[:]),
                             reads=[r_Pb, cx.rcst], writes=[cx.rps[base + g]])
                    P.op("act", lambda e, g=g: e.activation(out=PTs4[:, g * 8:(g + 1) * 8, qi * 128:(qi + 1) * 128], in_=pst.rearrange("p (a b) -> p a b", a=8), func=AF.Copy),
                         reads=[cx.rps[base + g]], writes=[r_PTs])
                if qi != 3:
                    return
                for c in range(NT):
                    P.op("pe", lambda e, c=c: e.matmul(cx.ps[base + 2][:, :], lhsT=vh[:, c, :], rhs=PTs4[:, c, :], start=(c == 0), stop=(c == NT - 1)),
                         reads=[r_vh, r_PTs], writes=[cx.rps[base + 2]])
                P.op("act", lambda e: e.activation(out=oT4[:], in_=cx.ps[base + 2][:, :], func=AF.Copy), reads=[cx.rps[base + 2]], writes=[r_oT])
                for q2 in range(4):
                    qq = qb - 3 + q2
                    m = sm[qq % 8]
                    r_m = r_sm[qq % 8]
                    for hh in range(2):
                        pb = base + 3 - hh
                        P.op("pe", lambda e, hh=hh, pb=pb, q2=q2: e.matmul(cx.ps[pb][:, :], lhsT=oT4[:, q2 * 128:(q2 + 1) * 128], rhs=wo[:, hh * 512:(hh + 1) * 512], start=True, stop=True),
                             reads=[r_oT, r_wo], writes=[cx.rps[pb]])
                        P.op("dve", lambda e, hh=hh, pb=pb, qq=qq, m=m: e.scalar_tensor_tensor(out=cx.R[:, qq, hh * 512:(hh + 1) * 512], in0=cx.ps[pb][:, :], scalar=m[:, 3:4],
                                                                                       in1=cx.R[:, qq, hh * 512:(hh + 1) * 512], op0=ALU.mult, op1=ALU.add),
                             reads=[cx.rps[pb], r_m, cx.rR[qq]], writes=[cx.rR[qq]])

            emit_S(0)
            for qb in range(NT):
                if qb + 1 < NT:
                    emit_S(qb + 1)
                emit_softmax(qb)
                emit_rest(qb)
    P.barrier()


LN_QSCALE = float(np.log(128.0 ** -0.5))


def emit_linattn(cx, kind):
    P = cx.P
    nc = cx.nc
    A = cx.A
    XT = xt_view(cx)
    ml = (kind == "mlstm")
    pre = "mlstm" if ml else "gla"
    W_in = A[pre + "_w_in"][0]
    DV1 = 257 if ml else 256
    cs = 1.0 if ml else 1.0 / 16.0
    with contextlib.ExitStack() as st:
        sb = lambda name, shape, dt: st.enter_context(nc.sbuf_tensor(un(name), shape, dt))
        ng = sb("ng", [128, 256], F32)
        r_ng = Reg()
        r_gw = Reg()
        if ml:
            wgates = sb("wgates", [128, NK, 16], BF16)
            P.dma("pool", wgates[:], W_in[:, 3072:3088].rearrange("(k p) c -> p k c", p=128), writes=[r_gw])
            gbias = sb("gbias", [128, 16], F32)
            P.dma("sp", gbias[:], A["mlstm_gate_b"][0].rearrange("a b c -> (a b c)").partition_broadcast(128), writes=[r_gw])
            ngbias = sb("ngbias", [128, 16], F32)
            P.op("dve", lambda e: e.tensor_scalar(out=ngbias[:], in0=gbias[:], scalar1=-1.0, scalar2=None, op0=ALU.mult), reads=[r_gw], writes=[r_gw])
            wrep = [sb("wrep", [128, NK, 128], BF16) for j in range(2)]
            r_wrep = [Reg(), Reg()]
            igb = sb("igb", [128, S], F32)
            r_igb = Reg()
        else:
            wglr = sb("wglr", [128, NK, 32], BF16)
            P.dma("pool", wglr[:], W_in[:, 3072:3104].rearrange("(k p) c -> p k c", p=128), writes=[r_gw])
            gwpad = sb("gwpad", [32, 2, 512], BF16)
            P.op("pool", lambda e: e.memset(gwpad[:], 0.0), writes=[r_gw])
            for d in range(2):
                P.dma("pool", gwpad[d * 16:(d + 1) * 16, d, :], A["gla_gate_w"][0, d], writes=[r_gw])
            ngb = sb("ngb", [128, 2, 4], F32)
            for d in range(2):
                small_load(cx, ngb[:, d, :], A["gla_gate_b"][0, d].rearrange("(h p) -> p h", p=128), r_gw)
            P.op("dve", lambda e: e.tensor_scalar(out=ngb[:], in0=ngb[:], scalar1=-1.0, scalar2=None, op0=ALU.mult), reads=[r_gw], writes=[r_gw])
            glrT = sb("glrT", [32, S], BF16)
            r_glrT = Reg()
            for tb in range(4):
                for k in range(NK):
                    P.op("pe", lambda e, tb=tb, k=k: e.matmul(cx.ps[tb][0:32, :], lhsT=wglr[:, k, :], rhs=XT[:, k, tb * 512:(tb + 1) * 512],
                                                              start=(k == 0), stop=(k == NK - 1)), reads=[r_gw] + cx.rXA[tb * 4:(tb + 1) * 4], writes=[cx.rps[tb]])
                P.op("act", lambda e, tb=tb: e.activation(out=glrT[:, tb * 512:(tb + 1) * 512], in_=cx.ps[tb][0:32, :], func=AF.Copy),
                     reads=[cx.rps[tb]], writes=[r_glrT])

        WA = sb("WA", [128, NK, 256], BF16)
        WB = sb("WB", [128, NK, 256], BF16)
        r_WA, r_WB = Reg(), Reg()
        wq = WA[:, :, 0:128]
        wk = WA[:, :, 128:256]
        wo = WA[:].rearrange("p k c -> p (k c)").rearrange("p (j f) -> p j f", j=2)
        wv = WB
        wor = WB
        r_wq, r_wk, r_wv, r_wor, r_wo = r_WA, r_WA, r_WB, r_WB, r_WA
        qT = sb("qT", [128, S], F32)
        kT = sb("kT", [128, S], F32)
        r_qT, r_kT = Reg(), Reg()
        vb = sb("vb", [128, NT, DV1], BF16)
        r_vb = Reg()
        buf1 = sb("buf1", [128, S], F32)
        buf2 = sb("buf2", [128, S], F32)
        r_b1, r_b2 = Reg(), Reg()
        qe = sb("qe", [128, S], BF16)
        ke = sb("ke", [128, S], BF16)
        kgT = sb("kgT", [128, NT, 128], BF16)
        r_qe, r_ke, r_kgT = Reg(), Reg(), Reg()
        Oacc = sb("Oacc", [128, NT, 256], F32)
        r_O = [Reg() for i in range(NT)]
        offs = sb("offs", [128, NT], F32)
        gsv = sb("gsv", [128, NT], F32)
        eg = sb("eg", [128, NT], F32)
        r_small = Reg()
        Sst = sb("Sst", [128, DV1], F32)
        Sb = sb("Sb", [128, DV1], BF16)
        r_S, r_Sb = Reg(), Reg()
        atm = [sb("atm", [128, 128], BF16) for j in range(2)]
        r_atm = [Reg(), Reg()]
        dn = [sb("dn", [128, 4], F32) for j in range(2)]
        r_dn = [Reg(), Reg()]
        hsall = sb("hsall", [128, NT, 10], F32)
        r_hsall = Reg()
        gact = [sb("gact", [128, 256], F32) for j in range(2)]
        r_gact = [Reg(), Reg()]
        xn = [sb("xn", [128, 256], F32) for j in range(2)]
        r_xn = [Reg(), Reg()]
        gbt = [sb("gbt", [128, 256], BF16) for j in range(2)]
        r_gbt = [Reg(), Reg()]
        gTt = [sb("gTt", [128, 2, 128], BF16) for j in range(2)]
        r_gTt = [Reg(), Reg()]
        P.op("pool", lambda e: e.memset(offs[:, 0:1], 0.0), writes=[r_small])
        if ml:
            P.op("pool", lambda e: e.memset(vb[:, :, 256:257], 1.0), writes=[r_vb])

        b1v = buf1[:].rearrange("p (c j) -> p c j", j=128)
        b2v = buf2[:].rearrange("p (c j) -> p c j", j=128)
        allXA = cx.rXA

        def wload(dst, reg, c0, c1):
            P.dma("pool", dst, W_in[:, c0:c1].rearrange("(k p) c -> p k c", p=128), writes=[reg])

        for h in range(4):
            wload(wq, r_wq, h * 128, (h + 1) * 128)
            wload(wk, r_wk, 512 + h * 128, 512 + (h + 1) * 128)
            wload(wv[:], r_wv, 1024 + h * 256, 1024 + (h + 1) * 256)
            P.dma("sp", ng[:], A[pre + "_norm_g"][0][h * 256:(h + 1) * 256].partition_broadcast(128), writes=[r_ng])
            for (wsrc, r_w, dst, r_dst, base) in ((wq, r_wq, qT, r_qT, 0), (wk, r_wk, kT, r_kT, 4)):
                for tb in range(4):
                    pb = base + tb
                    for k in range(NK):
                        P.op("pe", lambda e, k=k, pb=pb, tb=tb, wsrc=wsrc: e.matmul(cx.ps[pb][:, :], lhsT=wsrc[:, k, :], rhs=XT[:, k, tb * 512:(tb + 1) * 512],
                                                                                    start=(k == 0), stop=(k == NK - 1)),
                             reads=[r_w] + allXA[tb * 4:(tb + 1) * 4], writes=[cx.rps[pb]])
                    eng = "act" if base == 0 else "dve"
                    if eng == "act":
                        P.op("act", lambda e, pb=pb, tb=tb, dst=dst: e.activation(out=dst[:, tb * 512:(tb + 1) * 512], in_=cx.ps[pb][:, :], func=AF.Copy),
                             reads=[cx.rps[pb]], writes=[r_dst])
                    else:
                        P.op("dve", lambda e, pb=pb, tb=tb, dst=dst: e.tensor_copy(out=dst[:, tb * 512:(tb + 1) * 512], in_=cx.ps[pb][:, :]),
                             reads=[cx.rps[pb]], writes=[r_dst])
            for i in range(NT):
                pb = (i // 2) % 8
                off = (i % 2) * 256
                for k in range(NK):
                    P.op("pe", lambda e, k=k, pb=pb, off=off, i=i: e.matmul(cx.ps[pb][:, off:off + 256], lhsT=XT[:, k, i * 128:(i + 1) * 128], rhs=wv[:, k, :],
                                                                            start=(k == 0), stop=(k == NK - 1)), reads=[r_wv, allXA[i]], writes=[cx.rps[pb]])
                P.op("act", lambda e, pb=pb, off=off, i=i: e.activation(out=vb[:, i, 0:256], in_=cx.ps[pb][:, off:off + 256], func=AF.Copy),
                     reads=[cx.rps[pb]], writes=[r_vb])
            wload(wor[:], r_wor, 2048 + h * 256, 2048 + (h + 1) * 256)
            P.dma("pool", wo, A[pre + "_w_out"][0][h * 256:(h + 1) * 256, :].rearrange("(j p) f -> p j f", p=128), writes=[r_wo])
            for d in range(2):
                if ml:
                    jf = d * 8 + 4 + h
                    ji = d * 8 + h
                    for jj, gidx in enumerate((jf, ji)):
                        P.op("dve", lambda e, jj=jj, gidx=gidx: e.tensor_copy(out=wrep[jj][:], in_=wgates[:, :, gidx:gidx + 1].to_broadcast([128, NK, 128])),
                             reads=[r_gw], writes=[r_wrep[jj]])
                    for tb in range(4):
                        for k in range(NK):
                            P.op("pe", lambda e, k=k, tb=tb: e.matmul(cx.ps[tb][:, :], lhsT=wrep[0][:, k, :], rhs=XT[:, k, tb * 512:(tb + 1) * 512],
                                                                      start=(k == 0), stop=(k == NK - 1)), reads=[r_wrep[0]] + allXA[tb * 4:(tb + 1) * 4], writes=[cx.rps[tb]])
                        P.op("act", lambda e, tb=tb: e.activation(out=buf1[:, tb * 512:(tb + 1) * 512], in_=cx.ps[tb][:, :], func=AF.Exp, scale=-1.0,
                                                                  bias=ngbias[:, jf:jf + 1]), reads=[cx.rps[tb], r_gw], writes=[r_b1])
                    for tb in range(4):
                        for k in range(NK):
                            P.op("pe", lambda e, k=k, tb=tb: e.matmul(cx.ps[4 + tb][:, :], lhsT=wrep[1][:, k, :], rhs=XT[:, k, tb * 512:(tb + 1) * 512],
                                                                      start=(k == 0), stop=(k == NK - 1)), reads=[r_wrep[1]] + allXA[tb * 4:(tb + 1) * 4], writes=[cx.rps[4 + tb]])
                        P.op("act", lambda e, tb=tb: e.activation(out=igb[:, tb * 512:(tb + 1) * 512], in_=cx.ps[4 + tb][:, :], func=AF.Identity,
                                                                  bias=gbias[:, ji:ji + 1]), reads=[cx.rps[4 + tb], r_gw], writes=[r_igb])
                else:
                    for tb in range(4):
                        P.op("pe", lambda e, tb=tb: e.matmul(cx.ps[tb][:, :], lhsT=gwpad[:, d, h * 128:(h + 1) * 128], rhs=glrT[:, tb * 512:(tb + 1) * 512],
                                                             start=True, stop=True), reads=[r_gw, r_glrT], writes=[cx.rps[tb]])
                        P.op("act", lambda e, tb=tb: e.activation(out=buf1[:, tb * 512:(tb + 1) * 512], in_=cx.ps[tb][:, :], func=AF.Exp, scale=-1.0,
                                                                  bias=ngb[:, d, h:h + 1]), reads=[cx.rps[tb], r_gw], writes=[r_b1])
                P.op("act", lambda e: e.activation(out=buf1[:], in_=buf1[:], func=AF.Ln, bias=1.0), reads=[r_b1], writes=[r_b1])
                P.op("dve", lambda e: e.tensor_tensor_scan(out=buf2[:], data0=cx.cst[:, C_ONES:C_ONES + 1].to_broadcast([128, S]), data1=buf1[:], initial=0.0,
                                                            op0=ALU.mult, op1=ALU.add), reads=[r_b1, cx.rcst], writes=[r_b2])
                P.op("dve", lambda e: e.tensor_copy(out=offs[:, 1:NT], in_=b2v[:, 0:NT - 1, 127]), reads=[r_b2], writes=[r_small])
                P.op("dve", lambda e: e.tensor_tensor(out=b2v, in0=b2v, in1=offs[:, :, None].to_broadcast([128, NT, 128]), op=ALU.subtract),
                     reads=[r_b2, r_small], writes=[r_b2])
                P.op("dve", lambda e: e.tensor_copy(out=gsv[:], in_=b2v[:, :, 127]), reads=[r_b2], writes=[r_small])
                P.op("act", lambda e: e.activation(out=eg[:], in_=gsv[:], func=AF.Exp, scale=-cs), reads=[r_small], writes=[r_small])
                gs_bc = gsv[:, :, None].to_broadcast([128, NT, 128])
                if d == 1:
                    P.op("dve", lambda e: e.scalar_tensor_tensor(out=buf2[:], in0=buf2[:], scalar=-1.0, in1=buf1[:], op0=ALU.mult, op1=ALU.add),
                         reads=[r_b1, r_b2], writes=[r_b2])
                    P.op("dve", lambda e: e.tensor_tensor(out=b2v, in0=b2v, in1=gs_bc, op=ALU.add), reads=[r_b2, r_small], writes=[r_b2])
                P.op("dve", lambda e: e.scalar_tensor_tensor(out=b1v, in0=b2v, scalar=-1.0, in1=gs_bc, op0=ALU.mult, op1=ALU.add),
                     reads=[r_b2, r_small], writes=[r_b1])
                if ml:
                    P.op("dve", lambda e: e.scalar_tensor_tensor(out=buf1[:], in0=buf1[:], scalar=-cs, in1=igb[:], op0=ALU.mult, op1=ALU.add),
                         reads=[r_b1, r_igb], writes=[r_b1])
                    P.op("act", lambda e: e.activation(out=buf1[:], in_=buf1[:], func=AF.Exp), reads=[r_b1], writes=[r_b1])
                else:
                    P.op("act", lambda e: e.activation(out=buf1[:], in_=buf1[:], func=AF.Exp, scale=-cs), reads=[r_b1], writes=[r_b1])
                P.op("dve", lambda e: e.tensor_tensor(out=ke[:], in0=kT[:], in1=buf1[:], op=ALU.mult), reads=[r_kT, r_b1], writes=[r_ke])
                for g in range(2):
                    pst = cx.ps[6 + g][:].bitcast(BF16)
                    for j in range(8):
                        c = g * 8 + j
                        P.op("pe", lambda e, c=c, j=j: e.transpose(out=pst[:, j * 128:(j + 1) * 128], in_=ke[:, c * 128:(c + 1) * 128], identity=cx.identb[:]),
                             reads=[r_ke, cx.rcst], writes=[cx.rps[6 + g]])
                    P.op("act", lambda e, g=g: e.activation(out=kgT[:, g * 8:(g + 1) * 8, :], in_=pst.rearrange("p (a b) -> p a b", a=8), func=AF.Copy),
                         reads=[cx.rps[6 + g]], writes=[r_kgT])
                if ml:
                    P.op("dve", lambda e: e.scalar_tensor_tensor(out=buf1[:], in0=buf2[:], scalar=cs, in1=igb[:], op0=ALU.mult, op1=ALU.add),
                         reads=[r_b2, r_igb], writes=[r_b1])
                    P.op("act", lambda e: e.activation(out=buf1[:], in_=buf1[:], func=AF.Exp), reads=[r_b1], writes=[r_b1])
                else:
                    P.op("act", lambda e: e.activation(out=buf1[:], in_=buf2[:], func=AF.Exp, scale=cs), reads=[r_b2], writes=[r_b1])
                P.op("dve", lambda e: e.tensor_tensor(out=ke[:], in0=kT[:], in1=buf1[:], op=ALU.mult), reads=[r_kT, r_b1], writes=[r_ke])
                P.op("act", lambda e: e.activation(out=buf2[:], in_=buf2[:], func=AF.Exp, scale=-cs, bias=cx.cst[:, C_LNQ:C_LNQ + 1]), reads=[r_b2, cx.rcst], writes=[r_b2])
                P.op("dve", lambda e: e.tensor_tensor(out=qe[:], in0=qT[:], in1=buf2[:], op=ALU.mult), reads=[r_qT, r_b2], writes=[r_qe])
                order = list(range(NT)) if d == 0 else list(range(NT - 1, -1, -1))
                mcol = C_TRIU if d == 0 else C_TRIL
                for n_, c in enumerate(order):
                    csl = slice(c * 128, (c + 1) * 128)
                    pa = n_ % 2
                    po = 2 + (n_ % 2)
                    pss = 4 + (n_ % 2)
                    first = (n_ == 0)
                    last = (n_ == NT - 1)
                    P.op("pe", lambda e: e.matmul(cx.ps[pa][:, 0:128], lhsT=ke[:, csl], rhs=qe[:, csl], start=True, stop=True),
                         reads=[r_ke, r_qe], writes=[cx.rps[pa]])
                    am = atm[n_ % 2]
                    P.op("dve", lambda e: e.tensor_tensor(out=am[:], in0=cx.cst[:, mcol:mcol + 128], in1=cx.ps[pa][:, 0:128], op=ALU.mult),
                         reads=[cx.rps[pa], cx.rcst], writes=[r_atm[n_ % 2]])
                    P.op("pe", lambda e: e.matmul(cx.ps[po][:, 0:DV1], lhsT=am[:], rhs=vb[:, c, :], start=True, stop=first),
                         reads=[r_atm[n_ % 2], r_vb], writes=[cx.rps[po]])
                    if not first:
                        P.op("pe", lambda e: e.matmul(cx.ps[po][:, 0:DV1], lhsT=qe[:, csl], rhs=Sb[:], start=False, stop=True),
                             reads=[r_qe, r_Sb], writes=[cx.rps[po]])
                    if not last:
                        P.op("pe", lambda e: e.matmul(cx.ps[pss][:, 0:DV1], lhsT=kgT[:, c, :], rhs=vb[:, c, :], start=True, stop=True),
                             reads=[r_kgT, r_vb], writes=[cx.rps[pss]])
                        if first:
                            P.op("dve", lambda e: e.tensor_copy(out=Sst[:], in_=cx.ps[pss][:, 0:DV1]), reads=[cx.rps[pss]], writes=[r_S])
                        else:
                            P.op("dve", lambda e: e.scalar_tensor_tensor(out=Sst[:], in0=Sst[:], scalar=eg[:, c:c + 1], in1=cx.ps[pss][:, 0:DV1],
                                                                          op0=ALU.mult, op1=ALU.add), reads=[cx.rps[pss], r_S, r_small], writes=[r_S])
                        P.op("act", lambda e: e.activation(out=Sb[:], in_=Sst[:], func=AF.Copy), reads=[r_S], writes=[r_Sb])
                    oc = Oacc[:, c, :]
                    if ml:
                        dd = dn[n_ % 2]
                        r_dd = r_dn[n_ % 2]
                        P.op("act", lambda e: e.activation(out=dd[:, 0:1], in_=cx.ps[po][:, 256:257], func=AF.Abs), reads=[cx.rps[po]], writes=[r_dd])
                        P.op("dve", lambda e: e.tensor_scalar(out=dd[:, 0:1], in0=dd[:, 0:1], scalar1=1.0, scalar2=None, op0=ALU.max), reads=[r_dd], writes=[r_dd])
                        P.op("dve", lambda e: e.reciprocal(out=dd[:, 1:2], in_=dd[:, 0:1]), reads=[r_dd], writes=[r_dd])
                        if d == 0:
                            P.op("act", lambda e: e.activation(out=oc, in_=cx.ps[po][:, 0:256], func=AF.Copy, scale=dd[:, 1:2]),
                                 reads=[cx.rps[po], r_dd], writes=[r_O[c]])
                        else:
                            P.op("dve", lambda e: e.scalar_tensor_tensor(out=oc, in0=cx.ps[po][:, 0:256], scalar=dd[:, 1:2], in1=oc, op0=ALU.mult, op1=ALU.add),
                                 reads=[cx.rps[po], r_dd, r_O[c]], writes=[r_O[c]])
                    else:
                        if d == 0:
                            P.op("act", lambda e: e.activation(out=oc, in_=cx.ps[po][:, 0:256], func=AF.Copy), reads=[cx.rps[po]], writes=[r_O[c]])
                        else:
                            P.op("dve", lambda e: e.tensor_tensor(out=oc, in0=oc, in1=cx.ps[po][:, 0:256], op=ALU.add), reads=[cx.rps[po], r_O[c]], writes=[r_O[c]])
            for i in range(NT):
                P.op("dve", lambda e, i=i: e.bn_stats(out=hsall[:, i, 0:6], in_=Oacc[:, i, :]), reads=[r_O[i]], writes=[r_hsall])
            for i in range(NT):
                P.op("dve", lambda e, i=i: e.bn_aggr(out=hsall[:, i, 6:8], in_=hsall[:, i, 0:6]), reads=[r_hsall], writes=[r_hsall])
            P.op("dve", lambda e: e.tensor_scalar(out=hsall[:, :, 8], in0=hsall[:, :, 7], scalar1=LN_EPS, scalar2=None, op0=ALU.add), reads=[r_hsall], writes=[r_hsall])
            P.op("act", lambda e: e.activation(out=hsall[:, :, 8], in_=hsall[:, :, 8], func=AF.Sqrt), reads=[r_hsall], writes=[r_hsall])
            P.op("dve", lambda e: e.reciprocal(out=hsall[:, :, 9], in_=hsall[:, :, 8]), reads=[r_hsall], writes=[r_hsall])
            for i in range(NT):
                j2 = i % 2
                hs = hsall[:, i, :]
                r_hs = r_hsall
                oc = Oacc[:, i, :]
                x_ = xn[j2]
                P.op("dve", lambda e: e.tensor_scalar(out=x_[:], in0=oc, scalar1=hs[:, 6:7], scalar2=hs[:, 9:10], op0=ALU.subtract, op1=ALU.mult),
                     reads=[r_O[i], r_hs], writes=[r_xn[j2]])
                P.op("pool", lambda e: e.tensor_tensor(out=x_[:], in0=x_[:], in1=ng[:], op=ALU.mult), reads=[r_xn[j2], r_ng], writes=[r_xn[j2]])
                pg = 6 + j2
                for k in range(NK):
                    P.op("pe", lambda e, k=k: e.matmul(cx.ps[pg][:, 0:256], lhsT=XT[:, k, i * 128:(i + 1) * 128], rhs=wor[:, k, :],
                                                       start=(k == 0), stop=(k == NK - 1)), reads=[r_wor, allXA[i]], writes=[cx.rps[pg]])
                ga = gact[j2]
                P.op("act", lambda e: e.activation(out=ga[:], in_=cx.ps[pg][:, 0:256], func=(AF.Sigmoid if ml else AF.Silu)),
                     reads=[cx.rps[pg]], writes=[r_gact[j2]])
                gb_ = gbt[j2]
                P.op("dve", lambda e: e.tensor_tensor(out=gb_[:], in0=x_[:], in1=ga[:], op=ALU.mult), reads=[r_xn[j2], r_gact[j2]], writes=[r_gbt[j2]])
                pt = 4 + j2
                pst = cx.ps[pt][:].bitcast(BF16)
                for j in range(2):
                    P.op("pe", lambda e, j=j: e.transpose(out=pst[:, j * 128:(j + 1) * 128], in_=gb_[:, j * 128:(j + 1) * 128], identity=cx.identb[:]),
                         reads=[r_gbt[j2], cx.rcst], writes=[cx.rps[pt]])
                gT_ = gTt[j2]
                P.op("act", lambda e: e.activation(out=gT_[:], in_=pst[:, 0:256].rearrange("p (a b) -> p a b", a=2), func=AF.Copy),
                     reads=[cx.rps[pt]], writes=[r_gTt[j2]])
                for hh in range(2):
                    pb = (i * 2 + hh) % 4
                    for j in range(2):
                        P.op("pe", lambda e, j=j, hh=hh, pb=pb: e.matmul(cx.ps[pb][:, :], lhsT=gT_[:, j, :], rhs=wo[:, j, hh * 512:(hh + 1) * 512],
                                                                         start=(j == 0), stop=(j == 1)), reads=[r_gTt[j2], r_wo], writes=[cx.rps[pb]])
                    P.op("dve", lambda e, hh=hh, pb=pb: e.tensor_tensor(out=cx.R[:, i, hh * 512:(hh + 1) * 512], in0=cx.R[:, i, hh * 512:(hh + 1) * 512],
                                                                        in1=cx.ps[pb][:, :], op=ALU.add), reads=[cx.rps[pb], cx.rR[i]], writes=[cx.rR[i]])
    P.barrier()


def declare_inputs(nc, nseq, names_shapes):
    aps = {}
    for name, shape, dt in names_shapes:
        aps[name] = nc.dram_tensor(name, list(shape), dt, kind="ExternalInput").ap()
    return aps


WEIGHT_SPECS = [
    ("mlstm_w_in", (1, 1024, 3088)), ("mlstm_gate_b", (1, 2, 2, 4)), ("mlstm_norm_g", (1, 1024)), ("mlstm_w_out", (1, 1024, 1024)),
    ("gla_w_in", (1, 1024, 3104)), ("gla_gate_w", (1, 2, 16, 512)), ("gla_gate_b", (1, 2, 512)), ("gla_norm_g", (1, 1024)),
    ("gla_w_out", (1, 1024, 1024)),
    ("lru_w_in", (1, 1024, 2048)), ("lru_conv_w", (1, 4, 1024)), ("lru_conv_b", (1, 1024)),
    ("lru_gate_a_w", (1, 2, 4, 256, 256)), ("lru_gate_a_b", (1, 2, 1024)), ("lru_gate_x_w", (1, 2, 4, 256, 256)),
    ("lru_gate_x_b", (1, 2, 1024)), ("lru_lambda", (1, 2, 1024)), ("lru_w_out", (1, 1024, 1024)),
    ("mla_w_in", (1, 1024, 704)), ("mla_q_norm_g", (1, 384)), ("mla_kv_norm_g", (1, 256)), ("mla_w_uq", (1, 384, 1536)),
    ("mla_w_ukv", (1, 256, 2048)), ("mla_w_out", (1, 1024, 1024)),
    ("moe_router", (4, 1024, 16)), ("moe_w_gate", (4, 16, 1024, 1024)), ("moe_w_up", (4, 16, 1024, 1024)),
    ("moe_w_down", (4, 16, 1024, 1024)), ("ln_g", (4, 2, 1024)), ("ln_b", (4, 2, 1024)),
]


def build_program(nseq, stages):
    nc = bass.Bass("TRN2", target_bir_lowering=False)
    specs = [("x", (nseq, S, D), F32), ("positions", (nseq, S), I32)]
    specs += [(n, s, F32) for n, s in WEIGHT_SPECS]
    specs += [("consts", (128, C_END), F32), ("consts16", (16, 16 * 128), F32)]
    A = declare_inputs(nc, nseq, specs)
    out = nc.dram_tensor("out", [nseq, S, D], F32, kind="ExternalOutput").ap()
    xd = nc.dram_tensor("xd_scratch", [S, D], BF16, kind="ExternalOutput").ap()
    with contextlib.ExitStack() as stack:
        P = Prog(nc, stack)
        cx = setup_ctx(nc, stack, P)
        cx.A = A
        cx.xd = xd
        cx.yd = nc.dram_tensor("yd_scratch", [S, D], F32, kind="ExternalOutput").ap()
        cx.r_yd = Reg()
        cx.r_xd = [Reg() for i in range(NT)]
        load_consts(cx, A["consts"], A["consts16"])
        wr = stack.enter_context(nc.sbuf_tensor("wr", [128, NK, NE], BF16))
        rwr = Reg()
        lnscr = None
        for s in range(nseq):
            for stg in stages:
                kind = stg[0]
                if kind == "load":
                    xin = A["x"][s].rearrange("(i p) d -> p i d", p=128)
                    for i in range(NT):
                        P.dma("sp", cx.R[:, i, :], xin[:, i, :], writes=[cx.rR[i]])
                elif kind == "ln":
                    _, L, j, mode = stg
                    if mode == "xb":
                        P.dma("pool", wr[:], A["moe_router"][L].rearrange("(k p) e -> p k e", p=128), writes=[rwr])
                    emit_ln(cx, A["ln_g"][L, j], A["ln_b"][L, j], mode, wr=wr, rwr=rwr, do_ln=True,
                            out_ap=out[s].rearrange("(i p) d -> p i d", p=128), scratch=lnscr)
                elif kind == "prep":
                    _, L, mode = stg
                    if mode == "xb":
                        P.dma("pool", wr[:], A["moe_router"][L].rearrange("(k p) e -> p k e", p=128), writes=[rwr])
                    emit_ln(cx, None, None, mode, wr=wr, rwr=rwr, do_ln=False, scratch=lnscr)
                elif kind == "moe":
                    _, L = stg
                    emit_moe(cx, A["moe_w_gate"][L], A["moe_w_up"][L], A["moe_w_down"][L])
                elif kind == "mla":
                    emit_mla(cx, s)
                elif kind in ("mlstm", "gla"):
                    emit_linattn(cx, kind)
                elif kind == "lru":
                    emit_lru(cx)
                elif kind == "store":
                    oo = out[s].rearrange("(i p) d -> p i d", p=128)
                    for i in range(NT):
                        P.dma("sp", oo[:, i, :], cx.R[:, i, :], reads=[cx.rR[i]])
                else:
                    raise ValueError(kind)
        P.barrier(engines=["sp"])
        print("instructions", P.ninstr, "waits", P.nwaits)
    return nc


MIXERS = ("mlstm", "gla", "lru", "mla")


def full_stages():
    st = [("load",), ("prep", 0, "xt")]
    for L in range(DEPTH):
        st.append((MIXERS[L % 4],))
        st.append(("ln", L, 0, "xb"))
        st.append(("moe", L))
        st.append(("ln", L, 1, "xt" if L < DEPTH - 1 else "none"))
    return st


NSEQ_PER_LAUNCH = 4
_PROG_CACHE = {}


def kernel(**inputs):
    nseq = NSEQ_PER_LAUNCH
    x = np.ascontiguousarray(np.asarray(inputs["x"], dtype=np.float32))
    pos = np.ascontiguousarray(np.asarray(inputs["positions"], dtype=np.int32))
    B = x.shape[0]
    per_core = B // NCORES
    consts = make_consts()
    consts16 = make_consts16()
    weights = {n: np.ascontiguousarray(np.asarray(inputs[n], dtype=np.float32)) for n, _ in WEIGHT_SPECS}
    out = np.empty((B, S, D), np.float32)
    for l0 in range(0, per_core, nseq):
        if nseq not in _PROG_CACHE:
            _PROG_CACHE[nseq] = build_program(nseq, full_stages())
        nc = _PROG_CACHE[nseq]
        in_maps = []
        for c in range(NCORES):
            b0 = c * per_core + l0
            m = {"x": x[b0:b0 + nseq], "positions": pos[b0:b0 + nseq], "consts": consts, "consts16": consts16}
            m.update(weights)
            in_maps.append(m)
        res = run_bass_kernel_spmd(nc, in_maps, core_ids=list(range(NCORES)))
        for c in range(NCORES):
            b0 = c * per_core + l0
            out[b0:b0 + nseq] = np.asarray(res.results[c]["out"]).reshape(nseq, S, D)
    return out
```
